# Optimizing a Trainium2 kernel written in Bass

```python
import math
import jax
import jax.numpy as jnp
from jax import lax
import numpy as np

D_MODEL = 2048
BATCH = 2
SEQ = 8192
DEPTH = 4

GRID_W = 64
CTX_LEN = 256
NORM_EPS = 1e-6
D_FF = ((8 * D_MODEL // 3 + 255) // 256) * 256
MIX_A = D_MODEL // 2
M_HEADDIM = 64
M_HEADS = MIX_A // M_HEADDIM
M_GROUPS = 4
M_STATE = 128
M_CONV = 4
M_CHUNK = 128
M_XBC = MIX_A + 2 * M_GROUPS * M_STATE
M_NORM_EPS = 1e-5
MIX_B = D_MODEL // 2
R_HEADSIZE = 64
R_HEADS = MIX_B // R_HEADSIZE
R_DECAY_LORA = max(32, int(round(1.8 * D_MODEL ** 0.5 / 32)) * 32)
R_AAA_LORA = max(32, int(round(1.8 * D_MODEL ** 0.5 / 32)) * 32)
R_GATE_LORA = max(32, int(round(0.6 * D_MODEL ** 0.8 / 32)) * 32)
R_SHIFT_W = 3 * MIX_B + R_DECAY_LORA + R_AAA_LORA
R_LN_EPS = 64e-5
IN_AB = MIX_A + M_XBC + M_HEADS + R_SHIFT_W + R_GATE_LORA
C_WIDTH = D_MODEL
C_HEADS = 8
C_BLOCK = C_WIDTH // C_HEADS
C_CONV = 4
RG_C = 8.0

kernel_name = "hybrid_ssd_rwkv7_rglru_flow_trunk"


def rms_norm(x, w, eps=NORM_EPS):
    xf = x.astype(jnp.float32)
    y = xf * lax.rsqrt(jnp.mean(xf * xf, axis=-1, keepdims=True) + eps)
    return (y * w.astype(jnp.float32)).astype(x.dtype)


def modulate(x, w, shift, scale):
    return rms_norm(x, w) * (1.0 + scale) + shift


def swiglu(h, w_gate, w_up, w_down):
    return (jax.nn.silu(h @ w_gate) * (h @ w_up)) @ w_down


def causal_dwconv(x, w, b):
    k_w, ch = w.shape
    y = lax.conv_general_dilated(x, w[:, None, :].astype(x.dtype), window_strides=(1,),
                                 padding=[(k_w - 1, 0)], dimension_numbers=('NWC', 'WIO', 'NWC'),
                                 feature_group_count=ch)
    return y + b


def token_shift(f):
    return jnp.pad(f, ((0, 0), (1, 0), (0, 0)))[:, :-1]


def grid_to_col_major(t):
    bsz, n, ch = t.shape
    rows = n // GRID_W
    return t.reshape(bsz, rows, GRID_W, ch).transpose(0, 2, 1, 3).reshape(bsz, n, ch)


def grid_to_row_major(t):
    bsz, n, ch = t.shape
    rows = n // GRID_W
    return t.reshape(bsz, GRID_W, rows, ch).transpose(0, 2, 1, 3).reshape(bsz, n, ch)


def bidirectional(fn, p_fwd, p_bwd, feats_ctx, feats_lat, zero_state, with_ctx):
    flip = lambda t: jax.tree_util.tree_map(lambda a: jnp.flip(a, axis=1), t)
    add = lambda u, v: jax.tree_util.tree_map(jnp.add, u, v)
    out_cf, s_f = fn(p_fwd, feats_ctx, zero_state)
    out_cb, s_b = fn(p_bwd, flip(feats_ctx), zero_state)
    out_lf, _ = fn(p_fwd, feats_lat, s_f)
    out_lb, _ = fn(p_bwd, flip(feats_lat), s_b)
    out_ctx = add(out_cf, flip(out_cb)) if with_ctx else None
    return out_ctx, add(out_lf, flip(out_lb))


def ssd_chunked(xdt, dta, bm, cm, h0):
    bsz, seqlen, n_heads, hd = xdt.shape
    n_groups, d_state = bm.shape[2], bm.shape[3]
    hpg = n_heads // n_groups
    nc = seqlen // M_CHUNK
    X = xdt.reshape(bsz, nc, M_CHUNK, n_groups, hpg, hd)
    A = dta.astype(jnp.float32).reshape(bsz, nc, M_CHUNK, n_groups, hpg)
    Bc = bm.reshape(bsz, nc, M_CHUNK, n_groups, d_state)
    Cc = cm.reshape(bsz, nc, M_CHUNK, n_groups, d_state)
    a_cs = jnp.cumsum(A, axis=2)
    causal = jnp.tril(jnp.ones((M_CHUNK, M_CHUNK), dtype=bool))[:, :, None, None]
    seg = a_cs[:, :, :, None] - a_cs[:, :, None, :]
    decay_in = jnp.exp(jnp.where(causal, seg, -jnp.inf))
    cb = jnp.einsum('bclgn,bcsgn->bclsg', Cc, Bc)
    y_diag = jnp.einsum('bclsg,bclsgh,bcsghp->bclghp', cb, decay_in, X)
    decay_to_end = jnp.exp(a_cs[:, :, -1:] - a_cs)
    chunk_states = jnp.einsum('bclgn,bclgh,bclghp->bcghpn', Bc, decay_to_end, X)
    chunk_decay = jnp.exp(a_cs[:, :, -1])

    def step(h, inp):
        s, dec = inp
        return h * dec[..., None, None] + s, h

    h_last, h_in = lax.scan(step, h0, (jnp.moveaxis(chunk_states, 1, 0), jnp.moveaxis(chunk_decay, 1, 0)))
    h_in = jnp.moveaxis(h_in, 0, 1)
    y_off = jnp.einsum('bclgn,bcghpn,bclgh->bclghp', Cc, h_in, jnp.exp(a_cs))
    return (y_diag + y_off).reshape(bsz, seqlen, n_heads, hd), h_last


def mamba_direction(p, feats, h0):
    conv_w, conv_b, dt_bias, a_log, d_skip = p
    xbc_raw, dt_raw = feats
    bsz, seqlen, _ = xbc_raw.shape
    xbc = jax.nn.silu(causal_dwconv(xbc_raw, conv_w, conv_b))
    xs = xbc[..., :MIX_A].reshape(bsz, seqlen, M_HEADS, M_HEADDIM)
    bm = xbc[..., MIX_A:MIX_A + M_GROUPS * M_STATE].reshape(bsz, seqlen, M_GROUPS, M_STATE)
    cm = xbc[..., MIX_A + M_GROUPS * M_STATE:].reshape(bsz, seqlen, M_GROUPS, M_STATE)
    dt = jax.nn.softplus((dt_raw + dt_bias).astype(jnp.float32))
    a = -jnp.exp(a_log.astype(jnp.float32))
    y, h_last = ssd_chunked(xs * dt[..., None], dt * a, bm, cm, h0)
    y = y + xs * d_skip[:, None]
    return y.reshape(bsz, seqlen, MIX_A).astype(xbc_raw.dtype), h_last


def rwkv_direction(p, feats, s0):
    mu, w0, w2, a0, a2, k_k, k_a, r_k = p
    f = feats + (token_shift(feats) - feats) * mu
    bsz, seqlen, _ = f.shape
    r = f[..., :MIX_B]
    k = f[..., MIX_B:2 * MIX_B]
    v = f[..., 2 * MIX_B:3 * MIX_B]
    wl = f[..., 3 * MIX_B:3 * MIX_B + R_DECAY_LORA]
    al = f[..., 3 * MIX_B + R_DECAY_LORA:]
    w_log = -jax.nn.softplus(-(w0 + jnp.tanh(wl) @ w2).astype(jnp.float32)) - 0.5
    decay = jnp.exp(-jnp.exp(w_log))
    a = jax.nn.sigmoid((a0 + al @ a2).astype(jnp.float32))
    heads = lambda t: t.astype(jnp.float32).reshape(bsz, seqlen, R_HEADS, R_HEADSIZE)
    kk = heads(k * k_k)
    kk = kk / jnp.maximum(jnp.sqrt(jnp.sum(kk * kk, axis=-1, keepdims=True)), 1e-12)
    k = k * (1.0 + (a - 1.0) * k_a)
    r_h, k_h, v_h, w_h, a_h = heads(r), heads(k), heads(v), heads(decay), heads(a)
    b_h = kk * a_h

    def step(S, inp):
        r_t, w_t, k_t, v_t, kk_t, b_t = inp
        sa = jnp.einsum('bhvk,bhk->bhv', S, kk_t)
        S = S * w_t[:, :, None, :] - sa[..., None] * b_t[:, :, None, :] + v_t[..., None] * k_t[:, :, None, :]
        return S, jnp.einsum('bhvk,bhk->bhv', S, r_t)

    tm = lambda t: jnp.moveaxis(t, 1, 0)
    s_last, y = lax.scan(step, s0, (tm(r_h), tm(w_h), tm(k_h), tm(v_h), tm(kk), tm(b_h)))
    y = jnp.moveaxis(y, 0, 1)
    bonus = jnp.sum(r_h * k_h * r_k, axis=-1, keepdims=True) * v_h
    return (y, bonus), s_last


def rglru_direction(p, feats, h0):
    conv_w, conv_b, wa, ba, wx, bx, lam = p
    bsz, seqlen, _ = feats.shape
    xc = causal_dwconv(feats, conv_w, conv_b)
    xh = xc.reshape(bsz, seqlen, C_HEADS, C_BLOCK)
    gate_r = jax.nn.sigmoid((jnp.einsum('blhi,hij->blhj', xh, wa).reshape(bsz, seqlen, C_WIDTH) + ba).astype(jnp.float32))
    gate_i = jax.nn.sigmoid((jnp.einsum('blhi,hij->blhj', xh, wx).reshape(bsz, seqlen, C_WIDTH) + bx).astype(jnp.float32))
    log_a = -RG_C * gate_r * jax.nn.softplus(-lam.astype(jnp.float32))
    a = jnp.exp(log_a)
    mult = jnp.sqrt(jnp.maximum(-jnp.expm1(2.0 * log_a), 0.0))
    u = mult * gate_i * xc.astype(jnp.float32)
    u = u.at[:, 0].add(a[:, 0] * h0)

    def combine(left, right):
        return left[0] * right[0], right[0] * left[1] + right[1]

    _, h = lax.associative_scan(combine, (a, u), axis=1)
    return h.astype(feats.dtype), h[:, -1]


def grouped_rmsnorm(y, w, groups):
    bsz, seqlen, ch = y.shape
    yf = y.astype(jnp.float32).reshape(bsz, seqlen, groups, ch // groups)
    yf = yf * lax.rsqrt(jnp.mean(yf * yf, axis=-1, keepdims=True) + M_NORM_EPS)
    return (yf.reshape(bsz, seqlen, ch) * w).astype(y.dtype)


def rwkv_readout(y, bonus, gl, lnx_w, lnx_b, g2):
    bsz, seqlen = y.shape[:2]
    mu = jnp.mean(y, axis=-1, keepdims=True)
    var = jnp.mean(jnp.square(y - mu), axis=-1, keepdims=True)
    yn = ((y - mu) * lax.rsqrt(var + R_LN_EPS)).reshape(bsz, seqlen, MIX_B) * lnx_w + lnx_b
    out = yn + bonus.reshape(bsz, seqlen, MIX_B)
    gate = jax.nn.sigmoid(gl) @ g2
    return (out * gate).astype(gl.dtype)


def mixer_ssd_rwkv(h_ctx, h_lat, w_in, w_out, m_conv_w, m_conv_b, m_dt_bias, m_a_log, m_d, m_norm_w,
                   r_mu, r_w0, r_w2, r_a0, r_a2, r_kk, r_ka, r_rk, r_g2, r_lnx_w, r_lnx_b, with_ctx):
    bsz = h_lat.shape[0]
    o1 = MIX_A
    o2 = o1 + M_XBC
    o3 = o2 + M_HEADS
    o4 = o3 + R_SHIFT_W

    def project(h):
        pr = h @ w_in
        return pr[..., :o1], pr[..., o1:o2], pr[..., o2:o3], pr[..., o3:o4], pr[..., o4:]

    z_c, xbc_c, dt_c, rf_c, gl_c = project(h_ctx)
    z_l, xbc_l, dt_l, rf_l, gl_l = project(h_lat)
    m_dir = lambda d: (m_conv_w[d], m_conv_b[d], m_dt_bias[d], m_a_log[d], m_d[d])
    r_dir = lambda d: (r_mu[d], r_w0[d], r_w2[d], r_a0[d], r_a2[d], r_kk[d], r_ka[d], r_rk[d])
    m_zero = jnp.zeros((bsz, M_GROUPS, M_HEADS // M_GROUPS, M_HEADDIM, M_STATE), jnp.float32)
    r_zero = jnp.zeros((bsz, R_HEADS, R_HEADSIZE, R_HEADSIZE), jnp.float32)
    ym_c, ym_l = bidirectional(mamba_direction, m_dir(0), m_dir(1), (xbc_c, dt_c), (xbc_l, dt_l), m_zero, with_ctx)
    yr_c, yr_l = bidirectional(rwkv_direction, r_dir(0), r_dir(1), rf_c, rf_l, r_zero, with_ctx)

    def readout(ym, z, yr, gl):
        a_out = grouped_rmsnorm(ym * jax.nn.silu(z), m_norm_w, M_GROUPS)
        b_out = rwkv_readout(yr[0], yr[1], gl, r_lnx_w, r_lnx_b, r_g2)
        return jnp.concatenate([a_out, b_out], axis=-1) @ w_out

    out_lat = readout(ym_l, z_l, yr_l, gl_l)
    out_ctx = readout(ym_c, z_c, yr_c, gl_c) if with_ctx else None
    return out_ctx, out_lat


def mixer_rglru(h_ctx, h_lat, w_in, w_out, conv_w, conv_b, wa, ba, wx, bx, lam, with_ctx):
    bsz = h_lat.shape[0]
    p_lat = h_lat @ w_in
    gy_lat = jax.nn.gelu(p_lat[..., :C_WIDTH])
    xb_lat = grid_to_col_major(p_lat[..., C_WIDTH:])
    if with_ctx:
        p_ctx = h_ctx @ w_in
        gy_ctx = jax.nn.gelu(p_ctx[..., :C_WIDTH])
        xb_ctx = p_ctx[..., C_WIDTH:]
    else:
        xb_ctx = h_ctx @ w_in[:, C_WIDTH:]
    dirp = lambda d: (conv_w[d], conv_b[d], wa[d], ba[d], wx[d], bx[d], lam[d])
    zero = jnp.zeros((bsz, C_WIDTH), jnp.float32)
    hs_ctx, hs_lat = bidirectional(rglru_direction, dirp(0), dirp(1), xb_ctx, xb_lat, zero, with_ctx)
    out_lat = (grid_to_row_major(hs_lat) * gy_lat) @ w_out
    out_ctx = (hs_ctx * gy_ctx) @ w_out if with_ctx else None
    return out_ctx, out_lat


def setup_inputs(seed: int = 0) -> dict:
    key = jax.random.key(seed)
    keys = iter(jax.random.split(key, 64))

    def nrm(shape, scale):
        return scale * jax.random.normal(next(keys), shape, jnp.float32)

    def uni(shape, lo, hi):
        return jax.random.uniform(next(keys), shape, jnp.float32, lo, hi)

    ne, no = (DEPTH + 1) // 2, DEPTH // 2
    D = D_MODEL
    dt = jnp.exp(uni((ne, 2, M_HEADS), math.log(1e-3), math.log(1e-1)))
    a_target = uni((no, 2, C_WIDTH), 0.9, 0.999) ** (1.0 / RG_C)
    return {
        'x': nrm((BATCH, SEQ, D), 1.0),
        'c': nrm((BATCH, D), 1.0),
        'ctx': nrm((BATCH, CTX_LEN, D), 1.0),
        'c_ctx': nrm((D,), 1.0),
        'ada_w': nrm((DEPTH, D, 6 * D), 0.5 * D ** -0.5),
        'ada_b': nrm((DEPTH, 6 * D), 0.01),
        'norm1_w': 1.0 + nrm((DEPTH, D), 0.02),
        'norm2_w': 1.0 + nrm((DEPTH, D), 0.02),
        'ffn_w_gate': nrm((DEPTH, D, D_FF), D ** -0.5),
        'ffn_w_up': nrm((DEPTH, D, D_FF), D ** -0.5),
        'ffn_w_down': nrm((DEPTH, D_FF, D), D_FF ** -0.5),
        'final_norm_w': 1.0 + nrm((D,), 0.02),
        'ab_w_in': nrm((ne, D, IN_AB), D ** -0.5),
        'ab_w_out': nrm((ne, MIX_A + MIX_B, D), (MIX_A + MIX_B) ** -0.5),
        'm_conv_w': nrm((ne, 2, M_CONV, M_XBC), M_CONV ** -0.5),
        'm_conv_b': nrm((ne, 2, M_XBC), 0.01),
        'm_dt_bias': dt + jnp.log(-jnp.expm1(-dt)),
        'm_a_log': jnp.log(uni((ne, 2, M_HEADS), 1.0, 16.0)),
        'm_d': 1.0 + nrm((ne, 2, M_HEADS), 0.1),
        'm_norm_w': 1.0 + nrm((ne, MIX_A), 0.02),
        'r_mu': uni((ne, 2, R_SHIFT_W), 0.1, 0.9),
        'r_w0': uni((ne, 2, MIX_B), -6.0, -1.0),
        'r_w2': nrm((ne, 2, R_DECAY_LORA, MIX_B), 0.5 * R_DECAY_LORA ** -0.5),
        'r_a0': nrm((ne, 2, MIX_B), 0.1),
        'r_a2': nrm((ne, 2, R_AAA_LORA, MIX_B), 0.5 * R_AAA_LORA ** -0.5),
        'r_kk': 0.85 + nrm((ne, 2, MIX_B), 0.02),
        'r_ka': 1.0 + nrm((ne, 2, MIX_B), 0.02),
        'r_rk': nrm((ne, 2, R_HEADS, R_HEADSIZE), 0.1),
        'r_g2': nrm((ne, R_GATE_LORA, MIX_B), R_GATE_LORA ** -0.5),
        'r_lnx_w': 1.0 + nrm((ne, MIX_B), 0.02),
        'r_lnx_b': nrm((ne, MIX_B), 0.01),
        'c_w_in': nrm((no, D, 2 * C_WIDTH), D ** -0.5),
        'c_w_out': nrm((no, C_WIDTH, D), C_WIDTH ** -0.5),
        'c_conv_w': nrm((no, 2, C_CONV, C_WIDTH), C_CONV ** -0.5),
        'c_conv_b': nrm((no, 2, C_WIDTH), 0.01),
        'c_wa': nrm((no, 2, C_HEADS, C_BLOCK, C_BLOCK), C_BLOCK ** -0.5),
        'c_ba': nrm((no, 2, C_WIDTH), 0.01),
        'c_wx': nrm((no, 2, C_HEADS, C_BLOCK, C_BLOCK), C_BLOCK ** -0.5),
        'c_bx': nrm((no, 2, C_WIDTH), 0.01),
        'c_lambda': jnp.log(a_target) - jnp.log1p(-a_target),
    }


def reference(x, c, ctx, c_ctx, ada_w, ada_b, norm1_w, norm2_w, ffn_w_gate, ffn_w_up, ffn_w_down,
              final_norm_w, ab_w_in, ab_w_out, m_conv_w, m_conv_b, m_dt_bias, m_a_log, m_d, m_norm_w,
              r_mu, r_w0, r_w2, r_a0, r_a2, r_kk, r_ka, r_rk, r_g2, r_lnx_w, r_lnx_b,
              c_w_in, c_w_out, c_conv_w, c_conv_b, c_wa, c_ba, c_wx, c_bx, c_lambda):
    silu_c = jax.nn.silu(c)
    silu_cc = jax.nn.silu(c_ctx)
    for layer in range(DEPTH):
        with_ctx = layer < DEPTH - 1
        mod = silu_c @ ada_w[layer] + ada_b[layer]
        sh1, sc1, g1, sh2, sc2, g2 = [t[:, None, :] for t in jnp.split(mod, 6, axis=-1)]
        mod_c = silu_cc @ ada_w[layer] + ada_b[layer]
        csh1, csc1, cg1, csh2, csc2, cg2 = jnp.split(mod_c, 6)
        h_lat = modulate(x, norm1_w[layer], sh1, sc1)
        h_ctx = modulate(ctx, norm1_w[layer], csh1, csc1)
        if layer % 2 == 0:
            e = layer // 2
            o_ctx, o_lat = mixer_ssd_rwkv(h_ctx, h_lat, ab_w_in[e], ab_w_out[e], m_conv_w[e], m_conv_b[e],
                                          m_dt_bias[e], m_a_log[e], m_d[e], m_norm_w[e], r_mu[e], r_w0[e],
                                          r_w2[e], r_a0[e], r_a2[e], r_kk[e], r_ka[e], r_rk[e], r_g2[e],
                                          r_lnx_w[e], r_lnx_b[e], with_ctx)
        else:
            o = layer // 2
            o_ctx, o_lat = mixer_rglru(h_ctx, h_lat, c_w_in[o], c_w_out[o], c_conv_w[o], c_conv_b[o],
                                       c_wa[o], c_ba[o], c_wx[o], c_bx[o], c_lambda[o], with_ctx)
        x = x + g1 * o_lat
        x = x + g2 * swiglu(modulate(x, norm2_w[layer], sh2, sc2),
                            ffn_w_gate[layer], ffn_w_up[layer], ffn_w_down[layer])
        if with_ctx:
            ctx = ctx + cg1 * o_ctx
            ctx = ctx + cg2 * swiglu(modulate(ctx, norm2_w[layer], csh2, csc2),
                                     ffn_w_gate[layer], ffn_w_up[layer], ffn_w_down[layer])
    return rms_norm(x, final_norm_w)
```

```python
from contextlib import ExitStack
import math
import numpy as np
import concourse.bass as bass
import concourse.mybir as mybir
from concourse.bass_utils import run_bass_kernel_spmd

F32 = mybir.dt.float32
BF16 = mybir.dt.bfloat16
AF = mybir.ActivationFunctionType
ALU = mybir.AluOpType
AX = mybir.AxisListType

D = 2048
FC = 16
CTX = 256
DFF = 5632
DFFC = DFF // 8
NCORE = 8
EPS = 1e-6
GW = 64

SAME_SYNC = True
ROLL = 30000


class Buf:
    __slots__ = ("name", "wr", "rd", "dsem", "dcum")

    def __init__(self, name):
        self.name = name
        self.wr = None
        self.rd = {}
        self.dsem = None
        self.dcum = 0


class KB:
    def __init__(self, nc):
        self.nc = nc
        self.stack = ExitStack()
        self.eng = {"pe": nc.tensor, "dve": nc.vector, "act": nc.scalar, "pool": nc.gpsimd, "sp": nc.sync}
        self.sems = {}
        self.esem = {}
        self.ecnt = {}
        self.waited = {k: {} for k in self.eng}
        self.nsem = 0
        self.dmabufs = []
        self.free_dsems = []
        self.ccsem = None
        self.sem_cum = {}
        self.scopes = [[]]
        self.ninst = 0
        for e in self.eng:
            self._newesem(e)

    def newsem(self, name):
        h = self.stack.enter_context(self.nc.semaphore(name))
        self.sems[name] = h
        self.nsem += 1
        return name

    def _newesem(self, e):
        name = self.newsem(f"e_{e}_{self.nsem}")
        self.esem[e] = name
        self.ecnt[name] = 0

    def _deps(self, reads, writes):
        d = {}
        for b in reads:
            if b.wr and d.get(b.wr[0], 0) < b.wr[1]:
                d[b.wr[0]] = b.wr[1]
        for b in writes:
            if b.wr and d.get(b.wr[0], 0) < b.wr[1]:
                d[b.wr[0]] = b.wr[1]
            for sem, v in b.rd.items():
                if d.get(sem, 0) < v:
                    d[sem] = v
        return d

    def _wait(self, e, deps):
        w = self.waited[e]
        for sem, v in deps.items():
            if w.get(sem, 0) >= v:
                continue
            if sem == self.esem[e] and (e == "pe" or e == "sp" or not SAME_SYNC):
                continue
            self.eng[e].wait_ge(self.sems[sem], v)
            w[sem] = v

    def op(self, e, fn, reads=(), writes=()):
        self._wait(e, self._deps(reads, writes))
        ins = fn(self.eng[e])
        sem = self.esem[e]
        self.ecnt[sem] += 1
        v = self.ecnt[sem]
        ins.then_inc(self.sems[sem], 1)
        self.ninst += 1
        for b in writes:
            b.wr = (sem, v)
            b.rd = {}
        for b in reads:
            if b.wr != (sem, v):
                b.rd[sem] = v
        if v >= ROLL:
            self._newesem(e)
        return (sem, v)

    def _dsem(self, outbuf):
        if outbuf.dsem is not None and self.sem_cum[outbuf.dsem] >= ROLL * 16:
            outbuf.dsem = None
        if outbuf.dsem is None:
            if self.free_dsems:
                outbuf.dsem = self.free_dsems.pop()
            else:
                outbuf.dsem = self.newsem(f"d_{self.nsem}")
                self.sem_cum[outbuf.dsem] = 0

    def open_scope(self):
        self.scopes.append([])

    def close_scope(self):
        for b in self.scopes.pop():
            if b.dsem is not None:
                if self.sem_cum[b.dsem] < ROLL * 16:
                    self.free_dsems.append(b.dsem)
                b.dsem = None
            if b in self.dmabufs:
                self.dmabufs.remove(b)

    def dma(self, q, out, in_, outbuf, inbuf, **kw):
        outs = outbuf if isinstance(outbuf, (list, tuple)) else [outbuf]
        ins_ = [] if inbuf is None else (inbuf if isinstance(inbuf, (list, tuple)) else [inbuf])
        self._wait(q, self._deps(ins_, outs))
        prim = outs[0]
        self._dsem(prim)
        ins = self.eng[q].dma_start(out=out, in_=in_, **kw)
        self.sem_cum[prim.dsem] += 16
        ins.then_inc(self.sems[prim.dsem], 16)
        self.ninst += 1
        t = (prim.dsem, self.sem_cum[prim.dsem])
        for ob in outs:
            ob.wr = t
            ob.rd = {}
        for ib in ins_:
            ib.rd[t[0]] = t[1]
        if prim not in self.dmabufs:
            self.dmabufs.append(prim)
        return t

    def collective(self, kind, op, groups, in_ap, out_ap, inbuf, outbuf):
        import os
        if "nocc" in os.environ.get("DBG", ""):
            return self.dma("pool", out_ap, in_ap, outbuf, inbuf)
        e = "pool"
        self._wait(e, self._deps([inbuf], [outbuf]))
        if self.ccsem is None:
            self.ccsem = self.newsem("ccsem")
            self.sem_cum[self.ccsem] = 0
            self.ccbuf = Buf("ccbuf")
            self.dmabufs.append(self.ccbuf)
        ins = self.eng[e].collective_compute(kind, op, replica_groups=groups, ins=[in_ap], outs=[out_ap])
        self.sem_cum[self.ccsem] += 1
        ins.then_inc(self.sems[self.ccsem], 1)
        t = (self.ccsem, self.sem_cum[self.ccsem])
        outbuf.wr = t
        outbuf.rd = {}
        inbuf.rd[t[0]] = t[1]
        self.ccbuf.wr = t
        return t

    def barrier(self, engines=("pe", "dve", "act", "pool", "sp")):
        deps = {}
        for e, sem in self.esem.items():
            if self.ecnt[sem] > 0:
                deps[sem] = self.ecnt[sem]
        for b in self.dmabufs:
            if b.wr:
                deps[b.wr[0]] = max(deps.get(b.wr[0], 0), b.wr[1])
        for e in engines:
            w = self.waited[e]
            for sem, v in deps.items():
                if w.get(sem, 0) >= v or sem == self.esem[e]:
                    continue
                self.eng[e].wait_ge(self.sems[sem], v)
                w[sem] = v


class Tl:
    def __init__(self, kb, stack, name, shape, dtype=F32, psum=False):
        mk = kb.nc.psum_tensor if psum else kb.nc.sbuf_tensor
        kb.ntl = getattr(kb, "ntl", 0) + 1
        name = f"{name}_{kb.ntl}"
        self.t = stack.enter_context(mk(name, list(shape), dtype))
        self.b = Buf(name)
        kb.scopes[-1].append(self.b)

    def __getitem__(self, k):
        return self.t[k]


class Dr:
    def __init__(self, nc, name, shape, dtype=F32, kind="Internal", chunk_rows=None):
        self.t = nc.dram_tensor(name, list(shape), dtype, kind=kind)
        self.ap = self.t.ap()
        self.b = Buf(name)
        self.chunk_rows = chunk_rows
        if chunk_rows:
            self.nchunk = shape[0] // chunk_rows
            self.cb = [Buf(f"{name}_c{i}") for i in range(self.nchunk)]

    def bs(self, r0, r1):
        if not self.chunk_rows:
            return [self.b]
        return self.cb[r0 // self.chunk_rows:(r1 - 1) // self.chunk_rows + 1]

    def chunk_ap(self, k):
        return self.ap[k * self.chunk_rows:(k + 1) * self.chunk_rows, :]


def make_cfg(SEQ=8192, ltypes=("E", "O", "E", "O"), T=256, stop="", dbg=False):
    return dict(SEQ=SEQ, ltypes=tuple(ltypes), T=T, stop=stop, dbg=dbg)


def build(cfg):
    SEQ = cfg["SEQ"]
    LT = cfg["ltypes"]
    NL = len(LT)
    T = cfg["T"]
    TR = SEQ // 4
    LS = CTX + SEQ
    ROWS = SEQ // GW
    NE = sum(1 for t in LT if t == "E")
    NO = sum(1 for t in LT if t == "O")
    assert TR % T == 0 and CTX % min(T, CTX) == 0

    nc = bass.Bass("TRN2", target_bir_lowering=False)
    kb = KB(nc)
    top = kb.stack

    def din(name, shape, dt=F32):
        return Dr(nc, name, shape, dt, kind="ExternalInput")

    xs = din("xs", [D, TR])
    xc_in = din("xc", [D, 2 * CTX])
    cmine = din("cmine", [128, 2 * 3])
    adaw = din("adaw", [NL * 2 * 128, 6 * D])
    adab = din("adab", [128, NL * 96])
    nwv = din("nwv", [128, (2 * NL + 1) * FC])
    wg_d = din("wg", [NL * D, DFFC])
    wu_d = din("wu", [NL * D, DFFC])
    wd_d = din("wd", [NL * DFFC, D])
    if NO:
        wino_d = din("wino", [NO * D, 512])
        wouto_d = din("wouto", [NO * 256, D])
        gw_d = din("gw", [NO * 2 * 2 * 256, 256])
        ovec_d = din("ovec", [128, NO * 2 * 2 * 8])
    if NE:
        wine_d = din("wine", [NE * D, 1604])
        woute_d = din("woute", [NE * 384, D])
        evec_d = din("evec", [128, NE * 96])
        lora_d = din("lora", [NE * 8 * 96, 64])
        g2_d = din("g2", [NE * 256, 128])
        dvec_d = din("dvec", [64, NE * 8])
        sel4_d = din("sel4", [4, 4 * 128])
    sel_d = din("sel", [128, 12])
    yout = Dr(nc, "yout", [D, TR], F32, kind="ExternalOutput")

    xsi = Dr(nc, "xsi", [D, TR])
    CR = (1 << 20) // TR
    XT = Dr(nc, "XT", [8 * D, TR], chunk_rows=CR)
    XC = Dr(nc, "XCi", [D, 2 * CTX])
    PARTL = Dr(nc, "PARTL", [8 * D, TR], chunk_rows=CR)
    PARTC = Dr(nc, "PARTC", [D, 2 * CTX])
    REDL = Dr(nc, "REDL", [8 * D, TR], chunk_rows=CR)
    REDC = Dr(nc, "REDC", [D, 2 * CTX])
    MODP = Dr(nc, "MODP", [128, NL * 288])
    MODR = Dr(nc, "MODR", [128, NL * 288])
    NCH_MAX = 14 if NE else 4
    PT = Dr(nc, "PT", [NCH_MAX * 128, 2 * LS])
    HS = Dr(nc, "HS", [2 * 128, 2 * LS])
    HSC = Dr(nc, "HSC", [2 * 128, LS])
    if NE:
        YM = [Dr(nc, f"YM{d}", [2 * 128, LS]) for d in range(2)]
        YR = [Dr(nc, f"YR{d}", [128, 2 * LS]) for d in range(2)]
        BN = [Dr(nc, f"BN{d}", [128, 2 * LS]) for d in range(2)]
    G4 = [[0, 1, 2, 3], [4, 5, 6, 7]]
    G2 = [[0, 4], [1, 5], [2, 6], [3, 7]]

    TMPL = Dr(nc, "TMPL", [8 * D, TR], chunk_rows=CR)
    TMPC = Dr(nc, "TMPC", [D, 2 * CTX])
    MODT = Dr(nc, "MODT", [128, NL * 288])

    def allreduce(src, dst, tmp):
        if not src.chunk_rows:
            kb.collective("AllReduce", ALU.add, G4, src.ap, tmp.ap, src.b, tmp.b)
            kb.collective("AllReduce", ALU.add, G2, tmp.ap, dst.ap, tmp.b, dst.b)
            return
        for k in range(src.nchunk):
            kb.collective("AllReduce", ALU.add, G4, src.chunk_ap(k), tmp.chunk_ap(k), src.cb[k], tmp.cb[k])
        for k in range(src.nchunk):
            kb.collective("AllReduce", ALU.add, G2, tmp.chunk_ap(k), dst.chunk_ap(k), tmp.cb[k], dst.cb[k])

    def lat_view(dr):
        return dr.ap.rearrange("(r fc p) t -> p r fc t", r=8, fc=FC, p=128)

    def ctx_view(dr):
        return dr.ap.rearrange("(fc p) t -> p fc t", p=128)

    tiles = []
    for r in range(8):
        for t0 in range(0, TR, T):
            tiles.append(dict(kind="lat", r=r, t0=t0, n=T, b=r // 4, pos=CTX + (r % 4) * TR + t0, j=r // 4))
    TCX = min(T, CTX)
    for b in range(2):
        for t0 in range(0, CTX, TCX):
            tiles.append(dict(kind="ctx", b=b, c0=b * CTX + t0, n=TCX, pos=t0, j=2))

    def xb(drl, drc, tile):
        if tile["kind"] == "lat":
            return drl.bs(tile["r"] * D, (tile["r"] + 1) * D)
        return [drc.b]

    def xap(drl, drc, tile):
        if tile["kind"] == "lat":
            return lat_view(drl)[:, tile["r"], :, tile["t0"]:tile["t0"] + tile["n"]]
        return ctx_view(drc)[:, :, tile["c0"]:tile["c0"] + tile["n"]]

    def scr_ap(dr, nch, tile, c0=0):
        v = dr.ap.rearrange("(c p) (b s) -> p c b s", p=128, b=2)
        return v[:, c0:c0 + nch, tile["b"], tile["pos"]:tile["pos"] + tile["n"]]

    mod = Tl(kb, top, "mod", [128, NL * 288])
    nw = Tl(kb, top, "nw", [128, (2 * NL + 1) * FC])
    A12 = Tl(kb, top, "A12", [128, NL * 2 * FC * 3])
    ones_bf = Tl(kb, top, "ones_bf", [128, 128], BF16)
    ident = Tl(kb, top, "ident", [128, 128])
    cst = Tl(kb, top, "cst", [128, 8])
    psb = [Tl(kb, top, f"ps{i}", [128, 512], F32, psum=True) for i in range(8)]
    ps_i = [0]

    def nextps():
        p = psb[ps_i[0] % 8]
        ps_i[0] += 1
        return p

    def modcol(l, j6, fc, j):
        o = ((l * 6 + j6) * FC + fc) * 3 + j
        return mod[:, o:o + 1]

    def acol(l, which, fc, j):
        o = ((l * 2 + which) * FC + fc) * 3 + j
        return A12[:, o:o + 1]

    ev_i = [0]
    ev_dve = [False]

    def evac(out, in_, R, W):
        ev_i[0] += 1
        if ev_i[0] % 2 and not ev_dve[0]:
            kb.op("act", lambda E: E.activation(out=out, in_=in_, func=AF.Copy), reads=R, writes=W)
        else:
            kb.op("dve", lambda E: E.tensor_copy(out=out, in_=in_), reads=R, writes=W)

    wst = [Tl(kb, top, f"wst{i}", [128, D]) for i in range(2)]
    wst_i = [0]

    def load_w_bf16(dst_tile, dst_ap_fn, src_ap_fn, nk):
        for k in range(nk):
            dst = dst_ap_fn(k)
            npart, ncol = dst.shape[0], dst.shape[-1]
            stg = wst[wst_i[0] % 2]
            wst_i[0] += 1
            kb.dma("sp", stg[0:npart, 0:ncol], src_ap_fn(k), stg.b, None)
            kb.op("pool", lambda E: E.tensor_copy(out=dst, in_=stg[0:npart, 0:ncol]), reads=[stg.b], writes=[dst_tile.b])

    kb.op("dve", lambda E: E.memset(ones_bf[:], 1.0), writes=[ones_bf.b])
    kb.op("dve", lambda E: E.memset(cst[:, 0:1], EPS), writes=[cst.b])
    kb.op("dve", lambda E: E.memset(cst[:, 1:2], 1.0), writes=[cst.b])
    kb.op("dve", lambda E: E.memset(cst[:, 2:3], 0.0), writes=[cst.b])
    kb.op("pool", lambda E: E.memset(ident[:], 0.0), writes=[ident.b])
    kb.op("pool", lambda E: E.affine_select(out=ident[:], in_=ident[:], pattern=[[-1, 128]], compare_op=ALU.not_equal,
                                            fill=1.0, base=0, channel_multiplier=1), reads=[ident.b], writes=[ident.b])
    kb.dma("sp", nw[:], nwv.ap, nw.b, None)
    with ExitStack() as st:
        kb.open_scope()
        selt0 = Tl(kb, st, "selt0", [128, 12])
        kb.dma("sp", selt0[:], sel_d.ap, selt0.b, None)
        xg = [Tl(kb, st, f"xg{i}", [128, FC, T]) for i in range(2)]
        xo = [Tl(kb, st, f"xo{i}", [128, FC, T]) for i in range(2)]
        gi_ = 0
        for t0 in range(0, TR, T):
            g = xg[(t0 // T) % 2]
            kb.dma("sp", g[:], xs.ap.rearrange("(fc p) t -> p fc t", p=128)[:, :, t0:t0 + T], g.b, None)
            for r in range(8):
                o_ = xo[gi_ % 2]
                gi_ += 1
                kb.op("dve" if r % 2 else "pool", lambda E: E.tensor_scalar(out=o_[:], in0=g[:], scalar1=selt0[:, r:r + 1], scalar2=None, op0=ALU.mult),
                      reads=[g.b, selt0.b], writes=[o_.b])
                kb.dma("act", lat_view(PARTL)[:, r, :, t0:t0 + T], o_[:], PARTL.bs(r * D, (r + 1) * D), o_.b)
        allreduce(PARTL, XT, TMPL)
        kb.barrier()
        kb.close_scope()
    import os
    DBG = os.environ.get("DBG", "")
    kb.dma("sp", XC.ap, xc_in.ap, XC.b, None)

    with ExitStack() as st:
      kb.open_scope()
      if "noada" not in DBG:
          csb = Tl(kb, st, "csb", [128, 6])
          csl = Tl(kb, st, "csl", [128, 6])
          adb = Tl(kb, st, "adb", [128, NL * 96])
          modp = Tl(kb, st, "modp", [128, NL * 288])
          wbuf = [Tl(kb, st, f"adw{i}", [128, 6 * D]) for i in range(2)]
          kb.dma("sp", csb[:], cmine.ap, csb.b, None)
          kb.dma("sp", adb[:], adab.ap, adb.b, None)
          kb.op("act", lambda E: E.activation(out=csl[:], in_=csb[:], func=AF.Silu), reads=[csb.b], writes=[csl.b])
          for l in range(NL):
              ps = nextps()
              for kc in range(2):
                  wb = wbuf[kc]
                  row0 = (l * 2 + kc) * 128
                  kb.dma("sp", wb[:], adaw.ap[row0:row0 + 128, :], wb.b, None)
              for cc in range(96):
                  for kc in range(2):
                      wb = wbuf[kc]
                      kb.op("pe", lambda E: E.matmul(ps[:, cc * 3:cc * 3 + 3], lhsT=wb[:, cc * 128:(cc + 1) * 128],
                                                     rhs=csl[:, kc * 3:kc * 3 + 3], start=(kc == 0), stop=(kc == 1)),
                            reads=[wb.b, csl.b], writes=[ps.b])
              kb.op("dve", lambda E: E.scalar_tensor_tensor(
                  out=modp[:, l * 288:(l + 1) * 288].rearrange("p (c j) -> p c j", j=3),
                  in0=adb[:, l * 96:(l + 1) * 96].unsqueeze(2).to_broadcast([128, 96, 3]), scalar=1.0 / NCORE,
                  in1=ps[:, 0:288].rearrange("p (c j) -> p c j", j=3), op0=ALU.mult, op1=ALU.add),
                  reads=[adb.b, ps.b], writes=[modp.b])
          kb.dma("sp", MODP.ap, modp[:], MODP.b, modp.b)
          allreduce(MODP, MODR, MODT)
          kb.dma("sp", mod[:], MODR.ap, mod.b, MODR.b)
          for l in range(NL):
              for which, j6 in ((0, 1), (1, 4)):
                  o_m = ((l * 6 + j6) * FC) * 3
                  o_a = ((l * 2 + which) * FC) * 3
                  o_w = (l * 2 + which) * FC
                  kb.op("dve", lambda E: E.scalar_tensor_tensor(
                      out=A12[:, o_a:o_a + 48].rearrange("p (f j) -> p f j", j=3),
                      in0=mod[:, o_m:o_m + 48].rearrange("p (f j) -> p f j", j=3), scalar=1.0,
                      in1=nw[:, o_w:o_w + FC].unsqueeze(2).to_broadcast([128, FC, 3]), op0=ALU.add, op1=ALU.mult),
                      reads=[mod.b, nw.b], writes=[A12.b])
          kb.barrier()
      kb.close_scope()

    def rmsnorm_mod(xt, sq, rs, tmp, n, acols, bcols, ncols_scale=1.0):
        kb.op("act", lambda E: E.activation(out=sq[:, :, 0:n], in_=xt[:, :, 0:n], func=AF.Square), reads=[xt.b], writes=[sq.b])
        ps = nextps()
        for fc in range(FC):
            kb.op("pe", lambda E: E.matmul(ps[:, 0:n], lhsT=ones_bf[:], rhs=sq[:, fc, 0:n], start=(fc == 0), stop=(fc == FC - 1)),
                  reads=[ones_bf.b, sq.b], writes=[ps.b])
        kb.op("act", lambda E: E.activation(out=rs[:, 0:n], in_=ps[:, 0:n], func=AF.Sqrt, scale=1.0 / D, bias=cst[:, 0:1]),
              reads=[ps.b, cst.b], writes=[rs.b])
        kb.op("dve", lambda E: E.reciprocal(out=rs[:, 0:n], in_=rs[:, 0:n]), reads=[rs.b], writes=[rs.b])
        for fc in range(FC):
            tm = tmp[fc % len(tmp)]
            kb.op("dve", lambda E: E.tensor_tensor(out=tm[:, 0:n], in0=xt[:, fc, 0:n], in1=rs[:, 0:n], op=ALU.mult),
                  reads=[xt.b, rs.b], writes=[tm.b])
            kb.op("act", lambda E: E.activation(out=sq[:, fc, 0:n], in_=tm[:, 0:n], func=AF.Identity, scale=acols(fc), bias=bcols(fc)),
                  reads=[tm.b, A12.b, mod.b], writes=[sq.b])

    def resid_update(xt, rt, n, gcols):
        for fc in range(FC):
            kb.op("dve", lambda E: E.scalar_tensor_tensor(out=xt[:, fc, 0:n], in0=rt[:, fc, 0:n], scalar=gcols(fc),
                                                          in1=xt[:, fc, 0:n], op0=ALU.mult, op1=ALU.add),
                  reads=[rt.b, xt.b, mod.b], writes=[xt.b])

    def allreduce_parts():
        allreduce(PARTL, REDL, TMPL)
        allreduce(PARTC, REDC, TMPC)

    def phase_A(l, win_dr, row0, ncols, pending_g2_layer, chunks=None):
        if chunks is None:
            chunks = [(c * 128, min(128, ncols - c * 128)) for c in range((ncols + 127) // 128)]
        nch = len(chunks)
        with ExitStack() as st:
            kb.open_scope()
            W = Tl(kb, st, "Win", [128, FC, ncols], BF16)
            load_w_bf16(W, lambda k: W[:, k, :], lambda k: win_dr.ap[row0 + k * 128:row0 + (k + 1) * 128, :], FC)
            xts = [Tl(kb, st, f"xtA{i}", [128, FC, T]) for i in range(2)]
            rts = [Tl(kb, st, f"rtA{i}", [128, FC, T]) for i in range(2)]
            sqs = [Tl(kb, st, f"sqA{i}", [128, FC, T], BF16) for i in range(2)]
            rss = [Tl(kb, st, f"rsA{i}", [128, T]) for i in range(2)]
            tmp = [Tl(kb, st, f"tmA{i}", [128, T]) for i in range(4)]
            for ti, tile in enumerate(tiles):
                n = tile["n"]
                j = tile["j"]
                xt, rt, sq, rs = xts[ti % 2], rts[ti % 2], sqs[ti % 2], rss[ti % 2]
                kb.dma("sp", xt[:, :, 0:n], xap(XT, XC, tile), xt.b, xb(XT, XC, tile))
                if pending_g2_layer is not None:
                    kb.dma("sp", rt[:, :, 0:n], xap(REDL, REDC, tile), rt.b, xb(REDL, REDC, tile))
                    resid_update(xt, rt, n, lambda fc: modcol(pending_g2_layer, 5, fc, j))
                    kb.dma("act", xap(XT, XC, tile), xt[:, :, 0:n], xb(XT, XC, tile), xt.b)
                rmsnorm_mod(xt, sq, rs, tmp, n, lambda fc: acol(l, 0, fc, j), lambda fc: modcol(l, 0, fc, j))
                for c in range(nch):
                    cc0, cw = chunks[c]
                    ps = nextps()
                    for kc in range(FC):
                        kb.op("pe", lambda E: E.matmul(ps[0:cw, 0:n], lhsT=W[:, kc, cc0:cc0 + cw], rhs=sq[:, kc, 0:n],
                                                       start=(kc == 0), stop=(kc == FC - 1)), reads=[W.b, sq.b], writes=[ps.b])
                    evac(rt[0:cw, c, 0:n], ps[0:cw, 0:n], [ps.b], [rt.b])
                c = 0
                while c < nch:
                    if chunks[c][1] == 128:
                        c1 = c
                        while c1 < nch and chunks[c1][1] == 128:
                            c1 += 1
                        kb.dma("act", scr_ap(PT, c1 - c, tile, c), rt[:, c:c1, 0:n], PT.b, rt.b)
                        c = c1
                    else:
                        cw = chunks[c][1]
                        kb.dma("act", scr_ap(PT, 1, tile, c)[0:cw], rt[0:cw, c:c + 1, 0:n], PT.b, rt.b)
                        c += 1
            kb.barrier()
            kb.close_scope()

    def phase_C_odd(l, o):
        with ExitStack() as st:
            kb.open_scope()
            W = Tl(kb, st, "Wout", [128, 2, D], BF16)
            load_w_bf16(W, lambda k: W[:, k, :], lambda k: wouto_d.ap[o * 256 + k * 128:o * 256 + (k + 1) * 128, :], 2)
            hss = [Tl(kb, st, f"hsC{i}", [128, 2, T]) for i in range(2)]
            gys = [Tl(kb, st, f"gyC{i}", [128, 2, T]) for i in range(2)]
            t1 = Tl(kb, st, "t1C", [128, 2, T])
            mbf = [Tl(kb, st, f"mC{i}", [128, 2, T], BF16) for i in range(2)]
            stg = [Tl(kb, st, f"stC{i}", [128, FC, T]) for i in range(2)]
            for ti, tile in enumerate(tiles):
                n = tile["n"]
                hs, gy, m, sg = hss[ti % 2], gys[ti % 2], mbf[ti % 2], stg[ti % 2]
                kb.dma("sp", hs[:, :, 0:n], scr_ap(HS, 2, tile), hs.b, HS.b)
                kb.dma("sp", gy[:, :, 0:n], scr_ap(PT, 2, tile, 0), gy.b, PT.b)
                kb.op("dve", lambda E: E.tensor_tensor(out=t1[:, :, 0:n], in0=gy[:, :, 0:n], in1=gy[:, :, 0:n], op=ALU.mult), reads=[gy.b], writes=[t1.b])
                kb.op("dve", lambda E: E.tensor_scalar(out=t1[:, :, 0:n], in0=t1[:, :, 0:n], scalar1=0.044715, scalar2=1.0, op0=ALU.mult, op1=ALU.add),
                      reads=[t1.b], writes=[t1.b])
                kb.op("dve", lambda E: E.tensor_tensor(out=t1[:, :, 0:n], in0=t1[:, :, 0:n], in1=gy[:, :, 0:n], op=ALU.mult), reads=[t1.b, gy.b], writes=[t1.b])
                kb.op("act", lambda E: E.activation(out=t1[:, :, 0:n], in_=t1[:, :, 0:n], func=AF.Sigmoid, scale=2.0 * math.sqrt(2.0 / math.pi)),
                      reads=[t1.b], writes=[t1.b])
                kb.op("dve", lambda E: E.tensor_tensor(out=t1[:, :, 0:n], in0=t1[:, :, 0:n], in1=gy[:, :, 0:n], op=ALU.mult), reads=[t1.b, gy.b], writes=[t1.b])
                kb.op("dve", lambda E: E.tensor_tensor(out=m[:, :, 0:n], in0=t1[:, :, 0:n], in1=hs[:, :, 0:n], op=ALU.mult), reads=[t1.b, hs.b], writes=[m.b])
                out_proj(W, 2, [128, 128], m, sg, n, tile)
            kb.barrier()
            kb.close_scope()

    def out_proj(W, nk, ksz, m, sg, n, tile):
        for fo in range(FC):
            ps = nextps()
            for k in range(nk):
                kb.op("pe", lambda E: E.matmul(ps[:, 0:n], lhsT=W[0:ksz[k], k, fo * 128:(fo + 1) * 128], rhs=m[0:ksz[k], k, 0:n],
                                               start=(k == 0), stop=(k == nk - 1)), reads=[W.b, m.b], writes=[ps.b])
            evac(sg[:, fo, 0:n], ps[:, 0:n], [ps.b], [sg.b])
        lat = tile["kind"] == "lat"
        kb.dma("act", xap(PARTL, PARTC, tile), sg[:, :, 0:n], xb(PARTL, PARTC, tile), sg.b)

    def phase_D(l):
        csz = [128, 128, 128, 128, 128, 64]
        with ExitStack() as st:
            kb.open_scope()
            Wg = Tl(kb, st, "Wg", [128, FC, DFFC], BF16)
            Wu = Tl(kb, st, "Wu", [128, FC, DFFC], BF16)
            Wd = Tl(kb, st, "Wd", [128, 6, D], BF16)
            load_w_bf16(Wg, lambda k: Wg[:, k, :], lambda k: wg_d.ap[l * D + k * 128:l * D + (k + 1) * 128, :], FC)
            load_w_bf16(Wu, lambda k: Wu[:, k, :], lambda k: wu_d.ap[l * D + k * 128:l * D + (k + 1) * 128, :], FC)
            load_w_bf16(Wd, lambda k: Wd[0:csz[k], k, :], lambda k: wd_d.ap[l * DFFC + k * 128:l * DFFC + k * 128 + csz[k], :], 6)
            xts = [Tl(kb, st, f"xtD{i}", [128, FC, T]) for i in range(2)]
            rts = [Tl(kb, st, f"rtD{i}", [128, FC, T]) for i in range(2)]
            sqs = [Tl(kb, st, f"sqD{i}", [128, FC, T], BF16) for i in range(2)]
            rss = [Tl(kb, st, f"rsD{i}", [128, T]) for i in range(2)]
            tmp = [Tl(kb, st, f"tmD{i}", [128, T]) for i in range(4)]
            acts = [Tl(kb, st, f"acD{i}", [128, 6, T], BF16) for i in range(2)]
            sgl = [Tl(kb, st, f"sgD{i}", [128, T]) for i in range(2)]
            for ti, tile in enumerate(tiles):
                n = tile["n"]
                j = tile["j"]
                lat = tile["kind"] == "lat"
                xt, rt, sq, rs, ac = xts[ti % 2], rts[ti % 2], sqs[ti % 2], rss[ti % 2], acts[ti % 2]
                kb.dma("sp", xt[:, :, 0:n], xap(XT, XC, tile), xt.b, xb(XT, XC, tile))
                kb.dma("sp", rt[:, :, 0:n], xap(REDL, REDC, tile), rt.b, xb(REDL, REDC, tile))
                resid_update(xt, rt, n, lambda fc: modcol(l, 2, fc, j))
                kb.dma("act", xap(XT, XC, tile), xt[:, :, 0:n], xb(XT, XC, tile), xt.b)
                rmsnorm_mod(xt, sq, rs, tmp, n, lambda fc: acol(l, 1, fc, j), lambda fc: modcol(l, 3, fc, j))
                for c in range(6):
                    cw = csz[c]
                    pg, pu = nextps(), nextps()
                    for kc in range(FC):
                        kb.op("pe", lambda E: E.matmul(pg[0:cw, 0:n], lhsT=Wg[:, kc, c * 128:c * 128 + cw], rhs=sq[:, kc, 0:n],
                                                       start=(kc == 0), stop=(kc == FC - 1)), reads=[Wg.b, sq.b], writes=[pg.b])
                    for kc in range(FC):
                        kb.op("pe", lambda E: E.matmul(pu[0:cw, 0:n], lhsT=Wu[:, kc, c * 128:c * 128 + cw], rhs=sq[:, kc, 0:n],
                                                       start=(kc == 0), stop=(kc == FC - 1)), reads=[Wu.b, sq.b], writes=[pu.b])
                    s = sgl[c % 2]
                    kb.op("act", lambda E: E.activation(out=s[0:cw, 0:n], in_=pg[0:cw, 0:n], func=AF.Silu), reads=[pg.b], writes=[s.b])
                    kb.op("dve", lambda E: E.tensor_tensor(out=ac[0:cw, c, 0:n], in0=s[0:cw, 0:n], in1=pu[0:cw, 0:n], op=ALU.mult),
                          reads=[s.b, pu.b], writes=[ac.b])
                out_proj(Wd, 6, csz, ac, rt, n, tile)
            kb.barrier()
            kb.close_scope()

    def mixer_rglru(o):
        BL = 512
        with ExitStack() as st:
            kb.open_scope()
            GWt = Tl(kb, st, "GWt", [128, 2 * 2 * 2, 256], BF16)
            load_w_bf16(GWt, lambda k: GWt[:, k, :], lambda k: gw_d.ap[(o * 8 + k) * 128:(o * 8 + k + 1) * 128, :], 8)
            ov = Tl(kb, st, "ov", [128, 2 * 2 * 8])
            spn = Tl(kb, st, "spn", [128, 4])
            kb.dma("sp", ov[:], ovec_d.ap[:, o * 32:(o + 1) * 32], ov.b, None)
            for dc in range(4):
                kb.op("act", lambda E: E.activation(out=spn[:, dc:dc + 1], in_=ov[:, dc * 8 + 7:dc * 8 + 8], func=AF.Exp, scale=-1.0),
                      reads=[ov.b], writes=[spn.b])
                kb.op("act", lambda E: E.activation(out=spn[:, dc:dc + 1], in_=spn[:, dc:dc + 1], func=AF.Ln, scale=1.0, bias=cst[:, 1:2]),
                      reads=[spn.b, cst.b], writes=[spn.b])
            kb.op("dve", lambda E: E.tensor_scalar(out=spn[:], in0=spn[:], scalar1=-8.0, scalar2=None, op0=ALU.mult), reads=[spn.b], writes=[spn.b])
            XB = Tl(kb, st, "XB", [128, 2, LS])
            hfw = Tl(kb, st, "hfw", [128, 2, BL])
            RM = Tl(kb, st, "RM", [128, SEQ])
            xcf = Tl(kb, st, "xcf", [128, 2, BL])
            xcb = Tl(kb, st, "xcb", [128, 2, BL], BF16)
            gr = Tl(kb, st, "gr", [128, 2, BL])
            gi = Tl(kb, st, "gi", [128, 2, BL])
            aa = Tl(kb, st, "aa", [128, 2, BL])
            uu = Tl(kb, st, "uu", [128, 2, BL])
            hh = [Tl(kb, st, f"hh{i}", [128, 2, BL]) for i in range(2)]
            zero = cst[:, 2:3]
            pt_v = PT.ap.rearrange("(c p) (b s) -> p c b s", p=128, b=2)
            hs_v = HS.ap.rearrange("(c p) (b s) -> p c b s", p=128, b=2)
            for b in range(2):
                for ci in range(2):
                    kb.dma("sp", XB[:, ci, 0:CTX], pt_v[:, 2 + ci, b, 0:CTX], XB.b, PT.b)
                    kb.dma("sp", RM[:], pt_v[:, 2 + ci, b, CTX:LS], RM.b, PT.b)
                    kb.op("pool", lambda E: E.tensor_copy(out=XB[:, ci, CTX:LS].rearrange("p (c r) -> p c r", c=GW),
                                                          in_=RM[:].rearrange("p (r c) -> p c r", c=GW)), reads=[RM.b], writes=[XB.b])
                for d in range(2):
                    segs = [(0, CTX), (CTX, LS)]
                    hprev = None
                    blk_i = 0
                    for (S0, S1) in segs:
                        starts = list(range(S0, S1, BL))
                        if d == 1:
                            starts = starts[::-1]
                        for s0 in starts:
                            s1 = min(s0 + BL, S1)
                            n = s1 - s0
                            for ci in range(2):
                                vo = (d * 2 + ci) * 8
                                kb.op("act", lambda E: E.activation(out=xcf[:, ci, 0:n], in_=XB[:, ci, s0:s1], func=AF.Identity,
                                                                    scale=ov[:, vo + 3:vo + 4], bias=ov[:, vo + 4:vo + 5]),
                                      reads=[XB.b, ov.b], writes=[xcf.b])
                                for k in range(1, 4):
                                    if d == 0:
                                        lo = max(s0, S0 + k)
                                        if lo >= s1:
                                            continue
                                        kb.op("dve", lambda E: E.scalar_tensor_tensor(
                                            out=xcf[:, ci, lo - s0:n], in0=XB[:, ci, lo - k:s1 - k], scalar=ov[:, vo + 3 - k:vo + 4 - k],
                                            in1=xcf[:, ci, lo - s0:n], op0=ALU.mult, op1=ALU.add), reads=[XB.b, ov.b, xcf.b], writes=[xcf.b])
                                    else:
                                        hi = min(s1, S1 - k)
                                        if hi <= s0:
                                            continue
                                        kb.op("dve", lambda E: E.scalar_tensor_tensor(
                                            out=xcf[:, ci, 0:hi - s0], in0=XB[:, ci, s0 + k:hi + k], scalar=ov[:, vo + 3 - k:vo + 4 - k],
                                            in1=xcf[:, ci, 0:hi - s0], op0=ALU.mult, op1=ALU.add), reads=[XB.b, ov.b, xcf.b], writes=[xcf.b])
                            kb.op("pool", lambda E: E.tensor_copy(out=xcb[:, :, 0:n], in_=xcf[:, :, 0:n]), reads=[xcf.b], writes=[xcb.b])
                            for jc in range(2):
                                vo = (d * 2 + jc) * 8
                                for g, dst in ((0, gr), (1, gi)):
                                    ps = nextps()
                                    for ic in range(2):
                                        kb.op("pe", lambda E: E.matmul(ps[:, 0:n], lhsT=GWt[:, (d * 2 + g) * 2 + ic, jc * 128:(jc + 1) * 128],
                                                                       rhs=xcb[:, ic, 0:n], start=(ic == 0), stop=(ic == 1)),
                                              reads=[GWt.b, xcb.b], writes=[ps.b])
                                    kb.op("act", lambda E: E.activation(out=dst[:, jc, 0:n], in_=ps[:, 0:n], func=AF.Sigmoid,
                                                                        bias=ov[:, vo + 5 + g:vo + 6 + g]), reads=[ps.b, ov.b], writes=[dst.b])
                                kb.op("act", lambda E: E.activation(out=aa[:, jc, 0:n], in_=gr[:, jc, 0:n], func=AF.Exp, scale=spn[:, d * 2 + jc:d * 2 + jc + 1]),
                                      reads=[gr.b, spn.b], writes=[aa.b])
                            kb.op("dve", lambda E: E.tensor_tensor(out=uu[:, :, 0:n], in0=aa[:, :, 0:n], in1=aa[:, :, 0:n], op=ALU.mult), reads=[aa.b], writes=[uu.b])
                            kb.op("dve", lambda E: E.tensor_scalar(out=uu[:, :, 0:n], in0=uu[:, :, 0:n], scalar1=-1.0, scalar2=1.0, op0=ALU.mult, op1=ALU.add),
                                  reads=[uu.b], writes=[uu.b])
                            kb.op("dve", lambda E: E.tensor_scalar(out=uu[:, :, 0:n], in0=uu[:, :, 0:n], scalar1=1e-30, scalar2=None, op0=ALU.max),
                                  reads=[uu.b], writes=[uu.b])
                            kb.op("act", lambda E: E.activation(out=uu[:, :, 0:n], in_=uu[:, :, 0:n], func=AF.Sqrt), reads=[uu.b], writes=[uu.b])
                            kb.op("dve", lambda E: E.tensor_tensor(out=uu[:, :, 0:n], in0=uu[:, :, 0:n], in1=gi[:, :, 0:n], op=ALU.mult), reads=[uu.b, gi.b], writes=[uu.b])
                            kb.op("dve", lambda E: E.tensor_tensor(out=uu[:, :, 0:n], in0=uu[:, :, 0:n], in1=xcf[:, :, 0:n], op=ALU.mult), reads=[uu.b, xcf.b], writes=[uu.b])
                            h = hh[blk_i % 2]
                            for ci in range(2):
                                if hprev is None:
                                    init = zero
                                else:
                                    hp, pn = hprev
                                    init = hp[:, ci, pn - 1:pn] if d == 0 else hp[:, ci, 0:1]
                                if d == 0:
                                    kb.op("dve", lambda E: E.tensor_tensor_scan(out=h[:, ci, 0:n], data0=aa[:, ci, 0:n], data1=uu[:, ci, 0:n],
                                                                                initial=init, op0=ALU.mult, op1=ALU.add),
                                          reads=[aa.b, uu.b, cst.b] + ([hprev[0].b] if hprev else []), writes=[h.b])
                                else:
                                    kb.op("dve", lambda E: E.tensor_tensor_scan(out=h[:, ci, 0:n][:, ::-1], data0=aa[:, ci, 0:n][:, ::-1],
                                                                                data1=uu[:, ci, 0:n][:, ::-1], initial=init, op0=ALU.mult, op1=ALU.add),
                                          reads=[aa.b, uu.b, cst.b] + ([hprev[0].b] if hprev else []), writes=[h.b])
                            hsc_v = HSC.ap.rearrange("(c p) s -> p c s", p=128)
                            if d == 0:
                                kb.dma("act", hsc_v[:, :, s0:s1], h[:, :, 0:n], HSC.b, h.b)
                            else:
                                kb.dma("sp", hfw[:, :, 0:n], hsc_v[:, :, s0:s1], hfw.b, HSC.b)
                                kb.op("pool", lambda E: E.tensor_tensor(out=hfw[:, :, 0:n], in0=hfw[:, :, 0:n], in1=h[:, :, 0:n], op=ALU.add),
                                      reads=[h.b, hfw.b], writes=[hfw.b])
                                kb.dma("act", hsc_v[:, :, s0:s1], hfw[:, :, 0:n], HSC.b, hfw.b)
                            hprev = (h, n)
                            blk_i += 1
                for ci in range(2):
                    kb.dma("sp", RM[:, 0:CTX], hsc_v[:, ci, 0:CTX], RM.b, HSC.b)
                    kb.dma("act", hs_v[:, ci, b, 0:CTX], RM[:, 0:CTX], HS.b, RM.b)
                    kb.dma("sp", RM[:], hsc_v[:, ci, CTX:LS], RM.b, HSC.b)
                    kb.op("pool", lambda E: E.tensor_copy(out=XB[:, 0, 0:SEQ].rearrange("p (r c) -> p c r", c=GW),
                                                          in_=RM[:].rearrange("p (c r) -> p c r", c=GW)), reads=[RM.b], writes=[XB.b])
                    kb.dma("act", hs_v[:, ci, b, CTX:LS], XB[:, 0, 0:SEQ], HS.b, XB.b)
            kb.barrier()
            kb.close_scope()

    E05 = math.exp(-0.5)
    pt_v = PT.ap.rearrange("(c p) (b s) -> p c b s", p=128, b=2)

    def scan_blocks(nb):
        out = []
        for (S0, S1) in ((0, CTX), (CTX, LS)):
            for q0 in range(0, S1 - S0, nb):
                out.append((S0, S1, q0, min(nb, S1 - S0 - q0)))
        return out

    def load_scan(dst, tmp, rows_ap_fn, d, S0, S1, q0, n, halo, npart):
        if d == 0:
            lo = S0 + q0
            h = min(halo, q0)
            if h < halo:
                kb.op("pool", lambda E: E.memset(dst[0:npart, :, 0:halo - h], 0.0), writes=[dst.b])
            kb.dma("sp", dst[0:npart, :, halo - h:halo + n], rows_ap_fn(lo - h, lo + n), dst.b, PT.b)
        else:
            hi = S1 - q0
            h = min(halo, q0)
            if h < halo:
                kb.op("pool", lambda E: E.memset(tmp[0:npart, :, n + h:n + halo], 0.0), writes=[tmp.b])
            kb.dma("sp", tmp[0:npart, :, 0:n + h], rows_ap_fn(hi - n, hi + h), tmp.b, PT.b)
            kb.op("dve", lambda E: E.tensor_copy(out=dst[0:npart, :, 0:n + halo], in_=tmp[0:npart, :, 0:n + halo][:, :, ::-1]),
                  reads=[tmp.b], writes=[dst.b])

    def store_scan(dr, rows_ap_fn, src, tmp, d, S0, S1, q0, n, npart):
        if d == 0:
            kb.dma("act", rows_ap_fn(S0 + q0, S0 + q0 + n), src[0:npart, 0:n], dr.b, src.b)
        else:
            kb.op("dve", lambda E: E.tensor_copy(out=tmp[0:npart, 0:n], in_=src[0:npart, 0:n][:, ::-1]), reads=[src.b], writes=[tmp.b])
            kb.dma("act", rows_ap_fn(S1 - q0 - n, S1 - q0), tmp[0:npart, 0:n], dr.b, tmp.b)

    def mixer_rwkv(e):
        C = 64
        NBK = 512
        with ExitStack() as st:
            kb.open_scope()
            ev = Tl(kb, st, "ev", [128, 96])
            kb.dma("sp", ev[:], evec_d.ap[:, e * 96:(e + 1) * 96], ev.b, None)
            lwt = Tl(kb, st, "lwt", [96, 8, 64])
            for i in range(8):
                kb.dma("sp", lwt[:, i, :], lora_d.ap[(e * 8 + i) * 96:(e * 8 + i + 1) * 96, :], lwt.b, None)
            omk = Tl(kb, st, "omk", [64, 4])
            for i in range(4):
                kb.op("dve", lambda E: E.tensor_scalar(out=omk[:, i:i + 1], in0=ev[0:64, 40 + i * 8 + 6:40 + i * 8 + 7], scalar1=-1.0, scalar2=1.0,
                                                       op0=ALU.mult, op1=ALU.add), reads=[ev.b], writes=[omk.b])
            MK = Tl(kb, st, "MK", [64, 128])
            ML = Tl(kb, st, "ML", [64, 64])
            on64 = Tl(kb, st, "on64", [64, NBK])
            kb.op("pool", lambda E: E.memset(on64[:], 1.0), writes=[on64.b])
            kb.op("pool", lambda E: E.affine_select(out=MK[:, 0:64], in_=on64[:, 0:64], pattern=[[1, 64]], compare_op=ALU.is_gt, fill=0.0, base=0,
                                                    channel_multiplier=-1), reads=[on64.b], writes=[MK.b])
            kb.op("pool", lambda E: E.affine_select(out=MK[:, 64:128], in_=on64[:, 0:64], pattern=[[1, 64]], compare_op=ALU.is_ge, fill=0.0, base=0,
                                                    channel_multiplier=-1), reads=[on64.b], writes=[MK.b])
            kb.op("pool", lambda E: E.affine_select(out=ML[:], in_=on64[:, 0:64], pattern=[[-1, 64]], compare_op=ALU.is_gt, fill=0.0, base=0,
                                                    channel_multiplier=1), reads=[on64.b], writes=[ML.b])
            W = []
            for hh in range(2):
                w = {}
                for nm, shp in (("F", [64, 3, NBK + 1]), ("G", [64, 3, NBK + 1]), ("FL", [96, 2, NBK + 1]), ("GL", [96, 2, NBK + 1]),
                                ("f", [64, 3, NBK]), ("fl", [96, 2, NBK]), ("t1", [64, 3, NBK]), ("tl", [96, 2, NBK]),
                                ("lgw", [64, NBK]), ("a", [64, NBK]), ("kap", [64, NBK]), ("kp", [64, NBK]), ("bet", [64, NBK]), ("cs", [64, NBK]),
                                ("tA", [64, NBK]), ("tB", [64, NBK]), ("OB", [64, NBK]), ("OT", [64, NBK]),
                                ("lw", [64, C]), ("lm", [64, C]), ("g", [64, C]), ("gi", [64, C]), ("gm", [64, C]),
                                ("KR", [64, 2 * C]), ("kt", [64, C]), ("bt", [64, C]), ("AA0", [64, 128]), ("AA1", [64, 128]), ("BbT", [64, C]),
                                ("PBm", [64, 128]), ("X", [64, 128]), ("TM", [64, 192]), ("W2n", [64, 64]), ("RhT", [64, C]), ("MTn", [64, 64]),
                                ("Ha", [64, 64]), ("Hb", [64, 64]), ("ht", [64, 64])):
                    w[nm] = Tl(kb, st, f"{nm}{hh}", shp)
                W.append(w)
            zero = cst[0:64, 2:3]

            def mm(ps_ap, lhsT, rhs, R, Wb, start=True, stop=True):
                kb.op("pe", lambda E: E.matmul(ps_ap, lhsT=lhsT, rhs=rhs, start=start, stop=stop), reads=R, writes=Wb)

            def dve_tt(out, in0, in1, op, R, Wb):
                kb.op("dve", lambda E: E.tensor_tensor(out=out, in0=in0, in1=in1, op=op), reads=R, writes=Wb)

            for b in range(2):
                for d in range(2):
                    for hh in range(2):
                        kb.op("pool", lambda E: E.memset(W[hh]["Ha"][:], 0.0), writes=[W[hh]["Ha"].b])
                    Hcur = [W[0]["Ha"], W[1]["Ha"]]
                    Hnxt = [W[0]["Hb"], W[1]["Hb"]]
                    for (S0, S1, q0, n) in scan_blocks(NBK):
                        for hh in range(2):
                            w = W[hh]
                            vo = 40 + (d * 2 + hh) * 8
                            col = lambda j: ev[0:64, vo + j:vo + j + 1]
                            load_scan(w["F"], w["G"], lambda lo, hi: pt_v[hh * 64:hh * 64 + 64, 0:3, b, lo:hi], d, S0, S1, q0, n, 1, 64)
                            load_scan(w["FL"], w["GL"], lambda lo, hi: pt_v[0:96, 3:5, b, lo:hi], d, S0, S1, q0, n, 1, 96)
                            F, FL, f, fl, t1, tl = w["F"], w["FL"], w["f"], w["fl"], w["t1"], w["tl"]
                            dve_tt(t1[:, :, 0:n], F[:, :, 0:n], F[:, :, 1:n + 1], ALU.subtract, [F.b], [t1.b])
                            for q in range(3):
                                kb.op("dve", lambda E: E.scalar_tensor_tensor(out=f[:, q, 0:n], in0=t1[:, q, 0:n], scalar=col(q), in1=F[:, q, 1:n + 1],
                                                                              op0=ALU.mult, op1=ALU.add), reads=[t1.b, F.b, ev.b], writes=[f.b])
                            dve_tt(tl[:, :, 0:n], FL[:, :, 0:n], FL[:, :, 1:n + 1], ALU.subtract, [FL.b], [tl.b])
                            for q in range(2):
                                kb.op("dve", lambda E: E.scalar_tensor_tensor(out=fl[:, q, 0:n], in0=tl[:, q, 0:n], scalar=ev[0:96, 72 + d * 2 + q:73 + d * 2 + q],
                                                                              in1=FL[:, q, 1:n + 1], op0=ALU.mult, op1=ALU.add), reads=[tl.b, FL.b, ev.b], writes=[fl.b])
                            kb.op("act", lambda E: E.activation(out=tl[:, 0, 0:n], in_=fl[:, 0, 0:n], func=AF.Tanh), reads=[fl.b], writes=[tl.b])
                            ps = nextps()
                            mm(ps[0:64, 0:n], lwt[:, (d * 2 + hh) * 2 + 0, :], tl[:, 0, 0:n], [lwt.b, tl.b], [ps.b])
                            kb.op("act", lambda E: E.activation(out=w["lgw"][:, 0:n], in_=ps[0:64, 0:n], func=AF.Sigmoid, bias=col(3)), reads=[ps.b, ev.b], writes=[w["lgw"].b])
                            ps = nextps()
                            mm(ps[0:64, 0:n], lwt[:, (d * 2 + hh) * 2 + 1, :], fl[:, 1, 0:n], [lwt.b, fl.b], [ps.b])
                            kb.op("act", lambda E: E.activation(out=w["a"][:, 0:n], in_=ps[0:64, 0:n], func=AF.Sigmoid, bias=col(4)), reads=[ps.b, ev.b], writes=[w["a"].b])
                            kb.op("act", lambda E: E.activation(out=w["tA"][:, 0:n], in_=f[:, 1, 0:n], func=AF.Square, scale=col(5)), reads=[f.b, ev.b], writes=[w["tA"].b])
                            ps = nextps()
                            mm(ps[0:64, 0:n], on64[:, 0:64], w["tA"][:, 0:n], [on64.b, w["tA"].b], [ps.b])
                            kb.op("act", lambda E: E.activation(out=w["tB"][:, 0:n], in_=ps[0:64, 0:n], func=AF.Sqrt), reads=[ps.b], writes=[w["tB"].b])
                            kb.op("dve", lambda E: E.tensor_scalar(out=w["tB"][:, 0:n], in0=w["tB"][:, 0:n], scalar1=1e-12, scalar2=None, op0=ALU.max), reads=[w["tB"].b], writes=[w["tB"].b])
                            kb.op("dve", lambda E: E.reciprocal(out=w["tB"][:, 0:n], in_=w["tB"][:, 0:n]), reads=[w["tB"].b], writes=[w["tB"].b])
                            kb.op("dve", lambda E: E.scalar_tensor_tensor(out=w["kap"][:, 0:n], in0=f[:, 1, 0:n], scalar=col(5), in1=w["tB"][:, 0:n], op0=ALU.mult, op1=ALU.mult),
                                  reads=[f.b, ev.b, w["tB"].b], writes=[w["kap"].b])
                            kb.op("act", lambda E: E.activation(out=w["tA"][:, 0:n], in_=w["a"][:, 0:n], func=AF.Identity, scale=col(6), bias=omk[:, d * 2 + hh:d * 2 + hh + 1]),
                                  reads=[w["a"].b, ev.b, omk.b], writes=[w["tA"].b])
                            dve_tt(w["kp"][:, 0:n], f[:, 1, 0:n], w["tA"][:, 0:n], ALU.mult, [f.b, w["tA"].b], [w["kp"].b])
                            dve_tt(w["bet"][:, 0:n], w["kap"][:, 0:n], w["a"][:, 0:n], ALU.mult, [w["kap"].b, w["a"].b], [w["bet"].b])
                            kb.op("dve", lambda E: E.scalar_tensor_tensor(out=w["tA"][:, 0:n], in0=f[:, 0, 0:n], scalar=col(7), in1=w["kp"][:, 0:n], op0=ALU.mult, op1=ALU.mult),
                                  reads=[f.b, ev.b, w["kp"].b], writes=[w["tA"].b])
                            ps = nextps()
                            mm(ps[0:64, 0:n], on64[:, 0:64], w["tA"][:, 0:n], [on64.b, w["tA"].b], [ps.b])
                            dve_tt(w["tB"][:, 0:n], ps[0:64, 0:n], f[:, 2, 0:n], ALU.mult, [ps.b, f.b], [w["tB"].b])
                            store_scan(BN[d], lambda lo, hi: BN[d].ap[hh * 64:hh * 64 + 64, b * LS + lo:b * LS + hi], w["tB"], w["OT"], d, S0, S1, q0, n, 64)
                            kb.op("dve", lambda E: E.tensor_tensor_scan(out=w["cs"][:, 0:n], data0=on64[:, 0:n], data1=w["lgw"][:, 0:n], initial=0.0,
                                                                        op0=ALU.mult, op1=ALU.add), reads=[on64.b, w["lgw"].b], writes=[w["cs"].b])
                        for c0 in (range(0, n, C) if "nochunk" not in DBG else []):
                            for hh in range(2):
                                w = W[hh]
                                f = w["f"]
                                H = Hcur[hh]
                                Hn = Hnxt[hh]
                                cs_ = slice(c0, c0 + C)
                                off = w["cs"][:, c0 - 1:c0] if c0 > 0 else zero
                                kb.op("dve", lambda E: E.tensor_scalar(out=w["lw"][:], in0=w["cs"][:, cs_], scalar1=off, scalar2=-E05, op0=ALU.subtract, op1=ALU.mult),
                                      reads=[w["cs"].b, cst.b], writes=[w["lw"].b])
                                kb.op("dve", lambda E: E.scalar_tensor_tensor(out=w["lm"][:], in0=w["lgw"][:, cs_], scalar=E05, in1=w["lw"][:], op0=ALU.mult, op1=ALU.add),
                                      reads=[w["lgw"].b, w["lw"].b], writes=[w["lm"].b])
                                kb.op("act", lambda E: E.activation(out=w["g"][:], in_=w["lw"][:], func=AF.Exp), reads=[w["lw"].b], writes=[w["g"].b])
                                kb.op("act", lambda E: E.activation(out=w["gi"][:], in_=w["lw"][:], func=AF.Exp, scale=-1.0), reads=[w["lw"].b], writes=[w["gi"].b])
                                kb.op("act", lambda E: E.activation(out=w["gm"][:], in_=w["lm"][:], func=AF.Exp), reads=[w["lm"].b], writes=[w["gm"].b])
                                KR, kt, bt = w["KR"], w["kt"], w["bt"]
                                dve_tt(KR[:, 0:C], w["kap"][:, cs_], w["gm"][:], ALU.mult, [w["kap"].b, w["gm"].b], [KR.b])
                                dve_tt(KR[:, C:2 * C], f[:, 0, cs_], w["g"][:], ALU.mult, [f.b, w["g"].b], [KR.b])
                                dve_tt(kt[:], w["kp"][:, cs_], w["gi"][:], ALU.mult, [w["kp"].b, w["gi"].b], [kt.b])
                                dve_tt(bt[:], w["bet"][:, cs_], w["gi"][:], ALU.mult, [w["bet"].b, w["gi"].b], [bt.b])
                                CK = int(DBG[DBG.index("ck") + 2]) if "ck" in DBG else 9
                                if CK < 2:
                                    continue
                                pa, pb, pc = nextps(), nextps(), nextps()
                                mm(pa[0:64, 0:128], bt[:], KR[:], [bt.b, KR.b], [pa.b])
                                mm(pb[0:64, 0:128], kt[:], KR[:], [kt.b, KR.b], [pb.b])
                                mm(pc[0:64, 0:64], KR[:, 0:C], bt[:], [KR.b, bt.b], [pc.b])
                                AA = w["AA0"]
                                dve_tt(AA[:, 0:64], pa[0:64, 0:64], MK[:, 0:64], ALU.mult, [pa.b, MK.b], [AA.b])
                                dve_tt(w["BbT"][:], pa[0:64, 64:128], MK[:, 64:128], ALU.mult, [pa.b, MK.b], [w["BbT"].b])
                                dve_tt(w["PBm"][:], pb[0:64, 0:128], MK[:], ALU.mult, [pb.b, MK.b], [w["PBm"].b])
                                dve_tt(AA[:, 64:128], pc[0:64, 0:64], ML[:], ALU.mult, [pc.b, ML.b], [AA.b])
                                if CK < 3:
                                    continue
                                pt = nextps()
                                for i, src in enumerate((KR[:, 0:C], kt[:], bt[:], f[:, 2, cs_])):
                                    srcb = [KR.b, kt.b, bt.b, f.b][i]
                                    kb.op("pe", lambda E: E.matmul(pt[0:64, i * 64:(i + 1) * 64], lhsT=src, rhs=ident[0:64, 0:64], start=True, stop=True), reads=[srcb, ident.b], writes=[pt.b])
                                X, TM = w["X"], w["TM"]
                                if "ck3a" in DBG:
                                    continue
                                evac(X[:, 0:64], pt[0:64, 0:64], [pt.b], [X.b])
                                evac(TM[:], pt[0:64, 64:256], [pt.b], [TM.b])
                                Kt_, Bt_, V_ = TM[:, 0:64], TM[:, 64:128], TM[:, 128:192]
                                if "ck3b" in DBG:
                                    continue
                                pk = nextps()
                                mm(pk[0:64, 0:64], w["PBm"][:, 0:64], V_, [w["PBm"].b, TM.b], [pk.b])
                                evac(X[:, 64:128], pk[0:64, 0:64], [pk.b], [X.b])
                                if CK < 4:
                                    continue
                                for lev in range(6):
                                    if lev > 0:
                                        AAn = w["AA1"] if AA is w["AA0"] else w["AA0"]
                                        pq = nextps()
                                        mm(pq[0:64, 0:64], AA[:, 64:128], AA[:, 0:64], [AA.b], [pq.b])
                                        if lev < 5:
                                            mm(pq[0:64, 64:128], AA[:, 0:64], AA[:, 64:128], [AA.b], [pq.b])
                                            evac(AAn[:], pq[0:64, 0:128], [pq.b], [AAn.b])
                                        else:
                                            evac(AAn[:, 0:64], pq[0:64, 0:64], [pq.b], [AAn.b])
                                        AA = AAn
                                    px = nextps()
                                    mm(px[0:64, 0:128], AA[:, 0:64], X[:], [AA.b, X.b], [px.b])
                                    dve_tt(X[:], X[:], px[0:64, 0:128], ALU.subtract if lev == 0 else ALU.add, [X.b, px.b], [X.b])
                                if CK < 5:
                                    continue
                                kb.op("dve", lambda E: E.tensor_scalar(out=w["W2n"][:], in0=X[:, 64:128], scalar1=-1.0, scalar2=None, op0=ALU.mult), reads=[X.b], writes=[w["W2n"].b])
                                W1 = X[:, 0:64]
                                pr = nextps()
                                mm(pr[0:64, 0:64], W1, w["BbT"][:], [X.b, w["BbT"].b], [pr.b])
                                dve_tt(w["RhT"][:], KR[:, C:2 * C], pr[0:64, 0:64], ALU.subtract, [KR.b, pr.b], [w["RhT"].b])
                                pm = nextps()
                                mm(pm[0:64, 0:64], W1, Bt_, [X.b, TM.b], [pm.b])
                                kb.op("dve", lambda E: E.tensor_scalar(out=w["MTn"][:], in0=pm[0:64, 0:64], scalar1=-1.0, scalar2=None, op0=ALU.mult), reads=[pm.b], writes=[w["MTn"].b])
                                pg = nextps()
                                mm(pg[0:64, 0:64], Kt_, V_, [TM.b], [pg.b], start=True, stop=False)
                                mm(pg[0:64, 0:64], Bt_, w["W2n"][:], [TM.b, w["W2n"].b], [pg.b], start=False, stop=False)
                                mm(pg[0:64, 0:64], w["MTn"][:], H[:], [w["MTn"].b, H.b], [pg.b], start=False, stop=True)
                                py = nextps()
                                mm(py[0:64, 0:64], V_, w["PBm"][:, 64:128], [TM.b, w["PBm"].b], [py.b], start=True, stop=False)
                                mm(py[0:64, 0:64], w["W2n"][:], w["BbT"][:], [w["W2n"].b, w["BbT"].b], [py.b], start=False, stop=False)
                                mm(py[0:64, 0:64], H[:], w["RhT"][:], [H.b, w["RhT"].b], [py.b], start=False, stop=True)
                                evac(w["OB"][:, cs_], py[0:64, 0:64], [py.b], [w["OB"].b])
                                dve_tt(w["ht"][:], H[:], pg[0:64, 0:64], ALU.add, [H.b, pg.b], [w["ht"].b])
                                kb.op("dve", lambda E: E.tensor_scalar(out=Hn[:], in0=w["ht"][:], scalar1=w["g"][:, C - 1:C], scalar2=None, op0=ALU.mult),
                                      reads=[w["ht"].b, w["g"].b], writes=[Hn.b])
                                Hcur[hh], Hnxt[hh] = Hn, H
                        for hh in range(2):
                            w = W[hh]
                            store_scan(YR[d], lambda lo, hi: YR[d].ap[hh * 64:hh * 64 + 64, b * LS + lo:b * LS + hi], w["OB"], w["OT"], d, S0, S1, q0, n, 64)
            kb.barrier()
            kb.close_scope()

    def mixer_ssd(e):
        C = 128
        NBK = 512
        with ExitStack() as st:
            kb.open_scope()
            ev = Tl(kb, st, "evs", [128, 96])
            kb.dma("sp", ev[:], evec_d.ap[:, e * 96:(e + 1) * 96], ev.b, None)
            dv = Tl(kb, st, "dv", [128, 8])
            kb.dma("sp", dv[0:64, :], dvec_d.ap[:, e * 8:(e + 1) * 8], dv.b, None)
            kb.dma("sp", dv[64:128, :], dvec_d.ap[:, e * 8:(e + 1) * 8], dv.b, None)
            s4 = Tl(kb, st, "s4", [4, 512])
            kb.dma("sp", s4[:], sel4_d.ap, s4.b, None)
            selt = Tl(kb, st, "seltm", [128, 12])
            kb.dma("sp", selt[:], sel_d.ap, selt.b, None)
            MU = Tl(kb, st, "MU", [128, 128])
            on = Tl(kb, st, "onS", [128, 128])
            kb.op("pool", lambda E: E.memset(on[:], 1.0), writes=[on.b])
            kb.op("pool", lambda E: E.affine_select(out=MU[:], in_=on[:], pattern=[[1, 128]], compare_op=ALU.is_ge, fill=0.0, base=0, channel_multiplier=-1),
                  reads=[on.b], writes=[MU.b])
            F = Tl(kb, st, "Fs", [128, 4, NBK + 3])
            G0 = Tl(kb, st, "G0s", [128, 4, NBK + 3])
            G1 = Tl(kb, st, "G1s", [128, 4, NBK + 3])
            Fd = Tl(kb, st, "Fd", [4, 1, NBK])
            Gd0 = Tl(kb, st, "Gd0", [4, 1, NBK])
            Gd1 = Tl(kb, st, "Gd1", [4, 1, NBK])
            xc = Tl(kb, st, "xcs", [128, 4, NBK])
            xs = Tl(kb, st, "xss", [128, 4, NBK])
            dtt = Tl(kb, st, "dtt", [4, NBK])
            dta = Tl(kb, st, "dta", [4, NBK])
            acol = Tl(kb, st, "acolS", [4, 2])
            acs = Tl(kb, st, "acs", [4, C])
            DTA = Tl(kb, st, "DTA", [128, 8])
            CBm = Tl(kb, st, "CBm", [128, C])
            Btok = Tl(kb, st, "Btok", [128, 128])
            sg = [Tl(kb, st, f"sgS{i}", [128, C]) for i in range(2)]
            MT = [Tl(kb, st, f"MTs{i}", [128, C]) for i in range(2)]
            gb = [Tl(kb, st, f"gbS{i}", [128, C]) for i in range(2)]
            rT = [Tl(kb, st, f"rTs{i}", [128, C]) for i in range(2)]
            al = [Tl(kb, st, f"alS{i}", [128, 2]) for i in range(2)]
            xdt = [Tl(kb, st, f"xdt{i}", [128, 64]) for i in range(2)]
            xdw = [Tl(kb, st, f"xdw{i}", [128, 64]) for i in range(2)]
            xdd = [Tl(kb, st, f"xdd{i}", [128, 64]) for i in range(2)]
            Hs = [Tl(kb, st, f"Hs{i}", [128, 64]) for i in range(4)]
            OB = [Tl(kb, st, f"OBs{i}", [64, NBK]) for i in range(4)]
            OT = Tl(kb, st, "OTs", [64, NBK])

            def mm(ps_ap, lhsT, rhs, R, Wb, start=True, stop=True):
                kb.op("pe", lambda E: E.matmul(ps_ap, lhsT=lhsT, rhs=rhs, start=start, stop=stop), reads=R, writes=Wb)

            def blend(dst, g0, g1, npart, width):
                kb.op("dve", lambda E: E.tensor_scalar(out=dst[0:npart, :, 0:width], in0=g0[0:npart, :, 0:width], scalar1=selt[0:npart, 10:11], scalar2=None, op0=ALU.mult),
                      reads=[g0.b, selt.b], writes=[dst.b])
                kb.op("dve", lambda E: E.scalar_tensor_tensor(out=dst[0:npart, :, 0:width], in0=g1[0:npart, :, 0:width], scalar=selt[0:npart, 11:12], in1=dst[0:npart, :, 0:width],
                                                              op0=ALU.mult, op1=ALU.add), reads=[g1.b, selt.b, dst.b], writes=[dst.b])

            for d in range(2):
                kb.op("act", lambda E: E.activation(out=acol[:, d:d + 1], in_=ev[0:4, 80 + d * 2 + 1:80 + d * 2 + 2], func=AF.Exp), reads=[ev.b], writes=[acol.b])
                kb.op("dve", lambda E: E.tensor_scalar(out=acol[:, d:d + 1], in0=acol[:, d:d + 1], scalar1=-1.0, scalar2=None, op0=ALU.mult), reads=[acol.b], writes=[acol.b])
                for hd in range(4):
                    kb.op("pool", lambda E: E.memset(Hs[hd][:], 0.0), writes=[Hs[hd].b])
                for (S0, S1, q0, n) in scan_blocks(NBK):
                    if d == 0:
                        lo, hi = S0 + q0, S0 + q0 + n
                        h = min(3, q0)
                        for bsel, Gx, Gdx in ((0, G0, Gd0), (1, G1, Gd1)):
                            if h < 3:
                                kb.op("pool", lambda E: E.memset(Gx[:, :, 0:3 - h], 0.0), writes=[Gx.b])
                            kb.dma("sp", Gx[:, :, 3 - h:3 + n], pt_v[:, 9:13, bsel, lo - h:hi], Gx.b, PT.b)
                            kb.dma("sp", Gdx[:, :, 0:n], pt_v[0:4, 13:14, bsel, lo:hi], Gdx.b, PT.b)
                        blend(F, G0, G1, 128, n + 3)
                        blend(Fd, Gd0, Gd1, 4, n)
                    else:
                        hi = S1 - q0
                        lo = hi - n
                        h = min(3, q0)
                        for bsel, Gx, Gdx in ((0, G0, Gd0), (1, G1, Gd1)):
                            if h < 3:
                                kb.op("pool", lambda E: E.memset(Gx[:, :, n + h:n + 3], 0.0), writes=[Gx.b])
                            kb.dma("sp", Gx[:, :, 0:n + h], pt_v[:, 9:13, bsel, lo:hi + h], Gx.b, PT.b)
                            kb.dma("sp", Gdx[:, :, 0:n], pt_v[0:4, 13:14, bsel, lo:hi], Gdx.b, PT.b)
                        blend(G0, G0, G1, 128, n + 3)
                        blend(Gd0, Gd0, Gd1, 4, n)
                        kb.op("dve", lambda E: E.tensor_copy(out=F[:, :, 0:n + 3], in_=G0[:, :, 0:n + 3][:, :, ::-1]), reads=[G0.b], writes=[F.b])
                        kb.op("dve", lambda E: E.tensor_copy(out=Fd[:, :, 0:n], in_=Gd0[:, :, 0:n][:, :, ::-1]), reads=[Gd0.b], writes=[Fd.b])
                    for q in range(4):
                        vo = d * 20 + q * 5
                        kb.op("act", lambda E: E.activation(out=xc[:, q, 0:n], in_=F[:, q, 3:3 + n], func=AF.Identity, scale=ev[:, vo + 3:vo + 4], bias=ev[:, vo + 4:vo + 5]),
                              reads=[F.b, ev.b], writes=[xc.b])
                        for k in range(1, 4):
                            kb.op("dve", lambda E: E.scalar_tensor_tensor(out=xc[:, q, 0:n], in0=F[:, q, 3 - k:3 - k + n], scalar=ev[:, vo + 3 - k:vo + 4 - k], in1=xc[:, q, 0:n],
                                                                          op0=ALU.mult, op1=ALU.add), reads=[F.b, ev.b, xc.b], writes=[xc.b])
                    kb.op("act", lambda E: E.activation(out=xs[:, :, 0:n], in_=xc[:, :, 0:n], func=AF.Silu), reads=[xc.b], writes=[xs.b])
                    kb.op("act", lambda E: E.activation(out=dtt[:, 0:n], in_=Fd[:, 0, 0:n], func=AF.Exp, bias=ev[0:4, 80 + d * 2:80 + d * 2 + 1]), reads=[Fd.b, ev.b], writes=[dtt.b])
                    kb.op("act", lambda E: E.activation(out=dtt[:, 0:n], in_=dtt[:, 0:n], func=AF.Ln, bias=cst[0:4, 1:2]), reads=[dtt.b, cst.b], writes=[dtt.b])
                    kb.op("dve", lambda E: E.tensor_scalar(out=dta[:, 0:n], in0=dtt[:, 0:n], scalar1=acol[:, d:d + 1], scalar2=None, op0=ALU.mult), reads=[dtt.b, acol.b], writes=[dta.b])
                    for c0 in range(0, n, C):
                        cs_ = slice(c0, c0 + C)
                        kb.op("dve", lambda E: E.tensor_tensor_scan(out=acs[:], data0=on[0:4, 0:C], data1=dta[:, cs_], initial=0.0, op0=ALU.mult, op1=ALU.add),
                              reads=[on.b, dta.b], writes=[acs.b])
                        pt = nextps()
                        kb.op("pe", lambda E: E.matmul(pt[:, 0:4], lhsT=dtt[:, cs_], rhs=ident[0:4, 0:4], start=True, stop=True), reads=[dtt.b, ident.b], writes=[pt.b])
                        kb.op("pe", lambda E: E.matmul(pt[:, 4:8], lhsT=acs[:], rhs=ident[0:4, 0:4], start=True, stop=True), reads=[acs.b, ident.b], writes=[pt.b])
                        evac(DTA[:], pt[:, 0:8], [pt.b], [DTA.b])
                        pcb = nextps()
                        mm(pcb[:, 0:C], xs[:, 2, cs_], xs[:, 3, cs_], [xs.b], [pcb.b])
                        kb.op("dve", lambda E: E.tensor_tensor(out=CBm[:], in0=pcb[:, 0:C], in1=MU[:], op=ALU.mult), reads=[pcb.b, MU.b], writes=[CBm.b])
                        pbt = nextps()
                        kb.op("pe", lambda E: E.matmul(pbt[:, 0:128], lhsT=xs[:, 2, cs_], rhs=ident[:], start=True, stop=True), reads=[xs.b, ident.b], writes=[pbt.b])
                        evac(Btok[:], pbt[:, 0:128], [pbt.b], [Btok.b])
                        for hd in range(4):
                            i2 = hd % 2
                            H = Hs[hd]
                            pab = nextps()
                            mm(pab[:, 0:C], s4[:, hd * 128:(hd + 1) * 128], acs[:], [s4.b, acs.b], [pab.b])
                            kb.op("dve", lambda E: E.tensor_scalar(out=sg[i2][:], in0=pab[:, 0:C], scalar1=DTA[:, 4 + hd:5 + hd], scalar2=0.0, op0=ALU.subtract, op1=ALU.min),
                                  reads=[pab.b, DTA.b], writes=[sg[i2].b])
                            kb.op("act", lambda E: E.activation(out=sg[i2][:], in_=sg[i2][:], func=AF.Exp), reads=[sg[i2].b], writes=[sg[i2].b])
                            kb.op("dve", lambda E: E.tensor_tensor(out=MT[i2][:], in0=sg[i2][:], in1=CBm[:], op=ALU.mult), reads=[sg[i2].b, CBm.b], writes=[MT[i2].b])
                            kb.op("act", lambda E: E.activation(out=gb[i2][:], in_=pab[:, 0:C], func=AF.Exp), reads=[pab.b], writes=[gb[i2].b])
                            kb.op("dve", lambda E: E.tensor_tensor(out=rT[i2][:], in0=xs[:, 3, cs_], in1=gb[i2][:], op=ALU.mult), reads=[xs.b, gb[i2].b], writes=[rT[i2].b])
                            kb.op("dve", lambda E: E.tensor_copy(out=al[i2][:, 0:1], in_=pab[:, C - 1:C]), reads=[pab.b], writes=[al[i2].b])
                            kb.op("act", lambda E: E.activation(out=al[i2][:, 1:2], in_=DTA[:, 4 + hd:5 + hd], func=AF.Exp, scale=-1.0, bias=al[i2][:, 0:1]),
                                  reads=[DTA.b, al[i2].b], writes=[al[i2].b])
                            pxt = nextps()
                            pb0 = (hd % 2) * 64
                            kb.op("pe", lambda E: E.matmul(pxt[:, 0:64], lhsT=xs[pb0:pb0 + 64, hd // 2, cs_], rhs=ident[pb0:pb0 + 64, pb0:pb0 + 64], start=True, stop=True), reads=[xs.b, ident.b], writes=[pxt.b])
                            kb.op("dve", lambda E: E.tensor_scalar(out=xdt[i2][:], in0=pxt[:, 0:64], scalar1=DTA[:, hd:hd + 1], scalar2=None, op0=ALU.mult), reads=[pxt.b, DTA.b], writes=[xdt[i2].b])
                            kb.op("dve", lambda E: E.tensor_scalar(out=xdd[i2][:], in0=pxt[:, 0:64], scalar1=dv[:, d * 4 + hd:d * 4 + hd + 1], scalar2=None, op0=ALU.mult), reads=[pxt.b, dv.b], writes=[xdd[i2].b])
                            kb.op("dve", lambda E: E.tensor_scalar(out=xdw[i2][:], in0=xdt[i2][:], scalar1=al[i2][:, 1:2], scalar2=None, op0=ALU.mult), reads=[xdt[i2].b, al[i2].b], writes=[xdw[i2].b])
                            py = nextps()
                            mm(py[0:64, 0:C], xdt[i2][:], MT[i2][:], [xdt[i2].b, MT[i2].b], [py.b], start=True, stop=False)
                            mm(py[0:64, 0:C], xdd[i2][:], ident[:], [xdd[i2].b, ident.b], [py.b], start=False, stop=False)
                            mm(py[0:64, 0:C], H[:], rT[i2][:], [H.b, rT[i2].b], [py.b], start=False, stop=True)
                            evac(OB[hd][:, cs_], py[0:64, 0:C], [py.b], [OB[hd].b])
                            ph = nextps()
                            mm(ph[:, 0:64], Btok[:], xdw[i2][:], [Btok.b, xdw[i2].b], [ph.b])
                            kb.op("act", lambda E: E.activation(out=gb[i2][:, 0:1], in_=al[i2][:, 0:1], func=AF.Exp), reads=[al[i2].b, rT[i2].b], writes=[gb[i2].b])
                            kb.op("dve", lambda E: E.scalar_tensor_tensor(out=H[:], in0=H[:], scalar=gb[i2][:, 0:1], in1=ph[:, 0:64], op0=ALU.mult, op1=ALU.add),
                                  reads=[H.b, gb[i2].b, ph.b], writes=[H.b])
                    for hd in range(4):
                        r0 = (hd // 2) * 128 + (hd % 2) * 64
                        store_scan(YM[d], lambda lo_, hi_: YM[d].ap[r0:r0 + 64, lo_:hi_], OB[hd], OT, d, S0, S1, q0, n, 64)
            kb.barrier()
            kb.close_scope()

    def phase_C_even(l, e):
        with ExitStack() as st:
            kb.open_scope()
            W = Tl(kb, st, "WoutE", [128, 3, D], BF16)
            load_w_bf16(W, lambda k: W[:, k, :], lambda k: woute_d.ap[e * 384 + k * 128:e * 384 + (k + 1) * 128, :], 3)
            G2w = Tl(kb, st, "G2w", [128, 2, 128], BF16)
            load_w_bf16(G2w, lambda k: G2w[:, k, :], lambda k: g2_d.ap[e * 256 + k * 128:e * 256 + (k + 1) * 128, :], 2)
            ev = Tl(kb, st, "evC", [128, 96])
            kb.dma("sp", ev[:], evec_d.ap[:, e * 96:(e + 1) * 96], ev.b, None)
            selt = Tl(kb, st, "seltC", [128, 12])
            kb.dma("sp", selt[:], sel_d.ap, selt.b, None)
            on = Tl(kb, st, "onC", [128, 128])
            bo = Tl(kb, st, "boC", [128, 128])
            kb.op("pool", lambda E: E.memset(on[:], 1.0), writes=[on.b])
            kb.op("pool", lambda E: E.memset(bo[:], 0.0), writes=[bo.b])
            kb.op("pool", lambda E: E.memset(bo[0:64, 0:64], 1.0), writes=[bo.b])
            kb.op("pool", lambda E: E.memset(bo[64:128, 64:128], 1.0), writes=[bo.b])
            kb.op("dve", lambda E: E.memset(cst[:, 3:4], 1e-5), writes=[cst.b])
            kb.op("dve", lambda E: E.memset(cst[:, 4:5], 64e-5), writes=[cst.b])
            mns = Tl(kb, st, "mns", [128, 4])
            for q in range(2):
                for b in range(2):
                    kb.op("dve", lambda E: E.tensor_tensor(out=mns[:, q * 2 + b:q * 2 + b + 1], in0=ev[:, 78 + q:79 + q], in1=selt[:, 10 + b:11 + b], op=ALU.mult),
                          reads=[ev.b, selt.b], writes=[mns.b])
            ym = [[Tl(kb, st, f"ym{i}{d}", [128, 2, T]) for d in range(2)] for i in range(2)]
            zz = [Tl(kb, st, f"zz{i}", [128, 2, T]) for i in range(2)]
            gl = [Tl(kb, st, f"glC{i}", [128, 2, T]) for i in range(2)]
            yr = [[Tl(kb, st, f"yr{i}{d}", [128, T]) for d in range(2)] for i in range(2)]
            bn = [[Tl(kb, st, f"bn{i}{d}", [128, T]) for d in range(2)] for i in range(2)]
            sq2 = Tl(kb, st, "sq2C", [128, 2, T])
            rs = Tl(kb, st, "rsC", [128, T])
            t1 = Tl(kb, st, "t1E", [128, T])
            t2 = Tl(kb, st, "t2E", [128, T])
            sgl = Tl(kb, st, "sglC", [128, 2, T], BF16)
            mbf = [Tl(kb, st, f"mE{i}", [128, 3, T], BF16) for i in range(2)]
            stg = [Tl(kb, st, f"stE{i}", [128, FC, T]) for i in range(2)]
            ymv = [YM[d].ap.rearrange("(q p) s -> p q s", p=128) for d in range(2)]
            for ti, tile in enumerate(tiles):
                n, b, pos = tile["n"], tile["b"], tile["pos"]
                i = ti % 2
                m, sg = mbf[i], stg[i]
                for d in range(2):
                    kb.dma("sp", ym[i][d][:, :, 0:n], ymv[d][:, :, pos:pos + n], ym[i][d].b, YM[d].b)
                    kb.dma("sp", yr[i][d][:, 0:n], YR[d].ap[:, b * LS + pos:b * LS + pos + n], yr[i][d].b, YR[d].b)
                    kb.dma("sp", bn[i][d][:, 0:n], BN[d].ap[:, b * LS + pos:b * LS + pos + n], bn[i][d].b, BN[d].b)
                kb.dma("sp", zz[i][:, :, 0:n], scr_ap(PT, 2, tile, 7), zz[i].b, PT.b)
                kb.dma("sp", gl[i][:, :, 0:n], scr_ap(PT, 2, tile, 5), gl[i].b, PT.b)
                y0 = ym[i][0]
                kb.op("dve", lambda E: E.tensor_tensor(out=y0[:, :, 0:n], in0=y0[:, :, 0:n], in1=ym[i][1][:, :, 0:n], op=ALU.add), reads=[y0.b, ym[i][1].b], writes=[y0.b])
                kb.op("act", lambda E: E.activation(out=zz[i][:, :, 0:n], in_=zz[i][:, :, 0:n], func=AF.Silu), reads=[zz[i].b], writes=[zz[i].b])
                kb.op("dve", lambda E: E.tensor_tensor(out=y0[:, :, 0:n], in0=y0[:, :, 0:n], in1=zz[i][:, :, 0:n], op=ALU.mult), reads=[y0.b, zz[i].b], writes=[y0.b])
                kb.op("act", lambda E: E.activation(out=sq2[:, :, 0:n], in_=y0[:, :, 0:n], func=AF.Square), reads=[y0.b], writes=[sq2.b])
                ps = nextps()
                for q in range(2):
                    kb.op("pe", lambda E: E.matmul(ps[:, 0:n], lhsT=on[:], rhs=sq2[:, q, 0:n], start=(q == 0), stop=(q == 1)), reads=[on.b, sq2.b], writes=[ps.b])
                kb.op("act", lambda E: E.activation(out=rs[:, 0:n], in_=ps[:, 0:n], func=AF.Sqrt, scale=1.0 / 256, bias=cst[:, 3:4]), reads=[ps.b, cst.b], writes=[rs.b])
                kb.op("dve", lambda E: E.reciprocal(out=rs[:, 0:n], in_=rs[:, 0:n]), reads=[rs.b], writes=[rs.b])
                for q in range(2):
                    kb.op("dve", lambda E: E.scalar_tensor_tensor(out=m[:, q, 0:n], in0=y0[:, q, 0:n], scalar=mns[:, q * 2 + b:q * 2 + b + 1], in1=rs[:, 0:n],
                                                                  op0=ALU.mult, op1=ALU.mult), reads=[y0.b, mns.b, rs.b], writes=[m.b])
                yy = yr[i][0]
                kb.op("dve", lambda E: E.tensor_tensor(out=yy[:, 0:n], in0=yy[:, 0:n], in1=yr[i][1][:, 0:n], op=ALU.add), reads=[yy.b, yr[i][1].b], writes=[yy.b])
                ps = nextps()
                kb.op("pe", lambda E: E.matmul(ps[:, 0:n], lhsT=bo[:], rhs=yy[:, 0:n], start=True, stop=True), reads=[bo.b, yy.b], writes=[ps.b])
                kb.op("dve", lambda E: E.scalar_tensor_tensor(out=t1[:, 0:n], in0=ps[:, 0:n], scalar=-1.0 / 64, in1=yy[:, 0:n], op0=ALU.mult, op1=ALU.add),
                      reads=[ps.b, yy.b], writes=[t1.b])
                kb.op("act", lambda E: E.activation(out=t2[:, 0:n], in_=t1[:, 0:n], func=AF.Square), reads=[t1.b], writes=[t2.b])
                ps = nextps()
                kb.op("pe", lambda E: E.matmul(ps[:, 0:n], lhsT=bo[:], rhs=t2[:, 0:n], start=True, stop=True), reads=[bo.b, t2.b], writes=[ps.b])
                kb.op("act", lambda E: E.activation(out=t2[:, 0:n], in_=ps[:, 0:n], func=AF.Sqrt, scale=1.0 / 64, bias=cst[:, 4:5]), reads=[ps.b, cst.b], writes=[t2.b])
                kb.op("dve", lambda E: E.reciprocal(out=t2[:, 0:n], in_=t2[:, 0:n]), reads=[t2.b], writes=[t2.b])
                kb.op("dve", lambda E: E.tensor_tensor(out=t1[:, 0:n], in0=t1[:, 0:n], in1=t2[:, 0:n], op=ALU.mult), reads=[t1.b, t2.b], writes=[t1.b])
                kb.op("act", lambda E: E.activation(out=t1[:, 0:n], in_=t1[:, 0:n], func=AF.Identity, scale=ev[:, 76:77], bias=ev[:, 77:78]), reads=[t1.b, ev.b], writes=[t1.b])
                kb.op("dve", lambda E: E.tensor_tensor(out=t1[:, 0:n], in0=t1[:, 0:n], in1=bn[i][0][:, 0:n], op=ALU.add), reads=[t1.b, bn[i][0].b], writes=[t1.b])
                kb.op("dve", lambda E: E.tensor_tensor(out=t1[:, 0:n], in0=t1[:, 0:n], in1=bn[i][1][:, 0:n], op=ALU.add), reads=[t1.b, bn[i][1].b], writes=[t1.b])
                kb.op("act", lambda E: E.activation(out=sgl[:, :, 0:n], in_=gl[i][:, :, 0:n], func=AF.Sigmoid), reads=[gl[i].b], writes=[sgl.b])
                ps = nextps()
                for q in range(2):
                    kb.op("pe", lambda E: E.matmul(ps[:, 0:n], lhsT=G2w[:, q, :], rhs=sgl[:, q, 0:n], start=(q == 0), stop=(q == 1)), reads=[G2w.b, sgl.b], writes=[ps.b])
                kb.op("dve", lambda E: E.tensor_tensor(out=m[:, 2, 0:n], in0=t1[:, 0:n], in1=ps[:, 0:n], op=ALU.mult), reads=[t1.b, ps.b], writes=[m.b])
                out_proj(W, 3, [128, 128, 128], m, sg, n, tile)
            kb.barrier()
            kb.close_scope()

    ECH = [(0, 128), (128, 128), (256, 128), (384, 96), (480, 96), (576, 128), (704, 128), (832, 128), (960, 128), (1088, 128),
           (1216, 128), (1344, 128), (1472, 128), (1600, 4)]

    e_i = o_i = 0
    for l, lt in enumerate(LT):
        pend = (l - 1) if l > 0 else None
        stop = cfg.get("stop", "")
        if stop == "setup":
            break
        if lt == "O":
            phase_A(l, wino_d, o_i * D, 512, pend)
            if stop == "A":
                break
            mixer_rglru(o_i)
            if stop == "mix":
                break
            phase_C_odd(l, o_i)
            o_i += 1
        else:
            phase_A(l, wine_d, e_i * D, 1604, pend, chunks=ECH)
            if stop == "A":
                break
            ev_dve[0] = True
            if "nossd" not in DBG:
                mixer_ssd(e_i)
            if "norwkv" not in DBG:
                mixer_rwkv(e_i)
            ev_dve[0] = False
            if stop == "mix":
                break
            phase_C_even(l, e_i)
            e_i += 1
        if stop == "C":
            break
        allreduce_parts()
        if stop == "AR":
            break
        phase_D(l)
        allreduce_parts()

    with ExitStack() as st:
        kb.open_scope()
        selt = Tl(kb, st, "selt", [128, 12])
        g2c = Tl(kb, st, "g2c", [128, FC])
        kb.dma("sp", selt[:], sel_d.ap, selt.b, None)
        o5 = ((NL - 1) * 6 + 5) * FC * 3
        g2v = lambda j: mod[:, o5:o5 + FC * 3].rearrange("p (f j) -> p f j", j=3)[:, :, j]
        kb.op("dve", lambda E: E.tensor_scalar(out=g2c[:], in0=g2v(0), scalar1=selt[:, 8:9], scalar2=None, op0=ALU.mult), reads=[mod.b, selt.b], writes=[g2c.b])
        kb.op("dve", lambda E: E.scalar_tensor_tensor(out=g2c[:], in0=g2v(1), scalar=selt[:, 9:10], in1=g2c[:], op0=ALU.mult, op1=ALU.add),
              reads=[mod.b, selt.b, g2c.b], writes=[g2c.b])
        xts = [Tl(kb, st, f"xtF{i}", [128, FC, T]) for i in range(2)]
        rts = [Tl(kb, st, f"rtF{i}", [128, FC, T]) for i in range(2)]
        xa = Tl(kb, st, "xaF", [128, FC, T])
        ra = Tl(kb, st, "raF", [128, FC, T])
        sq = Tl(kb, st, "sqF", [128, FC, T], BF16)
        rs = Tl(kb, st, "rsF", [128, T])
        li = 0
        for t0 in range(0, TR, T):
            for r in range(8):
                xt, rt = xts[li % 2], rts[li % 2]
                li += 1
                kb.dma("sp", xt[:], lat_view(XT)[:, r, :, t0:t0 + T], xt.b, XT.bs(r * D, (r + 1) * D))
                kb.dma("sp", rt[:], lat_view(REDL)[:, r, :, t0:t0 + T], rt.b, REDL.bs(r * D, (r + 1) * D))
                if r == 0:
                    kb.op("dve", lambda E: E.tensor_scalar(out=xa[:], in0=xt[:], scalar1=selt[:, 0:1], scalar2=None, op0=ALU.mult), reads=[xt.b, selt.b], writes=[xa.b])
                    kb.op("pool", lambda E: E.tensor_scalar(out=ra[:], in0=rt[:], scalar1=selt[:, 0:1], scalar2=None, op0=ALU.mult), reads=[rt.b, selt.b], writes=[ra.b])
                else:
                    kb.op("dve", lambda E: E.scalar_tensor_tensor(out=xa[:], in0=xt[:], scalar=selt[:, r:r + 1], in1=xa[:], op0=ALU.mult, op1=ALU.add),
                          reads=[xt.b, selt.b, xa.b], writes=[xa.b])
                    kb.op("dve", lambda E: E.scalar_tensor_tensor(out=ra[:], in0=rt[:], scalar=selt[:, r:r + 1], in1=ra[:], op0=ALU.mult, op1=ALU.add),
                          reads=[rt.b, selt.b, ra.b], writes=[ra.b])
            for fc in range(FC):
                kb.op("dve", lambda E: E.scalar_tensor_tensor(out=xa[:, fc, :], in0=ra[:, fc, :], scalar=g2c[:, fc:fc + 1], in1=xa[:, fc, :],
                                                              op0=ALU.mult, op1=ALU.add), reads=[ra.b, g2c.b, xa.b], writes=[xa.b])
            kb.op("act", lambda E: E.activation(out=sq[:], in_=xa[:], func=AF.Square), reads=[xa.b], writes=[sq.b])
            ps = nextps()
            for fc in range(FC):
                kb.op("pe", lambda E: E.matmul(ps[:, 0:T], lhsT=ones_bf[:], rhs=sq[:, fc, :], start=(fc == 0), stop=(fc == FC - 1)),
                      reads=[ones_bf.b, sq.b], writes=[ps.b])
            kb.op("act", lambda E: E.activation(out=rs[:], in_=ps[:, 0:T], func=AF.Sqrt, scale=1.0 / D, bias=cst[:, 0:1]), reads=[ps.b, cst.b], writes=[rs.b])
            kb.op("dve", lambda E: E.reciprocal(out=rs[:], in_=rs[:]), reads=[rs.b], writes=[rs.b])
            for fc in range(FC):
                kb.op("dve", lambda E: E.scalar_tensor_tensor(out=ra[:, fc, :], in0=xa[:, fc, :], scalar=nw[:, 2 * NL * FC + fc:2 * NL * FC + fc + 1],
                                                              in1=rs[:], op0=ALU.mult, op1=ALU.mult), reads=[xa.b, rs.b, nw.b], writes=[ra.b])
            kb.dma("act", yout.ap.rearrange("(fc p) t -> p fc t", p=128)[:, :, t0:t0 + T], ra[:], yout.b, ra.b)
    if cfg.get("dbg"):
        kb.barrier()
        extra = ([("YM0", YM[0]), ("YM1", YM[1]), ("YR0", YR[0]), ("YR1", YR[1]), ("BN0", BN[0]), ("BN1", BN[1])] if NE else [])
        for nm, dr in [("MODR", MODR), ("PT", PT), ("HS", HS), ("REDL", REDL), ("REDC", REDC), ("XT", XT), ("XCi", XC)] + extra:
            od = Dr(nc, "dbg_" + nm, list(dr.t.shape), F32, kind="ExternalOutput")
            kb.dma("sp", od.ap, dr.ap, od.b, None)
        kb.barrier()
    deps = {yout.b.wr[0]: yout.b.wr[1]}
    kb._wait("sp", deps)
    kb._wait("act", deps)
    kb.barrier()
    ninst = kb.ninst
    kb.stack.close()
    return nc, ninst


def pack_inputs(inp, cfg):
    SEQ = cfg["SEQ"]
    LT = cfg["ltypes"]
    NL = len(LT)
    TR = SEQ // 4
    f32 = lambda a: np.ascontiguousarray(np.asarray(a, dtype=np.float32))
    x = f32(inp["x"]).reshape(2 * SEQ, D)
    xc = f32(f32(inp["ctx"]).reshape(2 * CTX, D).T)
    cc = np.stack([f32(inp["c"])[0], f32(inp["c"])[1], f32(inp["c_ctx"])], 0)
    ada_w = f32(inp["ada_w"])[:NL]
    ada_b = f32(inp["ada_b"])[:NL]
    adab = f32(ada_b.reshape(NL, 96, 128).transpose(2, 0, 1).reshape(128, NL * 96))
    nwv = np.zeros((128, (2 * NL + 1) * FC), np.float32)
    for l in range(NL):
        nwv[:, (l * 2) * FC:(l * 2 + 1) * FC] = f32(inp["norm1_w"])[l].reshape(FC, 128).T
        nwv[:, (l * 2 + 1) * FC:(l * 2 + 2) * FC] = f32(inp["norm2_w"])[l].reshape(FC, 128).T
    nwv[:, 2 * NL * FC:] = f32(inp["final_norm_w"]).reshape(FC, 128).T
    wgate, wup, wdown = f32(inp["ffn_w_gate"])[:NL], f32(inp["ffn_w_up"])[:NL], f32(inp["ffn_w_down"])[:NL]
    o_idx = [i for i, t in enumerate(LT) if t == "O"]
    NO = len(o_idx)
    NE = len(LT) - NO
    maps = []
    for c in range(NCORE):
        m = {}
        m["xs"] = f32(x[c * TR:(c + 1) * TR].T)
        m["xc"] = xc
        cm = np.zeros((128, 6), np.float32)
        for kc in range(2):
            cm[:, kc * 3:(kc + 1) * 3] = cc[:, c * 256 + kc * 128:c * 256 + (kc + 1) * 128].T
        m["cmine"] = cm
        m["adaw"] = f32(ada_w[:, c * 256:(c + 1) * 256, :].reshape(NL * 256, 6 * D))
        m["adab"] = adab
        m["nwv"] = nwv
        m["wg"] = f32(wgate[:, :, c * DFFC:(c + 1) * DFFC].reshape(NL * D, DFFC))
        m["wu"] = f32(wup[:, :, c * DFFC:(c + 1) * DFFC].reshape(NL * D, DFFC))
        m["wd"] = f32(wdown[:, c * DFFC:(c + 1) * DFFC, :].reshape(NL * DFFC, D))
        if NO:
            cw_in, cw_out = f32(inp["c_w_in"]), f32(inp["c_w_out"])
            m["wino"] = f32(np.concatenate([np.concatenate([cw_in[o][:, c * 256:(c + 1) * 256], cw_in[o][:, D + c * 256:D + (c + 1) * 256]], 1)
                                            for o in range(NO)], 0))
            m["wouto"] = f32(np.concatenate([cw_out[o][c * 256:(c + 1) * 256, :] for o in range(NO)], 0))
            gws = []
            ov = np.zeros((128, NO * 32), np.float32)
            for o in range(NO):
                for d in range(2):
                    gws.append(f32(inp["c_wa"])[o, d, c])
                    gws.append(f32(inp["c_wx"])[o, d, c])
                    for ci in range(2):
                        ch = slice(c * 256 + ci * 128, c * 256 + (ci + 1) * 128)
                        base = o * 32 + (d * 2 + ci) * 8
                        ov[:, base:base + 4] = f32(inp["c_conv_w"])[o, d][:, ch].T
                        ov[:, base + 4] = f32(inp["c_conv_b"])[o, d, ch]
                        ov[:, base + 5] = f32(inp["c_ba"])[o, d, ch]
                        ov[:, base + 6] = f32(inp["c_bx"])[o, d, ch]
                        ov[:, base + 7] = f32(inp["c_lambda"])[o, d, ch]
            m["gw"] = f32(np.concatenate(gws, 0))
            m["ovec"] = ov
        if NE:
            g, bm = c // 2, c % 2
            abw = f32(inp["ab_w_in"])
            cols = np.concatenate([np.arange(3088 + c * 128, 3088 + (c + 1) * 128), np.arange(4112 + c * 128, 4112 + (c + 1) * 128),
                                   np.arange(5136 + c * 128, 5136 + (c + 1) * 128), np.arange(6160, 6256), np.arange(6256, 6352), np.arange(6352, 6608),
                                   np.arange(g * 256, (g + 1) * 256), np.arange(1024 + g * 256, 1024 + (g + 1) * 256),
                                   np.arange(2048 + g * 128, 2048 + (g + 1) * 128), np.arange(2560 + g * 128, 2560 + (g + 1) * 128),
                                   np.arange(3072 + g * 4, 3072 + (g + 1) * 4)])
            m["wine"] = f32(np.concatenate([abw[e][:, cols] for e in range(NE)], 0))
            abo = f32(inp["ab_w_out"])
            m["woute"] = f32(np.concatenate([np.concatenate([abo[e][g * 256:(g + 1) * 256], abo[e][1024 + c * 128:1024 + (c + 1) * 128]], 0) for e in range(NE)], 0))
            evv = np.zeros((128, NE * 96), np.float32)
            lora = []
            dvec = np.zeros((64, NE * 8), np.float32)
            mch = [np.arange(g * 256, g * 256 + 128), np.arange(g * 256 + 128, (g + 1) * 256), np.arange(1024 + g * 128, 1024 + (g + 1) * 128),
                   np.arange(1536 + g * 128, 1536 + (g + 1) * 128)]
            for e in range(NE):
                o = e * 96
                for d in range(2):
                    for q in range(4):
                        evv[:, o + d * 20 + q * 5:o + d * 20 + q * 5 + 4] = f32(inp["m_conv_w"])[e, d][:, mch[q]].T
                        evv[:, o + d * 20 + q * 5 + 4] = f32(inp["m_conv_b"])[e, d, mch[q]]
                    mu = f32(inp["r_mu"])[e, d]
                    for hh in range(2):
                        ch = np.arange(c * 128 + hh * 64, c * 128 + (hh + 1) * 64)
                        base = o + 40 + (d * 2 + hh) * 8
                        evv[:64, base + 0] = mu[ch]
                        evv[:64, base + 1] = mu[1024 + ch]
                        evv[:64, base + 2] = mu[2048 + ch]
                        evv[:64, base + 3] = f32(inp["r_w0"])[e, d, ch]
                        evv[:64, base + 4] = f32(inp["r_a0"])[e, d, ch]
                        evv[:64, base + 5] = f32(inp["r_kk"])[e, d, ch]
                        evv[:64, base + 6] = f32(inp["r_ka"])[e, d, ch]
                        evv[:64, base + 7] = f32(inp["r_rk"])[e, d].reshape(-1)[ch]
                        lora.append(f32(inp["r_w2"])[e, d][:, ch])
                        lora.append(f32(inp["r_a2"])[e, d][:, ch])
                    evv[:96, o + 72 + d * 2] = mu[3072:3168]
                    evv[:96, o + 72 + d * 2 + 1] = mu[3168:3264]
                    evv[:4, o + 80 + d * 2] = f32(inp["m_dt_bias"])[e, d, g * 4:(g + 1) * 4]
                    evv[:4, o + 80 + d * 2 + 1] = f32(inp["m_a_log"])[e, d, g * 4:(g + 1) * 4]
                    for hd in range(4):
                        dvec[:, e * 8 + d * 4 + hd] = f32(inp["m_d"])[e, d, g * 4 + hd]
                evv[:, o + 76] = f32(inp["r_lnx_w"])[e, c * 128:(c + 1) * 128]
                evv[:, o + 77] = f32(inp["r_lnx_b"])[e, c * 128:(c + 1) * 128]
                evv[:, o + 78] = f32(inp["m_norm_w"])[e, g * 256:g * 256 + 128]
                evv[:, o + 79] = f32(inp["m_norm_w"])[e, g * 256 + 128:(g + 1) * 256]
            m["evec"] = evv
            m["lora"] = f32(np.concatenate(lora, 0))
            m["g2"] = f32(np.concatenate([f32(inp["r_g2"])[e][:, c * 128:(c + 1) * 128] for e in range(NE)], 0))
            m["dvec"] = dvec
            s4 = np.zeros((4, 512), np.float32)
            for hd in range(4):
                s4[hd, hd * 128:(hd + 1) * 128] = 1.0
            m["sel4"] = s4
        sel = np.zeros((128, 12), np.float32)
        sel[:, c] = 1.0
        sel[:, 8 + c // 4] = 1.0
        sel[:, 10 + c % 2] = 1.0
        m["sel"] = sel
        maps.append(m)
    return maps


_CACHE = {}


def run(inp, cfg, trace=False):
    key = (cfg["SEQ"], cfg["ltypes"], cfg["T"], cfg.get("stop", ""), cfg.get("dbg", False))
    if key not in _CACHE:
        _CACHE[key] = build(cfg)
    nc, ninst = _CACHE[key]
    maps = pack_inputs(inp, cfg)
    res = run_bass_kernel_spmd(nc, maps, core_ids=list(range(NCORE)), **({"trace": True} if trace else {}))
    SEQ = cfg["SEQ"]
    TR = SEQ // 4
    out = np.zeros((2 * SEQ, D), np.float32)
    for c in range(NCORE):
        out[c * TR:(c + 1) * TR] = np.asarray(res.results[c]["yout"]).T
    return out.reshape(2, SEQ, D), res


def kernel(**inputs):
    cfg = make_cfg()
    out, _ = run(inputs, cfg)
    return out
```

```python
from contextlib import ExitStack
import math
import numpy as np
import concourse.bass as bass
import concourse.mybir as mybir
from concourse.bass_utils import run_bass_kernel_spmd

F32 = mybir.dt.float32
BF16 = mybir.dt.bfloat16
AF = mybir.ActivationFunctionType
ALU = mybir.AluOpType
AX = mybir.AxisListType

D = 2048
FC = 16
CTX = 256
DFF = 5632
DFFC = DFF // 8
NCORE = 8
EPS = 1e-6
GW = 64

SAME_SYNC = True
ROLL = 30000


class Buf:
    __slots__ = ("name", "wr", "rd", "dsem", "dcum")

    def __init__(self, name):
        self.name = name
        self.wr = None
        self.rd = {}
        self.dsem = None
        self.dcum = 0


class KB:
    def __init__(self, nc):
        self.nc = nc
        self.stack = ExitStack()
        self.eng = {"pe": nc.tensor, "dve": nc.vector, "act": nc.scalar, "pool": nc.gpsimd, "sp": nc.sync}
        self.sems = {}
        self.esem = {}
        self.ecnt = {}
        self.waited = {k: {} for k in self.eng}
        self.nsem = 0
        self.dmabufs = []
        self.free_dsems = []
        self.ccsem = None
        self.sem_cum = {}
        self.scopes = [[]]
        self.ninst = 0
        for e in self.eng:
            self._newesem(e)

    def newsem(self, name):
        h = self.stack.enter_context(self.nc.semaphore(name))
        self.sems[name] = h
        self.nsem += 1
        return name

    def _newesem(self, e):
        name = self.newsem(f"e_{e}_{self.nsem}")
        self.esem[e] = name
        self.ecnt[name] = 0

    def _deps(self, reads, writes):
        d = {}
        for b in reads:
            if b.wr and d.get(b.wr[0], 0) < b.wr[1]:
                d[b.wr[0]] = b.wr[1]
        for b in writes:
            if b.wr and d.get(b.wr[0], 0) < b.wr[1]:
                d[b.wr[0]] = b.wr[1]
            for sem, v in b.rd.items():
                if d.get(sem, 0) < v:
                    d[sem] = v
        return d

    def _wait(self, e, deps):
        w = self.waited[e]
        for sem, v in deps.items():
            if w.get(sem, 0) >= v:
                continue
            if sem == self.esem[e] and (e == "pe" or e == "sp" or not SAME_SYNC):
                continue
            self.eng[e].wait_ge(self.sems[sem], v)
            w[sem] = v

    def op(self, e, fn, reads=(), writes=()):
        self._wait(e, self._deps(reads, writes))
        ins = fn(self.eng[e])
        sem = self.esem[e]
        self.ecnt[sem] += 1
        v = self.ecnt[sem]
        ins.then_inc(self.sems[sem], 1)
        self.ninst += 1
        for b in writes:
            b.wr = (sem, v)
            b.rd = {}
        for b in reads:
            if b.wr != (sem, v):
                b.rd[sem] = v
        if v >= ROLL:
            self._newesem(e)
        return (sem, v)

    def _dsem(self, outbuf):
        if outbuf.dsem is not None and self.sem_cum[outbuf.dsem] >= ROLL * 16:
            outbuf.dsem = None
        if outbuf.dsem is None:
            if self.free_dsems:
                outbuf.dsem = self.free_dsems.pop()
            else:
                outbuf.dsem = self.newsem(f"d_{self.nsem}")
                self.sem_cum[outbuf.dsem] = 0

    def open_scope(self):
        self.scopes.append([])

    def close_scope(self):
        for b in self.scopes.pop():
            if b.dsem is not None:
                if self.sem_cum[b.dsem] < ROLL * 16:
                    self.free_dsems.append(b.dsem)
                b.dsem = None
            if b in self.dmabufs:
                self.dmabufs.remove(b)

    def dma(self, q, out, in_, outbuf, inbuf, **kw):
        outs = outbuf if isinstance(outbuf, (list, tuple)) else [outbuf]
        ins_ = [] if inbuf is None else (inbuf if isinstance(inbuf, (list, tuple)) else [inbuf])
        self._wait(q, self._deps(ins_, outs))
        prim = outs[0]
        self._dsem(prim)
        ins = self.eng[q].dma_start(out=out, in_=in_, **kw)
        self.sem_cum[prim.dsem] += 16
        ins.then_inc(self.sems[prim.dsem], 16)
        self.ninst += 1
        t = (prim.dsem, self.sem_cum[prim.dsem])
        for ob in outs:
            ob.wr = t
            ob.rd = {}
        for ib in ins_:
            ib.rd[t[0]] = t[1]
        if prim not in self.dmabufs:
            self.dmabufs.append(prim)
        return t

    def collective(self, kind, op, groups, in_ap, out_ap, inbuf, outbuf):
        import os
        if "nocc" in os.environ.get("DBG", ""):
            return self.dma("pool", out_ap, in_ap, outbuf, inbuf)
        e = "pool"
        self._wait(e, self._deps([inbuf], [outbuf]))
        if self.ccsem is None:
            self.ccsem = self.newsem("ccsem")
            self.sem_cum[self.ccsem] = 0
            self.ccbuf = Buf("ccbuf")
            self.dmabufs.append(self.ccbuf)
        ins = self.eng[e].collective_compute(kind, op, replica_groups=groups, ins=[in_ap], outs=[out_ap])
        self.sem_cum[self.ccsem] += 1
        ins.then_inc(self.sems[self.ccsem], 1)
        t = (self.ccsem, self.sem_cum[self.ccsem])
        outbuf.wr = t
        outbuf.rd = {}
        inbuf.rd[t[0]] = t[1]
        self.ccbuf.wr = t
        return t

    def barrier(self, engines=("pe", "dve", "act", "pool", "sp")):
        deps = {}
        for e, sem in self.esem.items():
            if self.ecnt[sem] > 0:
                deps[sem] = self.ecnt[sem]
        for b in self.dmabufs:
            if b.wr and b is not getattr(self, "ccbuf", None):
                deps[b.wr[0]] = max(deps.get(b.wr[0], 0), b.wr[1])
        for e in engines:
            w = self.waited[e]
            for sem, v in deps.items():
                if w.get(sem, 0) >= v or sem == self.esem[e]:
                    continue
                self.eng[e].wait_ge(self.sems[sem], v)
                w[sem] = v


class Tl:
    def __init__(self, kb, stack, name, shape, dtype=F32, psum=False):
        mk = kb.nc.psum_tensor if psum else kb.nc.sbuf_tensor
        kb.ntl = getattr(kb, "ntl", 0) + 1
        name = f"{name}_{kb.ntl}"
        self.t = stack.enter_context(mk(name, list(shape), dtype))
        self.b = Buf(name)
        kb.scopes[-1].append(self.b)

    def __getitem__(self, k):
        return self.t[k]


class Dr:
    def __init__(self, nc, name, shape, dtype=F32, kind="Internal", chunk_rows=None):
        self.t = nc.dram_tensor(name, list(shape), dtype, kind=kind)
        self.ap = self.t.ap()
        self.b = Buf(name)
        self.chunk_rows = chunk_rows
        if chunk_rows:
            self.nchunk = shape[0] // chunk_rows
            self.cb = [Buf(f"{name}_c{i}") for i in range(self.nchunk)]

    def bs(self, r0, r1):
        if not self.chunk_rows:
            return [self.b]
        return self.cb[r0 // self.chunk_rows:(r1 - 1) // self.chunk_rows + 1]

    def chunk_ap(self, k):
        return self.ap[k * self.chunk_rows:(k + 1) * self.chunk_rows, :]


def make_cfg(SEQ=8192, ltypes=("E", "O", "E", "O"), T=256, stop="", dbg=False):
    return dict(SEQ=SEQ, ltypes=tuple(ltypes), T=T, stop=stop, dbg=dbg)


def build(cfg):
    SEQ = cfg["SEQ"]
    LT = cfg["ltypes"]
    NL = len(LT)
    T = cfg["T"]
    TR = SEQ // 4
    LS = CTX + SEQ
    ROWS = SEQ // GW
    NE = sum(1 for t in LT if t == "E")
    NO = sum(1 for t in LT if t == "O")
    assert TR % T == 0 and CTX % min(T, CTX) == 0

    nc = bass.Bass("TRN2", target_bir_lowering=False)
    kb = KB(nc)
    top = kb.stack

    def din(name, shape, dt=F32):
        return Dr(nc, name, shape, dt, kind="ExternalInput")

    xs = din("xs", [D, TR])
    xc_in = din("xc", [D, 2 * CTX])
    cmine = din("cmine", [128, 2 * 3])
    adaw = din("adaw", [NL * 2 * 128, 6 * D])
    adab = din("adab", [128, NL * 96])
    nwv = din("nwv", [128, (2 * NL + 1) * FC])
    wg_d = din("wg", [NL * D, DFFC])
    wu_d = din("wu", [NL * D, DFFC])
    wd_d = din("wd", [NL * DFFC, D])
    if NO:
        wino_d = din("wino", [NO * D, 512])
        wouto_d = din("wouto", [NO * 256, D])
        gw_d = din("gw", [NO * 2 * 2 * 256, 256])
        ovec_d = din("ovec", [128, NO * 2 * 2 * 8])
    if NE:
        wine_d = din("wine", [NE * D, 1604])
        woute_d = din("woute", [NE * 384, D])
        evec_d = din("evec", [128, NE * 96])
        lora_d = din("lora", [NE * 8 * 96, 64])
        g2_d = din("g2", [NE * 256, 128])
        dvec_d = din("dvec", [64, NE * 8])
        sel4_d = din("sel4", [4, 4 * 128])
    sel_d = din("sel", [128, 12])
    yout = Dr(nc, "yout", [D, TR], F32, kind="ExternalOutput")

    xsi = Dr(nc, "xsi", [D, TR])
    CR = (1 << 20) // TR
    XT = Dr(nc, "XT", [8 * D, TR], chunk_rows=CR)
    XC = Dr(nc, "XCi", [D, 2 * CTX])
    PARTL = Dr(nc, "PARTL", [8 * D, TR], chunk_rows=CR)
    PARTC = Dr(nc, "PARTC", [D, 2 * CTX])
    REDL = Dr(nc, "REDL", [8 * D, TR], chunk_rows=CR)
    REDC = Dr(nc, "REDC", [D, 2 * CTX])
    MODP = Dr(nc, "MODP", [128, NL * 288])
    MODR = Dr(nc, "MODR", [128, NL * 288])
    NCH_MAX = 14 if NE else 4
    PT = Dr(nc, "PT", [NCH_MAX * 128, 2 * LS])
    HS = Dr(nc, "HS", [2 * 128, 2 * LS])
    HSC = Dr(nc, "HSC", [2 * 128, LS])
    if NE:
        YM = [Dr(nc, f"YM{d}", [2 * 128, LS]) for d in range(2)]
        YR = [Dr(nc, f"YR{d}", [128, 2 * LS]) for d in range(2)]
        BN = [Dr(nc, f"BN{d}", [128, 2 * LS]) for d in range(2)]
    G4 = [[0, 1, 2, 3], [4, 5, 6, 7]]
    G2 = [[0, 4], [1, 5], [2, 6], [3, 7]]

    TMPL = Dr(nc, "TMPL", [8 * D, TR], chunk_rows=CR)
    TMPC = Dr(nc, "TMPC", [D, 2 * CTX])
    MODT = Dr(nc, "MODT", [128, NL * 288])

    def allreduce(src, dst, tmp):
        if not src.chunk_rows:
            kb.collective("AllReduce", ALU.add, G4, src.ap, tmp.ap, src.b, tmp.b)
            kb.collective("AllReduce", ALU.add, G2, tmp.ap, dst.ap, tmp.b, dst.b)
            return
        for k in range(src.nchunk):
            kb.collective("AllReduce", ALU.add, G4, src.chunk_ap(k), tmp.chunk_ap(k), src.cb[k], tmp.cb[k])
        for k in range(src.nchunk):
            kb.collective("AllReduce", ALU.add, G2, tmp.chunk_ap(k), dst.chunk_ap(k), tmp.cb[k], dst.cb[k])

    def lat_view(dr):
        return dr.ap.rearrange("(r fc p) t -> p r fc t", r=8, fc=FC, p=128)

    def ctx_view(dr):
        return dr.ap.rearrange("(fc p) t -> p fc t", p=128)

    tiles = []
    for r in range(8):
        for t0 in range(0, TR, T):
            tiles.append(dict(kind="lat", r=r, t0=t0, n=T, b=r // 4, pos=CTX + (r % 4) * TR + t0, j=r // 4, last=(t0 + T >= TR)))
    TCX = min(T, CTX)
    for b in range(2):
        for t0 in range(0, CTX, TCX):
            tiles.append(dict(kind="ctx", b=b, c0=b * CTX + t0, n=TCX, pos=t0, j=2, last=(b == 1 and t0 + TCX >= CTX)))

    def xb(drl, drc, tile):
        if tile["kind"] == "lat":
            return drl.bs(tile["r"] * D, (tile["r"] + 1) * D)
        return [drc.b]

    def xap(drl, drc, tile):
        if tile["kind"] == "lat":
            return lat_view(drl)[:, tile["r"], :, tile["t0"]:tile["t0"] + tile["n"]]
        return ctx_view(drc)[:, :, tile["c0"]:tile["c0"] + tile["n"]]

    def scr_ap(dr, nch, tile, c0=0):
        v = dr.ap.rearrange("(c p) (b s) -> p c b s", p=128, b=2)
        return v[:, c0:c0 + nch, tile["b"], tile["pos"]:tile["pos"] + tile["n"]]

    mod = Tl(kb, top, "mod", [128, NL * 288])
    nw = Tl(kb, top, "nw", [128, (2 * NL + 1) * FC])
    A12 = Tl(kb, top, "A12", [128, NL * 2 * FC * 3])
    ones_bf = Tl(kb, top, "ones_bf", [128, 128], BF16)
    ident = Tl(kb, top, "ident", [128, 128])
    cst = Tl(kb, top, "cst", [128, 8])
    psb = [Tl(kb, top, f"ps{i}", [128, 512], F32, psum=True) for i in range(8)]
    ps_i = [0]

    def nextps():
        p = psb[ps_i[0] % 8]
        ps_i[0] += 1
        return p

    def modcol(l, j6, fc, j):
        o = ((l * 6 + j6) * FC + fc) * 3 + j
        return mod[:, o:o + 1]

    def acol(l, which, fc, j):
        o = ((l * 2 + which) * FC + fc) * 3 + j
        return A12[:, o:o + 1]

    ev_i = [0]
    ev_dve = [False]

    def evac(out, in_, R, W):
        ev_i[0] += 1
        if ev_i[0] % 2 and not ev_dve[0]:
            kb.op("act", lambda E: E.activation(out=out, in_=in_, func=AF.Copy), reads=R, writes=W)
        else:
            kb.op("dve", lambda E: E.tensor_copy(out=out, in_=in_), reads=R, writes=W)

    wst = [Tl(kb, top, f"wst{i}", [128, D]) for i in range(2)]
    wst_i = [0]

    def load_w_bf16(dst_tile, dst_ap_fn, src_ap_fn, nk):
        for k in range(nk):
            dst = dst_ap_fn(k)
            npart, ncol = dst.shape[0], dst.shape[-1]
            stg = wst[wst_i[0] % 2]
            wst_i[0] += 1
            kb.dma("sp", stg[0:npart, 0:ncol], src_ap_fn(k), stg.b, None)
            kb.op("pool", lambda E: E.tensor_copy(out=dst, in_=stg[0:npart, 0:ncol]), reads=[stg.b], writes=[dst_tile.b])

    kb.op("dve", lambda E: E.memset(ones_bf[:], 1.0), writes=[ones_bf.b])
    kb.op("dve", lambda E: E.memset(cst[:, 0:1], EPS), writes=[cst.b])
    kb.op("dve", lambda E: E.memset(cst[:, 1:2], 1.0), writes=[cst.b])
    kb.op("dve", lambda E: E.memset(cst[:, 2:3], 0.0), writes=[cst.b])
    kb.op("pool", lambda E: E.memset(ident[:], 0.0), writes=[ident.b])
    kb.op("pool", lambda E: E.affine_select(out=ident[:], in_=ident[:], pattern=[[-1, 128]], compare_op=ALU.not_equal,
                                            fill=1.0, base=0, channel_multiplier=1), reads=[ident.b], writes=[ident.b])
    kb.dma("sp", nw[:], nwv.ap, nw.b, None)
    with ExitStack() as st:
        kb.open_scope()
        selt0 = Tl(kb, st, "selt0", [128, 12])
        kb.dma("sp", selt0[:], sel_d.ap, selt0.b, None)
        xg = [Tl(kb, st, f"xg{i}", [128, FC, T]) for i in range(2)]
        xo = [Tl(kb, st, f"xo{i}", [128, FC, T]) for i in range(2)]
        gi_ = 0
        for t0 in range(0, TR, T):
            g = xg[(t0 // T) % 2]
            kb.dma("sp", g[:], xs.ap.rearrange("(fc p) t -> p fc t", p=128)[:, :, t0:t0 + T], g.b, None)
            for r in range(8):
                o_ = xo[gi_ % 2]
                gi_ += 1
                kb.op("dve" if r % 2 else "pool", lambda E: E.tensor_scalar(out=o_[:], in0=g[:], scalar1=selt0[:, r:r + 1], scalar2=None, op0=ALU.mult),
                      reads=[g.b, selt0.b], writes=[o_.b])
                kb.dma("act", lat_view(PARTL)[:, r, :, t0:t0 + T], o_[:], PARTL.bs(r * D, (r + 1) * D), o_.b)
        allreduce(PARTL, XT, TMPL)
        kb.barrier()
        kb.close_scope()
    import os
    DBG = os.environ.get("DBG", "")
    kb.dma("sp", XC.ap, xc_in.ap, XC.b, None)

    with ExitStack() as st:
      kb.open_scope()
      if "noada" not in DBG:
          csb = Tl(kb, st, "csb", [128, 6])
          csl = Tl(kb, st, "csl", [128, 6])
          adb = Tl(kb, st, "adb", [128, NL * 96])
          modp = Tl(kb, st, "modp", [128, NL * 288])
          wbuf = [Tl(kb, st, f"adw{i}", [128, 6 * D]) for i in range(2)]
          kb.dma("sp", csb[:], cmine.ap, csb.b, None)
          kb.dma("sp", adb[:], adab.ap, adb.b, None)
          kb.op("act", lambda E: E.activation(out=csl[:], in_=csb[:], func=AF.Silu), reads=[csb.b], writes=[csl.b])
          for l in range(NL):
              ps = nextps()
              for kc in range(2):
                  wb = wbuf[kc]
                  row0 = (l * 2 + kc) * 128
                  kb.dma("sp", wb[:], adaw.ap[row0:row0 + 128, :], wb.b, None)
              for cc in range(96):
                  for kc in range(2):
                      wb = wbuf[kc]
                      kb.op("pe", lambda E: E.matmul(ps[:, cc * 3:cc * 3 + 3], lhsT=wb[:, cc * 128:(cc + 1) * 128],
                                                     rhs=csl[:, kc * 3:kc * 3 + 3], start=(kc == 0), stop=(kc == 1)),
                            reads=[wb.b, csl.b], writes=[ps.b])
              kb.op("dve", lambda E: E.scalar_tensor_tensor(
                  out=modp[:, l * 288:(l + 1) * 288].rearrange("p (c j) -> p c j", j=3),
                  in0=adb[:, l * 96:(l + 1) * 96].unsqueeze(2).to_broadcast([128, 96, 3]), scalar=1.0 / NCORE,
                  in1=ps[:, 0:288].rearrange("p (c j) -> p c j", j=3), op0=ALU.mult, op1=ALU.add),
                  reads=[adb.b, ps.b], writes=[modp.b])
          kb.dma("sp", MODP.ap, modp[:], MODP.b, modp.b)
          allreduce(MODP, MODR, MODT)
          kb.dma("sp", mod[:], MODR.ap, mod.b, MODR.b)
          for l in range(NL):
              for which, j6 in ((0, 1), (1, 4)):
                  o_m = ((l * 6 + j6) * FC) * 3
                  o_a = ((l * 2 + which) * FC) * 3
                  o_w = (l * 2 + which) * FC
                  kb.op("dve", lambda E: E.scalar_tensor_tensor(
                      out=A12[:, o_a:o_a + 48].rearrange("p (f j) -> p f j", j=3),
                      in0=mod[:, o_m:o_m + 48].rearrange("p (f j) -> p f j", j=3), scalar=1.0,
                      in1=nw[:, o_w:o_w + FC].unsqueeze(2).to_broadcast([128, FC, 3]), op0=ALU.add, op1=ALU.mult),
                      reads=[mod.b, nw.b], writes=[A12.b])
          kb.barrier()
      kb.close_scope()

    def rmsnorm_mod(xt, sq, rs, tmp, n, acols, bcols, ncols_scale=1.0):
        kb.op("act", lambda E: E.activation(out=sq[:, :, 0:n], in_=xt[:, :, 0:n], func=AF.Square), reads=[xt.b], writes=[sq.b])
        ps = nextps()
        for fc in range(FC):
            kb.op("pe", lambda E: E.matmul(ps[:, 0:n], lhsT=ones_bf[:], rhs=sq[:, fc, 0:n], start=(fc == 0), stop=(fc == FC - 1)),
                  reads=[ones_bf.b, sq.b], writes=[ps.b])
        kb.op("act", lambda E: E.activation(out=rs[:, 0:n], in_=ps[:, 0:n], func=AF.Sqrt, scale=1.0 / D, bias=cst[:, 0:1]),
              reads=[ps.b, cst.b], writes=[rs.b])
        kb.op("dve", lambda E: E.reciprocal(out=rs[:, 0:n], in_=rs[:, 0:n]), reads=[rs.b], writes=[rs.b])
        for fc in range(FC):
            tm = tmp[fc % len(tmp)]
            kb.op("dve", lambda E: E.tensor_tensor(out=tm[:, 0:n], in0=xt[:, fc, 0:n], in1=rs[:, 0:n], op=ALU.mult),
                  reads=[xt.b, rs.b], writes=[tm.b])
            kb.op("act", lambda E: E.activation(out=sq[:, fc, 0:n], in_=tm[:, 0:n], func=AF.Identity, scale=acols(fc), bias=bcols(fc)),
                  reads=[tm.b, A12.b, mod.b], writes=[sq.b])

    def resid_update(xt, rt, n, gcols):
        for fc in range(FC):
            kb.op("dve", lambda E: E.scalar_tensor_tensor(out=xt[:, fc, 0:n], in0=rt[:, fc, 0:n], scalar=gcols(fc),
                                                          in1=xt[:, fc, 0:n], op0=ALU.mult, op1=ALU.add),
                  reads=[rt.b, xt.b, mod.b], writes=[xt.b])

    def allreduce_parts():
        allreduce(PARTL, REDL, TMPL)
        allreduce(PARTC, REDC, TMPC)

    def phase_A(l, win_dr, row0, ncols, pending_g2_layer, chunks=None):
        if chunks is None:
            chunks = [(c * 128, min(128, ncols - c * 128)) for c in range((ncols + 127) // 128)]
        nch = len(chunks)
        with ExitStack() as st:
            kb.open_scope()
            W = Tl(kb, st, "Win", [128, FC, ncols], BF16)
            load_w_bf16(W, lambda k: W[:, k, :], lambda k: win_dr.ap[row0 + k * 128:row0 + (k + 1) * 128, :], FC)
            xts = [Tl(kb, st, f"xtA{i}", [128, FC, T]) for i in range(2)]
            rts = [Tl(kb, st, f"rtA{i}", [128, FC, T]) for i in range(2)]
            sqs = [Tl(kb, st, f"sqA{i}", [128, FC, T], BF16) for i in range(2)]
            rss = [Tl(kb, st, f"rsA{i}", [128, T]) for i in range(2)]
            tmp = [Tl(kb, st, f"tmA{i}", [128, T]) for i in range(4)]
            for ti, tile in enumerate(tiles):
                n = tile["n"]
                j = tile["j"]
                xt, rt, sq, rs = xts[ti % 2], rts[ti % 2], sqs[ti % 2], rss[ti % 2]
                kb.dma("sp", xt[:, :, 0:n], xap(XT, XC, tile), xt.b, xb(XT, XC, tile))
                if pending_g2_layer is not None:
                    kb.dma("sp", rt[:, :, 0:n], xap(REDL, REDC, tile), rt.b, xb(REDL, REDC, tile))
                    resid_update(xt, rt, n, lambda fc: modcol(pending_g2_layer, 5, fc, j))
                    kb.dma("act", xap(XT, XC, tile), xt[:, :, 0:n], xb(XT, XC, tile), xt.b)
                rmsnorm_mod(xt, sq, rs, tmp, n, lambda fc: acol(l, 0, fc, j), lambda fc: modcol(l, 0, fc, j))
                for c in range(nch):
                    cc0, cw = chunks[c]
                    ps = nextps()
                    for kc in range(FC):
                        kb.op("pe", lambda E: E.matmul(ps[0:cw, 0:n], lhsT=W[:, kc, cc0:cc0 + cw], rhs=sq[:, kc, 0:n],
                                                       start=(kc == 0), stop=(kc == FC - 1)), reads=[W.b, sq.b], writes=[ps.b])
                    evac(rt[0:cw, c, 0:n], ps[0:cw, 0:n], [ps.b], [rt.b])
                c = 0
                while c < nch:
                    if chunks[c][1] == 128:
                        c1 = c
                        while c1 < nch and chunks[c1][1] == 128:
                            c1 += 1
                        kb.dma("act", scr_ap(PT, c1 - c, tile, c), rt[:, c:c1, 0:n], PT.b, rt.b)
                        c = c1
                    else:
                        cw = chunks[c][1]
                        kb.dma("act", scr_ap(PT, 1, tile, c)[0:cw], rt[0:cw, c:c + 1, 0:n], PT.b, rt.b)
                        c += 1
            kb.barrier()
            kb.close_scope()

    def phase_C_odd(l, o):
        with ExitStack() as st:
            kb.open_scope()
            W = Tl(kb, st, "Wout", [128, 2, D], BF16)
            load_w_bf16(W, lambda k: W[:, k, :], lambda k: wouto_d.ap[o * 256 + k * 128:o * 256 + (k + 1) * 128, :], 2)
            hss = [Tl(kb, st, f"hsC{i}", [128, 2, T]) for i in range(2)]
            gys = [Tl(kb, st, f"gyC{i}", [128, 2, T]) for i in range(2)]
            t1 = Tl(kb, st, "t1C", [128, 2, T])
            mbf = [Tl(kb, st, f"mC{i}", [128, 2, T], BF16) for i in range(2)]
            stg = [Tl(kb, st, f"stC{i}", [128, FC, T]) for i in range(2)]
            for ti, tile in enumerate(tiles):
                n = tile["n"]
                hs, gy, m, sg = hss[ti % 2], gys[ti % 2], mbf[ti % 2], stg[ti % 2]
                kb.dma("sp", hs[:, :, 0:n], scr_ap(HS, 2, tile), hs.b, HS.b)
                kb.dma("sp", gy[:, :, 0:n], scr_ap(PT, 2, tile, 0), gy.b, PT.b)
                kb.op("dve", lambda E: E.tensor_tensor(out=t1[:, :, 0:n], in0=gy[:, :, 0:n], in1=gy[:, :, 0:n], op=ALU.mult), reads=[gy.b], writes=[t1.b])
                kb.op("dve", lambda E: E.tensor_scalar(out=t1[:, :, 0:n], in0=t1[:, :, 0:n], scalar1=0.044715, scalar2=1.0, op0=ALU.mult, op1=ALU.add),
                      reads=[t1.b], writes=[t1.b])
                kb.op("dve", lambda E: E.tensor_tensor(out=t1[:, :, 0:n], in0=t1[:, :, 0:n], in1=gy[:, :, 0:n], op=ALU.mult), reads=[t1.b, gy.b], writes=[t1.b])
                kb.op("act", lambda E: E.activation(out=t1[:, :, 0:n], in_=t1[:, :, 0:n], func=AF.Sigmoid, scale=2.0 * math.sqrt(2.0 / math.pi)),
                      reads=[t1.b], writes=[t1.b])
                kb.op("dve", lambda E: E.tensor_tensor(out=t1[:, :, 0:n], in0=t1[:, :, 0:n], in1=gy[:, :, 0:n], op=ALU.mult), reads=[t1.b, gy.b], writes=[t1.b])
                kb.op("dve", lambda E: E.tensor_tensor(out=m[:, :, 0:n], in0=t1[:, :, 0:n], in1=hs[:, :, 0:n], op=ALU.mult), reads=[t1.b, hs.b], writes=[m.b])
                out_proj(W, 2, [128, 128], m, sg, n, tile)
            kb.barrier()
            kb.close_scope()

    ar_pend = []

    def out_proj(W, nk, ksz, m, sg, n, tile):
        for fo in range(FC):
            ps = nextps()
            for k in range(nk):
                kb.op("pe", lambda E: E.matmul(ps[:, 0:n], lhsT=W[0:ksz[k], k, fo * 128:(fo + 1) * 128], rhs=m[0:ksz[k], k, 0:n],
                                               start=(k == 0), stop=(k == nk - 1)), reads=[W.b, m.b], writes=[ps.b])
            evac(sg[:, fo, 0:n], ps[:, 0:n], [ps.b], [sg.b])
        lat = tile["kind"] == "lat"
        kb.dma("act", xap(PARTL, PARTC, tile), sg[:, :, 0:n], xb(PARTL, PARTC, tile), sg.b)
        if tile.get("last"):
            if lat:
                ready = [k for k in range(PARTL.nchunk) if ((k + 1) * CR - 1) // D == tile["r"]]
                for k in ready:
                    kb.collective("AllReduce", ALU.add, G4, PARTL.chunk_ap(k), TMPL.chunk_ap(k), PARTL.cb[k], TMPL.cb[k])
                for k in ar_pend:
                    kb.collective("AllReduce", ALU.add, G2, TMPL.chunk_ap(k), REDL.chunk_ap(k), TMPL.cb[k], REDL.cb[k])
                ar_pend[:] = ready
            else:
                for k in ar_pend:
                    kb.collective("AllReduce", ALU.add, G2, TMPL.chunk_ap(k), REDL.chunk_ap(k), TMPL.cb[k], REDL.cb[k])
                ar_pend[:] = []
                allreduce(PARTC, REDC, TMPC)

    def phase_D(l):
        csz = [128, 128, 128, 128, 128, 64]
        with ExitStack() as st:
            kb.open_scope()
            Wg = Tl(kb, st, "Wg", [128, FC, DFFC], BF16)
            Wu = Tl(kb, st, "Wu", [128, FC, DFFC], BF16)
            Wd = Tl(kb, st, "Wd", [128, 6, D], BF16)
            load_w_bf16(Wg, lambda k: Wg[:, k, :], lambda k: wg_d.ap[l * D + k * 128:l * D + (k + 1) * 128, :], FC)
            load_w_bf16(Wu, lambda k: Wu[:, k, :], lambda k: wu_d.ap[l * D + k * 128:l * D + (k + 1) * 128, :], FC)
            load_w_bf16(Wd, lambda k: Wd[0:csz[k], k, :], lambda k: wd_d.ap[l * DFFC + k * 128:l * DFFC + k * 128 + csz[k], :], 6)
            xts = [Tl(kb, st, f"xtD{i}", [128, FC, T]) for i in range(2)]
            rts = [Tl(kb, st, f"rtD{i}", [128, FC, T]) for i in range(2)]
            sqs = [Tl(kb, st, f"sqD{i}", [128, FC, T], BF16) for i in range(2)]
            rss = [Tl(kb, st, f"rsD{i}", [128, T]) for i in range(2)]
            tmp = [Tl(kb, st, f"tmD{i}", [128, T]) for i in range(4)]
            acts = [Tl(kb, st, f"acD{i}", [128, 6, T], BF16) for i in range(2)]
            sgl = [Tl(kb, st, f"sgD{i}", [128, T]) for i in range(2)]
            for ti, tile in enumerate(tiles):
                n = tile["n"]
                j = tile["j"]
                lat = tile["kind"] == "lat"
                xt, rt, sq, rs, ac = xts[ti % 2], rts[ti % 2], sqs[ti % 2], rss[ti % 2], acts[ti % 2]
                kb.dma("sp", xt[:, :, 0:n], xap(XT, XC, tile), xt.b, xb(XT, XC, tile))
                kb.dma("sp", rt[:, :, 0:n], xap(REDL, REDC, tile), rt.b, xb(REDL, REDC, tile))
                resid_update(xt, rt, n, lambda fc: modcol(l, 2, fc, j))
                kb.dma("act", xap(XT, XC, tile), xt[:, :, 0:n], xb(XT, XC, tile), xt.b)
                rmsnorm_mod(xt, sq, rs, tmp, n, lambda fc: acol(l, 1, fc, j), lambda fc: modcol(l, 3, fc, j))
                for c in range(6):
                    cw = csz[c]
                    pg, pu = nextps(), nextps()
                    for kc in range(FC):
                        kb.op("pe", lambda E: E.matmul(pg[0:cw, 0:n], lhsT=Wg[:, kc, c * 128:c * 128 + cw], rhs=sq[:, kc, 0:n],
                                                       start=(kc == 0), stop=(kc == FC - 1)), reads=[Wg.b, sq.b], writes=[pg.b])
                    for kc in range(FC):
                        kb.op("pe", lambda E: E.matmul(pu[0:cw, 0:n], lhsT=Wu[:, kc, c * 128:c * 128 + cw], rhs=sq[:, kc, 0:n],
                                                       start=(kc == 0), stop=(kc == FC - 1)), reads=[Wu.b, sq.b], writes=[pu.b])
                    s = sgl[c % 2]
                    kb.op("act", lambda E: E.activation(out=s[0:cw, 0:n], in_=pg[0:cw, 0:n], func=AF.Silu), reads=[pg.b], writes=[s.b])
                    kb.op("dve", lambda E: E.tensor_tensor(out=ac[0:cw, c, 0:n], in0=s[0:cw, 0:n], in1=pu[0:cw, 0:n], op=ALU.mult),
                          reads=[s.b, pu.b], writes=[ac.b])
                out_proj(Wd, 6, csz, ac, rt, n, tile)
            kb.barrier()
            kb.close_scope()

    def mixer_rglru(o):
        BL = 512
        with ExitStack() as st:
            kb.open_scope()
            GWt = Tl(kb, st, "GWt", [128, 2 * 2 * 2, 256], BF16)
            load_w_bf16(GWt, lambda k: GWt[:, k, :], lambda k: gw_d.ap[(o * 8 + k) * 128:(o * 8 + k + 1) * 128, :], 8)
            ov = Tl(kb, st, "ov", [128, 2 * 2 * 8])
            spn = Tl(kb, st, "spn", [128, 4])
            kb.dma("sp", ov[:], ovec_d.ap[:, o * 32:(o + 1) * 32], ov.b, None)
            for dc in range(4):
                kb.op("act", lambda E: E.activation(out=spn[:, dc:dc + 1], in_=ov[:, dc * 8 + 7:dc * 8 + 8], func=AF.Exp, scale=-1.0),
                      reads=[ov.b], writes=[spn.b])
                kb.op("act", lambda E: E.activation(out=spn[:, dc:dc + 1], in_=spn[:, dc:dc + 1], func=AF.Ln, scale=1.0, bias=cst[:, 1:2]),
                      reads=[spn.b, cst.b], writes=[spn.b])
            kb.op("dve", lambda E: E.tensor_scalar(out=spn[:], in0=spn[:], scalar1=-8.0, scalar2=None, op0=ALU.mult), reads=[spn.b], writes=[spn.b])
            XB = Tl(kb, st, "XB", [128, 2, LS])
            hfw = Tl(kb, st, "hfw", [128, 2, BL])
            RM = Tl(kb, st, "RM", [128, SEQ])
            xcf = Tl(kb, st, "xcf", [128, 2, BL])
            xcb = Tl(kb, st, "xcb", [128, 2, BL], BF16)
            gr = Tl(kb, st, "gr", [128, 2, BL])
            gi = Tl(kb, st, "gi", [128, 2, BL])
            aa = Tl(kb, st, "aa", [128, 2, BL])
            uu = Tl(kb, st, "uu", [128, 2, BL])
            hh = [Tl(kb, st, f"hh{i}", [128, 2, BL]) for i in range(2)]
            zero = cst[:, 2:3]
            pt_v = PT.ap.rearrange("(c p) (b s) -> p c b s", p=128, b=2)
            hs_v = HS.ap.rearrange("(c p) (b s) -> p c b s", p=128, b=2)
            for b in range(2):
                for ci in range(2):
                    kb.dma("sp", XB[:, ci, 0:CTX], pt_v[:, 2 + ci, b, 0:CTX], XB.b, PT.b)
                    kb.dma("sp", RM[:], pt_v[:, 2 + ci, b, CTX:LS], RM.b, PT.b)
                    kb.op("pool", lambda E: E.tensor_copy(out=XB[:, ci, CTX:LS].rearrange("p (c r) -> p c r", c=GW),
                                                          in_=RM[:].rearrange("p (r c) -> p c r", c=GW)), reads=[RM.b], writes=[XB.b])
                for d in range(2):
                    segs = [(0, CTX), (CTX, LS)]
                    hprev = None
                    blk_i = 0
                    for (S0, S1) in segs:
                        starts = list(range(S0, S1, BL))
                        if d == 1:
                            starts = starts[::-1]
                        for s0 in starts:
                            s1 = min(s0 + BL, S1)
                            n = s1 - s0
                            for ci in range(2):
                                vo = (d * 2 + ci) * 8
                                kb.op("act", lambda E: E.activation(out=xcf[:, ci, 0:n], in_=XB[:, ci, s0:s1], func=AF.Identity,
                                                                    scale=ov[:, vo + 3:vo + 4], bias=ov[:, vo + 4:vo + 5]),
                                      reads=[XB.b, ov.b], writes=[xcf.b])
                                for k in range(1, 4):
                                    if d == 0:
                                        lo = max(s0, S0 + k)
                                        if lo >= s1:
                                            continue
                                        kb.op("dve", lambda E: E.scalar_tensor_tensor(
                                            out=xcf[:, ci, lo - s0:n], in0=XB[:, ci, lo - k:s1 - k], scalar=ov[:, vo + 3 - k:vo + 4 - k],
                                            in1=xcf[:, ci, lo - s0:n], op0=ALU.mult, op1=ALU.add), reads=[XB.b, ov.b, xcf.b], writes=[xcf.b])
                                    else:
                                        hi = min(s1, S1 - k)
                                        if hi <= s0:
                                            continue
                                        kb.op("dve", lambda E: E.scalar_tensor_tensor(
                                            out=xcf[:, ci, 0:hi - s0], in0=XB[:, ci, s0 + k:hi + k], scalar=ov[:, vo + 3 - k:vo + 4 - k],
                                            in1=xcf[:, ci, 0:hi - s0], op0=ALU.mult, op1=ALU.add), reads=[XB.b, ov.b, xcf.b], writes=[xcf.b])
                            kb.op("pool", lambda E: E.tensor_copy(out=xcb[:, :, 0:n], in_=xcf[:, :, 0:n]), reads=[xcf.b], writes=[xcb.b])
                            for jc in range(2):
                                vo = (d * 2 + jc) * 8
                                for g, dst in ((0, gr), (1, gi)):
                                    ps = nextps()
                                    for ic in range(2):
                                        kb.op("pe", lambda E: E.matmul(ps[:, 0:n], lhsT=GWt[:, (d * 2 + g) * 2 + ic, jc * 128:(jc + 1) * 128],
                                                                       rhs=xcb[:, ic, 0:n], start=(ic == 0), stop=(ic == 1)),
                                              reads=[GWt.b, xcb.b], writes=[ps.b])
                                    kb.op("act", lambda E: E.activation(out=dst[:, jc, 0:n], in_=ps[:, 0:n], func=AF.Sigmoid,
                                                                        bias=ov[:, vo + 5 + g:vo + 6 + g]), reads=[ps.b, ov.b], writes=[dst.b])
                                kb.op("act", lambda E: E.activation(out=aa[:, jc, 0:n], in_=gr[:, jc, 0:n], func=AF.Exp, scale=spn[:, d * 2 + jc:d * 2 + jc + 1]),
                                      reads=[gr.b, spn.b], writes=[aa.b])
                            kb.op("dve", lambda E: E.tensor_tensor(out=uu[:, :, 0:n], in0=aa[:, :, 0:n], in1=aa[:, :, 0:n], op=ALU.mult), reads=[aa.b], writes=[uu.b])
                            kb.op("dve", lambda E: E.tensor_scalar(out=uu[:, :, 0:n], in0=uu[:, :, 0:n], scalar1=-1.0, scalar2=1.0, op0=ALU.mult, op1=ALU.add),
                                  reads=[uu.b], writes=[uu.b])
                            kb.op("dve", lambda E: E.tensor_scalar(out=uu[:, :, 0:n], in0=uu[:, :, 0:n], scalar1=1e-30, scalar2=None, op0=ALU.max),
                                  reads=[uu.b], writes=[uu.b])
                            kb.op("act", lambda E: E.activation(out=uu[:, :, 0:n], in_=uu[:, :, 0:n], func=AF.Sqrt), reads=[uu.b], writes=[uu.b])
                            kb.op("dve", lambda E: E.tensor_tensor(out=uu[:, :, 0:n], in0=uu[:, :, 0:n], in1=gi[:, :, 0:n], op=ALU.mult), reads=[uu.b, gi.b], writes=[uu.b])
                            kb.op("dve", lambda E: E.tensor_tensor(out=uu[:, :, 0:n], in0=uu[:, :, 0:n], in1=xcf[:, :, 0:n], op=ALU.mult), reads=[uu.b, xcf.b], writes=[uu.b])
                            h = hh[blk_i % 2]
                            for ci in range(2):
                                if hprev is None:
                                    init = zero
                                else:
                                    hp, pn = hprev
                                    init = hp[:, ci, pn - 1:pn] if d == 0 else hp[:, ci, 0:1]
                                if d == 0:
                                    kb.op("dve", lambda E: E.tensor_tensor_scan(out=h[:, ci, 0:n], data0=aa[:, ci, 0:n], data1=uu[:, ci, 0:n],
                                                                                initial=init, op0=ALU.mult, op1=ALU.add),
                                          reads=[aa.b, uu.b, cst.b] + ([hprev[0].b] if hprev else []), writes=[h.b])
                                else:
                                    kb.op("dve", lambda E: E.tensor_tensor_scan(out=h[:, ci, 0:n][:, ::-1], data0=aa[:, ci, 0:n][:, ::-1],
                                                                                data1=uu[:, ci, 0:n][:, ::-1], initial=init, op0=ALU.mult, op1=ALU.add),
                                          reads=[aa.b, uu.b, cst.b] + ([hprev[0].b] if hprev else []), writes=[h.b])
                            hsc_v = HSC.ap.rearrange("(c p) s -> p c s", p=128)
                            if d == 0:
                                kb.dma("act", hsc_v[:, :, s0:s1], h[:, :, 0:n], HSC.b, h.b)
                            else:
                                kb.dma("sp", hfw[:, :, 0:n], hsc_v[:, :, s0:s1], hfw.b, HSC.b)
                                kb.op("pool", lambda E: E.tensor_tensor(out=hfw[:, :, 0:n], in0=hfw[:, :, 0:n], in1=h[:, :, 0:n], op=ALU.add),
                                      reads=[h.b, hfw.b], writes=[hfw.b])
                                kb.dma("act", hsc_v[:, :, s0:s1], hfw[:, :, 0:n], HSC.b, hfw.b)
                            hprev = (h, n)
                            blk_i += 1
                for ci in range(2):
                    kb.dma("sp", RM[:, 0:CTX], hsc_v[:, ci, 0:CTX], RM.b, HSC.b)
                    kb.dma("act", hs_v[:, ci, b, 0:CTX], RM[:, 0:CTX], HS.b, RM.b)
                    kb.dma("sp", RM[:], hsc_v[:, ci, CTX:LS], RM.b, HSC.b)
                    kb.op("pool", lambda E: E.tensor_copy(out=XB[:, 0, 0:SEQ].rearrange("p (r c) -> p c r", c=GW),
                                                          in_=RM[:].rearrange("p (c r) -> p c r", c=GW)), reads=[RM.b], writes=[XB.b])
                    kb.dma("act", hs_v[:, ci, b, CTX:LS], XB[:, 0, 0:SEQ], HS.b, XB.b)
            kb.barrier()
            kb.close_scope()

    E05 = math.exp(-0.5)
    pt_v = PT.ap.rearrange("(c p) (b s) -> p c b s", p=128, b=2)

    def scan_blocks(nb):
        out = []
        for (S0, S1) in ((0, CTX), (CTX, LS)):
            for q0 in range(0, S1 - S0, nb):
                out.append((S0, S1, q0, min(nb, S1 - S0 - q0)))
        return out

    def load_scan(dst, tmp, rows_ap_fn, d, S0, S1, q0, n, halo, npart):
        if d == 0:
            lo = S0 + q0
            h = min(halo, q0)
            if h < halo:
                kb.op("pool", lambda E: E.memset(dst[0:npart, :, 0:halo - h], 0.0), writes=[dst.b])
            kb.dma("sp", dst[0:npart, :, halo - h:halo + n], rows_ap_fn(lo - h, lo + n), dst.b, PT.b)
        else:
            hi = S1 - q0
            h = min(halo, q0)
            if h < halo:
                kb.op("pool", lambda E: E.memset(tmp[0:npart, :, n + h:n + halo], 0.0), writes=[tmp.b])
            kb.dma("sp", tmp[0:npart, :, 0:n + h], rows_ap_fn(hi - n, hi + h), tmp.b, PT.b)
            kb.op("dve", lambda E: E.tensor_copy(out=dst[0:npart, :, 0:n + halo], in_=tmp[0:npart, :, 0:n + halo][:, :, ::-1]),
                  reads=[tmp.b], writes=[dst.b])

    def store_scan(dr, rows_ap_fn, src, tmp, d, S0, S1, q0, n, npart):
        if d == 0:
            kb.dma("act", rows_ap_fn(S0 + q0, S0 + q0 + n), src[0:npart, 0:n], dr.b, src.b)
        else:
            kb.op("dve", lambda E: E.tensor_copy(out=tmp[0:npart, 0:n], in_=src[0:npart, 0:n][:, ::-1]), reads=[src.b], writes=[tmp.b])
            kb.dma("act", rows_ap_fn(S1 - q0 - n, S1 - q0), tmp[0:npart, 0:n], dr.b, tmp.b)

    def mixer_rwkv(e):
        C = 64
        NBK = 512
        with ExitStack() as st:
            kb.open_scope()
            ev = Tl(kb, st, "ev", [128, 96])
            kb.dma("sp", ev[:], evec_d.ap[:, e * 96:(e + 1) * 96], ev.b, None)
            lwt = Tl(kb, st, "lwt", [96, 8, 64])
            for i in range(8):
                kb.dma("sp", lwt[:, i, :], lora_d.ap[(e * 8 + i) * 96:(e * 8 + i + 1) * 96, :], lwt.b, None)
            omk = Tl(kb, st, "omk", [64, 4])
            for i in range(4):
                kb.op("dve", lambda E: E.tensor_scalar(out=omk[:, i:i + 1], in0=ev[0:64, 40 + i * 8 + 6:40 + i * 8 + 7], scalar1=-1.0, scalar2=1.0,
                                                       op0=ALU.mult, op1=ALU.add), reads=[ev.b], writes=[omk.b])
            MK = Tl(kb, st, "MK", [64, 128])
            ML = Tl(kb, st, "ML", [64, 64])
            on64 = Tl(kb, st, "on64", [64, NBK])
            kb.op("pool", lambda E: E.memset(on64[:], 1.0), writes=[on64.b])
            kb.op("pool", lambda E: E.affine_select(out=MK[:, 0:64], in_=on64[:, 0:64], pattern=[[1, 64]], compare_op=ALU.is_gt, fill=0.0, base=0,
                                                    channel_multiplier=-1), reads=[on64.b], writes=[MK.b])
            kb.op("pool", lambda E: E.affine_select(out=MK[:, 64:128], in_=on64[:, 0:64], pattern=[[1, 64]], compare_op=ALU.is_ge, fill=0.0, base=0,
                                                    channel_multiplier=-1), reads=[on64.b], writes=[MK.b])
            kb.op("pool", lambda E: E.affine_select(out=ML[:], in_=on64[:, 0:64], pattern=[[-1, 64]], compare_op=ALU.is_gt, fill=0.0, base=0,
                                                    channel_multiplier=1), reads=[on64.b], writes=[ML.b])
            W = []
            for hh in range(2):
                w = {}
                for nm, shp in (("F", [64, 3, NBK + 1]), ("G", [64, 3, NBK + 1]), ("FL", [96, 2, NBK + 1]), ("GL", [96, 2, NBK + 1]),
                                ("f", [64, 3, NBK]), ("fl", [96, 2, NBK]), ("t1", [64, 3, NBK]), ("tl", [96, 2, NBK]),
                                ("lgw", [64, NBK]), ("a", [64, NBK]), ("kap", [64, NBK]), ("kp", [64, NBK]), ("bet", [64, NBK]), ("cs", [64, NBK]),
                                ("tA", [64, NBK]), ("tB", [64, NBK]), ("OB", [64, NBK]), ("OT", [64, NBK]),
                                ("lw", [64, C]), ("lm", [64, C]), ("g", [64, C]), ("gi", [64, C]), ("gm", [64, C]),
                                ("KR", [64, 2 * C]), ("kt", [64, C]), ("bt", [64, C]), ("AA0", [64, 128]), ("AA1", [64, 128]), ("BbT", [64, C]),
                                ("PBm", [64, 128]), ("X", [64, 128]), ("TM", [64, 192]), ("W2n", [64, 64]), ("RhT", [64, C]), ("MTn", [64, 64]),
                                ("Ha", [64, 64]), ("Hb", [64, 64]), ("ht", [64, 64])):
                    w[nm] = Tl(kb, st, f"{nm}{hh}", shp)
                W.append(w)
            zero = cst[0:64, 2:3]

            def mm(ps_ap, lhsT, rhs, R, Wb, start=True, stop=True):
                kb.op("pe", lambda E: E.matmul(ps_ap, lhsT=lhsT, rhs=rhs, start=start, stop=stop), reads=R, writes=Wb)

            def dve_tt(out, in0, in1, op, R, Wb):
                kb.op("dve", lambda E: E.tensor_tensor(out=out, in0=in0, in1=in1, op=op), reads=R, writes=Wb)

            for b in range(2):
                for d in range(2):
                    for hh in range(2):
                        kb.op("pool", lambda E: E.memset(W[hh]["Ha"][:], 0.0), writes=[W[hh]["Ha"].b])
                    Hcur = [W[0]["Ha"], W[1]["Ha"]]
                    Hnxt = [W[0]["Hb"], W[1]["Hb"]]
                    for (S0, S1, q0, n) in scan_blocks(NBK):
                        for hh in range(2):
                            w = W[hh]
                            vo = 40 + (d * 2 + hh) * 8
                            col = lambda j: ev[0:64, vo + j:vo + j + 1]
                            load_scan(w["F"], w["G"], lambda lo, hi: pt_v[hh * 64:hh * 64 + 64, 0:3, b, lo:hi], d, S0, S1, q0, n, 1, 64)
                            load_scan(w["FL"], w["GL"], lambda lo, hi: pt_v[0:96, 3:5, b, lo:hi], d, S0, S1, q0, n, 1, 96)
                            F, FL, f, fl, t1, tl = w["F"], w["FL"], w["f"], w["fl"], w["t1"], w["tl"]
                            dve_tt(t1[:, :, 0:n], F[:, :, 0:n], F[:, :, 1:n + 1], ALU.subtract, [F.b], [t1.b])
                            for q in range(3):
                                kb.op("dve", lambda E: E.scalar_tensor_tensor(out=f[:, q, 0:n], in0=t1[:, q, 0:n], scalar=col(q), in1=F[:, q, 1:n + 1],
                                                                              op0=ALU.mult, op1=ALU.add), reads=[t1.b, F.b, ev.b], writes=[f.b])
                            dve_tt(tl[:, :, 0:n], FL[:, :, 0:n], FL[:, :, 1:n + 1], ALU.subtract, [FL.b], [tl.b])
                            for q in range(2):
                                kb.op("dve", lambda E: E.scalar_tensor_tensor(out=fl[:, q, 0:n], in0=tl[:, q, 0:n], scalar=ev[0:96, 72 + d * 2 + q:73 + d * 2 + q],
                                                                              in1=FL[:, q, 1:n + 1], op0=ALU.mult, op1=ALU.add), reads=[tl.b, FL.b, ev.b], writes=[fl.b])
                            kb.op("act", lambda E: E.activation(out=tl[:, 0, 0:n], in_=fl[:, 0, 0:n], func=AF.Tanh), reads=[fl.b], writes=[tl.b])
                            ps = nextps()
                            mm(ps[0:64, 0:n], lwt[:, (d * 2 + hh) * 2 + 0, :], tl[:, 0, 0:n], [lwt.b, tl.b], [ps.b])
                            kb.op("act", lambda E: E.activation(out=w["lgw"][:, 0:n], in_=ps[0:64, 0:n], func=AF.Sigmoid, bias=col(3)), reads=[ps.b, ev.b], writes=[w["lgw"].b])
                            ps = nextps()
                            mm(ps[0:64, 0:n], lwt[:, (d * 2 + hh) * 2 + 1, :], fl[:, 1, 0:n], [lwt.b, fl.b], [ps.b])
                            kb.op("act", lambda E: E.activation(out=w["a"][:, 0:n], in_=ps[0:64, 0:n], func=AF.Sigmoid, bias=col(4)), reads=[ps.b, ev.b], writes=[w["a"].b])
                            kb.op("act", lambda E: E.activation(out=w["tA"][:, 0:n], in_=f[:, 1, 0:n], func=AF.Square, scale=col(5)), reads=[f.b, ev.b], writes=[w["tA"].b])
                            ps = nextps()
                            mm(ps[0:64, 0:n], on64[:, 0:64], w["tA"][:, 0:n], [on64.b, w["tA"].b], [ps.b])
                            kb.op("act", lambda E: E.activation(out=w["tB"][:, 0:n], in_=ps[0:64, 0:n], func=AF.Sqrt), reads=[ps.b], writes=[w["tB"].b])
                            kb.op("dve", lambda E: E.tensor_scalar(out=w["tB"][:, 0:n], in0=w["tB"][:, 0:n], scalar1=1e-12, scalar2=None, op0=ALU.max), reads=[w["tB"].b], writes=[w["tB"].b])
                            kb.op("dve", lambda E: E.reciprocal(out=w["tB"][:, 0:n], in_=w["tB"][:, 0:n]), reads=[w["tB"].b], writes=[w["tB"].b])
                            kb.op("dve", lambda E: E.scalar_tensor_tensor(out=w["kap"][:, 0:n], in0=f[:, 1, 0:n], scalar=col(5), in1=w["tB"][:, 0:n], op0=ALU.mult, op1=ALU.mult),
                                  reads=[f.b, ev.b, w["tB"].b], writes=[w["kap"].b])
                            kb.op("act", lambda E: E.activation(out=w["tA"][:, 0:n], in_=w["a"][:, 0:n], func=AF.Identity, scale=col(6), bias=omk[:, d * 2 + hh:d * 2 + hh + 1]),
                                  reads=[w["a"].b, ev.b, omk.b], writes=[w["tA"].b])
                            dve_tt(w["kp"][:, 0:n], f[:, 1, 0:n], w["tA"][:, 0:n], ALU.mult, [f.b, w["tA"].b], [w["kp"].b])
                            dve_tt(w["bet"][:, 0:n], w["kap"][:, 0:n], w["a"][:, 0:n], ALU.mult, [w["kap"].b, w["a"].b], [w["bet"].b])
                            kb.op("dve", lambda E: E.scalar_tensor_tensor(out=w["tA"][:, 0:n], in0=f[:, 0, 0:n], scalar=col(7), in1=w["kp"][:, 0:n], op0=ALU.mult, op1=ALU.mult),
                                  reads=[f.b, ev.b, w["kp"].b], writes=[w["tA"].b])
                            ps = nextps()
                            mm(ps[0:64, 0:n], on64[:, 0:64], w["tA"][:, 0:n], [on64.b, w["tA"].b], [ps.b])
                            dve_tt(w["tB"][:, 0:n], ps[0:64, 0:n], f[:, 2, 0:n], ALU.mult, [ps.b, f.b], [w["tB"].b])
                            store_scan(BN[d], lambda lo, hi: BN[d].ap[hh * 64:hh * 64 + 64, b * LS + lo:b * LS + hi], w["tB"], w["OT"], d, S0, S1, q0, n, 64)
                            kb.op("dve", lambda E: E.tensor_tensor_scan(out=w["cs"][:, 0:n], data0=on64[:, 0:n], data1=w["lgw"][:, 0:n], initial=0.0,
                                                                        op0=ALU.mult, op1=ALU.add), reads=[on64.b, w["lgw"].b], writes=[w["cs"].b])
                        for c0 in (range(0, n, C) if "nochunk" not in DBG else []):
                            for hh in range(2):
                                w = W[hh]
                                f = w["f"]
                                H = Hcur[hh]
                                Hn = Hnxt[hh]
                                cs_ = slice(c0, c0 + C)
                                off = w["cs"][:, c0 - 1:c0] if c0 > 0 else zero
                                kb.op("dve", lambda E: E.tensor_scalar(out=w["lw"][:], in0=w["cs"][:, cs_], scalar1=off, scalar2=-E05, op0=ALU.subtract, op1=ALU.mult),
                                      reads=[w["cs"].b, cst.b], writes=[w["lw"].b])
                                kb.op("dve", lambda E: E.scalar_tensor_tensor(out=w["lm"][:], in0=w["lgw"][:, cs_], scalar=E05, in1=w["lw"][:], op0=ALU.mult, op1=ALU.add),
                                      reads=[w["lgw"].b, w["lw"].b], writes=[w["lm"].b])
                                kb.op("act", lambda E: E.activation(out=w["g"][:], in_=w["lw"][:], func=AF.Exp), reads=[w["lw"].b], writes=[w["g"].b])
                                kb.op("act", lambda E: E.activation(out=w["gi"][:], in_=w["lw"][:], func=AF.Exp, scale=-1.0), reads=[w["lw"].b], writes=[w["gi"].b])
                                kb.op("act", lambda E: E.activation(out=w["gm"][:], in_=w["lm"][:], func=AF.Exp), reads=[w["lm"].b], writes=[w["gm"].b])
                                KR, kt, bt = w["KR"], w["kt"], w["bt"]
                                dve_tt(KR[:, 0:C], w["kap"][:, cs_], w["gm"][:], ALU.mult, [w["kap"].b, w["gm"].b], [KR.b])
                                dve_tt(KR[:, C:2 * C], f[:, 0, cs_], w["g"][:], ALU.mult, [f.b, w["g"].b], [KR.b])
                                dve_tt(kt[:], w["kp"][:, cs_], w["gi"][:], ALU.mult, [w["kp"].b, w["gi"].b], [kt.b])
                                dve_tt(bt[:], w["bet"][:, cs_], w["gi"][:], ALU.mult, [w["bet"].b, w["gi"].b], [bt.b])
                                CK = int(DBG[DBG.index("ck") + 2]) if "ck" in DBG else 9
                                if CK < 2:
                                    continue
                                pa, pb, pc = nextps(), nextps(), nextps()
                                mm(pa[0:64, 0:128], bt[:], KR[:], [bt.b, KR.b], [pa.b])
                                mm(pb[0:64, 0:128], kt[:], KR[:], [kt.b, KR.b], [pb.b])
                                mm(pc[0:64, 0:64], KR[:, 0:C], bt[:], [KR.b, bt.b], [pc.b])
                                AA = w["AA0"]
                                dve_tt(AA[:, 0:64], pa[0:64, 0:64], MK[:, 0:64], ALU.mult, [pa.b, MK.b], [AA.b])
                                dve_tt(w["BbT"][:], pa[0:64, 64:128], MK[:, 64:128], ALU.mult, [pa.b, MK.b], [w["BbT"].b])
                                dve_tt(w["PBm"][:], pb[0:64, 0:128], MK[:], ALU.mult, [pb.b, MK.b], [w["PBm"].b])
                                dve_tt(AA[:, 64:128], pc[0:64, 0:64], ML[:], ALU.mult, [pc.b, ML.b], [AA.b])
                                if CK < 3:
                                    continue
                                pt = nextps()
                                for i, src in enumerate((KR[:, 0:C], kt[:], bt[:], f[:, 2, cs_])):
                                    srcb = [KR.b, kt.b, bt.b, f.b][i]
                                    kb.op("pe", lambda E: E.matmul(pt[0:64, i * 64:(i + 1) * 64], lhsT=src, rhs=ident[0:64, 0:64], start=True, stop=True), reads=[srcb, ident.b], writes=[pt.b])
                                X, TM = w["X"], w["TM"]
                                if "ck3a" in DBG:
                                    continue
                                evac(X[:, 0:64], pt[0:64, 0:64], [pt.b], [X.b])
                                evac(TM[:], pt[0:64, 64:256], [pt.b], [TM.b])
                                Kt_, Bt_, V_ = TM[:, 0:64], TM[:, 64:128], TM[:, 128:192]
                                if "ck3b" in DBG:
                                    continue
                                pk = nextps()
                                mm(pk[0:64, 0:64], w["PBm"][:, 0:64], V_, [w["PBm"].b, TM.b], [pk.b])
                                evac(X[:, 64:128], pk[0:64, 0:64], [pk.b], [X.b])
                                if CK < 4:
                                    continue
                                for lev in range(6):
                                    if lev > 0:
                                        AAn = w["AA1"] if AA is w["AA0"] else w["AA0"]
                                        pq = nextps()
                                        mm(pq[0:64, 0:64], AA[:, 64:128], AA[:, 0:64], [AA.b], [pq.b])
                                        if lev < 5:
                                            mm(pq[0:64, 64:128], AA[:, 0:64], AA[:, 64:128], [AA.b], [pq.b])
                                            evac(AAn[:], pq[0:64, 0:128], [pq.b], [AAn.b])
                                        else:
                                            evac(AAn[:, 0:64], pq[0:64, 0:64], [pq.b], [AAn.b])
                                        AA = AAn
                                    px = nextps()
                                    mm(px[0:64, 0:128], AA[:, 0:64], X[:], [AA.b, X.b], [px.b])
                                    dve_tt(X[:], X[:], px[0:64, 0:128], ALU.subtract if lev == 0 else ALU.add, [X.b, px.b], [X.b])
                                if CK < 5:
                                    continue
                                kb.op("dve", lambda E: E.tensor_scalar(out=w["W2n"][:], in0=X[:, 64:128], scalar1=-1.0, scalar2=None, op0=ALU.mult), reads=[X.b], writes=[w["W2n"].b])
                                W1 = X[:, 0:64]
                                pr = nextps()
                                mm(pr[0:64, 0:64], W1, w["BbT"][:], [X.b, w["BbT"].b], [pr.b])
                                dve_tt(w["RhT"][:], KR[:, C:2 * C], pr[0:64, 0:64], ALU.subtract, [KR.b, pr.b], [w["RhT"].b])
                                pm = nextps()
                                mm(pm[0:64, 0:64], W1, Bt_, [X.b, TM.b], [pm.b])
                                kb.op("dve", lambda E: E.tensor_scalar(out=w["MTn"][:], in0=pm[0:64, 0:64], scalar1=-1.0, scalar2=None, op0=ALU.mult), reads=[pm.b], writes=[w["MTn"].b])
                                pg = nextps()
                                mm(pg[0:64, 0:64], Kt_, V_, [TM.b], [pg.b], start=True, stop=False)
                                mm(pg[0:64, 0:64], Bt_, w["W2n"][:], [TM.b, w["W2n"].b], [pg.b], start=False, stop=False)
                                mm(pg[0:64, 0:64], w["MTn"][:], H[:], [w["MTn"].b, H.b], [pg.b], start=False, stop=True)
                                py = nextps()
                                mm(py[0:64, 0:64], V_, w["PBm"][:, 64:128], [TM.b, w["PBm"].b], [py.b], start=True, stop=False)
                                mm(py[0:64, 0:64], w["W2n"][:], w["BbT"][:], [w["W2n"].b, w["BbT"].b], [py.b], start=False, stop=False)
                                mm(py[0:64, 0:64], H[:], w["RhT"][:], [H.b, w["RhT"].b], [py.b], start=False, stop=True)
                                evac(w["OB"][:, cs_], py[0:64, 0:64], [py.b], [w["OB"].b])
                                dve_tt(w["ht"][:], H[:], pg[0:64, 0:64], ALU.add, [H.b, pg.b], [w["ht"].b])
                                kb.op("dve", lambda E: E.tensor_scalar(out=Hn[:], in0=w["ht"][:], scalar1=w["g"][:, C - 1:C], scalar2=None, op0=ALU.mult),
                                      reads=[w["ht"].b, w["g"].b], writes=[Hn.b])
                                Hcur[hh], Hnxt[hh] = Hn, H
                        for hh in range(2):
                            w = W[hh]
                            store_scan(YR[d], lambda lo, hi: YR[d].ap[hh * 64:hh * 64 + 64, b * LS + lo:b * LS + hi], w["OB"], w["OT"], d, S0, S1, q0, n, 64)
            kb.barrier()
            kb.close_scope()

    def mixer_ssd(e):
        C = 128
        NBK = 512
        with ExitStack() as st:
            kb.open_scope()
            ev = Tl(kb, st, "evs", [128, 96])
            kb.dma("sp", ev[:], evec_d.ap[:, e * 96:(e + 1) * 96], ev.b, None)
            dv = Tl(kb, st, "dv", [128, 8])
            kb.dma("sp", dv[0:64, :], dvec_d.ap[:, e * 8:(e + 1) * 8], dv.b, None)
            kb.dma("sp", dv[64:128, :], dvec_d.ap[:, e * 8:(e + 1) * 8], dv.b, None)
            s4 = Tl(kb, st, "s4", [4, 512])
            kb.dma("sp", s4[:], sel4_d.ap, s4.b, None)
            selt = Tl(kb, st, "seltm", [128, 12])
            kb.dma("sp", selt[:], sel_d.ap, selt.b, None)
            MU = Tl(kb, st, "MU", [128, 128])
            on = Tl(kb, st, "onS", [128, 128])
            kb.op("pool", lambda E: E.memset(on[:], 1.0), writes=[on.b])
            kb.op("pool", lambda E: E.affine_select(out=MU[:], in_=on[:], pattern=[[1, 128]], compare_op=ALU.is_ge, fill=0.0, base=0, channel_multiplier=-1),
                  reads=[on.b], writes=[MU.b])
            F = Tl(kb, st, "Fs", [128, 4, NBK + 3])
            G0 = Tl(kb, st, "G0s", [128, 4, NBK + 3])
            G1 = Tl(kb, st, "G1s", [128, 4, NBK + 3])
            Fd = Tl(kb, st, "Fd", [4, 1, NBK])
            Gd0 = Tl(kb, st, "Gd0", [4, 1, NBK])
            Gd1 = Tl(kb, st, "Gd1", [4, 1, NBK])
            xc = Tl(kb, st, "xcs", [128, 4, NBK])
            xs = Tl(kb, st, "xss", [128, 4, NBK])
            dtt = Tl(kb, st, "dtt", [4, NBK])
            dta = Tl(kb, st, "dta", [4, NBK])
            acol = Tl(kb, st, "acolS", [4, 2])
            acs = Tl(kb, st, "acs", [4, C])
            DTA = Tl(kb, st, "DTA", [128, 8])
            CBm = Tl(kb, st, "CBm", [128, C])
            Btok = Tl(kb, st, "Btok", [128, 128])
            sg = [Tl(kb, st, f"sgS{i}", [128, C]) for i in range(2)]
            MT = [Tl(kb, st, f"MTs{i}", [128, C]) for i in range(2)]
            gb = [Tl(kb, st, f"gbS{i}", [128, C]) for i in range(2)]
            rT = [Tl(kb, st, f"rTs{i}", [128, C]) for i in range(2)]
            al = [Tl(kb, st, f"alS{i}", [128, 2]) for i in range(2)]
            xdt = [Tl(kb, st, f"xdt{i}", [128, 64]) for i in range(2)]
            xdw = [Tl(kb, st, f"xdw{i}", [128, 64]) for i in range(2)]
            xdd = [Tl(kb, st, f"xdd{i}", [128, 64]) for i in range(2)]
            Hs = [Tl(kb, st, f"Hs{i}", [128, 64]) for i in range(4)]
            OB = [Tl(kb, st, f"OBs{i}", [64, NBK]) for i in range(4)]
            OT = Tl(kb, st, "OTs", [64, NBK])

            def mm(ps_ap, lhsT, rhs, R, Wb, start=True, stop=True):
                kb.op("pe", lambda E: E.matmul(ps_ap, lhsT=lhsT, rhs=rhs, start=start, stop=stop), reads=R, writes=Wb)

            def blend(dst, g0, g1, npart, width):
                kb.op("dve", lambda E: E.tensor_scalar(out=dst[0:npart, :, 0:width], in0=g0[0:npart, :, 0:width], scalar1=selt[0:npart, 10:11], scalar2=None, op0=ALU.mult),
                      reads=[g0.b, selt.b], writes=[dst.b])
                kb.op("dve", lambda E: E.scalar_tensor_tensor(out=dst[0:npart, :, 0:width], in0=g1[0:npart, :, 0:width], scalar=selt[0:npart, 11:12], in1=dst[0:npart, :, 0:width],
                                                              op0=ALU.mult, op1=ALU.add), reads=[g1.b, selt.b, dst.b], writes=[dst.b])

            for d in range(2):
                kb.op("act", lambda E: E.activation(out=acol[:, d:d + 1], in_=ev[0:4, 80 + d * 2 + 1:80 + d * 2 + 2], func=AF.Exp), reads=[ev.b], writes=[acol.b])
                kb.op("dve", lambda E: E.tensor_scalar(out=acol[:, d:d + 1], in0=acol[:, d:d + 1], scalar1=-1.0, scalar2=None, op0=ALU.mult), reads=[acol.b], writes=[acol.b])
                for hd in range(4):
                    kb.op("pool", lambda E: E.memset(Hs[hd][:], 0.0), writes=[Hs[hd].b])
                for (S0, S1, q0, n) in scan_blocks(NBK):
                    if d == 0:
                        lo, hi = S0 + q0, S0 + q0 + n
                        h = min(3, q0)
                        for bsel, Gx, Gdx in ((0, G0, Gd0), (1, G1, Gd1)):
                            if h < 3:
                                kb.op("pool", lambda E: E.memset(Gx[:, :, 0:3 - h], 0.0), writes=[Gx.b])
                            kb.dma("sp", Gx[:, :, 3 - h:3 + n], pt_v[:, 9:13, bsel, lo - h:hi], Gx.b, PT.b)
                            kb.dma("sp", Gdx[:, :, 0:n], pt_v[0:4, 13:14, bsel, lo:hi], Gdx.b, PT.b)
                        blend(F, G0, G1, 128, n + 3)
                        blend(Fd, Gd0, Gd1, 4, n)
                    else:
                        hi = S1 - q0
                        lo = hi - n
                        h = min(3, q0)
                        for bsel, Gx, Gdx in ((0, G0, Gd0), (1, G1, Gd1)):
                            if h < 3:
                                kb.op("pool", lambda E: E.memset(Gx[:, :, n + h:n + 3], 0.0), writes=[Gx.b])
                            kb.dma("sp", Gx[:, :, 0:n + h], pt_v[:, 9:13, bsel, lo:hi + h], Gx.b, PT.b)
                            kb.dma("sp", Gdx[:, :, 0:n], pt_v[0:4, 13:14, bsel, lo:hi], Gdx.b, PT.b)
                        blend(G0, G0, G1, 128, n + 3)
                        blend(Gd0, Gd0, Gd1, 4, n)
                        kb.op("dve", lambda E: E.tensor_copy(out=F[:, :, 0:n + 3], in_=G0[:, :, 0:n + 3][:, :, ::-1]), reads=[G0.b], writes=[F.b])
                        kb.op("dve", lambda E: E.tensor_copy(out=Fd[:, :, 0:n], in_=Gd0[:, :, 0:n][:, :, ::-1]), reads=[Gd0.b], writes=[Fd.b])
                    for q in range(4):
                        vo = d * 20 + q * 5
                        kb.op("act", lambda E: E.activation(out=xc[:, q, 0:n], in_=F[:, q, 3:3 + n], func=AF.Identity, scale=ev[:, vo + 3:vo + 4], bias=ev[:, vo + 4:vo + 5]),
                              reads=[F.b, ev.b], writes=[xc.b])
                        for k in range(1, 4):
                            kb.op("dve", lambda E: E.scalar_tensor_tensor(out=xc[:, q, 0:n], in0=F[:, q, 3 - k:3 - k + n], scalar=ev[:, vo + 3 - k:vo + 4 - k], in1=xc[:, q, 0:n],
                                                                          op0=ALU.mult, op1=ALU.add), reads=[F.b, ev.b, xc.b], writes=[xc.b])
                    kb.op("act", lambda E: E.activation(out=xs[:, :, 0:n], in_=xc[:, :, 0:n], func=AF.Silu), reads=[xc.b], writes=[xs.b])
                    kb.op("act", lambda E: E.activation(out=dtt[:, 0:n], in_=Fd[:, 0, 0:n], func=AF.Exp, bias=ev[0:4, 80 + d * 2:80 + d * 2 + 1]), reads=[Fd.b, ev.b], writes=[dtt.b])
                    kb.op("act", lambda E: E.activation(out=dtt[:, 0:n], in_=dtt[:, 0:n], func=AF.Ln, bias=cst[0:4, 1:2]), reads=[dtt.b, cst.b], writes=[dtt.b])
                    kb.op("dve", lambda E: E.tensor_scalar(out=dta[:, 0:n], in0=dtt[:, 0:n], scalar1=acol[:, d:d + 1], scalar2=None, op0=ALU.mult), reads=[dtt.b, acol.b], writes=[dta.b])
                    for c0 in range(0, n, C):
                        cs_ = slice(c0, c0 + C)
                        kb.op("dve", lambda E: E.tensor_tensor_scan(out=acs[:], data0=on[0:4, 0:C], data1=dta[:, cs_], initial=0.0, op0=ALU.mult, op1=ALU.add),
                              reads=[on.b, dta.b], writes=[acs.b])
                        pt = nextps()
                        kb.op("pe", lambda E: E.matmul(pt[:, 0:4], lhsT=dtt[:, cs_], rhs=ident[0:4, 0:4], start=True, stop=True), reads=[dtt.b, ident.b], writes=[pt.b])
                        kb.op("pe", lambda E: E.matmul(pt[:, 4:8], lhsT=acs[:], rhs=ident[0:4, 0:4], start=True, stop=True), reads=[acs.b, ident.b], writes=[pt.b])
                        evac(DTA[:], pt[:, 0:8], [pt.b], [DTA.b])
                        pcb = nextps()
                        mm(pcb[:, 0:C], xs[:, 2, cs_], xs[:, 3, cs_], [xs.b], [pcb.b])
                        kb.op("dve", lambda E: E.tensor_tensor(out=CBm[:], in0=pcb[:, 0:C], in1=MU[:], op=ALU.mult), reads=[pcb.b, MU.b], writes=[CBm.b])
                        pbt = nextps()
                        kb.op("pe", lambda E: E.matmul(pbt[:, 0:128], lhsT=xs[:, 2, cs_], rhs=ident[:], start=True, stop=True), reads=[xs.b, ident.b], writes=[pbt.b])
                        evac(Btok[:], pbt[:, 0:128], [pbt.b], [Btok.b])
                        for hd in range(4):
                            i2 = hd % 2
                            H = Hs[hd]
                            pab = nextps()
                            mm(pab[:, 0:C], s4[:, hd * 128:(hd + 1) * 128], acs[:], [s4.b, acs.b], [pab.b])
                            kb.op("dve", lambda E: E.tensor_scalar(out=sg[i2][:], in0=pab[:, 0:C], scalar1=DTA[:, 4 + hd:5 + hd], scalar2=0.0, op0=ALU.subtract, op1=ALU.min),
                                  reads=[pab.b, DTA.b], writes=[sg[i2].b])
                            kb.op("act", lambda E: E.activation(out=sg[i2][:], in_=sg[i2][:], func=AF.Exp), reads=[sg[i2].b], writes=[sg[i2].b])
                            kb.op("dve", lambda E: E.tensor_tensor(out=MT[i2][:], in0=sg[i2][:], in1=CBm[:], op=ALU.mult), reads=[sg[i2].b, CBm.b], writes=[MT[i2].b])
                            kb.op("act", lambda E: E.activation(out=gb[i2][:], in_=pab[:, 0:C], func=AF.Exp), reads=[pab.b], writes=[gb[i2].b])
                            kb.op("dve", lambda E: E.tensor_tensor(out=rT[i2][:], in0=xs[:, 3, cs_], in1=gb[i2][:], op=ALU.mult), reads=[xs.b, gb[i2].b], writes=[rT[i2].b])
                            kb.op("dve", lambda E: E.tensor_copy(out=al[i2][:, 0:1], in_=pab[:, C - 1:C]), reads=[pab.b], writes=[al[i2].b])
                            kb.op("act", lambda E: E.activation(out=al[i2][:, 1:2], in_=DTA[:, 4 + hd:5 + hd], func=AF.Exp, scale=-1.0, bias=al[i2][:, 0:1]),
                                  reads=[DTA.b, al[i2].b], writes=[al[i2].b])
                            pxt = nextps()
                            pb0 = (hd % 2) * 64
                            kb.op("pe", lambda E: E.matmul(pxt[:, 0:64], lhsT=xs[pb0:pb0 + 64, hd // 2, cs_], rhs=ident[pb0:pb0 + 64, pb0:pb0 + 64], start=True, stop=True), reads=[xs.b, ident.b], writes=[pxt.b])
                            kb.op("dve", lambda E: E.tensor_scalar(out=xdt[i2][:], in0=pxt[:, 0:64], scalar1=DTA[:, hd:hd + 1], scalar2=None, op0=ALU.mult), reads=[pxt.b, DTA.b], writes=[xdt[i2].b])
                            kb.op("dve", lambda E: E.tensor_scalar(out=xdd[i2][:], in0=pxt[:, 0:64], scalar1=dv[:, d * 4 + hd:d * 4 + hd + 1], scalar2=None, op0=ALU.mult), reads=[pxt.b, dv.b], writes=[xdd[i2].b])
                            kb.op("dve", lambda E: E.tensor_scalar(out=xdw[i2][:], in0=xdt[i2][:], scalar1=al[i2][:, 1:2], scalar2=None, op0=ALU.mult), reads=[xdt[i2].b, al[i2].b], writes=[xdw[i2].b])
                            py = nextps()
                            mm(py[0:64, 0:C], xdt[i2][:], MT[i2][:], [xdt[i2].b, MT[i2].b], [py.b], start=True, stop=False)
                            mm(py[0:64, 0:C], xdd[i2][:], ident[:], [xdd[i2].b, ident.b], [py.b], start=False, stop=False)
                            mm(py[0:64, 0:C], H[:], rT[i2][:], [H.b, rT[i2].b], [py.b], start=False, stop=True)
                            evac(OB[hd][:, cs_], py[0:64, 0:C], [py.b], [OB[hd].b])
                            ph = nextps()
                            mm(ph[:, 0:64], Btok[:], xdw[i2][:], [Btok.b, xdw[i2].b], [ph.b])
                            kb.op("act", lambda E: E.activation(out=gb[i2][:, 0:1], in_=al[i2][:, 0:1], func=AF.Exp), reads=[al[i2].b, rT[i2].b], writes=[gb[i2].b])
                            kb.op("dve", lambda E: E.scalar_tensor_tensor(out=H[:], in0=H[:], scalar=gb[i2][:, 0:1], in1=ph[:, 0:64], op0=ALU.mult, op1=ALU.add),
                                  reads=[H.b, gb[i2].b, ph.b], writes=[H.b])
                    for hd in range(4):
                        r0 = (hd // 2) * 128 + (hd % 2) * 64
                        store_scan(YM[d], lambda lo_, hi_: YM[d].ap[r0:r0 + 64, lo_:hi_], OB[hd], OT, d, S0, S1, q0, n, 64)
            kb.barrier()
            kb.close_scope()

    def phase_C_even(l, e):
        with ExitStack() as st:
            kb.open_scope()
            W = Tl(kb, st, "WoutE", [128, 3, D], BF16)
            load_w_bf16(W, lambda k: W[:, k, :], lambda k: woute_d.ap[e * 384 + k * 128:e * 384 + (k + 1) * 128, :], 3)
            G2w = Tl(kb, st, "G2w", [128, 2, 128], BF16)
            load_w_bf16(G2w, lambda k: G2w[:, k, :], lambda k: g2_d.ap[e * 256 + k * 128:e * 256 + (k + 1) * 128, :], 2)
            ev = Tl(kb, st, "evC", [128, 96])
            kb.dma("sp", ev[:], evec_d.ap[:, e * 96:(e + 1) * 96], ev.b, None)
            selt = Tl(kb, st, "seltC", [128, 12])
            kb.dma("sp", selt[:], sel_d.ap, selt.b, None)
            on = Tl(kb, st, "onC", [128, 128])
            bo = Tl(kb, st, "boC", [128, 128])
            kb.op("pool", lambda E: E.memset(on[:], 1.0), writes=[on.b])
            kb.op("pool", lambda E: E.memset(bo[:], 0.0), writes=[bo.b])
            kb.op("pool", lambda E: E.memset(bo[0:64, 0:64], 1.0), writes=[bo.b])
            kb.op("pool", lambda E: E.memset(bo[64:128, 64:128], 1.0), writes=[bo.b])
            kb.op("dve", lambda E: E.memset(cst[:, 3:4], 1e-5), writes=[cst.b])
            kb.op("dve", lambda E: E.memset(cst[:, 4:5], 64e-5), writes=[cst.b])
            mns = Tl(kb, st, "mns", [128, 4])
            for q in range(2):
                for b in range(2):
                    kb.op("dve", lambda E: E.tensor_tensor(out=mns[:, q * 2 + b:q * 2 + b + 1], in0=ev[:, 78 + q:79 + q], in1=selt[:, 10 + b:11 + b], op=ALU.mult),
                          reads=[ev.b, selt.b], writes=[mns.b])
            ym = [[Tl(kb, st, f"ym{i}{d}", [128, 2, T]) for d in range(2)] for i in range(2)]
            zz = [Tl(kb, st, f"zz{i}", [128, 2, T]) for i in range(2)]
            gl = [Tl(kb, st, f"glC{i}", [128, 2, T]) for i in range(2)]
            yr = [[Tl(kb, st, f"yr{i}{d}", [128, T]) for d in range(2)] for i in range(2)]
            bn = [[Tl(kb, st, f"bn{i}{d}", [128, T]) for d in range(2)] for i in range(2)]
            sq2 = Tl(kb, st, "sq2C", [128, 2, T])
            rs = Tl(kb, st, "rsC", [128, T])
            t1 = Tl(kb, st, "t1E", [128, T])
            t2 = Tl(kb, st, "t2E", [128, T])
            sgl = Tl(kb, st, "sglC", [128, 2, T], BF16)
            mbf = [Tl(kb, st, f"mE{i}", [128, 3, T], BF16) for i in range(2)]
            stg = [Tl(kb, st, f"stE{i}", [128, FC, T]) for i in range(2)]
            ymv = [YM[d].ap.rearrange("(q p) s -> p q s", p=128) for d in range(2)]
            for ti, tile in enumerate(tiles):
                n, b, pos = tile["n"], tile["b"], tile["pos"]
                i = ti % 2
                m, sg = mbf[i], stg[i]
                for d in range(2):
                    kb.dma("sp", ym[i][d][:, :, 0:n], ymv[d][:, :, pos:pos + n], ym[i][d].b, YM[d].b)
                    kb.dma("sp", yr[i][d][:, 0:n], YR[d].ap[:, b * LS + pos:b * LS + pos + n], yr[i][d].b, YR[d].b)
                    kb.dma("sp", bn[i][d][:, 0:n], BN[d].ap[:, b * LS + pos:b * LS + pos + n], bn[i][d].b, BN[d].b)
                kb.dma("sp", zz[i][:, :, 0:n], scr_ap(PT, 2, tile, 7), zz[i].b, PT.b)
                kb.dma("sp", gl[i][:, :, 0:n], scr_ap(PT, 2, tile, 5), gl[i].b, PT.b)
                y0 = ym[i][0]
                kb.op("dve", lambda E: E.tensor_tensor(out=y0[:, :, 0:n], in0=y0[:, :, 0:n], in1=ym[i][1][:, :, 0:n], op=ALU.add), reads=[y0.b, ym[i][1].b], writes=[y0.b])
                kb.op("act", lambda E: E.activation(out=zz[i][:, :, 0:n], in_=zz[i][:, :, 0:n], func=AF.Silu), reads=[zz[i].b], writes=[zz[i].b])
                kb.op("dve", lambda E: E.tensor_tensor(out=y0[:, :, 0:n], in0=y0[:, :, 0:n], in1=zz[i][:, :, 0:n], op=ALU.mult), reads=[y0.b, zz[i].b], writes=[y0.b])
                kb.op("act", lambda E: E.activation(out=sq2[:, :, 0:n], in_=y0[:, :, 0:n], func=AF.Square), reads=[y0.b], writes=[sq2.b])
                ps = nextps()
                for q in range(2):
                    kb.op("pe", lambda E: E.matmul(ps[:, 0:n], lhsT=on[:], rhs=sq2[:, q, 0:n], start=(q == 0), stop=(q == 1)), reads=[on.b, sq2.b], writes=[ps.b])
                kb.op("act", lambda E: E.activation(out=rs[:, 0:n], in_=ps[:, 0:n], func=AF.Sqrt, scale=1.0 / 256, bias=cst[:, 3:4]), reads=[ps.b, cst.b], writes=[rs.b])
                kb.op("dve", lambda E: E.reciprocal(out=rs[:, 0:n], in_=rs[:, 0:n]), reads=[rs.b], writes=[rs.b])
                for q in range(2):
                    kb.op("dve", lambda E: E.scalar_tensor_tensor(out=m[:, q, 0:n], in0=y0[:, q, 0:n], scalar=mns[:, q * 2 + b:q * 2 + b + 1], in1=rs[:, 0:n],
                                                                  op0=ALU.mult, op1=ALU.mult), reads=[y0.b, mns.b, rs.b], writes=[m.b])
                yy = yr[i][0]
                kb.op("dve", lambda E: E.tensor_tensor(out=yy[:, 0:n], in0=yy[:, 0:n], in1=yr[i][1][:, 0:n], op=ALU.add), reads=[yy.b, yr[i][1].b], writes=[yy.b])
                ps = nextps()
                kb.op("pe", lambda E: E.matmul(ps[:, 0:n], lhsT=bo[:], rhs=yy[:, 0:n], start=True, stop=True), reads=[bo.b, yy.b], writes=[ps.b])
                kb.op("dve", lambda E: E.scalar_tensor_tensor(out=t1[:, 0:n], in0=ps[:, 0:n], scalar=-1.0 / 64, in1=yy[:, 0:n], op0=ALU.mult, op1=ALU.add),
                      reads=[ps.b, yy.b], writes=[t1.b])
                kb.op("act", lambda E: E.activation(out=t2[:, 0:n], in_=t1[:, 0:n], func=AF.Square), reads=[t1.b], writes=[t2.b])
                ps = nextps()
                kb.op("pe", lambda E: E.matmul(ps[:, 0:n], lhsT=bo[:], rhs=t2[:, 0:n], start=True, stop=True), reads=[bo.b, t2.b], writes=[ps.b])
                kb.op("act", lambda E: E.activation(out=t2[:, 0:n], in_=ps[:, 0:n], func=AF.Sqrt, scale=1.0 / 64, bias=cst[:, 4:5]), reads=[ps.b, cst.b], writes=[t2.b])
                kb.op("dve", lambda E: E.reciprocal(out=t2[:, 0:n], in_=t2[:, 0:n]), reads=[t2.b], writes=[t2.b])
                kb.op("dve", lambda E: E.tensor_tensor(out=t1[:, 0:n], in0=t1[:, 0:n], in1=t2[:, 0:n], op=ALU.mult), reads=[t1.b, t2.b], writes=[t1.b])
                kb.op("act", lambda E: E.activation(out=t1[:, 0:n], in_=t1[:, 0:n], func=AF.Identity, scale=ev[:, 76:77], bias=ev[:, 77:78]), reads=[t1.b, ev.b], writes=[t1.b])
                kb.op("dve", lambda E: E.tensor_tensor(out=t1[:, 0:n], in0=t1[:, 0:n], in1=bn[i][0][:, 0:n], op=ALU.add), reads=[t1.b, bn[i][0].b], writes=[t1.b])
                kb.op("dve", lambda E: E.tensor_tensor(out=t1[:, 0:n], in0=t1[:, 0:n], in1=bn[i][1][:, 0:n], op=ALU.add), reads=[t1.b, bn[i][1].b], writes=[t1.b])
                kb.op("act", lambda E: E.activation(out=sgl[:, :, 0:n], in_=gl[i][:, :, 0:n], func=AF.Sigmoid), reads=[gl[i].b], writes=[sgl.b])
                ps = nextps()
                for q in range(2):
                    kb.op("pe", lambda E: E.matmul(ps[:, 0:n], lhsT=G2w[:, q, :], rhs=sgl[:, q, 0:n], start=(q == 0), stop=(q == 1)), reads=[G2w.b, sgl.b], writes=[ps.b])
                kb.op("dve", lambda E: E.tensor_tensor(out=m[:, 2, 0:n], in0=t1[:, 0:n], in1=ps[:, 0:n], op=ALU.mult), reads=[t1.b, ps.b], writes=[m.b])
                out_proj(W, 3, [128, 128, 128], m, sg, n, tile)
            kb.barrier()
            kb.close_scope()

    ECH = [(0, 128), (128, 128), (256, 128), (384, 96), (480, 96), (576, 128), (704, 128), (832, 128), (960, 128), (1088, 128),
           (1216, 128), (1344, 128), (1472, 128), (1600, 4)]

    e_i = o_i = 0
    for l, lt in enumerate(LT):
        pend = (l - 1) if l > 0 else None
        stop = cfg.get("stop", "")
        if stop == "setup":
            break
        if lt == "O":
            phase_A(l, wino_d, o_i * D, 512, pend)
            if stop == "A":
                break
            mixer_rglru(o_i)
            if stop == "mix":
                break
            phase_C_odd(l, o_i)
            o_i += 1
        else:
            phase_A(l, wine_d, e_i * D, 1604, pend, chunks=ECH)
            if stop == "A":
                break
            ev_dve[0] = True
            if "nossd" not in DBG:
                mixer_ssd(e_i)
            if "norwkv" not in DBG:
                mixer_rwkv(e_i)
            ev_dve[0] = False
            if stop == "mix":
                break
            phase_C_even(l, e_i)
            e_i += 1
        if stop == "C":
            break
        if stop == "AR":
            break
        phase_D(l)

    with ExitStack() as st:
        kb.open_scope()
        selt = Tl(kb, st, "selt", [128, 12])
        g2c = Tl(kb, st, "g2c", [128, FC])
        kb.dma("sp", selt[:], sel_d.ap, selt.b, None)
        o5 = ((NL - 1) * 6 + 5) * FC * 3
        g2v = lambda j: mod[:, o5:o5 + FC * 3].rearrange("p (f j) -> p f j", j=3)[:, :, j]
        kb.op("dve", lambda E: E.tensor_scalar(out=g2c[:], in0=g2v(0), scalar1=selt[:, 8:9], scalar2=None, op0=ALU.mult), reads=[mod.b, selt.b], writes=[g2c.b])
        kb.op("dve", lambda E: E.scalar_tensor_tensor(out=g2c[:], in0=g2v(1), scalar=selt[:, 9:10], in1=g2c[:], op0=ALU.mult, op1=ALU.add),
              reads=[mod.b, selt.b, g2c.b], writes=[g2c.b])
        xts = [Tl(kb, st, f"xtF{i}", [128, FC, T]) for i in range(2)]
        rts = [Tl(kb, st, f"rtF{i}", [128, FC, T]) for i in range(2)]
        xa = Tl(kb, st, "xaF", [128, FC, T])
        ra = Tl(kb, st, "raF", [128, FC, T])
        sq = Tl(kb, st, "sqF", [128, FC, T], BF16)
        rs = Tl(kb, st, "rsF", [128, T])
        li = 0
        for t0 in range(0, TR, T):
            for r in range(8):
                xt, rt = xts[li % 2], rts[li % 2]
                li += 1
                kb.dma("sp", xt[:], lat_view(XT)[:, r, :, t0:t0 + T], xt.b, XT.bs(r * D, (r + 1) * D))
                kb.dma("sp", rt[:], lat_view(REDL)[:, r, :, t0:t0 + T], rt.b, REDL.bs(r * D, (r + 1) * D))
                if r == 0:
                    kb.op("dve", lambda E: E.tensor_scalar(out=xa[:], in0=xt[:], scalar1=selt[:, 0:1], scalar2=None, op0=ALU.mult), reads=[xt.b, selt.b], writes=[xa.b])
                    kb.op("pool", lambda E: E.tensor_scalar(out=ra[:], in0=rt[:], scalar1=selt[:, 0:1], scalar2=None, op0=ALU.mult), reads=[rt.b, selt.b], writes=[ra.b])
                else:
                    kb.op("dve", lambda E: E.scalar_tensor_tensor(out=xa[:], in0=xt[:], scalar=selt[:, r:r + 1], in1=xa[:], op0=ALU.mult, op1=ALU.add),
                          reads=[xt.b, selt.b, xa.b], writes=[xa.b])
                    kb.op("dve", lambda E: E.scalar_tensor_tensor(out=ra[:], in0=rt[:], scalar=selt[:, r:r + 1], in1=ra[:], op0=ALU.mult, op1=ALU.add),
                          reads=[rt.b, selt.b, ra.b], writes=[ra.b])
            for fc in range(FC):
                kb.op("dve", lambda E: E.scalar_tensor_tensor(out=xa[:, fc, :], in0=ra[:, fc, :], scalar=g2c[:, fc:fc + 1], in1=xa[:, fc, :],
                                                              op0=ALU.mult, op1=ALU.add), reads=[ra.b, g2c.b, xa.b], writes=[xa.b])
            kb.op("act", lambda E: E.activation(out=sq[:], in_=xa[:], func=AF.Square), reads=[xa.b], writes=[sq.b])
            ps = nextps()
            for fc in range(FC):
                kb.op("pe", lambda E: E.matmul(ps[:, 0:T], lhsT=ones_bf[:], rhs=sq[:, fc, :], start=(fc == 0), stop=(fc == FC - 1)),
                      reads=[ones_bf.b, sq.b], writes=[ps.b])
            kb.op("act", lambda E: E.activation(out=rs[:], in_=ps[:, 0:T], func=AF.Sqrt, scale=1.0 / D, bias=cst[:, 0:1]), reads=[ps.b, cst.b], writes=[rs.b])
            kb.op("dve", lambda E: E.reciprocal(out=rs[:], in_=rs[:]), reads=[rs.b], writes=[rs.b])
            for fc in range(FC):
                kb.op("dve", lambda E: E.scalar_tensor_tensor(out=ra[:, fc, :], in0=xa[:, fc, :], scalar=nw[:, 2 * NL * FC + fc:2 * NL * FC + fc + 1],
                                                              in1=rs[:], op0=ALU.mult, op1=ALU.mult), reads=[xa.b, rs.b, nw.b], writes=[ra.b])
            kb.dma("act", yout.ap.rearrange("(fc p) t -> p fc t", p=128)[:, :, t0:t0 + T], ra[:], yout.b, ra.b)
    if cfg.get("dbg"):
        kb.barrier()
        extra = ([("YM0", YM[0]), ("YM1", YM[1]), ("YR0", YR[0]), ("YR1", YR[1]), ("BN0", BN[0]), ("BN1", BN[1])] if NE else [])
        for nm, dr in [("MODR", MODR), ("PT", PT), ("HS", HS), ("REDL", REDL), ("REDC", REDC), ("XT", XT), ("XCi", XC)] + extra:
            od = Dr(nc, "dbg_" + nm, list(dr.t.shape), F32, kind="ExternalOutput")
            kb.dma("sp", od.ap, dr.ap, od.b, None)
        kb.barrier()
    deps = {yout.b.wr[0]: yout.b.wr[1]}
    kb._wait("sp", deps)
    kb._wait("act", deps)
    kb.barrier()
    ninst = kb.ninst
    kb.stack.close()
    return nc, ninst


def pack_inputs(inp, cfg):
    SEQ = cfg["SEQ"]
    LT = cfg["ltypes"]
    NL = len(LT)
    TR = SEQ // 4
    f32 = lambda a: np.ascontiguousarray(np.asarray(a, dtype=np.float32))
    x = f32(inp["x"]).reshape(2 * SEQ, D)
    xc = f32(f32(inp["ctx"]).reshape(2 * CTX, D).T)
    cc = np.stack([f32(inp["c"])[0], f32(inp["c"])[1], f32(inp["c_ctx"])], 0)
    ada_w = f32(inp["ada_w"])[:NL]
    ada_b = f32(inp["ada_b"])[:NL]
    adab = f32(ada_b.reshape(NL, 96, 128).transpose(2, 0, 1).reshape(128, NL * 96))
    nwv = np.zeros((128, (2 * NL + 1) * FC), np.float32)
    for l in range(NL):
        nwv[:, (l * 2) * FC:(l * 2 + 1) * FC] = f32(inp["norm1_w"])[l].reshape(FC, 128).T
        nwv[:, (l * 2 + 1) * FC:(l * 2 + 2) * FC] = f32(inp["norm2_w"])[l].reshape(FC, 128).T
    nwv[:, 2 * NL * FC:] = f32(inp["final_norm_w"]).reshape(FC, 128).T
    wgate, wup, wdown = f32(inp["ffn_w_gate"])[:NL], f32(inp["ffn_w_up"])[:NL], f32(inp["ffn_w_down"])[:NL]
    o_idx = [i for i, t in enumerate(LT) if t == "O"]
    NO = len(o_idx)
    NE = len(LT) - NO
    maps = []
    for c in range(NCORE):
        m = {}
        m["xs"] = f32(x[c * TR:(c + 1) * TR].T)
        m["xc"] = xc
        cm = np.zeros((128, 6), np.float32)
        for kc in range(2):
            cm[:, kc * 3:(kc + 1) * 3] = cc[:, c * 256 + kc * 128:c * 256 + (kc + 1) * 128].T
        m["cmine"] = cm
        m["adaw"] = f32(ada_w[:, c * 256:(c + 1) * 256, :].reshape(NL * 256, 6 * D))
        m["adab"] = adab
        m["nwv"] = nwv
        m["wg"] = f32(wgate[:, :, c * DFFC:(c + 1) * DFFC].reshape(NL * D, DFFC))
        m["wu"] = f32(wup[:, :, c * DFFC:(c + 1) * DFFC].reshape(NL * D, DFFC))
        m["wd"] = f32(wdown[:, c * DFFC:(c + 1) * DFFC, :].reshape(NL * DFFC, D))
        if NO:
            cw_in, cw_out = f32(inp["c_w_in"]), f32(inp["c_w_out"])
            m["wino"] = f32(np.concatenate([np.concatenate([cw_in[o][:, c * 256:(c + 1) * 256], cw_in[o][:, D + c * 256:D + (c + 1) * 256]], 1)
                                            for o in range(NO)], 0))
            m["wouto"] = f32(np.concatenate([cw_out[o][c * 256:(c + 1) * 256, :] for o in range(NO)], 0))
            gws = []
            ov = np.zeros((128, NO * 32), np.float32)
            for o in range(NO):
                for d in range(2):
                    gws.append(f32(inp["c_wa"])[o, d, c])
                    gws.append(f32(inp["c_wx"])[o, d, c])
                    for ci in range(2):
                        ch = slice(c * 256 + ci * 128, c * 256 + (ci + 1) * 128)
                        base = o * 32 + (d * 2 + ci) * 8
                        ov[:, base:base + 4] = f32(inp["c_conv_w"])[o, d][:, ch].T
                        ov[:, base + 4] = f32(inp["c_conv_b"])[o, d, ch]
                        ov[:, base + 5] = f32(inp["c_ba"])[o, d, ch]
                        ov[:, base + 6] = f32(inp["c_bx"])[o, d, ch]
                        ov[:, base + 7] = f32(inp["c_lambda"])[o, d, ch]
            m["gw"] = f32(np.concatenate(gws, 0))
            m["ovec"] = ov
        if NE:
            g, bm = c // 2, c % 2
            abw = f32(inp["ab_w_in"])
            cols = np.concatenate([np.arange(3088 + c * 128, 3088 + (c + 1) * 128), np.arange(4112 + c * 128, 4112 + (c + 1) * 128),
                                   np.arange(5136 + c * 128, 5136 + (c + 1) * 128), np.arange(6160, 6256), np.arange(6256, 6352), np.arange(6352, 6608),
                                   np.arange(g * 256, (g + 1) * 256), np.arange(1024 + g * 256, 1024 + (g + 1) * 256),
                                   np.arange(2048 + g * 128, 2048 + (g + 1) * 128), np.arange(2560 + g * 128, 2560 + (g + 1) * 128),
                                   np.arange(3072 + g * 4, 3072 + (g + 1) * 4)])
            m["wine"] = f32(np.concatenate([abw[e][:, cols] for e in range(NE)], 0))
            abo = f32(inp["ab_w_out"])
            m["woute"] = f32(np.concatenate([np.concatenate([abo[e][g * 256:(g + 1) * 256], abo[e][1024 + c * 128:1024 + (c + 1) * 128]], 0) for e in range(NE)], 0))
            evv = np.zeros((128, NE * 96), np.float32)
            lora = []
            dvec = np.zeros((64, NE * 8), np.float32)
            mch = [np.arange(g * 256, g * 256 + 128), np.arange(g * 256 + 128, (g + 1) * 256), np.arange(1024 + g * 128, 1024 + (g + 1) * 128),
                   np.arange(1536 + g * 128, 1536 + (g + 1) * 128)]
            for e in range(NE):
                o = e * 96
                for d in range(2):
                    for q in range(4):
                        evv[:, o + d * 20 + q * 5:o + d * 20 + q * 5 + 4] = f32(inp["m_conv_w"])[e, d][:, mch[q]].T
                        evv[:, o + d * 20 + q * 5 + 4] = f32(inp["m_conv_b"])[e, d, mch[q]]
                    mu = f32(inp["r_mu"])[e, d]
                    for hh in range(2):
                        ch = np.arange(c * 128 + hh * 64, c * 128 + (hh + 1) * 64)
                        base = o + 40 + (d * 2 + hh) * 8
                        evv[:64, base + 0] = mu[ch]
                        evv[:64, base + 1] = mu[1024 + ch]
                        evv[:64, base + 2] = mu[2048 + ch]
                        evv[:64, base + 3] = f32(inp["r_w0"])[e, d, ch]
                        evv[:64, base + 4] = f32(inp["r_a0"])[e, d, ch]
                        evv[:64, base + 5] = f32(inp["r_kk"])[e, d, ch]
                        evv[:64, base + 6] = f32(inp["r_ka"])[e, d, ch]
                        evv[:64, base + 7] = f32(inp["r_rk"])[e, d].reshape(-1)[ch]
                        lora.append(f32(inp["r_w2"])[e, d][:, ch])
                        lora.append(f32(inp["r_a2"])[e, d][:, ch])
                    evv[:96, o + 72 + d * 2] = mu[3072:3168]
                    evv[:96, o + 72 + d * 2 + 1] = mu[3168:3264]
                    evv[:4, o + 80 + d * 2] = f32(inp["m_dt_bias"])[e, d, g * 4:(g + 1) * 4]
                    evv[:4, o + 80 + d * 2 + 1] = f32(inp["m_a_log"])[e, d, g * 4:(g + 1) * 4]
                    for hd in range(4):
                        dvec[:, e * 8 + d * 4 + hd] = f32(inp["m_d"])[e, d, g * 4 + hd]
                evv[:, o + 76] = f32(inp["r_lnx_w"])[e, c * 128:(c + 1) * 128]
                evv[:, o + 77] = f32(inp["r_lnx_b"])[e, c * 128:(c + 1) * 128]
                evv[:, o + 78] = f32(inp["m_norm_w"])[e, g * 256:g * 256 + 128]
                evv[:, o + 79] = f32(inp["m_norm_w"])[e, g * 256 + 128:(g + 1) * 256]
            m["evec"] = evv
            m["lora"] = f32(np.concatenate(lora, 0))
            m["g2"] = f32(np.concatenate([f32(inp["r_g2"])[e][:, c * 128:(c + 1) * 128] for e in range(NE)], 0))
            m["dvec"] = dvec
            s4 = np.zeros((4, 512), np.float32)
            for hd in range(4):
                s4[hd, hd * 128:(hd + 1) * 128] = 1.0
            m["sel4"] = s4
        sel = np.zeros((128, 12), np.float32)
        sel[:, c] = 1.0
        sel[:, 8 + c // 4] = 1.0
        sel[:, 10 + c % 2] = 1.0
        m["sel"] = sel
        maps.append(m)
    return maps


_CACHE = {}


def run(inp, cfg, trace=False):
    key = (cfg["SEQ"], cfg["ltypes"], cfg["T"], cfg.get("stop", ""), cfg.get("dbg", False))
    if key not in _CACHE:
        _CACHE[key] = build(cfg)
    nc, ninst = _CACHE[key]
    maps = pack_inputs(inp, cfg)
    res = run_bass_kernel_spmd(nc, maps, core_ids=list(range(NCORE)), **({"trace": True} if trace else {}))
    SEQ = cfg["SEQ"]
    TR = SEQ // 4
    out = np.zeros((2 * SEQ, D), np.float32)
    for c in range(NCORE):
        out[c * TR:(c + 1) * TR] = np.asarray(res.results[c]["yout"]).T
    return out.reshape(2, SEQ, D), res


def kernel(**inputs):
    cfg = make_cfg()
    out, _ = run(inputs, cfg)
    return out
```

```python
from contextlib import ExitStack
import math
import numpy as np
import concourse.bass as bass
import concourse.mybir as mybir
from concourse.bass_utils import run_bass_kernel_spmd

F32 = mybir.dt.float32
BF16 = mybir.dt.bfloat16
AF = mybir.ActivationFunctionType
ALU = mybir.AluOpType
AX = mybir.AxisListType

D = 2048
FC = 16
CTX = 256
DFF = 5632
DFFC = DFF // 8
NCORE = 8
EPS = 1e-6
GW = 64

SAME_SYNC = True
ROLL = 30000


class Buf:
    __slots__ = ("name", "wr", "rd", "dsem", "dcum")

    def __init__(self, name):
        self.name = name
        self.wr = None
        self.rd = {}
        self.dsem = None
        self.dcum = 0


class KB:
    def __init__(self, nc):
        self.nc = nc
        self.stack = ExitStack()
        self.eng = {"pe": nc.tensor, "dve": nc.vector, "act": nc.scalar, "pool": nc.gpsimd, "sp": nc.sync}
        self.sems = {}
        self.esem = {}
        self.ecnt = {}
        self.waited = {k: {} for k in self.eng}
        self.nsem = 0
        self.dmabufs = []
        self.free_dsems = []
        self.ccsem = None
        self.sem_cum = {}
        self.scopes = [[]]
        self.ninst = 0
        for e in self.eng:
            self._newesem(e)

    def newsem(self, name):
        h = self.stack.enter_context(self.nc.semaphore(name))
        self.sems[name] = h
        self.nsem += 1
        return name

    def _newesem(self, e):
        name = self.newsem(f"e_{e}_{self.nsem}")
        self.esem[e] = name
        self.ecnt[name] = 0

    def _deps(self, reads, writes):
        d = {}
        for b in reads:
            if b.wr and d.get(b.wr[0], 0) < b.wr[1]:
                d[b.wr[0]] = b.wr[1]
        for b in writes:
            if b.wr and d.get(b.wr[0], 0) < b.wr[1]:
                d[b.wr[0]] = b.wr[1]
            for sem, v in b.rd.items():
                if d.get(sem, 0) < v:
                    d[sem] = v
        return d

    def _wait(self, e, deps):
        w = self.waited[e]
        for sem, v in deps.items():
            if w.get(sem, 0) >= v:
                continue
            if sem == self.esem[e] and (e == "pe" or e == "sp" or not SAME_SYNC):
                continue
            self.eng[e].wait_ge(self.sems[sem], v)
            w[sem] = v

    def op(self, e, fn, reads=(), writes=()):
        self._wait(e, self._deps(reads, writes))
        ins = fn(self.eng[e])
        sem = self.esem[e]
        self.ecnt[sem] += 1
        v = self.ecnt[sem]
        ins.then_inc(self.sems[sem], 1)
        self.ninst += 1
        for b in writes:
            b.wr = (sem, v)
            b.rd = {}
        for b in reads:
            if b.wr != (sem, v):
                b.rd[sem] = v
        if v >= ROLL:
            self._newesem(e)
        return (sem, v)

    def _dsem(self, outbuf):
        if outbuf.dsem is not None and self.sem_cum[outbuf.dsem] >= ROLL * 16:
            outbuf.dsem = None
        if outbuf.dsem is None:
            if self.free_dsems:
                outbuf.dsem = self.free_dsems.pop()
            else:
                outbuf.dsem = self.newsem(f"d_{self.nsem}")
                self.sem_cum[outbuf.dsem] = 0

    def open_scope(self):
        self.scopes.append([])

    def close_scope(self):
        for b in self.scopes.pop():
            if b.dsem is not None:
                if self.sem_cum[b.dsem] < ROLL * 16:
                    self.free_dsems.append(b.dsem)
                b.dsem = None
            if b in self.dmabufs:
                self.dmabufs.remove(b)

    def dma(self, q, out, in_, outbuf, inbuf, **kw):
        outs = outbuf if isinstance(outbuf, (list, tuple)) else [outbuf]
        ins_ = [] if inbuf is None else (inbuf if isinstance(inbuf, (list, tuple)) else [inbuf])
        self._wait(q, self._deps(ins_, outs))
        prim = outs[0]
        self._dsem(prim)
        ins = self.eng[q].dma_start(out=out, in_=in_, **kw)
        self.sem_cum[prim.dsem] += 16
        ins.then_inc(self.sems[prim.dsem], 16)
        self.ninst += 1
        t = (prim.dsem, self.sem_cum[prim.dsem])
        for ob in outs:
            ob.wr = t
            ob.rd = {}
        for ib in ins_:
            ib.rd[t[0]] = t[1]
        if prim not in self.dmabufs:
            self.dmabufs.append(prim)
        return t

    def collective(self, kind, op, groups, in_ap, out_ap, inbuf, outbuf):
        import os
        if "nocc" in os.environ.get("DBG", ""):
            return self.dma("pool", out_ap, in_ap, outbuf, inbuf)
        e = "pool"
        self._wait(e, self._deps([inbuf], [outbuf]))
        if self.ccsem is None:
            self.ccsem = self.newsem("ccsem")
            self.sem_cum[self.ccsem] = 0
            self.ccbuf = Buf("ccbuf")
            self.dmabufs.append(self.ccbuf)
        ins = self.eng[e].collective_compute(kind, op, replica_groups=groups, ins=[in_ap], outs=[out_ap])
        self.sem_cum[self.ccsem] += 1
        ins.then_inc(self.sems[self.ccsem], 1)
        t = (self.ccsem, self.sem_cum[self.ccsem])
        outbuf.wr = t
        outbuf.rd = {}
        inbuf.rd[t[0]] = t[1]
        self.ccbuf.wr = t
        return t

    def barrier(self, engines=("pe", "dve", "act", "pool", "sp")):
        deps = {}
        for e, sem in self.esem.items():
            if self.ecnt[sem] > 0:
                deps[sem] = self.ecnt[sem]
        for b in self.dmabufs:
            if b.wr and b is not getattr(self, "ccbuf", None):
                deps[b.wr[0]] = max(deps.get(b.wr[0], 0), b.wr[1])
        for e in engines:
            w = self.waited[e]
            for sem, v in deps.items():
                if w.get(sem, 0) >= v or sem == self.esem[e]:
                    continue
                self.eng[e].wait_ge(self.sems[sem], v)
                w[sem] = v


class Tl:
    def __init__(self, kb, stack, name, shape, dtype=F32, psum=False):
        mk = kb.nc.psum_tensor if psum else kb.nc.sbuf_tensor
        kb.ntl = getattr(kb, "ntl", 0) + 1
        name = f"{name}_{kb.ntl}"
        self.t = stack.enter_context(mk(name, list(shape), dtype))
        self.b = Buf(name)
        kb.scopes[-1].append(self.b)

    def __getitem__(self, k):
        return self.t[k]


class Dr:
    def __init__(self, nc, name, shape, dtype=F32, kind="Internal", chunk_rows=None):
        self.t = nc.dram_tensor(name, list(shape), dtype, kind=kind)
        self.ap = self.t.ap()
        self.b = Buf(name)
        self.chunk_rows = chunk_rows
        if chunk_rows:
            self.nchunk = shape[0] // chunk_rows
            self.cb = [Buf(f"{name}_c{i}") for i in range(self.nchunk)]

    def bs(self, r0, r1):
        if not self.chunk_rows:
            return [self.b]
        return self.cb[r0 // self.chunk_rows:(r1 - 1) // self.chunk_rows + 1]

    def chunk_ap(self, k):
        return self.ap[k * self.chunk_rows:(k + 1) * self.chunk_rows, :]


def make_cfg(SEQ=8192, ltypes=("E", "O", "E", "O"), T=256, stop="", dbg=False):
    return dict(SEQ=SEQ, ltypes=tuple(ltypes), T=T, stop=stop, dbg=dbg)


def build(cfg):
    SEQ = cfg["SEQ"]
    LT = cfg["ltypes"]
    NL = len(LT)
    T = cfg["T"]
    TR = SEQ // 4
    LS = CTX + SEQ
    ROWS = SEQ // GW
    NE = sum(1 for t in LT if t == "E")
    NO = sum(1 for t in LT if t == "O")
    assert TR % T == 0 and CTX % min(T, CTX) == 0

    nc = bass.Bass("TRN2", target_bir_lowering=False)
    kb = KB(nc)
    top = kb.stack

    def din(name, shape, dt=F32):
        return Dr(nc, name, shape, dt, kind="ExternalInput")

    xs = din("xs", [D, TR])
    xc_in = din("xc", [D, 2 * CTX])
    cmine = din("cmine", [128, 2 * 3])
    adaw = din("adaw", [NL * 2 * 128, 6 * D])
    adab = din("adab", [128, NL * 96])
    nwv = din("nwv", [128, (2 * NL + 1) * FC])
    wg_d = din("wg", [NL * D, DFFC])
    wu_d = din("wu", [NL * D, DFFC])
    wd_d = din("wd", [NL * DFFC, D])
    if NO:
        wino_d = din("wino", [NO * D, 512])
        wouto_d = din("wouto", [NO * 256, D])
        gw_d = din("gw", [NO * 2 * 2 * 256, 256])
        ovec_d = din("ovec", [128, NO * 2 * 2 * 8])
    if NE:
        wine_d = din("wine", [NE * D, 1604])
        woute_d = din("woute", [NE * 384, D])
        evec_d = din("evec", [128, NE * 96])
        lora_d = din("lora", [NE * 8 * 96, 64])
        g2_d = din("g2", [NE * 256, 128])
        dvec_d = din("dvec", [64, NE * 8])
        sel4_d = din("sel4", [4, 4 * 128])
    sel_d = din("sel", [128, 12])
    yout = Dr(nc, "yout", [D, TR], F32, kind="ExternalOutput")

    xsi = Dr(nc, "xsi", [D, TR])
    CR = (1 << 20) // TR
    XT = Dr(nc, "XT", [8 * D, TR], chunk_rows=CR)
    XC = Dr(nc, "XCi", [D, 2 * CTX])
    PARTL = Dr(nc, "PARTL", [8 * D, TR], chunk_rows=CR)
    PARTC = Dr(nc, "PARTC", [D, 2 * CTX])
    REDL = Dr(nc, "REDL", [8 * D, TR], chunk_rows=CR)
    REDC = Dr(nc, "REDC", [D, 2 * CTX])
    MODP = Dr(nc, "MODP", [128, NL * 288])
    MODR = Dr(nc, "MODR", [128, NL * 288])
    NCH_MAX = 14 if NE else 4
    PT = Dr(nc, "PT", [NCH_MAX * 128, 2 * LS])
    HS = Dr(nc, "HS", [2 * 128, 2 * LS])
    HSC = Dr(nc, "HSC", [2 * 128, LS])
    if NE:
        YM = [Dr(nc, f"YM{d}", [2 * 128, LS]) for d in range(2)]
        YR = [Dr(nc, f"YR{d}", [128, 2 * LS]) for d in range(2)]
        BN = [Dr(nc, f"BN{d}", [128, 2 * LS]) for d in range(2)]
    G4 = [[0, 1, 2, 3], [4, 5, 6, 7]]
    G2 = [[0, 4], [1, 5], [2, 6], [3, 7]]

    TMPL = Dr(nc, "TMPL", [8 * D, TR], chunk_rows=CR)
    TMPC = Dr(nc, "TMPC", [D, 2 * CTX])
    MODT = Dr(nc, "MODT", [128, NL * 288])

    def allreduce(src, dst, tmp):
        if not src.chunk_rows:
            kb.collective("AllReduce", ALU.add, G4, src.ap, tmp.ap, src.b, tmp.b)
            kb.collective("AllReduce", ALU.add, G2, tmp.ap, dst.ap, tmp.b, dst.b)
            return
        for k in range(src.nchunk):
            kb.collective("AllReduce", ALU.add, G4, src.chunk_ap(k), tmp.chunk_ap(k), src.cb[k], tmp.cb[k])
        for k in range(src.nchunk):
            kb.collective("AllReduce", ALU.add, G2, tmp.chunk_ap(k), dst.chunk_ap(k), tmp.cb[k], dst.cb[k])

    def lat_view(dr):
        return dr.ap.rearrange("(r fc p) t -> p r fc t", r=8, fc=FC, p=128)

    def ctx_view(dr):
        return dr.ap.rearrange("(fc p) t -> p fc t", p=128)

    tiles = []
    for r in range(8):
        for t0 in range(0, TR, T):
            tiles.append(dict(kind="lat", r=r, t0=t0, n=T, b=r // 4, pos=CTX + (r % 4) * TR + t0, j=r // 4, last=(t0 + T >= TR)))
    TCX = min(T, CTX)
    for b in range(2):
        for t0 in range(0, CTX, TCX):
            tiles.append(dict(kind="ctx", b=b, c0=b * CTX + t0, n=TCX, pos=t0, j=2, last=(b == 1 and t0 + TCX >= CTX)))

    def xb(drl, drc, tile):
        if tile["kind"] == "lat":
            return drl.bs(tile["r"] * D, (tile["r"] + 1) * D)
        return [drc.b]

    def xap(drl, drc, tile):
        if tile["kind"] == "lat":
            return lat_view(drl)[:, tile["r"], :, tile["t0"]:tile["t0"] + tile["n"]]
        return ctx_view(drc)[:, :, tile["c0"]:tile["c0"] + tile["n"]]

    def scr_ap(dr, nch, tile, c0=0):
        v = dr.ap.rearrange("(c p) (b s) -> p c b s", p=128, b=2)
        return v[:, c0:c0 + nch, tile["b"], tile["pos"]:tile["pos"] + tile["n"]]

    mod = Tl(kb, top, "mod", [128, NL * 288])
    nw = Tl(kb, top, "nw", [128, (2 * NL + 1) * FC])
    A12 = Tl(kb, top, "A12", [128, NL * 2 * FC * 3])
    ones_bf = Tl(kb, top, "ones_bf", [128, 128], BF16)
    ident = Tl(kb, top, "ident", [128, 128])
    cst = Tl(kb, top, "cst", [128, 8])
    psb = [Tl(kb, top, f"ps{i}", [128, 512], F32, psum=True) for i in range(8)]
    ps_i = [0]

    def nextps():
        p = psb[ps_i[0] % 8]
        ps_i[0] += 1
        return p

    def modcol(l, j6, fc, j):
        o = ((l * 6 + j6) * FC + fc) * 3 + j
        return mod[:, o:o + 1]

    def acol(l, which, fc, j):
        o = ((l * 2 + which) * FC + fc) * 3 + j
        return A12[:, o:o + 1]

    ev_i = [0]
    ev_dve = [False]

    def evac(out, in_, R, W):
        ev_i[0] += 1
        if ev_i[0] % 2 and not ev_dve[0]:
            kb.op("act", lambda E: E.activation(out=out, in_=in_, func=AF.Copy), reads=R, writes=W)
        else:
            kb.op("dve", lambda E: E.tensor_copy(out=out, in_=in_), reads=R, writes=W)

    wst = [Tl(kb, top, f"wst{i}", [128, D]) for i in range(2)]
    wst_i = [0]

    def load_w_bf16(dst_tile, dst_ap_fn, src_ap_fn, nk):
        for k in range(nk):
            dst = dst_ap_fn(k)
            npart, ncol = dst.shape[0], dst.shape[-1]
            stg = wst[wst_i[0] % 2]
            wst_i[0] += 1
            kb.dma("sp", stg[0:npart, 0:ncol], src_ap_fn(k), stg.b, None)
            kb.op("pool", lambda E: E.tensor_copy(out=dst, in_=stg[0:npart, 0:ncol]), reads=[stg.b], writes=[dst_tile.b])

    kb.op("dve", lambda E: E.memset(ones_bf[:], 1.0), writes=[ones_bf.b])
    kb.op("dve", lambda E: E.memset(cst[:, 0:1], EPS), writes=[cst.b])
    kb.op("dve", lambda E: E.memset(cst[:, 1:2], 1.0), writes=[cst.b])
    kb.op("dve", lambda E: E.memset(cst[:, 2:3], 0.0), writes=[cst.b])
    kb.op("pool", lambda E: E.memset(ident[:], 0.0), writes=[ident.b])
    kb.op("pool", lambda E: E.affine_select(out=ident[:], in_=ident[:], pattern=[[-1, 128]], compare_op=ALU.not_equal,
                                            fill=1.0, base=0, channel_multiplier=1), reads=[ident.b], writes=[ident.b])
    kb.dma("sp", nw[:], nwv.ap, nw.b, None)
    with ExitStack() as st:
        kb.open_scope()
        selt0 = Tl(kb, st, "selt0", [128, 12])
        kb.dma("sp", selt0[:], sel_d.ap, selt0.b, None)
        xg = [Tl(kb, st, f"xg{i}", [128, FC, T]) for i in range(2)]
        xo = [Tl(kb, st, f"xo{i}", [128, FC, T]) for i in range(2)]
        gi_ = 0
        for t0 in range(0, TR, T):
            g = xg[(t0 // T) % 2]
            kb.dma("sp", g[:], xs.ap.rearrange("(fc p) t -> p fc t", p=128)[:, :, t0:t0 + T], g.b, None)
            for r in range(8):
                o_ = xo[gi_ % 2]
                gi_ += 1
                kb.op("dve" if r % 2 else "pool", lambda E: E.tensor_scalar(out=o_[:], in0=g[:], scalar1=selt0[:, r:r + 1], scalar2=None, op0=ALU.mult),
                      reads=[g.b, selt0.b], writes=[o_.b])
                kb.dma("act", lat_view(PARTL)[:, r, :, t0:t0 + T], o_[:], PARTL.bs(r * D, (r + 1) * D), o_.b)
        allreduce(PARTL, XT, TMPL)
        kb.barrier()
        kb.close_scope()
    import os
    DBG = os.environ.get("DBG", "")
    kb.dma("sp", XC.ap, xc_in.ap, XC.b, None)

    with ExitStack() as st:
      kb.open_scope()
      if "noada" not in DBG:
          csb = Tl(kb, st, "csb", [128, 6])
          csl = Tl(kb, st, "csl", [128, 6])
          adb = Tl(kb, st, "adb", [128, NL * 96])
          modp = Tl(kb, st, "modp", [128, NL * 288])
          wbuf = [Tl(kb, st, f"adw{i}", [128, 6 * D]) for i in range(2)]
          kb.dma("sp", csb[:], cmine.ap, csb.b, None)
          kb.dma("sp", adb[:], adab.ap, adb.b, None)
          kb.op("act", lambda E: E.activation(out=csl[:], in_=csb[:], func=AF.Silu), reads=[csb.b], writes=[csl.b])
          for l in range(NL):
              ps = nextps()
              for kc in range(2):
                  wb = wbuf[kc]
                  row0 = (l * 2 + kc) * 128
                  kb.dma("sp", wb[:], adaw.ap[row0:row0 + 128, :], wb.b, None)
              for cc in range(96):
                  for kc in range(2):
                      wb = wbuf[kc]
                      kb.op("pe", lambda E: E.matmul(ps[:, cc * 3:cc * 3 + 3], lhsT=wb[:, cc * 128:(cc + 1) * 128],
                                                     rhs=csl[:, kc * 3:kc * 3 + 3], start=(kc == 0), stop=(kc == 1)),
                            reads=[wb.b, csl.b], writes=[ps.b])
              kb.op("dve", lambda E: E.scalar_tensor_tensor(
                  out=modp[:, l * 288:(l + 1) * 288].rearrange("p (c j) -> p c j", j=3),
                  in0=adb[:, l * 96:(l + 1) * 96].unsqueeze(2).to_broadcast([128, 96, 3]), scalar=1.0 / NCORE,
                  in1=ps[:, 0:288].rearrange("p (c j) -> p c j", j=3), op0=ALU.mult, op1=ALU.add),
                  reads=[adb.b, ps.b], writes=[modp.b])
          kb.dma("sp", MODP.ap, modp[:], MODP.b, modp.b)
          allreduce(MODP, MODR, MODT)
          kb.dma("sp", mod[:], MODR.ap, mod.b, MODR.b)
          for l in range(NL):
              for which, j6 in ((0, 1), (1, 4)):
                  o_m = ((l * 6 + j6) * FC) * 3
                  o_a = ((l * 2 + which) * FC) * 3
                  o_w = (l * 2 + which) * FC
                  kb.op("dve", lambda E: E.scalar_tensor_tensor(
                      out=A12[:, o_a:o_a + 48].rearrange("p (f j) -> p f j", j=3),
                      in0=mod[:, o_m:o_m + 48].rearrange("p (f j) -> p f j", j=3), scalar=1.0,
                      in1=nw[:, o_w:o_w + FC].unsqueeze(2).to_broadcast([128, FC, 3]), op0=ALU.add, op1=ALU.mult),
                      reads=[mod.b, nw.b], writes=[A12.b])
          kb.barrier()
      kb.close_scope()

    def rmsnorm_mod(xt, sq, rs, tmp, n, acols, bcols, ncols_scale=1.0):
        kb.op("act", lambda E: E.activation(out=sq[:, :, 0:n], in_=xt[:, :, 0:n], func=AF.Square), reads=[xt.b], writes=[sq.b])
        ps = nextps()
        for fc in range(FC):
            kb.op("pe", lambda E: E.matmul(ps[:, 0:n], lhsT=ones_bf[:], rhs=sq[:, fc, 0:n], start=(fc == 0), stop=(fc == FC - 1)),
                  reads=[ones_bf.b, sq.b], writes=[ps.b])
        kb.op("act", lambda E: E.activation(out=rs[:, 0:n], in_=ps[:, 0:n], func=AF.Sqrt, scale=1.0 / D, bias=cst[:, 0:1]),
              reads=[ps.b, cst.b], writes=[rs.b])
        kb.op("dve", lambda E: E.reciprocal(out=rs[:, 0:n], in_=rs[:, 0:n]), reads=[rs.b], writes=[rs.b])
        for fc in range(FC):
            tm = tmp[fc % len(tmp)]
            kb.op("dve", lambda E: E.tensor_tensor(out=tm[:, 0:n], in0=xt[:, fc, 0:n], in1=rs[:, 0:n], op=ALU.mult),
                  reads=[xt.b, rs.b], writes=[tm.b])
            kb.op("act", lambda E: E.activation(out=sq[:, fc, 0:n], in_=tm[:, 0:n], func=AF.Identity, scale=acols(fc), bias=bcols(fc)),
                  reads=[tm.b, A12.b, mod.b], writes=[sq.b])

    def resid_update(xt, rt, n, gcols):
        for fc in range(FC):
            kb.op("dve", lambda E: E.scalar_tensor_tensor(out=xt[:, fc, 0:n], in0=rt[:, fc, 0:n], scalar=gcols(fc),
                                                          in1=xt[:, fc, 0:n], op0=ALU.mult, op1=ALU.add),
                  reads=[rt.b, xt.b, mod.b], writes=[xt.b])

    def allreduce_parts():
        allreduce(PARTL, REDL, TMPL)
        allreduce(PARTC, REDC, TMPC)

    def phase_A(l, win_dr, row0, ncols, pending_g2_layer, chunks=None):
        if chunks is None:
            chunks = [(c * 128, min(128, ncols - c * 128)) for c in range((ncols + 127) // 128)]
        nch = len(chunks)
        with ExitStack() as st:
            kb.open_scope()
            W = Tl(kb, st, "Win", [128, FC, ncols], BF16)
            load_w_bf16(W, lambda k: W[:, k, :], lambda k: win_dr.ap[row0 + k * 128:row0 + (k + 1) * 128, :], FC)
            xts = [Tl(kb, st, f"xtA{i}", [128, FC, T]) for i in range(2)]
            rts = [Tl(kb, st, f"rtA{i}", [128, FC, T]) for i in range(2)]
            sqs = [Tl(kb, st, f"sqA{i}", [128, FC, T], BF16) for i in range(2)]
            rss = [Tl(kb, st, f"rsA{i}", [128, T]) for i in range(2)]
            tmp = [Tl(kb, st, f"tmA{i}", [128, T]) for i in range(4)]
            for ti, tile in enumerate(tiles):
                n = tile["n"]
                j = tile["j"]
                xt, rt, sq, rs = xts[ti % 2], rts[ti % 2], sqs[ti % 2], rss[ti % 2]

                def ld(tj):
                    tl_ = tiles[tj]
                    kb.dma("sp", xts[tj % 2][:, :, 0:tl_["n"]], xap(XT, XC, tl_), xts[tj % 2].b, xb(XT, XC, tl_))
                    if pending_g2_layer is not None:
                        kb.dma("sp", rts[tj % 2][:, :, 0:tl_["n"]], xap(REDL, REDC, tl_), rts[tj % 2].b, xb(REDL, REDC, tl_))
                if ti == 0:
                    ld(0)
                if ti + 1 < len(tiles):
                    ld(ti + 1)
                if pending_g2_layer is not None:
                    resid_update(xt, rt, n, lambda fc: modcol(pending_g2_layer, 5, fc, j))
                    kb.dma("sp", xap(XT, XC, tile), xt[:, :, 0:n], xb(XT, XC, tile), xt.b)
                rmsnorm_mod(xt, sq, rs, tmp, n, lambda fc: acol(l, 0, fc, j), lambda fc: modcol(l, 0, fc, j))
                for c in range(nch):
                    cc0, cw = chunks[c]
                    ps = nextps()
                    for kc in range(FC):
                        kb.op("pe", lambda E: E.matmul(ps[0:cw, 0:n], lhsT=W[:, kc, cc0:cc0 + cw], rhs=sq[:, kc, 0:n],
                                                       start=(kc == 0), stop=(kc == FC - 1)), reads=[W.b, sq.b], writes=[ps.b])
                    evac(rt[0:cw, c, 0:n], ps[0:cw, 0:n], [ps.b], [rt.b])
                c = 0
                while c < nch:
                    if chunks[c][1] == 128:
                        c1 = c
                        while c1 < nch and chunks[c1][1] == 128:
                            c1 += 1
                        kb.dma("sp", scr_ap(PT, c1 - c, tile, c), rt[:, c:c1, 0:n], PT.b, rt.b)
                        c = c1
                    else:
                        cw = chunks[c][1]
                        kb.dma("sp", scr_ap(PT, 1, tile, c)[0:cw], rt[0:cw, c:c + 1, 0:n], PT.b, rt.b)
                        c += 1
            kb.barrier()
            kb.close_scope()

    def phase_C_odd(l, o):
        with ExitStack() as st:
            kb.open_scope()
            W = Tl(kb, st, "Wout", [128, 2, D], BF16)
            load_w_bf16(W, lambda k: W[:, k, :], lambda k: wouto_d.ap[o * 256 + k * 128:o * 256 + (k + 1) * 128, :], 2)
            hss = [Tl(kb, st, f"hsC{i}", [128, 2, T]) for i in range(2)]
            gys = [Tl(kb, st, f"gyC{i}", [128, 2, T]) for i in range(2)]
            t1 = Tl(kb, st, "t1C", [128, 2, T])
            mbf = [Tl(kb, st, f"mC{i}", [128, 2, T], BF16) for i in range(2)]
            stg = [Tl(kb, st, f"stC{i}", [128, FC, T]) for i in range(2)]
            for ti, tile in enumerate(tiles):
                n = tile["n"]
                hs, gy, m, sg = hss[ti % 2], gys[ti % 2], mbf[ti % 2], stg[ti % 2]

                def ld(tj):
                    tl_ = tiles[tj]
                    kb.dma("sp", hss[tj % 2][:, :, 0:tl_["n"]], scr_ap(HS, 2, tl_), hss[tj % 2].b, HS.b)
                    kb.dma("sp", gys[tj % 2][:, :, 0:tl_["n"]], scr_ap(PT, 2, tl_, 0), gys[tj % 2].b, PT.b)
                if ti == 0:
                    ld(0)
                if ti + 1 < len(tiles):
                    ld(ti + 1)
                kb.op("dve", lambda E: E.tensor_tensor(out=t1[:, :, 0:n], in0=gy[:, :, 0:n], in1=gy[:, :, 0:n], op=ALU.mult), reads=[gy.b], writes=[t1.b])
                kb.op("dve", lambda E: E.tensor_scalar(out=t1[:, :, 0:n], in0=t1[:, :, 0:n], scalar1=0.044715, scalar2=1.0, op0=ALU.mult, op1=ALU.add),
                      reads=[t1.b], writes=[t1.b])
                kb.op("dve", lambda E: E.tensor_tensor(out=t1[:, :, 0:n], in0=t1[:, :, 0:n], in1=gy[:, :, 0:n], op=ALU.mult), reads=[t1.b, gy.b], writes=[t1.b])
                kb.op("act", lambda E: E.activation(out=t1[:, :, 0:n], in_=t1[:, :, 0:n], func=AF.Sigmoid, scale=2.0 * math.sqrt(2.0 / math.pi)),
                      reads=[t1.b], writes=[t1.b])
                kb.op("dve", lambda E: E.tensor_tensor(out=t1[:, :, 0:n], in0=t1[:, :, 0:n], in1=gy[:, :, 0:n], op=ALU.mult), reads=[t1.b, gy.b], writes=[t1.b])
                kb.op("dve", lambda E: E.tensor_tensor(out=m[:, :, 0:n], in0=t1[:, :, 0:n], in1=hs[:, :, 0:n], op=ALU.mult), reads=[t1.b, hs.b], writes=[m.b])
                out_proj(W, 2, [128, 128], m, sg, n, tile)
            kb.barrier()
            kb.close_scope()

    ar_pend = []

    def out_proj(W, nk, ksz, m, sg, n, tile):
        for fo in range(FC):
            ps = nextps()
            for k in range(nk):
                kb.op("pe", lambda E: E.matmul(ps[:, 0:n], lhsT=W[0:ksz[k], k, fo * 128:(fo + 1) * 128], rhs=m[0:ksz[k], k, 0:n],
                                               start=(k == 0), stop=(k == nk - 1)), reads=[W.b, m.b], writes=[ps.b])
            evac(sg[:, fo, 0:n], ps[:, 0:n], [ps.b], [sg.b])
        lat = tile["kind"] == "lat"
        kb.dma("sp", xap(PARTL, PARTC, tile), sg[:, :, 0:n], xb(PARTL, PARTC, tile), sg.b)
        if tile.get("last"):
            if lat:
                ready = [k for k in range(PARTL.nchunk) if ((k + 1) * CR - 1) // D == tile["r"]]
                for k in ready:
                    kb.collective("AllReduce", ALU.add, G4, PARTL.chunk_ap(k), TMPL.chunk_ap(k), PARTL.cb[k], TMPL.cb[k])
                for k in ar_pend:
                    kb.collective("AllReduce", ALU.add, G2, TMPL.chunk_ap(k), REDL.chunk_ap(k), TMPL.cb[k], REDL.cb[k])
                ar_pend[:] = ready
            else:
                for k in ar_pend:
                    kb.collective("AllReduce", ALU.add, G2, TMPL.chunk_ap(k), REDL.chunk_ap(k), TMPL.cb[k], REDL.cb[k])
                ar_pend[:] = []
                allreduce(PARTC, REDC, TMPC)

    def phase_D(l):
        csz = [128, 128, 128, 128, 128, 64]
        with ExitStack() as st:
            kb.open_scope()
            Wg = Tl(kb, st, "Wg", [128, FC, DFFC], BF16)
            Wu = Tl(kb, st, "Wu", [128, FC, DFFC], BF16)
            Wd = Tl(kb, st, "Wd", [128, 6, D], BF16)
            load_w_bf16(Wg, lambda k: Wg[:, k, :], lambda k: wg_d.ap[l * D + k * 128:l * D + (k + 1) * 128, :], FC)
            load_w_bf16(Wu, lambda k: Wu[:, k, :], lambda k: wu_d.ap[l * D + k * 128:l * D + (k + 1) * 128, :], FC)
            load_w_bf16(Wd, lambda k: Wd[0:csz[k], k, :], lambda k: wd_d.ap[l * DFFC + k * 128:l * DFFC + k * 128 + csz[k], :], 6)
            xts = [Tl(kb, st, f"xtD{i}", [128, FC, T]) for i in range(2)]
            rts = [Tl(kb, st, f"rtD{i}", [128, FC, T]) for i in range(2)]
            sqs = [Tl(kb, st, f"sqD{i}", [128, FC, T], BF16) for i in range(2)]
            rss = [Tl(kb, st, f"rsD{i}", [128, T]) for i in range(2)]
            tmp = [Tl(kb, st, f"tmD{i}", [128, T]) for i in range(4)]
            acts = [Tl(kb, st, f"acD{i}", [128, 6, T], BF16) for i in range(2)]
            sgl = [Tl(kb, st, f"sgD{i}", [128, T]) for i in range(2)]
            for ti, tile in enumerate(tiles):
                n = tile["n"]
                j = tile["j"]
                lat = tile["kind"] == "lat"
                xt, rt, sq, rs, ac = xts[ti % 2], rts[ti % 2], sqs[ti % 2], rss[ti % 2], acts[ti % 2]

                def ld(tj):
                    tl_ = tiles[tj]
                    kb.dma("sp", xts[tj % 2][:, :, 0:tl_["n"]], xap(XT, XC, tl_), xts[tj % 2].b, xb(XT, XC, tl_))
                    kb.dma("sp", rts[tj % 2][:, :, 0:tl_["n"]], xap(REDL, REDC, tl_), rts[tj % 2].b, xb(REDL, REDC, tl_))
                if ti == 0:
                    ld(0)
                if ti + 1 < len(tiles):
                    ld(ti + 1)
                resid_update(xt, rt, n, lambda fc: modcol(l, 2, fc, j))
                kb.dma("sp", xap(XT, XC, tile), xt[:, :, 0:n], xb(XT, XC, tile), xt.b)
                rmsnorm_mod(xt, sq, rs, tmp, n, lambda fc: acol(l, 1, fc, j), lambda fc: modcol(l, 3, fc, j))
                for c in range(6):
                    cw = csz[c]
                    pg, pu = nextps(), nextps()
                    for kc in range(FC):
                        kb.op("pe", lambda E: E.matmul(pg[0:cw, 0:n], lhsT=Wg[:, kc, c * 128:c * 128 + cw], rhs=sq[:, kc, 0:n],
                                                       start=(kc == 0), stop=(kc == FC - 1)), reads=[Wg.b, sq.b], writes=[pg.b])
                    for kc in range(FC):
                        kb.op("pe", lambda E: E.matmul(pu[0:cw, 0:n], lhsT=Wu[:, kc, c * 128:c * 128 + cw], rhs=sq[:, kc, 0:n],
                                                       start=(kc == 0), stop=(kc == FC - 1)), reads=[Wu.b, sq.b], writes=[pu.b])
                    s = sgl[c % 2]
                    kb.op("act", lambda E: E.activation(out=s[0:cw, 0:n], in_=pg[0:cw, 0:n], func=AF.Silu), reads=[pg.b], writes=[s.b])
                    kb.op("dve", lambda E: E.tensor_tensor(out=ac[0:cw, c, 0:n], in0=s[0:cw, 0:n], in1=pu[0:cw, 0:n], op=ALU.mult),
                          reads=[s.b, pu.b], writes=[ac.b])
                out_proj(Wd, 6, csz, ac, rt, n, tile)
            kb.barrier()
            kb.close_scope()

    def mixer_rglru(o):
        BL = 512
        with ExitStack() as st:
            kb.open_scope()
            GWt = Tl(kb, st, "GWt", [128, 2 * 2 * 2, 256], BF16)
            load_w_bf16(GWt, lambda k: GWt[:, k, :], lambda k: gw_d.ap[(o * 8 + k) * 128:(o * 8 + k + 1) * 128, :], 8)
            ov = Tl(kb, st, "ov", [128, 2 * 2 * 8])
            spn = Tl(kb, st, "spn", [128, 4])
            kb.dma("sp", ov[:], ovec_d.ap[:, o * 32:(o + 1) * 32], ov.b, None)
            for dc in range(4):
                kb.op("act", lambda E: E.activation(out=spn[:, dc:dc + 1], in_=ov[:, dc * 8 + 7:dc * 8 + 8], func=AF.Exp, scale=-1.0),
                      reads=[ov.b], writes=[spn.b])
                kb.op("act", lambda E: E.activation(out=spn[:, dc:dc + 1], in_=spn[:, dc:dc + 1], func=AF.Ln, scale=1.0, bias=cst[:, 1:2]),
                      reads=[spn.b, cst.b], writes=[spn.b])
            kb.op("dve", lambda E: E.tensor_scalar(out=spn[:], in0=spn[:], scalar1=-8.0, scalar2=None, op0=ALU.mult), reads=[spn.b], writes=[spn.b])
            XB = Tl(kb, st, "XB", [128, 2, LS])
            hfw = Tl(kb, st, "hfw", [128, 2, BL])
            RM = Tl(kb, st, "RM", [128, SEQ])
            xcf = Tl(kb, st, "xcf", [128, 2, BL])
            xcb = Tl(kb, st, "xcb", [128, 2, BL], BF16)
            gr = Tl(kb, st, "gr", [128, 2, BL])
            gi = Tl(kb, st, "gi", [128, 2, BL])
            aa = Tl(kb, st, "aa", [128, 2, BL])
            uu = Tl(kb, st, "uu", [128, 2, BL])
            hh = [Tl(kb, st, f"hh{i}", [128, 2, BL]) for i in range(2)]
            zero = cst[:, 2:3]
            pt_v = PT.ap.rearrange("(c p) (b s) -> p c b s", p=128, b=2)
            hs_v = HS.ap.rearrange("(c p) (b s) -> p c b s", p=128, b=2)
            for b in range(2):
                for ci in range(2):
                    kb.dma("sp", XB[:, ci, 0:CTX], pt_v[:, 2 + ci, b, 0:CTX], XB.b, PT.b)
                    kb.dma("sp", RM[:], pt_v[:, 2 + ci, b, CTX:LS], RM.b, PT.b)
                    kb.op("pool", lambda E: E.tensor_copy(out=XB[:, ci, CTX:LS].rearrange("p (c r) -> p c r", c=GW),
                                                          in_=RM[:].rearrange("p (r c) -> p c r", c=GW)), reads=[RM.b], writes=[XB.b])
                for d in range(2):
                    segs = [(0, CTX), (CTX, LS)]
                    hprev = None
                    blk_i = 0
                    for (S0, S1) in segs:
                        starts = list(range(S0, S1, BL))
                        if d == 1:
                            starts = starts[::-1]
                        for s0 in starts:
                            s1 = min(s0 + BL, S1)
                            n = s1 - s0
                            for ci in range(2):
                                vo = (d * 2 + ci) * 8
                                kb.op("act", lambda E: E.activation(out=xcf[:, ci, 0:n], in_=XB[:, ci, s0:s1], func=AF.Identity,
                                                                    scale=ov[:, vo + 3:vo + 4], bias=ov[:, vo + 4:vo + 5]),
                                      reads=[XB.b, ov.b], writes=[xcf.b])
                                for k in range(1, 4):
                                    if d == 0:
                                        lo = max(s0, S0 + k)
                                        if lo >= s1:
                                            continue
                                        kb.op("dve", lambda E: E.scalar_tensor_tensor(
                                            out=xcf[:, ci, lo - s0:n], in0=XB[:, ci, lo - k:s1 - k], scalar=ov[:, vo + 3 - k:vo + 4 - k],
                                            in1=xcf[:, ci, lo - s0:n], op0=ALU.mult, op1=ALU.add), reads=[XB.b, ov.b, xcf.b], writes=[xcf.b])
                                    else:
                                        hi = min(s1, S1 - k)
                                        if hi <= s0:
                                            continue
                                        kb.op("dve", lambda E: E.scalar_tensor_tensor(
                                            out=xcf[:, ci, 0:hi - s0], in0=XB[:, ci, s0 + k:hi + k], scalar=ov[:, vo + 3 - k:vo + 4 - k],
                                            in1=xcf[:, ci, 0:hi - s0], op0=ALU.mult, op1=ALU.add), reads=[XB.b, ov.b, xcf.b], writes=[xcf.b])
                            kb.op("pool", lambda E: E.tensor_copy(out=xcb[:, :, 0:n], in_=xcf[:, :, 0:n]), reads=[xcf.b], writes=[xcb.b])
                            for jc in range(2):
                                vo = (d * 2 + jc) * 8
                                for g, dst in ((0, gr), (1, gi)):
                                    ps = nextps()
                                    for ic in range(2):
                                        kb.op("pe", lambda E: E.matmul(ps[:, 0:n], lhsT=GWt[:, (d * 2 + g) * 2 + ic, jc * 128:(jc + 1) * 128],
                                                                       rhs=xcb[:, ic, 0:n], start=(ic == 0), stop=(ic == 1)),
                                              reads=[GWt.b, xcb.b], writes=[ps.b])
                                    kb.op("act", lambda E: E.activation(out=dst[:, jc, 0:n], in_=ps[:, 0:n], func=AF.Sigmoid,
                                                                        bias=ov[:, vo + 5 + g:vo + 6 + g]), reads=[ps.b, ov.b], writes=[dst.b])
                                kb.op("act", lambda E: E.activation(out=aa[:, jc, 0:n], in_=gr[:, jc, 0:n], func=AF.Exp, scale=spn[:, d * 2 + jc:d * 2 + jc + 1]),
                                      reads=[gr.b, spn.b], writes=[aa.b])
                            kb.op("dve", lambda E: E.tensor_tensor(out=uu[:, :, 0:n], in0=aa[:, :, 0:n], in1=aa[:, :, 0:n], op=ALU.mult), reads=[aa.b], writes=[uu.b])
                            kb.op("dve", lambda E: E.tensor_scalar(out=uu[:, :, 0:n], in0=uu[:, :, 0:n], scalar1=-1.0, scalar2=1.0, op0=ALU.mult, op1=ALU.add),
                                  reads=[uu.b], writes=[uu.b])
                            kb.op("dve", lambda E: E.tensor_scalar(out=uu[:, :, 0:n], in0=uu[:, :, 0:n], scalar1=1e-30, scalar2=None, op0=ALU.max),
                                  reads=[uu.b], writes=[uu.b])
                            kb.op("act", lambda E: E.activation(out=uu[:, :, 0:n], in_=uu[:, :, 0:n], func=AF.Sqrt), reads=[uu.b], writes=[uu.b])
                            kb.op("dve", lambda E: E.tensor_tensor(out=uu[:, :, 0:n], in0=uu[:, :, 0:n], in1=gi[:, :, 0:n], op=ALU.mult), reads=[uu.b, gi.b], writes=[uu.b])
                            kb.op("dve", lambda E: E.tensor_tensor(out=uu[:, :, 0:n], in0=uu[:, :, 0:n], in1=xcf[:, :, 0:n], op=ALU.mult), reads=[uu.b, xcf.b], writes=[uu.b])
                            h = hh[blk_i % 2]
                            for ci in range(2):
                                if hprev is None:
                                    init = zero
                                else:
                                    hp, pn = hprev
                                    init = hp[:, ci, pn - 1:pn] if d == 0 else hp[:, ci, 0:1]
                                if d == 0:
                                    kb.op("dve", lambda E: E.tensor_tensor_scan(out=h[:, ci, 0:n], data0=aa[:, ci, 0:n], data1=uu[:, ci, 0:n],
                                                                                initial=init, op0=ALU.mult, op1=ALU.add),
                                          reads=[aa.b, uu.b, cst.b] + ([hprev[0].b] if hprev else []), writes=[h.b])
                                else:
                                    kb.op("dve", lambda E: E.tensor_tensor_scan(out=h[:, ci, 0:n][:, ::-1], data0=aa[:, ci, 0:n][:, ::-1],
                                                                                data1=uu[:, ci, 0:n][:, ::-1], initial=init, op0=ALU.mult, op1=ALU.add),
                                          reads=[aa.b, uu.b, cst.b] + ([hprev[0].b] if hprev else []), writes=[h.b])
                            hsc_v = HSC.ap.rearrange("(c p) s -> p c s", p=128)
                            if d == 0:
                                kb.dma("act", hsc_v[:, :, s0:s1], h[:, :, 0:n], HSC.b, h.b)
                            else:
                                kb.dma("sp", hfw[:, :, 0:n], hsc_v[:, :, s0:s1], hfw.b, HSC.b)
                                kb.op("pool", lambda E: E.tensor_tensor(out=hfw[:, :, 0:n], in0=hfw[:, :, 0:n], in1=h[:, :, 0:n], op=ALU.add),
                                      reads=[h.b, hfw.b], writes=[hfw.b])
                                kb.dma("act", hsc_v[:, :, s0:s1], hfw[:, :, 0:n], HSC.b, hfw.b)
                            hprev = (h, n)
                            blk_i += 1
                for ci in range(2):
                    kb.dma("sp", RM[:, 0:CTX], hsc_v[:, ci, 0:CTX], RM.b, HSC.b)
                    kb.dma("act", hs_v[:, ci, b, 0:CTX], RM[:, 0:CTX], HS.b, RM.b)
                    kb.dma("sp", RM[:], hsc_v[:, ci, CTX:LS], RM.b, HSC.b)
                    kb.op("pool", lambda E: E.tensor_copy(out=XB[:, 0, 0:SEQ].rearrange("p (r c) -> p c r", c=GW),
                                                          in_=RM[:].rearrange("p (c r) -> p c r", c=GW)), reads=[RM.b], writes=[XB.b])
                    kb.dma("act", hs_v[:, ci, b, CTX:LS], XB[:, 0, 0:SEQ], HS.b, XB.b)
            kb.barrier()
            kb.close_scope()

    E05 = math.exp(-0.5)
    pt_v = PT.ap.rearrange("(c p) (b s) -> p c b s", p=128, b=2)

    def scan_blocks(nb):
        out = []
        for (S0, S1) in ((0, CTX), (CTX, LS)):
            for q0 in range(0, S1 - S0, nb):
                out.append((S0, S1, q0, min(nb, S1 - S0 - q0)))
        return out

    def load_scan(dst, tmp, rows_ap_fn, d, S0, S1, q0, n, halo, npart):
        if d == 0:
            lo = S0 + q0
            h = min(halo, q0)
            if h < halo:
                kb.op("pool", lambda E: E.memset(dst[0:npart, :, 0:halo - h], 0.0), writes=[dst.b])
            kb.dma("sp", dst[0:npart, :, halo - h:halo + n], rows_ap_fn(lo - h, lo + n), dst.b, PT.b)
        else:
            hi = S1 - q0
            h = min(halo, q0)
            if h < halo:
                kb.op("pool", lambda E: E.memset(tmp[0:npart, :, n + h:n + halo], 0.0), writes=[tmp.b])
            kb.dma("sp", tmp[0:npart, :, 0:n + h], rows_ap_fn(hi - n, hi + h), tmp.b, PT.b)
            kb.op("dve", lambda E: E.tensor_copy(out=dst[0:npart, :, 0:n + halo], in_=tmp[0:npart, :, 0:n + halo][:, :, ::-1]),
                  reads=[tmp.b], writes=[dst.b])

    def store_scan(dr, rows_ap_fn, src, tmp, d, S0, S1, q0, n, npart):
        if d == 0:
            kb.dma("act", rows_ap_fn(S0 + q0, S0 + q0 + n), src[0:npart, 0:n], dr.b, src.b)
        else:
            kb.op("dve", lambda E: E.tensor_copy(out=tmp[0:npart, 0:n], in_=src[0:npart, 0:n][:, ::-1]), reads=[src.b], writes=[tmp.b])
            kb.dma("act", rows_ap_fn(S1 - q0 - n, S1 - q0), tmp[0:npart, 0:n], dr.b, tmp.b)

    def mixer_rwkv(e):
        C = 64
        NBK = 512
        with ExitStack() as st:
            kb.open_scope()
            ev = Tl(kb, st, "ev", [128, 96])
            kb.dma("sp", ev[:], evec_d.ap[:, e * 96:(e + 1) * 96], ev.b, None)
            lwt = Tl(kb, st, "lwt", [96, 8, 64])
            for i in range(8):
                kb.dma("sp", lwt[:, i, :], lora_d.ap[(e * 8 + i) * 96:(e * 8 + i + 1) * 96, :], lwt.b, None)
            omk = Tl(kb, st, "omk", [64, 4])
            for i in range(4):
                kb.op("dve", lambda E: E.tensor_scalar(out=omk[:, i:i + 1], in0=ev[0:64, 40 + i * 8 + 6:40 + i * 8 + 7], scalar1=-1.0, scalar2=1.0,
                                                       op0=ALU.mult, op1=ALU.add), reads=[ev.b], writes=[omk.b])
            MK = Tl(kb, st, "MK", [64, 128])
            ML = Tl(kb, st, "ML", [64, 64])
            on64 = Tl(kb, st, "on64", [64, NBK])
            kb.op("pool", lambda E: E.memset(on64[:], 1.0), writes=[on64.b])
            kb.op("pool", lambda E: E.affine_select(out=MK[:, 0:64], in_=on64[:, 0:64], pattern=[[1, 64]], compare_op=ALU.is_gt, fill=0.0, base=0,
                                                    channel_multiplier=-1), reads=[on64.b], writes=[MK.b])
            kb.op("pool", lambda E: E.affine_select(out=MK[:, 64:128], in_=on64[:, 0:64], pattern=[[1, 64]], compare_op=ALU.is_ge, fill=0.0, base=0,
                                                    channel_multiplier=-1), reads=[on64.b], writes=[MK.b])
            kb.op("pool", lambda E: E.affine_select(out=ML[:], in_=on64[:, 0:64], pattern=[[-1, 64]], compare_op=ALU.is_gt, fill=0.0, base=0,
                                                    channel_multiplier=1), reads=[on64.b], writes=[ML.b])
            W = []
            for hh in range(2):
                w = {}
                for nm, shp in (("F", [64, 3, NBK + 1]), ("G", [64, 3, NBK + 1]), ("FL", [96, 2, NBK + 1]), ("GL", [96, 2, NBK + 1]),
                                ("f", [64, 3, NBK]), ("fl", [96, 2, NBK]), ("t1", [64, 3, NBK]), ("tl", [96, 2, NBK]),
                                ("lgw", [64, NBK]), ("a", [64, NBK]), ("kap", [64, NBK]), ("kp", [64, NBK]), ("bet", [64, NBK]), ("cs", [64, NBK]),
                                ("tA", [64, NBK]), ("tB", [64, NBK]), ("OB", [64, NBK]), ("OT", [64, NBK]),
                                ("lw", [64, C]), ("lm", [64, C]), ("g", [64, C]), ("gi", [64, C]), ("gm", [64, C]),
                                ("KR", [64, 2 * C]), ("kt", [64, C]), ("bt", [64, C]), ("AA0", [64, 128]), ("AA1", [64, 128]), ("BbT", [64, C]),
                                ("PBm", [64, 128]), ("X", [64, 128]), ("TM", [64, 192]), ("W2n", [64, 64]), ("RhT", [64, C]), ("MTn", [64, 64]),
                                ("Ha", [64, 64]), ("Hb", [64, 64]), ("ht", [64, 64])):
                    w[nm] = Tl(kb, st, f"{nm}{hh}", shp)
                W.append(w)
            zero = cst[0:64, 2:3]

            def mm(ps_ap, lhsT, rhs, R, Wb, start=True, stop=True):
                kb.op("pe", lambda E: E.matmul(ps_ap, lhsT=lhsT, rhs=rhs, start=start, stop=stop), reads=R, writes=Wb)

            def dve_tt(out, in0, in1, op, R, Wb):
                kb.op("dve", lambda E: E.tensor_tensor(out=out, in0=in0, in1=in1, op=op), reads=R, writes=Wb)

            for b in range(2):
                for d in range(2):
                    for hh in range(2):
                        kb.op("pool", lambda E: E.memset(W[hh]["Ha"][:], 0.0), writes=[W[hh]["Ha"].b])
                    Hcur = [W[0]["Ha"], W[1]["Ha"]]
                    Hnxt = [W[0]["Hb"], W[1]["Hb"]]
                    for (S0, S1, q0, n) in scan_blocks(NBK):
                        for hh in range(2):
                            w = W[hh]
                            vo = 40 + (d * 2 + hh) * 8
                            col = lambda j: ev[0:64, vo + j:vo + j + 1]
                            load_scan(w["F"], w["G"], lambda lo, hi: pt_v[hh * 64:hh * 64 + 64, 0:3, b, lo:hi], d, S0, S1, q0, n, 1, 64)
                            load_scan(w["FL"], w["GL"], lambda lo, hi: pt_v[0:96, 3:5, b, lo:hi], d, S0, S1, q0, n, 1, 96)
                            F, FL, f, fl, t1, tl = w["F"], w["FL"], w["f"], w["fl"], w["t1"], w["tl"]
                            dve_tt(t1[:, :, 0:n], F[:, :, 0:n], F[:, :, 1:n + 1], ALU.subtract, [F.b], [t1.b])
                            for q in range(3):
                                kb.op("dve", lambda E: E.scalar_tensor_tensor(out=f[:, q, 0:n], in0=t1[:, q, 0:n], scalar=col(q), in1=F[:, q, 1:n + 1],
                                                                              op0=ALU.mult, op1=ALU.add), reads=[t1.b, F.b, ev.b], writes=[f.b])
                            dve_tt(tl[:, :, 0:n], FL[:, :, 0:n], FL[:, :, 1:n + 1], ALU.subtract, [FL.b], [tl.b])
                            for q in range(2):
                                kb.op("dve", lambda E: E.scalar_tensor_tensor(out=fl[:, q, 0:n], in0=tl[:, q, 0:n], scalar=ev[0:96, 72 + d * 2 + q:73 + d * 2 + q],
                                                                              in1=FL[:, q, 1:n + 1], op0=ALU.mult, op1=ALU.add), reads=[tl.b, FL.b, ev.b], writes=[fl.b])
                            kb.op("act", lambda E: E.activation(out=tl[:, 0, 0:n], in_=fl[:, 0, 0:n], func=AF.Tanh), reads=[fl.b], writes=[tl.b])
                            ps = nextps()
                            mm(ps[0:64, 0:n], lwt[:, (d * 2 + hh) * 2 + 0, :], tl[:, 0, 0:n], [lwt.b, tl.b], [ps.b])
                            kb.op("act", lambda E: E.activation(out=w["lgw"][:, 0:n], in_=ps[0:64, 0:n], func=AF.Sigmoid, bias=col(3)), reads=[ps.b, ev.b], writes=[w["lgw"].b])
                            ps = nextps()
                            mm(ps[0:64, 0:n], lwt[:, (d * 2 + hh) * 2 + 1, :], fl[:, 1, 0:n], [lwt.b, fl.b], [ps.b])
                            kb.op("act", lambda E: E.activation(out=w["a"][:, 0:n], in_=ps[0:64, 0:n], func=AF.Sigmoid, bias=col(4)), reads=[ps.b, ev.b], writes=[w["a"].b])
                            kb.op("act", lambda E: E.activation(out=w["tA"][:, 0:n], in_=f[:, 1, 0:n], func=AF.Square, scale=col(5)), reads=[f.b, ev.b], writes=[w["tA"].b])
                            ps = nextps()
                            mm(ps[0:64, 0:n], on64[:, 0:64], w["tA"][:, 0:n], [on64.b, w["tA"].b], [ps.b])
                            kb.op("act", lambda E: E.activation(out=w["tB"][:, 0:n], in_=ps[0:64, 0:n], func=AF.Sqrt), reads=[ps.b], writes=[w["tB"].b])
                            kb.op("dve", lambda E: E.tensor_scalar(out=w["tB"][:, 0:n], in0=w["tB"][:, 0:n], scalar1=1e-12, scalar2=None, op0=ALU.max), reads=[w["tB"].b], writes=[w["tB"].b])
                            kb.op("dve", lambda E: E.reciprocal(out=w["tB"][:, 0:n], in_=w["tB"][:, 0:n]), reads=[w["tB"].b], writes=[w["tB"].b])
                            kb.op("dve", lambda E: E.scalar_tensor_tensor(out=w["kap"][:, 0:n], in0=f[:, 1, 0:n], scalar=col(5), in1=w["tB"][:, 0:n], op0=ALU.mult, op1=ALU.mult),
                                  reads=[f.b, ev.b, w["tB"].b], writes=[w["kap"].b])
                            kb.op("act", lambda E: E.activation(out=w["tA"][:, 0:n], in_=w["a"][:, 0:n], func=AF.Identity, scale=col(6), bias=omk[:, d * 2 + hh:d * 2 + hh + 1]),
                                  reads=[w["a"].b, ev.b, omk.b], writes=[w["tA"].b])
                            dve_tt(w["kp"][:, 0:n], f[:, 1, 0:n], w["tA"][:, 0:n], ALU.mult, [f.b, w["tA"].b], [w["kp"].b])
                            dve_tt(w["bet"][:, 0:n], w["kap"][:, 0:n], w["a"][:, 0:n], ALU.mult, [w["kap"].b, w["a"].b], [w["bet"].b])
                            kb.op("dve", lambda E: E.scalar_tensor_tensor(out=w["tA"][:, 0:n], in0=f[:, 0, 0:n], scalar=col(7), in1=w["kp"][:, 0:n], op0=ALU.mult, op1=ALU.mult),
                                  reads=[f.b, ev.b, w["kp"].b], writes=[w["tA"].b])
                            ps = nextps()
                            mm(ps[0:64, 0:n], on64[:, 0:64], w["tA"][:, 0:n], [on64.b, w["tA"].b], [ps.b])
                            dve_tt(w["tB"][:, 0:n], ps[0:64, 0:n], f[:, 2, 0:n], ALU.mult, [ps.b, f.b], [w["tB"].b])
                            store_scan(BN[d], lambda lo, hi: BN[d].ap[hh * 64:hh * 64 + 64, b * LS + lo:b * LS + hi], w["tB"], w["OT"], d, S0, S1, q0, n, 64)
                            kb.op("dve", lambda E: E.tensor_tensor_scan(out=w["cs"][:, 0:n], data0=on64[:, 0:n], data1=w["lgw"][:, 0:n], initial=0.0,
                                                                        op0=ALU.mult, op1=ALU.add), reads=[on64.b, w["lgw"].b], writes=[w["cs"].b])
                        for c0 in (range(0, n, C) if "nochunk" not in DBG else []):
                            for hh in range(2):
                                w = W[hh]
                                f = w["f"]
                                H = Hcur[hh]
                                Hn = Hnxt[hh]
                                cs_ = slice(c0, c0 + C)
                                off = w["cs"][:, c0 - 1:c0] if c0 > 0 else zero
                                kb.op("dve", lambda E: E.tensor_scalar(out=w["lw"][:], in0=w["cs"][:, cs_], scalar1=off, scalar2=-E05, op0=ALU.subtract, op1=ALU.mult),
                                      reads=[w["cs"].b, cst.b], writes=[w["lw"].b])
                                kb.op("dve", lambda E: E.scalar_tensor_tensor(out=w["lm"][:], in0=w["lgw"][:, cs_], scalar=E05, in1=w["lw"][:], op0=ALU.mult, op1=ALU.add),
                                      reads=[w["lgw"].b, w["lw"].b], writes=[w["lm"].b])
                                kb.op("act", lambda E: E.activation(out=w["g"][:], in_=w["lw"][:], func=AF.Exp), reads=[w["lw"].b], writes=[w["g"].b])
                                kb.op("act", lambda E: E.activation(out=w["gi"][:], in_=w["lw"][:], func=AF.Exp, scale=-1.0), reads=[w["lw"].b], writes=[w["gi"].b])
                                kb.op("act", lambda E: E.activation(out=w["gm"][:], in_=w["lm"][:], func=AF.Exp), reads=[w["lm"].b], writes=[w["gm"].b])
                                KR, kt, bt = w["KR"], w["kt"], w["bt"]
                                dve_tt(KR[:, 0:C], w["kap"][:, cs_], w["gm"][:], ALU.mult, [w["kap"].b, w["gm"].b], [KR.b])
                                dve_tt(KR[:, C:2 * C], f[:, 0, cs_], w["g"][:], ALU.mult, [f.b, w["g"].b], [KR.b])
                                dve_tt(kt[:], w["kp"][:, cs_], w["gi"][:], ALU.mult, [w["kp"].b, w["gi"].b], [kt.b])
                                dve_tt(bt[:], w["bet"][:, cs_], w["gi"][:], ALU.mult, [w["bet"].b, w["gi"].b], [bt.b])
                                CK = int(DBG[DBG.index("ck") + 2]) if "ck" in DBG else 9
                                if CK < 2:
                                    continue
                                pa, pb, pc = nextps(), nextps(), nextps()
                                mm(pa[0:64, 0:128], bt[:], KR[:], [bt.b, KR.b], [pa.b])
                                mm(pb[0:64, 0:128], kt[:], KR[:], [kt.b, KR.b], [pb.b])
                                mm(pc[0:64, 0:64], KR[:, 0:C], bt[:], [KR.b, bt.b], [pc.b])
                                AA = w["AA0"]
                                dve_tt(AA[:, 0:64], pa[0:64, 0:64], MK[:, 0:64], ALU.mult, [pa.b, MK.b], [AA.b])
                                dve_tt(w["BbT"][:], pa[0:64, 64:128], MK[:, 64:128], ALU.mult, [pa.b, MK.b], [w["BbT"].b])
                                dve_tt(w["PBm"][:], pb[0:64, 0:128], MK[:], ALU.mult, [pb.b, MK.b], [w["PBm"].b])
                                dve_tt(AA[:, 64:128], pc[0:64, 0:64], ML[:], ALU.mult, [pc.b, ML.b], [AA.b])
                                if CK < 3:
                                    continue
                                pt = nextps()
                                for i, src in enumerate((KR[:, 0:C], kt[:], bt[:], f[:, 2, cs_])):
                                    srcb = [KR.b, kt.b, bt.b, f.b][i]
                                    kb.op("pe", lambda E: E.matmul(pt[0:64, i * 64:(i + 1) * 64], lhsT=src, rhs=ident[0:64, 0:64], start=True, stop=True), reads=[srcb, ident.b], writes=[pt.b])
                                X, TM = w["X"], w["TM"]
                                if "ck3a" in DBG:
                                    continue
                                evac(X[:, 0:64], pt[0:64, 0:64], [pt.b], [X.b])
                                evac(TM[:], pt[0:64, 64:256], [pt.b], [TM.b])
                                Kt_, Bt_, V_ = TM[:, 0:64], TM[:, 64:128], TM[:, 128:192]
                                if "ck3b" in DBG:
                                    continue
                                pk = nextps()
                                mm(pk[0:64, 0:64], w["PBm"][:, 0:64], V_, [w["PBm"].b, TM.b], [pk.b])
                                evac(X[:, 64:128], pk[0:64, 0:64], [pk.b], [X.b])
                                if CK < 4:
                                    continue
                                for lev in range(6):
                                    if lev > 0:
                                        AAn = w["AA1"] if AA is w["AA0"] else w["AA0"]
                                        pq = nextps()
                                        mm(pq[0:64, 0:64], AA[:, 64:128], AA[:, 0:64], [AA.b], [pq.b])
                                        if lev < 5:
                                            mm(pq[0:64, 64:128], AA[:, 0:64], AA[:, 64:128], [AA.b], [pq.b])
                                            evac(AAn[:], pq[0:64, 0:128], [pq.b], [AAn.b])
                                        else:
                                            evac(AAn[:, 0:64], pq[0:64, 0:64], [pq.b], [AAn.b])
                                        AA = AAn
                                    px = nextps()
                                    mm(px[0:64, 0:128], AA[:, 0:64], X[:], [AA.b, X.b], [px.b])
                                    dve_tt(X[:], X[:], px[0:64, 0:128], ALU.subtract if lev == 0 else ALU.add, [X.b, px.b], [X.b])
                                if CK < 5:
                                    continue
                                kb.op("dve", lambda E: E.tensor_scalar(out=w["W2n"][:], in0=X[:, 64:128], scalar1=-1.0, scalar2=None, op0=ALU.mult), reads=[X.b], writes=[w["W2n"].b])
                                W1 = X[:, 0:64]
                                pr = nextps()
                                mm(pr[0:64, 0:64], W1, w["BbT"][:], [X.b, w["BbT"].b], [pr.b])
                                dve_tt(w["RhT"][:], KR[:, C:2 * C], pr[0:64, 0:64], ALU.subtract, [KR.b, pr.b], [w["RhT"].b])
                                pm = nextps()
                                mm(pm[0:64, 0:64], W1, Bt_, [X.b, TM.b], [pm.b])
                                kb.op("dve", lambda E: E.tensor_scalar(out=w["MTn"][:], in0=pm[0:64, 0:64], scalar1=-1.0, scalar2=None, op0=ALU.mult), reads=[pm.b], writes=[w["MTn"].b])
                                pg = nextps()
                                mm(pg[0:64, 0:64], Kt_, V_, [TM.b], [pg.b], start=True, stop=False)
                                mm(pg[0:64, 0:64], Bt_, w["W2n"][:], [TM.b, w["W2n"].b], [pg.b], start=False, stop=False)
                                mm(pg[0:64, 0:64], w["MTn"][:], H[:], [w["MTn"].b, H.b], [pg.b], start=False, stop=True)
                                py = nextps()
                                mm(py[0:64, 0:64], V_, w["PBm"][:, 64:128], [TM.b, w["PBm"].b], [py.b], start=True, stop=False)
                                mm(py[0:64, 0:64], w["W2n"][:], w["BbT"][:], [w["W2n"].b, w["BbT"].b], [py.b], start=False, stop=False)
                                mm(py[0:64, 0:64], H[:], w["RhT"][:], [H.b, w["RhT"].b], [py.b], start=False, stop=True)
                                evac(w["OB"][:, cs_], py[0:64, 0:64], [py.b], [w["OB"].b])
                                dve_tt(w["ht"][:], H[:], pg[0:64, 0:64], ALU.add, [H.b, pg.b], [w["ht"].b])
                                kb.op("dve", lambda E: E.tensor_scalar(out=Hn[:], in0=w["ht"][:], scalar1=w["g"][:, C - 1:C], scalar2=None, op0=ALU.mult),
                                      reads=[w["ht"].b, w["g"].b], writes=[Hn.b])
                                Hcur[hh], Hnxt[hh] = Hn, H
                        for hh in range(2):
                            w = W[hh]
                            store_scan(YR[d], lambda lo, hi: YR[d].ap[hh * 64:hh * 64 + 64, b * LS + lo:b * LS + hi], w["OB"], w["OT"], d, S0, S1, q0, n, 64)
            kb.barrier()
            kb.close_scope()

    def mixer_ssd(e):
        C = 128
        NBK = 512
        with ExitStack() as st:
            kb.open_scope()
            ev = Tl(kb, st, "evs", [128, 96])
            kb.dma("sp", ev[:], evec_d.ap[:, e * 96:(e + 1) * 96], ev.b, None)
            dv = Tl(kb, st, "dv", [128, 8])
            kb.dma("sp", dv[0:64, :], dvec_d.ap[:, e * 8:(e + 1) * 8], dv.b, None)
            kb.dma("sp", dv[64:128, :], dvec_d.ap[:, e * 8:(e + 1) * 8], dv.b, None)
            s4 = Tl(kb, st, "s4", [4, 512])
            kb.dma("sp", s4[:], sel4_d.ap, s4.b, None)
            selt = Tl(kb, st, "seltm", [128, 12])
            kb.dma("sp", selt[:], sel_d.ap, selt.b, None)
            MU = Tl(kb, st, "MU", [128, 128])
            on = Tl(kb, st, "onS", [128, 128])
            kb.op("pool", lambda E: E.memset(on[:], 1.0), writes=[on.b])
            kb.op("pool", lambda E: E.affine_select(out=MU[:], in_=on[:], pattern=[[1, 128]], compare_op=ALU.is_ge, fill=0.0, base=0, channel_multiplier=-1),
                  reads=[on.b], writes=[MU.b])
            F = Tl(kb, st, "Fs", [128, 4, NBK + 3])
            G0 = Tl(kb, st, "G0s", [128, 4, NBK + 3])
            G1 = Tl(kb, st, "G1s", [128, 4, NBK + 3])
            Fd = Tl(kb, st, "Fd", [4, 1, NBK])
            Gd0 = Tl(kb, st, "Gd0", [4, 1, NBK])
            Gd1 = Tl(kb, st, "Gd1", [4, 1, NBK])
            xc = Tl(kb, st, "xcs", [128, 4, NBK])
            xs = Tl(kb, st, "xss", [128, 4, NBK])
            dtt = Tl(kb, st, "dtt", [4, NBK])
            dta = Tl(kb, st, "dta", [4, NBK])
            acol = Tl(kb, st, "acolS", [4, 2])
            acs = Tl(kb, st, "acs", [4, C])
            DTA = Tl(kb, st, "DTA", [128, 8])
            CBm = Tl(kb, st, "CBm", [128, C])
            Btok = Tl(kb, st, "Btok", [128, 128])
            sg = [Tl(kb, st, f"sgS{i}", [128, C]) for i in range(2)]
            MT = [Tl(kb, st, f"MTs{i}", [128, C]) for i in range(2)]
            gb = [Tl(kb, st, f"gbS{i}", [128, C]) for i in range(2)]
            rT = [Tl(kb, st, f"rTs{i}", [128, C]) for i in range(2)]
            al = [Tl(kb, st, f"alS{i}", [128, 2]) for i in range(2)]
            xdt = [Tl(kb, st, f"xdt{i}", [128, 64]) for i in range(2)]
            xdw = [Tl(kb, st, f"xdw{i}", [128, 64]) for i in range(2)]
            xdd = [Tl(kb, st, f"xdd{i}", [128, 64]) for i in range(2)]
            Hs = [Tl(kb, st, f"Hs{i}", [128, 64]) for i in range(4)]
            OB = [Tl(kb, st, f"OBs{i}", [64, NBK]) for i in range(4)]
            OT = Tl(kb, st, "OTs", [64, NBK])

            def mm(ps_ap, lhsT, rhs, R, Wb, start=True, stop=True):
                kb.op("pe", lambda E: E.matmul(ps_ap, lhsT=lhsT, rhs=rhs, start=start, stop=stop), reads=R, writes=Wb)

            def blend(dst, g0, g1, npart, width):
                kb.op("dve", lambda E: E.tensor_scalar(out=dst[0:npart, :, 0:width], in0=g0[0:npart, :, 0:width], scalar1=selt[0:npart, 10:11], scalar2=None, op0=ALU.mult),
                      reads=[g0.b, selt.b], writes=[dst.b])
                kb.op("dve", lambda E: E.scalar_tensor_tensor(out=dst[0:npart, :, 0:width], in0=g1[0:npart, :, 0:width], scalar=selt[0:npart, 11:12], in1=dst[0:npart, :, 0:width],
                                                              op0=ALU.mult, op1=ALU.add), reads=[g1.b, selt.b, dst.b], writes=[dst.b])

            for d in range(2):
                kb.op("act", lambda E: E.activation(out=acol[:, d:d + 1], in_=ev[0:4, 80 + d * 2 + 1:80 + d * 2 + 2], func=AF.Exp), reads=[ev.b], writes=[acol.b])
                kb.op("dve", lambda E: E.tensor_scalar(out=acol[:, d:d + 1], in0=acol[:, d:d + 1], scalar1=-1.0, scalar2=None, op0=ALU.mult), reads=[acol.b], writes=[acol.b])
                for hd in range(4):
                    kb.op("pool", lambda E: E.memset(Hs[hd][:], 0.0), writes=[Hs[hd].b])
                for (S0, S1, q0, n) in scan_blocks(NBK):
                    if d == 0:
                        lo, hi = S0 + q0, S0 + q0 + n
                        h = min(3, q0)
                        for bsel, Gx, Gdx in ((0, G0, Gd0), (1, G1, Gd1)):
                            if h < 3:
                                kb.op("pool", lambda E: E.memset(Gx[:, :, 0:3 - h], 0.0), writes=[Gx.b])
                            kb.dma("sp", Gx[:, :, 3 - h:3 + n], pt_v[:, 9:13, bsel, lo - h:hi], Gx.b, PT.b)
                            kb.dma("sp", Gdx[:, :, 0:n], pt_v[0:4, 13:14, bsel, lo:hi], Gdx.b, PT.b)
                        blend(F, G0, G1, 128, n + 3)
                        blend(Fd, Gd0, Gd1, 4, n)
                    else:
                        hi = S1 - q0
                        lo = hi - n
                        h = min(3, q0)
                        for bsel, Gx, Gdx in ((0, G0, Gd0), (1, G1, Gd1)):
                            if h < 3:
                                kb.op("pool", lambda E: E.memset(Gx[:, :, n + h:n + 3], 0.0), writes=[Gx.b])
                            kb.dma("sp", Gx[:, :, 0:n + h], pt_v[:, 9:13, bsel, lo:hi + h], Gx.b, PT.b)
                            kb.dma("sp", Gdx[:, :, 0:n], pt_v[0:4, 13:14, bsel, lo:hi], Gdx.b, PT.b)
                        blend(G0, G0, G1, 128, n + 3)
                        blend(Gd0, Gd0, Gd1, 4, n)
                        kb.op("dve", lambda E: E.tensor_copy(out=F[:, :, 0:n + 3], in_=G0[:, :, 0:n + 3][:, :, ::-1]), reads=[G0.b], writes=[F.b])
                        kb.op("dve", lambda E: E.tensor_copy(out=Fd[:, :, 0:n], in_=Gd0[:, :, 0:n][:, :, ::-1]), reads=[Gd0.b], writes=[Fd.b])
                    for q in range(4):
                        vo = d * 20 + q * 5
                        kb.op("act", lambda E: E.activation(out=xc[:, q, 0:n], in_=F[:, q, 3:3 + n], func=AF.Identity, scale=ev[:, vo + 3:vo + 4], bias=ev[:, vo + 4:vo + 5]),
                              reads=[F.b, ev.b], writes=[xc.b])
                        for k in range(1, 4):
                            kb.op("dve", lambda E: E.scalar_tensor_tensor(out=xc[:, q, 0:n], in0=F[:, q, 3 - k:3 - k + n], scalar=ev[:, vo + 3 - k:vo + 4 - k], in1=xc[:, q, 0:n],
                                                                          op0=ALU.mult, op1=ALU.add), reads=[F.b, ev.b, xc.b], writes=[xc.b])
                    kb.op("act", lambda E: E.activation(out=xs[:, :, 0:n], in_=xc[:, :, 0:n], func=AF.Silu), reads=[xc.b], writes=[xs.b])
                    kb.op("act", lambda E: E.activation(out=dtt[:, 0:n], in_=Fd[:, 0, 0:n], func=AF.Exp, bias=ev[0:4, 80 + d * 2:80 + d * 2 + 1]), reads=[Fd.b, ev.b], writes=[dtt.b])
                    kb.op("act", lambda E: E.activation(out=dtt[:, 0:n], in_=dtt[:, 0:n], func=AF.Ln, bias=cst[0:4, 1:2]), reads=[dtt.b, cst.b], writes=[dtt.b])
                    kb.op("dve", lambda E: E.tensor_scalar(out=dta[:, 0:n], in0=dtt[:, 0:n], scalar1=acol[:, d:d + 1], scalar2=None, op0=ALU.mult), reads=[dtt.b, acol.b], writes=[dta.b])
                    for c0 in range(0, n, C):
                        cs_ = slice(c0, c0 + C)
                        kb.op("dve", lambda E: E.tensor_tensor_scan(out=acs[:], data0=on[0:4, 0:C], data1=dta[:, cs_], initial=0.0, op0=ALU.mult, op1=ALU.add),
                              reads=[on.b, dta.b], writes=[acs.b])
                        pt = nextps()
                        kb.op("pe", lambda E: E.matmul(pt[:, 0:4], lhsT=dtt[:, cs_], rhs=ident[0:4, 0:4], start=True, stop=True), reads=[dtt.b, ident.b], writes=[pt.b])
                        kb.op("pe", lambda E: E.matmul(pt[:, 4:8], lhsT=acs[:], rhs=ident[0:4, 0:4], start=True, stop=True), reads=[acs.b, ident.b], writes=[pt.b])
                        evac(DTA[:], pt[:, 0:8], [pt.b], [DTA.b])
                        pcb = nextps()
                        mm(pcb[:, 0:C], xs[:, 2, cs_], xs[:, 3, cs_], [xs.b], [pcb.b])
                        kb.op("dve", lambda E: E.tensor_tensor(out=CBm[:], in0=pcb[:, 0:C], in1=MU[:], op=ALU.mult), reads=[pcb.b, MU.b], writes=[CBm.b])
                        pbt = nextps()
                        kb.op("pe", lambda E: E.matmul(pbt[:, 0:128], lhsT=xs[:, 2, cs_], rhs=ident[:], start=True, stop=True), reads=[xs.b, ident.b], writes=[pbt.b])
                        evac(Btok[:], pbt[:, 0:128], [pbt.b], [Btok.b])
                        for hd in range(4):
                            i2 = hd % 2
                            H = Hs[hd]
                            pab = nextps()
                            mm(pab[:, 0:C], s4[:, hd * 128:(hd + 1) * 128], acs[:], [s4.b, acs.b], [pab.b])
                            kb.op("dve", lambda E: E.tensor_scalar(out=sg[i2][:], in0=pab[:, 0:C], scalar1=DTA[:, 4 + hd:5 + hd], scalar2=0.0, op0=ALU.subtract, op1=ALU.min),
                                  reads=[pab.b, DTA.b], writes=[sg[i2].b])
                            kb.op("act", lambda E: E.activation(out=sg[i2][:], in_=sg[i2][:], func=AF.Exp), reads=[sg[i2].b], writes=[sg[i2].b])
                            kb.op("dve", lambda E: E.tensor_tensor(out=MT[i2][:], in0=sg[i2][:], in1=CBm[:], op=ALU.mult), reads=[sg[i2].b, CBm.b], writes=[MT[i2].b])
                            kb.op("act", lambda E: E.activation(out=gb[i2][:], in_=pab[:, 0:C], func=AF.Exp), reads=[pab.b], writes=[gb[i2].b])
                            kb.op("dve", lambda E: E.tensor_tensor(out=rT[i2][:], in0=xs[:, 3, cs_], in1=gb[i2][:], op=ALU.mult), reads=[xs.b, gb[i2].b], writes=[rT[i2].b])
                            kb.op("dve", lambda E: E.tensor_copy(out=al[i2][:, 0:1], in_=pab[:, C - 1:C]), reads=[pab.b], writes=[al[i2].b])
                            kb.op("act", lambda E: E.activation(out=al[i2][:, 1:2], in_=DTA[:, 4 + hd:5 + hd], func=AF.Exp, scale=-1.0, bias=al[i2][:, 0:1]),
                                  reads=[DTA.b, al[i2].b], writes=[al[i2].b])
                            pxt = nextps()
                            pb0 = (hd % 2) * 64
                            kb.op("pe", lambda E: E.matmul(pxt[:, 0:64], lhsT=xs[pb0:pb0 + 64, hd // 2, cs_], rhs=ident[pb0:pb0 + 64, pb0:pb0 + 64], start=True, stop=True), reads=[xs.b, ident.b], writes=[pxt.b])
                            kb.op("dve", lambda E: E.tensor_scalar(out=xdt[i2][:], in0=pxt[:, 0:64], scalar1=DTA[:, hd:hd + 1], scalar2=None, op0=ALU.mult), reads=[pxt.b, DTA.b], writes=[xdt[i2].b])
                            kb.op("dve", lambda E: E.tensor_scalar(out=xdd[i2][:], in0=pxt[:, 0:64], scalar1=dv[:, d * 4 + hd:d * 4 + hd + 1], scalar2=None, op0=ALU.mult), reads=[pxt.b, dv.b], writes=[xdd[i2].b])
                            kb.op("dve", lambda E: E.tensor_scalar(out=xdw[i2][:], in0=xdt[i2][:], scalar1=al[i2][:, 1:2], scalar2=None, op0=ALU.mult), reads=[xdt[i2].b, al[i2].b], writes=[xdw[i2].b])
                            py = nextps()
                            mm(py[0:64, 0:C], xdt[i2][:], MT[i2][:], [xdt[i2].b, MT[i2].b], [py.b], start=True, stop=False)
                            mm(py[0:64, 0:C], xdd[i2][:], ident[:], [xdd[i2].b, ident.b], [py.b], start=False, stop=False)
                            mm(py[0:64, 0:C], H[:], rT[i2][:], [H.b, rT[i2].b], [py.b], start=False, stop=True)
                            evac(OB[hd][:, cs_], py[0:64, 0:C], [py.b], [OB[hd].b])
                            ph = nextps()
                            mm(ph[:, 0:64], Btok[:], xdw[i2][:], [Btok.b, xdw[i2].b], [ph.b])
                            kb.op("act", lambda E: E.activation(out=gb[i2][:, 0:1], in_=al[i2][:, 0:1], func=AF.Exp), reads=[al[i2].b, rT[i2].b], writes=[gb[i2].b])
                            kb.op("dve", lambda E: E.scalar_tensor_tensor(out=H[:], in0=H[:], scalar=gb[i2][:, 0:1], in1=ph[:, 0:64], op0=ALU.mult, op1=ALU.add),
                                  reads=[H.b, gb[i2].b, ph.b], writes=[H.b])
                    for hd in range(4):
                        r0 = (hd // 2) * 128 + (hd % 2) * 64
                        store_scan(YM[d], lambda lo_, hi_: YM[d].ap[r0:r0 + 64, lo_:hi_], OB[hd], OT, d, S0, S1, q0, n, 64)
            kb.barrier()
            kb.close_scope()

    def phase_C_even(l, e):
        with ExitStack() as st:
            kb.open_scope()
            W = Tl(kb, st, "WoutE", [128, 3, D], BF16)
            load_w_bf16(W, lambda k: W[:, k, :], lambda k: woute_d.ap[e * 384 + k * 128:e * 384 + (k + 1) * 128, :], 3)
            G2w = Tl(kb, st, "G2w", [128, 2, 128], BF16)
            load_w_bf16(G2w, lambda k: G2w[:, k, :], lambda k: g2_d.ap[e * 256 + k * 128:e * 256 + (k + 1) * 128, :], 2)
            ev = Tl(kb, st, "evC", [128, 96])
            kb.dma("sp", ev[:], evec_d.ap[:, e * 96:(e + 1) * 96], ev.b, None)
            selt = Tl(kb, st, "seltC", [128, 12])
            kb.dma("sp", selt[:], sel_d.ap, selt.b, None)
            on = Tl(kb, st, "onC", [128, 128])
            bo = Tl(kb, st, "boC", [128, 128])
            kb.op("pool", lambda E: E.memset(on[:], 1.0), writes=[on.b])
            kb.op("pool", lambda E: E.memset(bo[:], 0.0), writes=[bo.b])
            kb.op("pool", lambda E: E.memset(bo[0:64, 0:64], 1.0), writes=[bo.b])
            kb.op("pool", lambda E: E.memset(bo[64:128, 64:128], 1.0), writes=[bo.b])
            kb.op("dve", lambda E: E.memset(cst[:, 3:4], 1e-5), writes=[cst.b])
            kb.op("dve", lambda E: E.memset(cst[:, 4:5], 64e-5), writes=[cst.b])
            mns = Tl(kb, st, "mns", [128, 4])
            for q in range(2):
                for b in range(2):
                    kb.op("dve", lambda E: E.tensor_tensor(out=mns[:, q * 2 + b:q * 2 + b + 1], in0=ev[:, 78 + q:79 + q], in1=selt[:, 10 + b:11 + b], op=ALU.mult),
                          reads=[ev.b, selt.b], writes=[mns.b])
            ym = [[Tl(kb, st, f"ym{i}{d}", [128, 2, T]) for d in range(2)] for i in range(2)]
            zz = [Tl(kb, st, f"zz{i}", [128, 2, T]) for i in range(2)]
            gl = [Tl(kb, st, f"glC{i}", [128, 2, T]) for i in range(2)]
            yr = [[Tl(kb, st, f"yr{i}{d}", [128, T]) for d in range(2)] for i in range(2)]
            bn = [[Tl(kb, st, f"bn{i}{d}", [128, T]) for d in range(2)] for i in range(2)]
            sq2 = Tl(kb, st, "sq2C", [128, 2, T])
            rs = Tl(kb, st, "rsC", [128, T])
            t1 = Tl(kb, st, "t1E", [128, T])
            t2 = Tl(kb, st, "t2E", [128, T])
            sgl = Tl(kb, st, "sglC", [128, 2, T], BF16)
            mbf = [Tl(kb, st, f"mE{i}", [128, 3, T], BF16) for i in range(2)]
            stg = [Tl(kb, st, f"stE{i}", [128, FC, T]) for i in range(2)]
            ymv = [YM[d].ap.rearrange("(q p) s -> p q s", p=128) for d in range(2)]
            for ti, tile in enumerate(tiles):
                n, b, pos = tile["n"], tile["b"], tile["pos"]
                i = ti % 2
                m, sg = mbf[i], stg[i]

                def ld(tj):
                    tl_ = tiles[tj]
                    i_, n_, b_, p_ = tj % 2, tl_["n"], tl_["b"], tl_["pos"]
                    for d in range(2):
                        kb.dma("sp", ym[i_][d][:, :, 0:n_], ymv[d][:, :, p_:p_ + n_], ym[i_][d].b, YM[d].b)
                        kb.dma("sp", yr[i_][d][:, 0:n_], YR[d].ap[:, b_ * LS + p_:b_ * LS + p_ + n_], yr[i_][d].b, YR[d].b)
                        kb.dma("sp", bn[i_][d][:, 0:n_], BN[d].ap[:, b_ * LS + p_:b_ * LS + p_ + n_], bn[i_][d].b, BN[d].b)
                    kb.dma("sp", zz[i_][:, :, 0:n_], scr_ap(PT, 2, tl_, 7), zz[i_].b, PT.b)
                    kb.dma("sp", gl[i_][:, :, 0:n_], scr_ap(PT, 2, tl_, 5), gl[i_].b, PT.b)
                if ti == 0:
                    ld(0)
                if ti + 1 < len(tiles):
                    ld(ti + 1)
                y0 = ym[i][0]
                kb.op("dve", lambda E: E.tensor_tensor(out=y0[:, :, 0:n], in0=y0[:, :, 0:n], in1=ym[i][1][:, :, 0:n], op=ALU.add), reads=[y0.b, ym[i][1].b], writes=[y0.b])
                kb.op("act", lambda E: E.activation(out=zz[i][:, :, 0:n], in_=zz[i][:, :, 0:n], func=AF.Silu), reads=[zz[i].b], writes=[zz[i].b])
                kb.op("dve", lambda E: E.tensor_tensor(out=y0[:, :, 0:n], in0=y0[:, :, 0:n], in1=zz[i][:, :, 0:n], op=ALU.mult), reads=[y0.b, zz[i].b], writes=[y0.b])
                kb.op("act", lambda E: E.activation(out=sq2[:, :, 0:n], in_=y0[:, :, 0:n], func=AF.Square), reads=[y0.b], writes=[sq2.b])
                ps = nextps()
                for q in range(2):
                    kb.op("pe", lambda E: E.matmul(ps[:, 0:n], lhsT=on[:], rhs=sq2[:, q, 0:n], start=(q == 0), stop=(q == 1)), reads=[on.b, sq2.b], writes=[ps.b])
                kb.op("act", lambda E: E.activation(out=rs[:, 0:n], in_=ps[:, 0:n], func=AF.Sqrt, scale=1.0 / 256, bias=cst[:, 3:4]), reads=[ps.b, cst.b], writes=[rs.b])
                kb.op("dve", lambda E: E.reciprocal(out=rs[:, 0:n], in_=rs[:, 0:n]), reads=[rs.b], writes=[rs.b])
                for q in range(2):
                    kb.op("dve", lambda E: E.scalar_tensor_tensor(out=m[:, q, 0:n], in0=y0[:, q, 0:n], scalar=mns[:, q * 2 + b:q * 2 + b + 1], in1=rs[:, 0:n],
                                                                  op0=ALU.mult, op1=ALU.mult), reads=[y0.b, mns.b, rs.b], writes=[m.b])
                yy = yr[i][0]
                kb.op("dve", lambda E: E.tensor_tensor(out=yy[:, 0:n], in0=yy[:, 0:n], in1=yr[i][1][:, 0:n], op=ALU.add), reads=[yy.b, yr[i][1].b], writes=[yy.b])
                ps = nextps()
                kb.op("pe", lambda E: E.matmul(ps[:, 0:n], lhsT=bo[:], rhs=yy[:, 0:n], start=True, stop=True), reads=[bo.b, yy.b], writes=[ps.b])
                kb.op("dve", lambda E: E.scalar_tensor_tensor(out=t1[:, 0:n], in0=ps[:, 0:n], scalar=-1.0 / 64, in1=yy[:, 0:n], op0=ALU.mult, op1=ALU.add),
                      reads=[ps.b, yy.b], writes=[t1.b])
                kb.op("act", lambda E: E.activation(out=t2[:, 0:n], in_=t1[:, 0:n], func=AF.Square), reads=[t1.b], writes=[t2.b])
                ps = nextps()
                kb.op("pe", lambda E: E.matmul(ps[:, 0:n], lhsT=bo[:], rhs=t2[:, 0:n], start=True, stop=True), reads=[bo.b, t2.b], writes=[ps.b])
                kb.op("act", lambda E: E.activation(out=t2[:, 0:n], in_=ps[:, 0:n], func=AF.Sqrt, scale=1.0 / 64, bias=cst[:, 4:5]), reads=[ps.b, cst.b], writes=[t2.b])
                kb.op("dve", lambda E: E.reciprocal(out=t2[:, 0:n], in_=t2[:, 0:n]), reads=[t2.b], writes=[t2.b])
                kb.op("dve", lambda E: E.tensor_tensor(out=t1[:, 0:n], in0=t1[:, 0:n], in1=t2[:, 0:n], op=ALU.mult), reads=[t1.b, t2.b], writes=[t1.b])
                kb.op("act", lambda E: E.activation(out=t1[:, 0:n], in_=t1[:, 0:n], func=AF.Identity, scale=ev[:, 76:77], bias=ev[:, 77:78]), reads=[t1.b, ev.b], writes=[t1.b])
                kb.op("dve", lambda E: E.tensor_tensor(out=t1[:, 0:n], in0=t1[:, 0:n], in1=bn[i][0][:, 0:n], op=ALU.add), reads=[t1.b, bn[i][0].b], writes=[t1.b])
                kb.op("dve", lambda E: E.tensor_tensor(out=t1[:, 0:n], in0=t1[:, 0:n], in1=bn[i][1][:, 0:n], op=ALU.add), reads=[t1.b, bn[i][1].b], writes=[t1.b])
                kb.op("act", lambda E: E.activation(out=sgl[:, :, 0:n], in_=gl[i][:, :, 0:n], func=AF.Sigmoid), reads=[gl[i].b], writes=[sgl.b])
                ps = nextps()
                for q in range(2):
                    kb.op("pe", lambda E: E.matmul(ps[:, 0:n], lhsT=G2w[:, q, :], rhs=sgl[:, q, 0:n], start=(q == 0), stop=(q == 1)), reads=[G2w.b, sgl.b], writes=[ps.b])
                kb.op("dve", lambda E: E.tensor_tensor(out=m[:, 2, 0:n], in0=t1[:, 0:n], in1=ps[:, 0:n], op=ALU.mult), reads=[t1.b, ps.b], writes=[m.b])
                out_proj(W, 3, [128, 128, 128], m, sg, n, tile)
            kb.barrier()
            kb.close_scope()

    ECH = [(0, 128), (128, 128), (256, 128), (384, 96), (480, 96), (576, 128), (704, 128), (832, 128), (960, 128), (1088, 128),
           (1216, 128), (1344, 128), (1472, 128), (1600, 4)]

    e_i = o_i = 0
    for l, lt in enumerate(LT):
        pend = (l - 1) if l > 0 else None
        stop = cfg.get("stop", "")
        if stop == "setup":
            break
        if lt == "O":
            phase_A(l, wino_d, o_i * D, 512, pend)
            if stop == "A":
                break
            mixer_rglru(o_i)
            if stop == "mix":
                break
            phase_C_odd(l, o_i)
            o_i += 1
        else:
            phase_A(l, wine_d, e_i * D, 1604, pend, chunks=ECH)
            if stop == "A":
                break
            ev_dve[0] = True
            if "nossd" not in DBG:
                mixer_ssd(e_i)
            if "norwkv" not in DBG:
                mixer_rwkv(e_i)
            ev_dve[0] = False
            if stop == "mix":
                break
            phase_C_even(l, e_i)
            e_i += 1
        if stop == "C":
            break
        if stop == "AR":
            break
        phase_D(l)

    with ExitStack() as st:
        kb.open_scope()
        selt = Tl(kb, st, "selt", [128, 12])
        g2c = Tl(kb, st, "g2c", [128, FC])
        kb.dma("sp", selt[:], sel_d.ap, selt.b, None)
        o5 = ((NL - 1) * 6 + 5) * FC * 3
        g2v = lambda j: mod[:, o5:o5 + FC * 3].rearrange("p (f j) -> p f j", j=3)[:, :, j]
        kb.op("dve", lambda E: E.tensor_scalar(out=g2c[:], in0=g2v(0), scalar1=selt[:, 8:9], scalar2=None, op0=ALU.mult), reads=[mod.b, selt.b], writes=[g2c.b])
        kb.op("dve", lambda E: E.scalar_tensor_tensor(out=g2c[:], in0=g2v(1), scalar=selt[:, 9:10], in1=g2c[:], op0=ALU.mult, op1=ALU.add),
              reads=[mod.b, selt.b, g2c.b], writes=[g2c.b])
        xts = [Tl(kb, st, f"xtF{i}", [128, FC, T]) for i in range(2)]
        rts = [Tl(kb, st, f"rtF{i}", [128, FC, T]) for i in range(2)]
        xa = Tl(kb, st, "xaF", [128, FC, T])
        ra = Tl(kb, st, "raF", [128, FC, T])
        sq = Tl(kb, st, "sqF", [128, FC, T], BF16)
        rs = Tl(kb, st, "rsF", [128, T])
        li = 0
        for t0 in range(0, TR, T):
            for r in range(8):
                xt, rt = xts[li % 2], rts[li % 2]
                li += 1
                kb.dma("sp", xt[:], lat_view(XT)[:, r, :, t0:t0 + T], xt.b, XT.bs(r * D, (r + 1) * D))
                kb.dma("sp", rt[:], lat_view(REDL)[:, r, :, t0:t0 + T], rt.b, REDL.bs(r * D, (r + 1) * D))
                if r == 0:
                    kb.op("dve", lambda E: E.tensor_scalar(out=xa[:], in0=xt[:], scalar1=selt[:, 0:1], scalar2=None, op0=ALU.mult), reads=[xt.b, selt.b], writes=[xa.b])
                    kb.op("pool", lambda E: E.tensor_scalar(out=ra[:], in0=rt[:], scalar1=selt[:, 0:1], scalar2=None, op0=ALU.mult), reads=[rt.b, selt.b], writes=[ra.b])
                else:
                    kb.op("dve", lambda E: E.scalar_tensor_tensor(out=xa[:], in0=xt[:], scalar=selt[:, r:r + 1], in1=xa[:], op0=ALU.mult, op1=ALU.add),
                          reads=[xt.b, selt.b, xa.b], writes=[xa.b])
                    kb.op("dve", lambda E: E.scalar_tensor_tensor(out=ra[:], in0=rt[:], scalar=selt[:, r:r + 1], in1=ra[:], op0=ALU.mult, op1=ALU.add),
                          reads=[rt.b, selt.b, ra.b], writes=[ra.b])
            for fc in range(FC):
                kb.op("dve", lambda E: E.scalar_tensor_tensor(out=xa[:, fc, :], in0=ra[:, fc, :], scalar=g2c[:, fc:fc + 1], in1=xa[:, fc, :],
                                                              op0=ALU.mult, op1=ALU.add), reads=[ra.b, g2c.b, xa.b], writes=[xa.b])
            kb.op("act", lambda E: E.activation(out=sq[:], in_=xa[:], func=AF.Square), reads=[xa.b], writes=[sq.b])
            ps = nextps()
            for fc in range(FC):
                kb.op("pe", lambda E: E.matmul(ps[:, 0:T], lhsT=ones_bf[:], rhs=sq[:, fc, :], start=(fc == 0), stop=(fc == FC - 1)),
                      reads=[ones_bf.b, sq.b], writes=[ps.b])
            kb.op("act", lambda E: E.activation(out=rs[:], in_=ps[:, 0:T], func=AF.Sqrt, scale=1.0 / D, bias=cst[:, 0:1]), reads=[ps.b, cst.b], writes=[rs.b])
            kb.op("dve", lambda E: E.reciprocal(out=rs[:], in_=rs[:]), reads=[rs.b], writes=[rs.b])
            for fc in range(FC):
                kb.op("dve", lambda E: E.scalar_tensor_tensor(out=ra[:, fc, :], in0=xa[:, fc, :], scalar=nw[:, 2 * NL * FC + fc:2 * NL * FC + fc + 1],
                                                              in1=rs[:], op0=ALU.mult, op1=ALU.mult), reads=[xa.b, rs.b, nw.b], writes=[ra.b])
            kb.dma("act", yout.ap.rearrange("(fc p) t -> p fc t", p=128)[:, :, t0:t0 + T], ra[:], yout.b, ra.b)
    if cfg.get("dbg"):
        kb.barrier()
        extra = ([("YM0", YM[0]), ("YM1", YM[1]), ("YR0", YR[0]), ("YR1", YR[1]), ("BN0", BN[0]), ("BN1", BN[1])] if NE else [])
        for nm, dr in [("MODR", MODR), ("PT", PT), ("HS", HS), ("REDL", REDL), ("REDC", REDC), ("XT", XT), ("XCi", XC)] + extra:
            od = Dr(nc, "dbg_" + nm, list(dr.t.shape), F32, kind="ExternalOutput")
            kb.dma("sp", od.ap, dr.ap, od.b, None)
        kb.barrier()
    deps = {yout.b.wr[0]: yout.b.wr[1]}
    kb._wait("sp", deps)
    kb._wait("act", deps)
    kb.barrier()
    ninst = kb.ninst
    kb.stack.close()
    return nc, ninst


def pack_inputs(inp, cfg):
    SEQ = cfg["SEQ"]
    LT = cfg["ltypes"]
    NL = len(LT)
    TR = SEQ // 4
    f32 = lambda a: np.ascontiguousarray(np.asarray(a, dtype=np.float32))
    x = f32(inp["x"]).reshape(2 * SEQ, D)
    xc = f32(f32(inp["ctx"]).reshape(2 * CTX, D).T)
    cc = np.stack([f32(inp["c"])[0], f32(inp["c"])[1], f32(inp["c_ctx"])], 0)
    ada_w = f32(inp["ada_w"])[:NL]
    ada_b = f32(inp["ada_b"])[:NL]
    adab = f32(ada_b.reshape(NL, 96, 128).transpose(2, 0, 1).reshape(128, NL * 96))
    nwv = np.zeros((128, (2 * NL + 1) * FC), np.float32)
    for l in range(NL):
        nwv[:, (l * 2) * FC:(l * 2 + 1) * FC] = f32(inp["norm1_w"])[l].reshape(FC, 128).T
        nwv[:, (l * 2 + 1) * FC:(l * 2 + 2) * FC] = f32(inp["norm2_w"])[l].reshape(FC, 128).T
    nwv[:, 2 * NL * FC:] = f32(inp["final_norm_w"]).reshape(FC, 128).T
    wgate, wup, wdown = f32(inp["ffn_w_gate"])[:NL], f32(inp["ffn_w_up"])[:NL], f32(inp["ffn_w_down"])[:NL]
    o_idx = [i for i, t in enumerate(LT) if t == "O"]
    NO = len(o_idx)
    NE = len(LT) - NO
    maps = []
    for c in range(NCORE):
        m = {}
        m["xs"] = f32(x[c * TR:(c + 1) * TR].T)
        m["xc"] = xc
        cm = np.zeros((128, 6), np.float32)
        for kc in range(2):
            cm[:, kc * 3:(kc + 1) * 3] = cc[:, c * 256 + kc * 128:c * 256 + (kc + 1) * 128].T
        m["cmine"] = cm
        m["adaw"] = f32(ada_w[:, c * 256:(c + 1) * 256, :].reshape(NL * 256, 6 * D))
        m["adab"] = adab
        m["nwv"] = nwv
        m["wg"] = f32(wgate[:, :, c * DFFC:(c + 1) * DFFC].reshape(NL * D, DFFC))
        m["wu"] = f32(wup[:, :, c * DFFC:(c + 1) * DFFC].reshape(NL * D, DFFC))
        m["wd"] = f32(wdown[:, c * DFFC:(c + 1) * DFFC, :].reshape(NL * DFFC, D))
        if NO:
            cw_in, cw_out = f32(inp["c_w_in"]), f32(inp["c_w_out"])
            m["wino"] = f32(np.concatenate([np.concatenate([cw_in[o][:, c * 256:(c + 1) * 256], cw_in[o][:, D + c * 256:D + (c + 1) * 256]], 1)
                                            for o in range(NO)], 0))
            m["wouto"] = f32(np.concatenate([cw_out[o][c * 256:(c + 1) * 256, :] for o in range(NO)], 0))
            gws = []
            ov = np.zeros((128, NO * 32), np.float32)
            for o in range(NO):
                for d in range(2):
                    gws.append(f32(inp["c_wa"])[o, d, c])
                    gws.append(f32(inp["c_wx"])[o, d, c])
                    for ci in range(2):
                        ch = slice(c * 256 + ci * 128, c * 256 + (ci + 1) * 128)
                        base = o * 32 + (d * 2 + ci) * 8
                        ov[:, base:base + 4] = f32(inp["c_conv_w"])[o, d][:, ch].T
                        ov[:, base + 4] = f32(inp["c_conv_b"])[o, d, ch]
                        ov[:, base + 5] = f32(inp["c_ba"])[o, d, ch]
                        ov[:, base + 6] = f32(inp["c_bx"])[o, d, ch]
                        ov[:, base + 7] = f32(inp["c_lambda"])[o, d, ch]
            m["gw"] = f32(np.concatenate(gws, 0))
            m["ovec"] = ov
        if NE:
            g, bm = c // 2, c % 2
            abw = f32(inp["ab_w_in"])
            cols = np.concatenate([np.arange(3088 + c * 128, 3088 + (c + 1) * 128), np.arange(4112 + c * 128, 4112 + (c + 1) * 128),
                                   np.arange(5136 + c * 128, 5136 + (c + 1) * 128), np.arange(6160, 6256), np.arange(6256, 6352), np.arange(6352, 6608),
                                   np.arange(g * 256, (g + 1) * 256), np.arange(1024 + g * 256, 1024 + (g + 1) * 256),
                                   np.arange(2048 + g * 128, 2048 + (g + 1) * 128), np.arange(2560 + g * 128, 2560 + (g + 1) * 128),
                                   np.arange(3072 + g * 4, 3072 + (g + 1) * 4)])
            m["wine"] = f32(np.concatenate([abw[e][:, cols] for e in range(NE)], 0))
            abo = f32(inp["ab_w_out"])
            m["woute"] = f32(np.concatenate([np.concatenate([abo[e][g * 256:(g + 1) * 256], abo[e][1024 + c * 128:1024 + (c + 1) * 128]], 0) for e in range(NE)], 0))
            evv = np.zeros((128, NE * 96), np.float32)
            lora = []
            dvec = np.zeros((64, NE * 8), np.float32)
            mch = [np.arange(g * 256, g * 256 + 128), np.arange(g * 256 + 128, (g + 1) * 256), np.arange(1024 + g * 128, 1024 + (g + 1) * 128),
                   np.arange(1536 + g * 128, 1536 + (g + 1) * 128)]
            for e in range(NE):
                o = e * 96
                for d in range(2):
                    for q in range(4):
                        evv[:, o + d * 20 + q * 5:o + d * 20 + q * 5 + 4] = f32(inp["m_conv_w"])[e, d][:, mch[q]].T
                        evv[:, o + d * 20 + q * 5 + 4] = f32(inp["m_conv_b"])[e, d, mch[q]]
                    mu = f32(inp["r_mu"])[e, d]
                    for hh in range(2):
                        ch = np.arange(c * 128 + hh * 64, c * 128 + (hh + 1) * 64)
                        base = o + 40 + (d * 2 + hh) * 8
                        evv[:64, base + 0] = mu[ch]
                        evv[:64, base + 1] = mu[1024 + ch]
                        evv[:64, base + 2] = mu[2048 + ch]
                        evv[:64, base + 3] = f32(inp["r_w0"])[e, d, ch]
                        evv[:64, base + 4] = f32(inp["r_a0"])[e, d, ch]
                        evv[:64, base + 5] = f32(inp["r_kk"])[e, d, ch]
                        evv[:64, base + 6] = f32(inp["r_ka"])[e, d, ch]
                        evv[:64, base + 7] = f32(inp["r_rk"])[e, d].reshape(-1)[ch]
                        lora.append(f32(inp["r_w2"])[e, d][:, ch])
                        lora.append(f32(inp["r_a2"])[e, d][:, ch])
                    evv[:96, o + 72 + d * 2] = mu[3072:3168]
                    evv[:96, o + 72 + d * 2 + 1] = mu[3168:3264]
                    evv[:4, o + 80 + d * 2] = f32(inp["m_dt_bias"])[e, d, g * 4:(g + 1) * 4]
                    evv[:4, o + 80 + d * 2 + 1] = f32(inp["m_a_log"])[e, d, g * 4:(g + 1) * 4]
                    for hd in range(4):
                        dvec[:, e * 8 + d * 4 + hd] = f32(inp["m_d"])[e, d, g * 4 + hd]
                evv[:, o + 76] = f32(inp["r_lnx_w"])[e, c * 128:(c + 1) * 128]
                evv[:, o + 77] = f32(inp["r_lnx_b"])[e, c * 128:(c + 1) * 128]
                evv[:, o + 78] = f32(inp["m_norm_w"])[e, g * 256:g * 256 + 128]
                evv[:, o + 79] = f32(inp["m_norm_w"])[e, g * 256 + 128:(g + 1) * 256]
            m["evec"] = evv
            m["lora"] = f32(np.concatenate(lora, 0))
            m["g2"] = f32(np.concatenate([f32(inp["r_g2"])[e][:, c * 128:(c + 1) * 128] for e in range(NE)], 0))
            m["dvec"] = dvec
            s4 = np.zeros((4, 512), np.float32)
            for hd in range(4):
                s4[hd, hd * 128:(hd + 1) * 128] = 1.0
            m["sel4"] = s4
        sel = np.zeros((128, 12), np.float32)
        sel[:, c] = 1.0
        sel[:, 8 + c // 4] = 1.0
        sel[:, 10 + c % 2] = 1.0
        m["sel"] = sel
        maps.append(m)
    return maps


_CACHE = {}


def run(inp, cfg, trace=False):
    key = (cfg["SEQ"], cfg["ltypes"], cfg["T"], cfg.get("stop", ""), cfg.get("dbg", False))
    if key not in _CACHE:
        _CACHE[key] = build(cfg)
    nc, ninst = _CACHE[key]
    maps = pack_inputs(inp, cfg)
    res = run_bass_kernel_spmd(nc, maps, core_ids=list(range(NCORE)), **({"trace": True} if trace else {}))
    SEQ = cfg["SEQ"]
    TR = SEQ // 4
    out = np.zeros((2 * SEQ, D), np.float32)
    for c in range(NCORE):
        out[c * TR:(c + 1) * TR] = np.asarray(res.results[c]["yout"]).T
    return out.reshape(2, SEQ, D), res


def kernel(**inputs):
    cfg = make_cfg()
    out, _ = run(inputs, cfg)
    return out
```

```python
from contextlib import ExitStack
import math
import numpy as np
import concourse.bass as bass
import concourse.mybir as mybir
from concourse.bass_utils import run_bass_kernel_spmd

F32 = mybir.dt.float32
BF16 = mybir.dt.bfloat16
AF = mybir.ActivationFunctionType
ALU = mybir.AluOpType
AX = mybir.AxisListType

D = 2048
FC = 16
CTX = 256
DFF = 5632
DFFC = DFF // 8
NCORE = 8
EPS = 1e-6
GW = 64

SAME_SYNC = True
ROLL = 30000


class Buf:
    __slots__ = ("name", "wr", "rd", "dsem", "dcum")

    def __init__(self, name):
        self.name = name
        self.wr = None
        self.rd = {}
        self.dsem = None
        self.dcum = 0


class KB:
    def __init__(self, nc):
        self.nc = nc
        self.stack = ExitStack()
        self.eng = {"pe": nc.tensor, "dve": nc.vector, "act": nc.scalar, "pool": nc.gpsimd, "sp": nc.sync}
        self.sems = {}
        self.esem = {}
        self.ecnt = {}
        self.waited = {k: {} for k in self.eng}
        self.nsem = 0
        self.dmabufs = []
        self.free_dsems = []
        self.ccsem = None
        self.sem_cum = {}
        self.scopes = [[]]
        self.ninst = 0
        for e in self.eng:
            self._newesem(e)

    def newsem(self, name):
        h = self.stack.enter_context(self.nc.semaphore(name))
        self.sems[name] = h
        self.nsem += 1
        return name

    def _newesem(self, e):
        name = self.newsem(f"e_{e}_{self.nsem}")
        self.esem[e] = name
        self.ecnt[name] = 0

    def _deps(self, reads, writes):
        d = {}
        for b in reads:
            if b.wr and d.get(b.wr[0], 0) < b.wr[1]:
                d[b.wr[0]] = b.wr[1]
        for b in writes:
            if b.wr and d.get(b.wr[0], 0) < b.wr[1]:
                d[b.wr[0]] = b.wr[1]
            for sem, v in b.rd.items():
                if d.get(sem, 0) < v:
                    d[sem] = v
        return d

    def _wait(self, e, deps):
        w = self.waited[e]
        for sem, v in deps.items():
            if w.get(sem, 0) >= v:
                continue
            if sem == self.esem[e] and (e == "pe" or e == "sp" or not SAME_SYNC):
                continue
            self.eng[e].wait_ge(self.sems[sem], v)
            w[sem] = v

    def op(self, e, fn, reads=(), writes=()):
        self._wait(e, self._deps(reads, writes))
        ins = fn(self.eng[e])
        sem = self.esem[e]
        self.ecnt[sem] += 1
        v = self.ecnt[sem]
        ins.then_inc(self.sems[sem], 1)
        self.ninst += 1
        for b in writes:
            b.wr = (sem, v)
            b.rd = {}
        for b in reads:
            if b.wr != (sem, v):
                b.rd[sem] = v
        if v >= ROLL:
            self._newesem(e)
        return (sem, v)

    def _dsem(self, outbuf):
        if outbuf.dsem is not None and self.sem_cum[outbuf.dsem] >= ROLL * 16:
            outbuf.dsem = None
        if outbuf.dsem is None:
            if self.free_dsems:
                outbuf.dsem = self.free_dsems.pop()
            else:
                outbuf.dsem = self.newsem(f"d_{self.nsem}")
                self.sem_cum[outbuf.dsem] = 0

    def open_scope(self):
        self.scopes.append([])

    def close_scope(self):
        for b in self.scopes.pop():
            if b.dsem is not None:
                if self.sem_cum[b.dsem] < ROLL * 16:
                    self.free_dsems.append(b.dsem)
                b.dsem = None
            if b in self.dmabufs:
                self.dmabufs.remove(b)

    def dma(self, q, out, in_, outbuf, inbuf, **kw):
        outs = outbuf if isinstance(outbuf, (list, tuple)) else [outbuf]
        ins_ = [] if inbuf is None else (inbuf if isinstance(inbuf, (list, tuple)) else [inbuf])
        self._wait(q, self._deps(ins_, outs))
        prim = outs[0]
        self._dsem(prim)
        ins = self.eng[q].dma_start(out=out, in_=in_, **kw)
        self.sem_cum[prim.dsem] += 16
        ins.then_inc(self.sems[prim.dsem], 16)
        self.ninst += 1
        t = (prim.dsem, self.sem_cum[prim.dsem])
        for ob in outs:
            ob.wr = t
            ob.rd = {}
        for ib in ins_:
            ib.rd[t[0]] = t[1]
        if prim not in self.dmabufs:
            self.dmabufs.append(prim)
        return t

    def collective(self, kind, op, groups, in_ap, out_ap, inbuf, outbuf):
        import os
        if "nocc" in os.environ.get("DBG", ""):
            return self.dma("pool", out_ap, in_ap, outbuf, inbuf)
        e = "pool"
        self._wait(e, self._deps([inbuf], [outbuf]))
        if self.ccsem is None:
            self.ccsem = self.newsem("ccsem")
            self.sem_cum[self.ccsem] = 0
            self.ccbuf = Buf("ccbuf")
            self.dmabufs.append(self.ccbuf)
        ins = self.eng[e].collective_compute(kind, op, replica_groups=groups, ins=[in_ap], outs=[out_ap])
        self.sem_cum[self.ccsem] += 1
        ins.then_inc(self.sems[self.ccsem], 1)
        t = (self.ccsem, self.sem_cum[self.ccsem])
        outbuf.wr = t
        outbuf.rd = {}
        inbuf.rd[t[0]] = t[1]
        self.ccbuf.wr = t
        return t

    def barrier(self, engines=("pe", "dve", "act", "pool", "sp")):
        deps = {}
        for e, sem in self.esem.items():
            if self.ecnt[sem] > 0:
                deps[sem] = self.ecnt[sem]
        for b in self.dmabufs:
            if b.wr and b is not getattr(self, "ccbuf", None):
                deps[b.wr[0]] = max(deps.get(b.wr[0], 0), b.wr[1])
        for e in engines:
            w = self.waited[e]
            for sem, v in deps.items():
                if w.get(sem, 0) >= v or sem == self.esem[e]:
                    continue
                self.eng[e].wait_ge(self.sems[sem], v)
                w[sem] = v


class Tl:
    def __init__(self, kb, stack, name, shape, dtype=F32, psum=False):
        mk = kb.nc.psum_tensor if psum else kb.nc.sbuf_tensor
        kb.ntl = getattr(kb, "ntl", 0) + 1
        name = f"{name}_{kb.ntl}"
        self.t = stack.enter_context(mk(name, list(shape), dtype))
        self.b = Buf(name)
        kb.scopes[-1].append(self.b)

    def __getitem__(self, k):
        return self.t[k]


class Dr:
    def __init__(self, nc, name, shape, dtype=F32, kind="Internal", chunk_rows=None):
        self.t = nc.dram_tensor(name, list(shape), dtype, kind=kind)
        self.ap = self.t.ap()
        self.b = Buf(name)
        self.chunk_rows = chunk_rows
        if chunk_rows:
            self.nchunk = shape[0] // chunk_rows
            self.cb = [Buf(f"{name}_c{i}") for i in range(self.nchunk)]

    def bs(self, r0, r1):
        if not self.chunk_rows:
            return [self.b]
        return self.cb[r0 // self.chunk_rows:(r1 - 1) // self.chunk_rows + 1]

    def chunk_ap(self, k):
        return self.ap[k * self.chunk_rows:(k + 1) * self.chunk_rows, :]


def make_cfg(SEQ=8192, ltypes=("E", "O", "E", "O"), T=256, stop="", dbg=False):
    return dict(SEQ=SEQ, ltypes=tuple(ltypes), T=T, stop=stop, dbg=dbg)


def build(cfg):
    SEQ = cfg["SEQ"]
    LT = cfg["ltypes"]
    NL = len(LT)
    T = cfg["T"]
    TR = SEQ // 4
    LS = CTX + SEQ
    ROWS = SEQ // GW
    NE = sum(1 for t in LT if t == "E")
    NO = sum(1 for t in LT if t == "O")
    assert TR % T == 0 and CTX % min(T, CTX) == 0

    nc = bass.Bass("TRN2", target_bir_lowering=False)
    kb = KB(nc)
    top = kb.stack

    def din(name, shape, dt=F32):
        return Dr(nc, name, shape, dt, kind="ExternalInput")

    xs = din("xs", [D, TR])
    xc_in = din("xc", [D, 2 * CTX])
    cmine = din("cmine", [128, 2 * 3])
    adaw = din("adaw", [NL * 2 * 128, 6 * D])
    adab = din("adab", [128, NL * 96])
    nwv = din("nwv", [128, (2 * NL + 1) * FC])
    wg_d = din("wg", [NL * D, DFFC])
    wu_d = din("wu", [NL * D, DFFC])
    wd_d = din("wd", [NL * DFFC, D])
    if NO:
        wino_d = din("wino", [NO * D, 512])
        wouto_d = din("wouto", [NO * 256, D])
        gw_d = din("gw", [NO * 2 * 2 * 256, 256])
        ovec_d = din("ovec", [128, NO * 2 * 2 * 8])
    if NE:
        wine_d = din("wine", [NE * D, 1604])
        woute_d = din("woute", [NE * 384, D])
        evec_d = din("evec", [128, NE * 96])
        lora_d = din("lora", [NE * 8 * 96, 64])
        g2_d = din("g2", [NE * 256, 128])
        dvec_d = din("dvec", [64, NE * 8])
        sel4_d = din("sel4", [4, 4 * 128])
    sel_d = din("sel", [128, 12])
    yout = Dr(nc, "yout", [D, TR], F32, kind="ExternalOutput")

    xsi = Dr(nc, "xsi", [D, TR])
    CR = (1 << 20) // TR
    XT = Dr(nc, "XT", [8 * D, TR], chunk_rows=CR)
    XC = Dr(nc, "XCi", [D, 2 * CTX])
    PARTL = Dr(nc, "PARTL", [8 * D, TR], chunk_rows=CR)
    PARTC = Dr(nc, "PARTC", [D, 2 * CTX])
    REDL = Dr(nc, "REDL", [8 * D, TR], chunk_rows=CR)
    REDC = Dr(nc, "REDC", [D, 2 * CTX])
    MODP = Dr(nc, "MODP", [128, NL * 288])
    MODR = Dr(nc, "MODR", [128, NL * 288])
    NCH_MAX = 14 if NE else 4
    PT = Dr(nc, "PT", [NCH_MAX * 128, 2 * LS])
    HS = Dr(nc, "HS", [2 * 128, 2 * LS])
    HSC = Dr(nc, "HSC", [2 * 128, LS])
    if NE:
        YM = [Dr(nc, f"YM{d}", [2 * 128, LS]) for d in range(2)]
        YR = [Dr(nc, f"YR{d}", [128, 2 * LS]) for d in range(2)]
        BN = [Dr(nc, f"BN{d}", [128, 2 * LS]) for d in range(2)]
    G4 = [[0, 1, 2, 3], [4, 5, 6, 7]]
    G2 = [[0, 4], [1, 5], [2, 6], [3, 7]]

    TMPL = Dr(nc, "TMPL", [8 * D, TR], chunk_rows=CR)
    TMPC = Dr(nc, "TMPC", [D, 2 * CTX])
    MODT = Dr(nc, "MODT", [128, NL * 288])

    def allreduce(src, dst, tmp):
        if not src.chunk_rows:
            kb.collective("AllReduce", ALU.add, G4, src.ap, tmp.ap, src.b, tmp.b)
            kb.collective("AllReduce", ALU.add, G2, tmp.ap, dst.ap, tmp.b, dst.b)
            return
        for k in range(src.nchunk):
            kb.collective("AllReduce", ALU.add, G4, src.chunk_ap(k), tmp.chunk_ap(k), src.cb[k], tmp.cb[k])
        for k in range(src.nchunk):
            kb.collective("AllReduce", ALU.add, G2, tmp.chunk_ap(k), dst.chunk_ap(k), tmp.cb[k], dst.cb[k])

    def lat_view(dr):
        return dr.ap.rearrange("(r fc p) t -> p r fc t", r=8, fc=FC, p=128)

    def ctx_view(dr):
        return dr.ap.rearrange("(fc p) t -> p fc t", p=128)

    tiles = []
    for r in range(8):
        for t0 in range(0, TR, T):
            tiles.append(dict(kind="lat", r=r, t0=t0, n=T, b=r // 4, pos=CTX + (r % 4) * TR + t0, j=r // 4, last=(t0 + T >= TR)))
    TCX = min(T, CTX)
    for b in range(2):
        for t0 in range(0, CTX, TCX):
            tiles.append(dict(kind="ctx", b=b, c0=b * CTX + t0, n=TCX, pos=t0, j=2, last=(b == 1 and t0 + TCX >= CTX)))

    def xb(drl, drc, tile):
        if tile["kind"] == "lat":
            return drl.bs(tile["r"] * D, (tile["r"] + 1) * D)
        return [drc.b]

    def xap(drl, drc, tile):
        if tile["kind"] == "lat":
            return lat_view(drl)[:, tile["r"], :, tile["t0"]:tile["t0"] + tile["n"]]
        return ctx_view(drc)[:, :, tile["c0"]:tile["c0"] + tile["n"]]

    def scr_ap(dr, nch, tile, c0=0):
        v = dr.ap.rearrange("(c p) (b s) -> p c b s", p=128, b=2)
        return v[:, c0:c0 + nch, tile["b"], tile["pos"]:tile["pos"] + tile["n"]]

    mod = Tl(kb, top, "mod", [128, NL * 288])
    nw = Tl(kb, top, "nw", [128, (2 * NL + 1) * FC])
    A12 = Tl(kb, top, "A12", [128, NL * 2 * FC * 3])
    ones_bf = Tl(kb, top, "ones_bf", [128, 128], BF16)
    ident = Tl(kb, top, "ident", [128, 128])
    cst = Tl(kb, top, "cst", [128, 8])
    psb = [Tl(kb, top, f"ps{i}", [128, 512], F32, psum=True) for i in range(8)]
    ps_i = [0]

    def nextps():
        p = psb[ps_i[0] % 8]
        ps_i[0] += 1
        return p

    def modcol(l, j6, fc, j):
        o = ((l * 6 + j6) * FC + fc) * 3 + j
        return mod[:, o:o + 1]

    def acol(l, which, fc, j):
        o = ((l * 2 + which) * FC + fc) * 3 + j
        return A12[:, o:o + 1]

    ev_i = [0]
    ev_dve = [False]

    def evac(out, in_, R, W):
        ev_i[0] += 1
        if ev_i[0] % 2 and not ev_dve[0]:
            kb.op("act", lambda E: E.activation(out=out, in_=in_, func=AF.Copy), reads=R, writes=W)
        else:
            kb.op("dve", lambda E: E.tensor_copy(out=out, in_=in_), reads=R, writes=W)

    wst = [Tl(kb, top, f"wst{i}", [128, D]) for i in range(2)]
    wst_i = [0]

    def load_w_bf16(dst_tile, dst_ap_fn, src_ap_fn, nk):
        for k in range(nk):
            dst = dst_ap_fn(k)
            npart, ncol = dst.shape[0], dst.shape[-1]
            stg = wst[wst_i[0] % 2]
            wst_i[0] += 1
            kb.dma("sp", stg[0:npart, 0:ncol], src_ap_fn(k), stg.b, None)
            kb.op("dve", lambda E: E.tensor_copy(out=dst, in_=stg[0:npart, 0:ncol]), reads=[stg.b], writes=[dst_tile.b])

    kb.op("dve", lambda E: E.memset(ones_bf[:], 1.0), writes=[ones_bf.b])
    kb.op("dve", lambda E: E.memset(cst[:, 0:1], EPS), writes=[cst.b])
    kb.op("dve", lambda E: E.memset(cst[:, 1:2], 1.0), writes=[cst.b])
    kb.op("dve", lambda E: E.memset(cst[:, 2:3], 0.0), writes=[cst.b])
    kb.op("pool", lambda E: E.memset(ident[:], 0.0), writes=[ident.b])
    kb.op("pool", lambda E: E.affine_select(out=ident[:], in_=ident[:], pattern=[[-1, 128]], compare_op=ALU.not_equal,
                                            fill=1.0, base=0, channel_multiplier=1), reads=[ident.b], writes=[ident.b])
    kb.dma("sp", nw[:], nwv.ap, nw.b, None)
    with ExitStack() as st:
        kb.open_scope()
        selt0 = Tl(kb, st, "selt0", [128, 12])
        kb.dma("sp", selt0[:], sel_d.ap, selt0.b, None)
        xg = [Tl(kb, st, f"xg{i}", [128, FC, T]) for i in range(2)]
        xo = [Tl(kb, st, f"xo{i}", [128, FC, T]) for i in range(2)]
        gi_ = 0
        for t0 in range(0, TR, T):
            g = xg[(t0 // T) % 2]
            kb.dma("sp", g[:], xs.ap.rearrange("(fc p) t -> p fc t", p=128)[:, :, t0:t0 + T], g.b, None)
            for r in range(8):
                o_ = xo[gi_ % 2]
                gi_ += 1
                kb.op("dve" if r % 2 else "pool", lambda E: E.tensor_scalar(out=o_[:], in0=g[:], scalar1=selt0[:, r:r + 1], scalar2=None, op0=ALU.mult),
                      reads=[g.b, selt0.b], writes=[o_.b])
                kb.dma("act", lat_view(PARTL)[:, r, :, t0:t0 + T], o_[:], PARTL.bs(r * D, (r + 1) * D), o_.b)
        allreduce(PARTL, XT, TMPL)
        kb.barrier()
        kb.close_scope()
    import os
    DBG = os.environ.get("DBG", "")
    kb.dma("sp", XC.ap, xc_in.ap, XC.b, None)

    with ExitStack() as st:
      kb.open_scope()
      if "noada" not in DBG:
          csb = Tl(kb, st, "csb", [128, 6])
          csl = Tl(kb, st, "csl", [128, 6])
          adb = Tl(kb, st, "adb", [128, NL * 96])
          modp = Tl(kb, st, "modp", [128, NL * 288])
          wbuf = [Tl(kb, st, f"adw{i}", [128, 6 * D]) for i in range(2)]
          kb.dma("sp", csb[:], cmine.ap, csb.b, None)
          kb.dma("sp", adb[:], adab.ap, adb.b, None)
          kb.op("act", lambda E: E.activation(out=csl[:], in_=csb[:], func=AF.Silu), reads=[csb.b], writes=[csl.b])
          for l in range(NL):
              ps = nextps()
              for kc in range(2):
                  wb = wbuf[kc]
                  row0 = (l * 2 + kc) * 128
                  kb.dma("sp", wb[:], adaw.ap[row0:row0 + 128, :], wb.b, None)
              for cc in range(96):
                  for kc in range(2):
                      wb = wbuf[kc]
                      kb.op("pe", lambda E: E.matmul(ps[:, cc * 3:cc * 3 + 3], lhsT=wb[:, cc * 128:(cc + 1) * 128],
                                                     rhs=csl[:, kc * 3:kc * 3 + 3], start=(kc == 0), stop=(kc == 1)),
                            reads=[wb.b, csl.b], writes=[ps.b])
              kb.op("dve", lambda E: E.scalar_tensor_tensor(
                  out=modp[:, l * 288:(l + 1) * 288].rearrange("p (c j) -> p c j", j=3),
                  in0=adb[:, l * 96:(l + 1) * 96].unsqueeze(2).to_broadcast([128, 96, 3]), scalar=1.0 / NCORE,
                  in1=ps[:, 0:288].rearrange("p (c j) -> p c j", j=3), op0=ALU.mult, op1=ALU.add),
                  reads=[adb.b, ps.b], writes=[modp.b])
          kb.dma("sp", MODP.ap, modp[:], MODP.b, modp.b)
          allreduce(MODP, MODR, MODT)
          kb.dma("sp", mod[:], MODR.ap, mod.b, MODR.b)
          for l in range(NL):
              for which, j6 in ((0, 1), (1, 4)):
                  o_m = ((l * 6 + j6) * FC) * 3
                  o_a = ((l * 2 + which) * FC) * 3
                  o_w = (l * 2 + which) * FC
                  kb.op("dve", lambda E: E.scalar_tensor_tensor(
                      out=A12[:, o_a:o_a + 48].rearrange("p (f j) -> p f j", j=3),
                      in0=mod[:, o_m:o_m + 48].rearrange("p (f j) -> p f j", j=3), scalar=1.0,
                      in1=nw[:, o_w:o_w + FC].unsqueeze(2).to_broadcast([128, FC, 3]), op0=ALU.add, op1=ALU.mult),
                      reads=[mod.b, nw.b], writes=[A12.b])
          kb.barrier()
      kb.close_scope()

    def rmsnorm_mod(xt, sq, rs, tmp, n, acols, bcols, ncols_scale=1.0):
        kb.op("act", lambda E: E.activation(out=sq[:, :, 0:n], in_=xt[:, :, 0:n], func=AF.Square), reads=[xt.b], writes=[sq.b])
        ps = nextps()
        for fc in range(FC):
            kb.op("pe", lambda E: E.matmul(ps[:, 0:n], lhsT=ones_bf[:], rhs=sq[:, fc, 0:n], start=(fc == 0), stop=(fc == FC - 1)),
                  reads=[ones_bf.b, sq.b], writes=[ps.b])
        kb.op("act", lambda E: E.activation(out=rs[:, 0:n], in_=ps[:, 0:n], func=AF.Sqrt, scale=1.0 / D, bias=cst[:, 0:1]),
              reads=[ps.b, cst.b], writes=[rs.b])
        kb.op("dve", lambda E: E.reciprocal(out=rs[:, 0:n], in_=rs[:, 0:n]), reads=[rs.b], writes=[rs.b])
        for fc in range(FC):
            tm = tmp[fc % len(tmp)]
            kb.op("dve", lambda E: E.tensor_tensor(out=tm[:, 0:n], in0=xt[:, fc, 0:n], in1=rs[:, 0:n], op=ALU.mult),
                  reads=[xt.b, rs.b], writes=[tm.b])
            kb.op("act", lambda E: E.activation(out=sq[:, fc, 0:n], in_=tm[:, 0:n], func=AF.Identity, scale=acols(fc), bias=bcols(fc)),
                  reads=[tm.b, A12.b, mod.b], writes=[sq.b])

    def resid_update(xt, rt, n, gcols):
        for fc in range(FC):
            kb.op("dve", lambda E: E.scalar_tensor_tensor(out=xt[:, fc, 0:n], in0=rt[:, fc, 0:n], scalar=gcols(fc),
                                                          in1=xt[:, fc, 0:n], op0=ALU.mult, op1=ALU.add),
                  reads=[rt.b, xt.b, mod.b], writes=[xt.b])

    def allreduce_parts():
        allreduce(PARTL, REDL, TMPL)
        allreduce(PARTC, REDC, TMPC)

    def phase_A(l, win_dr, row0, ncols, pending_g2_layer, chunks=None):
        if chunks is None:
            chunks = [(c * 128, min(128, ncols - c * 128)) for c in range((ncols + 127) // 128)]
        nch = len(chunks)
        with ExitStack() as st:
            kb.open_scope()
            W = Tl(kb, st, "Win", [128, FC, ncols], BF16)
            load_w_bf16(W, lambda k: W[:, k, :], lambda k: win_dr.ap[row0 + k * 128:row0 + (k + 1) * 128, :], FC)
            xts = [Tl(kb, st, f"xtA{i}", [128, FC, T]) for i in range(2)]
            rts = [Tl(kb, st, f"rtA{i}", [128, FC, T]) for i in range(2)]
            sqs = [Tl(kb, st, f"sqA{i}", [128, FC, T], BF16) for i in range(2)]
            rss = [Tl(kb, st, f"rsA{i}", [128, T]) for i in range(2)]
            tmp = [Tl(kb, st, f"tmA{i}", [128, T]) for i in range(4)]
            for ti, tile in enumerate(tiles):
                n = tile["n"]
                j = tile["j"]
                xt, rt, sq, rs = xts[ti % 2], rts[ti % 2], sqs[ti % 2], rss[ti % 2]

                def ld(tj):
                    tl_ = tiles[tj]
                    kb.dma("sp", xts[tj % 2][:, :, 0:tl_["n"]], xap(XT, XC, tl_), xts[tj % 2].b, xb(XT, XC, tl_))
                    if pending_g2_layer is not None:
                        kb.dma("sp", rts[tj % 2][:, :, 0:tl_["n"]], xap(REDL, REDC, tl_), rts[tj % 2].b, xb(REDL, REDC, tl_))
                if ti == 0:
                    ld(0)
                if ti + 1 < len(tiles):
                    ld(ti + 1)
                if pending_g2_layer is not None:
                    resid_update(xt, rt, n, lambda fc: modcol(pending_g2_layer, 5, fc, j))
                    kb.dma("sp", xap(XT, XC, tile), xt[:, :, 0:n], xb(XT, XC, tile), xt.b)
                rmsnorm_mod(xt, sq, rs, tmp, n, lambda fc: acol(l, 0, fc, j), lambda fc: modcol(l, 0, fc, j))
                for c in range(nch):
                    cc0, cw = chunks[c]
                    ps = nextps()
                    for kc in range(FC):
                        kb.op("pe", lambda E: E.matmul(ps[0:cw, 0:n], lhsT=W[:, kc, cc0:cc0 + cw], rhs=sq[:, kc, 0:n],
                                                       start=(kc == 0), stop=(kc == FC - 1)), reads=[W.b, sq.b], writes=[ps.b])
                    evac(rt[0:cw, c, 0:n], ps[0:cw, 0:n], [ps.b], [rt.b])
                c = 0
                while c < nch:
                    if chunks[c][1] == 128:
                        c1 = c
                        while c1 < nch and chunks[c1][1] == 128:
                            c1 += 1
                        kb.dma("sp", scr_ap(PT, c1 - c, tile, c), rt[:, c:c1, 0:n], PT.b, rt.b)
                        c = c1
                    else:
                        cw = chunks[c][1]
                        kb.dma("sp", scr_ap(PT, 1, tile, c)[0:cw], rt[0:cw, c:c + 1, 0:n], PT.b, rt.b)
                        c += 1
            kb.barrier()
            kb.close_scope()

    def phase_C_odd(l, o):
        with ExitStack() as st:
            kb.open_scope()
            W = Tl(kb, st, "Wout", [128, 2, D], BF16)
            load_w_bf16(W, lambda k: W[:, k, :], lambda k: wouto_d.ap[o * 256 + k * 128:o * 256 + (k + 1) * 128, :], 2)
            hss = [Tl(kb, st, f"hsC{i}", [128, 2, T]) for i in range(2)]
            gys = [Tl(kb, st, f"gyC{i}", [128, 2, T]) for i in range(2)]
            t1 = Tl(kb, st, "t1C", [128, 2, T])
            mbf = [Tl(kb, st, f"mC{i}", [128, 2, T], BF16) for i in range(2)]
            stg = [Tl(kb, st, f"stC{i}", [128, FC, T]) for i in range(2)]
            for ti, tile in enumerate(tiles):
                n = tile["n"]
                hs, gy, m, sg = hss[ti % 2], gys[ti % 2], mbf[ti % 2], stg[ti % 2]

                def ld(tj):
                    tl_ = tiles[tj]
                    kb.dma("sp", hss[tj % 2][:, :, 0:tl_["n"]], scr_ap(HS, 2, tl_), hss[tj % 2].b, HS.b)
                    kb.dma("sp", gys[tj % 2][:, :, 0:tl_["n"]], scr_ap(PT, 2, tl_, 0), gys[tj % 2].b, PT.b)
                if ti == 0:
                    ld(0)
                if ti + 1 < len(tiles):
                    ld(ti + 1)
                kb.op("dve", lambda E: E.tensor_tensor(out=t1[:, :, 0:n], in0=gy[:, :, 0:n], in1=gy[:, :, 0:n], op=ALU.mult), reads=[gy.b], writes=[t1.b])
                kb.op("dve", lambda E: E.tensor_scalar(out=t1[:, :, 0:n], in0=t1[:, :, 0:n], scalar1=0.044715, scalar2=1.0, op0=ALU.mult, op1=ALU.add),
                      reads=[t1.b], writes=[t1.b])
                kb.op("dve", lambda E: E.tensor_tensor(out=t1[:, :, 0:n], in0=t1[:, :, 0:n], in1=gy[:, :, 0:n], op=ALU.mult), reads=[t1.b, gy.b], writes=[t1.b])
                kb.op("act", lambda E: E.activation(out=t1[:, :, 0:n], in_=t1[:, :, 0:n], func=AF.Sigmoid, scale=2.0 * math.sqrt(2.0 / math.pi)),
                      reads=[t1.b], writes=[t1.b])
                kb.op("dve", lambda E: E.tensor_tensor(out=t1[:, :, 0:n], in0=t1[:, :, 0:n], in1=gy[:, :, 0:n], op=ALU.mult), reads=[t1.b, gy.b], writes=[t1.b])
                kb.op("dve", lambda E: E.tensor_tensor(out=m[:, :, 0:n], in0=t1[:, :, 0:n], in1=hs[:, :, 0:n], op=ALU.mult), reads=[t1.b, hs.b], writes=[m.b])
                out_proj(W, 2, [128, 128], m, sg, n, tile)
            kb.barrier()
            kb.close_scope()

    ar_pend = []

    def out_proj(W, nk, ksz, m, sg, n, tile):
        for fo in range(FC):
            ps = nextps()
            for k in range(nk):
                kb.op("pe", lambda E: E.matmul(ps[:, 0:n], lhsT=W[0:ksz[k], k, fo * 128:(fo + 1) * 128], rhs=m[0:ksz[k], k, 0:n],
                                               start=(k == 0), stop=(k == nk - 1)), reads=[W.b, m.b], writes=[ps.b])
            evac(sg[:, fo, 0:n], ps[:, 0:n], [ps.b], [sg.b])
        lat = tile["kind"] == "lat"
        kb.dma("sp", xap(PARTL, PARTC, tile), sg[:, :, 0:n], xb(PARTL, PARTC, tile), sg.b)
        if tile.get("last"):
            if lat:
                ready = [k for k in range(PARTL.nchunk) if ((k + 1) * CR - 1) // D == tile["r"]]
                for k in ready:
                    kb.collective("AllReduce", ALU.add, G4, PARTL.chunk_ap(k), TMPL.chunk_ap(k), PARTL.cb[k], TMPL.cb[k])
                for k in ar_pend:
                    kb.collective("AllReduce", ALU.add, G2, TMPL.chunk_ap(k), REDL.chunk_ap(k), TMPL.cb[k], REDL.cb[k])
                ar_pend[:] = ready
            else:
                for k in ar_pend:
                    kb.collective("AllReduce", ALU.add, G2, TMPL.chunk_ap(k), REDL.chunk_ap(k), TMPL.cb[k], REDL.cb[k])
                ar_pend[:] = []
                allreduce(PARTC, REDC, TMPC)

    def phase_D(l):
        csz = [128, 128, 128, 128, 128, 64]
        with ExitStack() as st:
            kb.open_scope()
            Wg = Tl(kb, st, "Wg", [128, FC, DFFC], BF16)
            Wu = Tl(kb, st, "Wu", [128, FC, DFFC], BF16)
            Wd = Tl(kb, st, "Wd", [128, 6, D], BF16)
            load_w_bf16(Wg, lambda k: Wg[:, k, :], lambda k: wg_d.ap[l * D + k * 128:l * D + (k + 1) * 128, :], FC)
            load_w_bf16(Wu, lambda k: Wu[:, k, :], lambda k: wu_d.ap[l * D + k * 128:l * D + (k + 1) * 128, :], FC)
            load_w_bf16(Wd, lambda k: Wd[0:csz[k], k, :], lambda k: wd_d.ap[l * DFFC + k * 128:l * DFFC + k * 128 + csz[k], :], 6)
            xts = [Tl(kb, st, f"xtD{i}", [128, FC, T]) for i in range(2)]
            rts = [Tl(kb, st, f"rtD{i}", [128, FC, T]) for i in range(2)]
            sqs = [Tl(kb, st, f"sqD{i}", [128, FC, T], BF16) for i in range(2)]
            rss = [Tl(kb, st, f"rsD{i}", [128, T]) for i in range(2)]
            tmp = [Tl(kb, st, f"tmD{i}", [128, T]) for i in range(4)]
            acts = [Tl(kb, st, f"acD{i}", [128, 6, T], BF16) for i in range(2)]
            sgl = [Tl(kb, st, f"sgD{i}", [128, T]) for i in range(2)]
            for ti, tile in enumerate(tiles):
                n = tile["n"]
                j = tile["j"]
                lat = tile["kind"] == "lat"
                xt, rt, sq, rs, ac = xts[ti % 2], rts[ti % 2], sqs[ti % 2], rss[ti % 2], acts[ti % 2]

                def ld(tj):
                    tl_ = tiles[tj]
                    kb.dma("sp", xts[tj % 2][:, :, 0:tl_["n"]], xap(XT, XC, tl_), xts[tj % 2].b, xb(XT, XC, tl_))
                    kb.dma("sp", rts[tj % 2][:, :, 0:tl_["n"]], xap(REDL, REDC, tl_), rts[tj % 2].b, xb(REDL, REDC, tl_))
                if ti == 0:
                    ld(0)
                if ti + 1 < len(tiles):
                    ld(ti + 1)
                resid_update(xt, rt, n, lambda fc: modcol(l, 2, fc, j))
                kb.dma("sp", xap(XT, XC, tile), xt[:, :, 0:n], xb(XT, XC, tile), xt.b)
                rmsnorm_mod(xt, sq, rs, tmp, n, lambda fc: acol(l, 1, fc, j), lambda fc: modcol(l, 3, fc, j))
                for c in range(6):
                    cw = csz[c]
                    pg, pu = nextps(), nextps()
                    for kc in range(FC):
                        kb.op("pe", lambda E: E.matmul(pg[0:cw, 0:n], lhsT=Wg[:, kc, c * 128:c * 128 + cw], rhs=sq[:, kc, 0:n],
                                                       start=(kc == 0), stop=(kc == FC - 1)), reads=[Wg.b, sq.b], writes=[pg.b])
                    for kc in range(FC):
                        kb.op("pe", lambda E: E.matmul(pu[0:cw, 0:n], lhsT=Wu[:, kc, c * 128:c * 128 + cw], rhs=sq[:, kc, 0:n],
                                                       start=(kc == 0), stop=(kc == FC - 1)), reads=[Wu.b, sq.b], writes=[pu.b])
                    s = sgl[c % 2]
                    kb.op("act", lambda E: E.activation(out=s[0:cw, 0:n], in_=pg[0:cw, 0:n], func=AF.Silu), reads=[pg.b], writes=[s.b])
                    kb.op("dve", lambda E: E.tensor_tensor(out=ac[0:cw, c, 0:n], in0=s[0:cw, 0:n], in1=pu[0:cw, 0:n], op=ALU.mult),
                          reads=[s.b, pu.b], writes=[ac.b])
                out_proj(Wd, 6, csz, ac, rt, n, tile)
            kb.barrier()
            kb.close_scope()

    def mixer_rglru(o):
        BL = 512
        with ExitStack() as st:
            kb.open_scope()
            GWt = Tl(kb, st, "GWt", [128, 2 * 2 * 2, 256], BF16)
            load_w_bf16(GWt, lambda k: GWt[:, k, :], lambda k: gw_d.ap[(o * 8 + k) * 128:(o * 8 + k + 1) * 128, :], 8)
            ov = Tl(kb, st, "ov", [128, 2 * 2 * 8])
            spn = Tl(kb, st, "spn", [128, 4])
            kb.dma("sp", ov[:], ovec_d.ap[:, o * 32:(o + 1) * 32], ov.b, None)
            for dc in range(4):
                kb.op("act", lambda E: E.activation(out=spn[:, dc:dc + 1], in_=ov[:, dc * 8 + 7:dc * 8 + 8], func=AF.Exp, scale=-1.0),
                      reads=[ov.b], writes=[spn.b])
                kb.op("act", lambda E: E.activation(out=spn[:, dc:dc + 1], in_=spn[:, dc:dc + 1], func=AF.Ln, scale=1.0, bias=cst[:, 1:2]),
                      reads=[spn.b, cst.b], writes=[spn.b])
            kb.op("dve", lambda E: E.tensor_scalar(out=spn[:], in0=spn[:], scalar1=-8.0, scalar2=None, op0=ALU.mult), reads=[spn.b], writes=[spn.b])
            XB = Tl(kb, st, "XB", [128, 2, LS])
            hfw = Tl(kb, st, "hfw", [128, 2, BL])
            RM = Tl(kb, st, "RM", [128, SEQ])
            xcf = Tl(kb, st, "xcf", [128, 2, BL])
            xcb = Tl(kb, st, "xcb", [128, 2, BL], BF16)
            gr = Tl(kb, st, "gr", [128, 2, BL])
            gi = Tl(kb, st, "gi", [128, 2, BL])
            aa = Tl(kb, st, "aa", [128, 2, BL])
            uu = Tl(kb, st, "uu", [128, 2, BL])
            hh = [Tl(kb, st, f"hh{i}", [128, 2, BL]) for i in range(2)]
            zero = cst[:, 2:3]
            pt_v = PT.ap.rearrange("(c p) (b s) -> p c b s", p=128, b=2)
            hs_v = HS.ap.rearrange("(c p) (b s) -> p c b s", p=128, b=2)
            for b in range(2):
                for ci in range(2):
                    kb.dma("sp", XB[:, ci, 0:CTX], pt_v[:, 2 + ci, b, 0:CTX], XB.b, PT.b)
                    kb.dma("sp", RM[:], pt_v[:, 2 + ci, b, CTX:LS], RM.b, PT.b)
                    kb.op("pool", lambda E: E.tensor_copy(out=XB[:, ci, CTX:LS].rearrange("p (c r) -> p c r", c=GW),
                                                          in_=RM[:].rearrange("p (r c) -> p c r", c=GW)), reads=[RM.b], writes=[XB.b])
                for d in range(2):
                    segs = [(0, CTX), (CTX, LS)]
                    hprev = None
                    blk_i = 0
                    for (S0, S1) in segs:
                        starts = list(range(S0, S1, BL))
                        if d == 1:
                            starts = starts[::-1]
                        for s0 in starts:
                            s1 = min(s0 + BL, S1)
                            n = s1 - s0
                            for ci in range(2):
                                vo = (d * 2 + ci) * 8
                                kb.op("act", lambda E: E.activation(out=xcf[:, ci, 0:n], in_=XB[:, ci, s0:s1], func=AF.Identity,
                                                                    scale=ov[:, vo + 3:vo + 4], bias=ov[:, vo + 4:vo + 5]),
                                      reads=[XB.b, ov.b], writes=[xcf.b])
                                for k in range(1, 4):
                                    if d == 0:
                                        lo = max(s0, S0 + k)
                                        if lo >= s1:
                                            continue
                                        kb.op("dve", lambda E: E.scalar_tensor_tensor(
                                            out=xcf[:, ci, lo - s0:n], in0=XB[:, ci, lo - k:s1 - k], scalar=ov[:, vo + 3 - k:vo + 4 - k],
                                            in1=xcf[:, ci, lo - s0:n], op0=ALU.mult, op1=ALU.add), reads=[XB.b, ov.b, xcf.b], writes=[xcf.b])
                                    else:
                                        hi = min(s1, S1 - k)
                                        if hi <= s0:
                                            continue
                                        kb.op("dve", lambda E: E.scalar_tensor_tensor(
                                            out=xcf[:, ci, 0:hi - s0], in0=XB[:, ci, s0 + k:hi + k], scalar=ov[:, vo + 3 - k:vo + 4 - k],
                                            in1=xcf[:, ci, 0:hi - s0], op0=ALU.mult, op1=ALU.add), reads=[XB.b, ov.b, xcf.b], writes=[xcf.b])
                            kb.op("pool", lambda E: E.tensor_copy(out=xcb[:, :, 0:n], in_=xcf[:, :, 0:n]), reads=[xcf.b], writes=[xcb.b])
                            for jc in range(2):
                                vo = (d * 2 + jc) * 8
                                for g, dst in ((0, gr), (1, gi)):
                                    ps = nextps()
                                    for ic in range(2):
                                        kb.op("pe", lambda E: E.matmul(ps[:, 0:n], lhsT=GWt[:, (d * 2 + g) * 2 + ic, jc * 128:(jc + 1) * 128],
                                                                       rhs=xcb[:, ic, 0:n], start=(ic == 0), stop=(ic == 1)),
                                              reads=[GWt.b, xcb.b], writes=[ps.b])
                                    kb.op("act", lambda E: E.activation(out=dst[:, jc, 0:n], in_=ps[:, 0:n], func=AF.Sigmoid,
                                                                        bias=ov[:, vo + 5 + g:vo + 6 + g]), reads=[ps.b, ov.b], writes=[dst.b])
                                kb.op("act", lambda E: E.activation(out=aa[:, jc, 0:n], in_=gr[:, jc, 0:n], func=AF.Exp, scale=spn[:, d * 2 + jc:d * 2 + jc + 1]),
                                      reads=[gr.b, spn.b], writes=[aa.b])
                            kb.op("dve", lambda E: E.tensor_tensor(out=uu[:, :, 0:n], in0=aa[:, :, 0:n], in1=aa[:, :, 0:n], op=ALU.mult), reads=[aa.b], writes=[uu.b])
                            kb.op("dve", lambda E: E.tensor_scalar(out=uu[:, :, 0:n], in0=uu[:, :, 0:n], scalar1=-1.0, scalar2=1.0, op0=ALU.mult, op1=ALU.add),
                                  reads=[uu.b], writes=[uu.b])
                            kb.op("dve", lambda E: E.tensor_scalar(out=uu[:, :, 0:n], in0=uu[:, :, 0:n], scalar1=1e-30, scalar2=None, op0=ALU.max),
                                  reads=[uu.b], writes=[uu.b])
                            kb.op("act", lambda E: E.activation(out=uu[:, :, 0:n], in_=uu[:, :, 0:n], func=AF.Sqrt), reads=[uu.b], writes=[uu.b])
                            kb.op("dve", lambda E: E.tensor_tensor(out=uu[:, :, 0:n], in0=uu[:, :, 0:n], in1=gi[:, :, 0:n], op=ALU.mult), reads=[uu.b, gi.b], writes=[uu.b])
                            kb.op("dve", lambda E: E.tensor_tensor(out=uu[:, :, 0:n], in0=uu[:, :, 0:n], in1=xcf[:, :, 0:n], op=ALU.mult), reads=[uu.b, xcf.b], writes=[uu.b])
                            h = hh[blk_i % 2]
                            for ci in range(2):
                                if hprev is None:
                                    init = zero
                                else:
                                    hp, pn = hprev
                                    init = hp[:, ci, pn - 1:pn] if d == 0 else hp[:, ci, 0:1]
                                if d == 0:
                                    kb.op("dve", lambda E: E.tensor_tensor_scan(out=h[:, ci, 0:n], data0=aa[:, ci, 0:n], data1=uu[:, ci, 0:n],
                                                                                initial=init, op0=ALU.mult, op1=ALU.add),
                                          reads=[aa.b, uu.b, cst.b] + ([hprev[0].b] if hprev else []), writes=[h.b])
                                else:
                                    kb.op("dve", lambda E: E.tensor_tensor_scan(out=h[:, ci, 0:n][:, ::-1], data0=aa[:, ci, 0:n][:, ::-1],
                                                                                data1=uu[:, ci, 0:n][:, ::-1], initial=init, op0=ALU.mult, op1=ALU.add),
                                          reads=[aa.b, uu.b, cst.b] + ([hprev[0].b] if hprev else []), writes=[h.b])
                            hsc_v = HSC.ap.rearrange("(c p) s -> p c s", p=128)
                            if d == 0:
                                kb.dma("act", hsc_v[:, :, s0:s1], h[:, :, 0:n], HSC.b, h.b)
                            else:
                                kb.dma("sp", hfw[:, :, 0:n], hsc_v[:, :, s0:s1], hfw.b, HSC.b)
                                kb.op("pool", lambda E: E.tensor_tensor(out=hfw[:, :, 0:n], in0=hfw[:, :, 0:n], in1=h[:, :, 0:n], op=ALU.add),
                                      reads=[h.b, hfw.b], writes=[hfw.b])
                                kb.dma("act", hsc_v[:, :, s0:s1], hfw[:, :, 0:n], HSC.b, hfw.b)
                            hprev = (h, n)
                            blk_i += 1
                for ci in range(2):
                    kb.dma("sp", RM[:, 0:CTX], hsc_v[:, ci, 0:CTX], RM.b, HSC.b)
                    kb.dma("act", hs_v[:, ci, b, 0:CTX], RM[:, 0:CTX], HS.b, RM.b)
                    kb.dma("sp", RM[:], hsc_v[:, ci, CTX:LS], RM.b, HSC.b)
                    kb.op("pool", lambda E: E.tensor_copy(out=XB[:, 0, 0:SEQ].rearrange("p (r c) -> p c r", c=GW),
                                                          in_=RM[:].rearrange("p (c r) -> p c r", c=GW)), reads=[RM.b], writes=[XB.b])
                    kb.dma("act", hs_v[:, ci, b, CTX:LS], XB[:, 0, 0:SEQ], HS.b, XB.b)
            kb.barrier()
            kb.close_scope()

    E05 = math.exp(-0.5)
    pt_v = PT.ap.rearrange("(c p) (b s) -> p c b s", p=128, b=2)

    def scan_blocks(nb):
        out = []
        for (S0, S1) in ((0, CTX), (CTX, LS)):
            for q0 in range(0, S1 - S0, nb):
                out.append((S0, S1, q0, min(nb, S1 - S0 - q0)))
        return out

    def load_scan(dst, tmp, rows_ap_fn, d, S0, S1, q0, n, halo, npart):
        if d == 0:
            lo = S0 + q0
            h = min(halo, q0)
            if h < halo:
                kb.op("pool", lambda E: E.memset(dst[0:npart, :, 0:halo - h], 0.0), writes=[dst.b])
            kb.dma("sp", dst[0:npart, :, halo - h:halo + n], rows_ap_fn(lo - h, lo + n), dst.b, PT.b)
        else:
            hi = S1 - q0
            h = min(halo, q0)
            if h < halo:
                kb.op("pool", lambda E: E.memset(tmp[0:npart, :, n + h:n + halo], 0.0), writes=[tmp.b])
            kb.dma("sp", tmp[0:npart, :, 0:n + h], rows_ap_fn(hi - n, hi + h), tmp.b, PT.b)
            kb.op("dve", lambda E: E.tensor_copy(out=dst[0:npart, :, 0:n + halo], in_=tmp[0:npart, :, 0:n + halo][:, :, ::-1]),
                  reads=[tmp.b], writes=[dst.b])

    def store_scan(dr, rows_ap_fn, src, tmp, d, S0, S1, q0, n, npart):
        if d == 0:
            kb.dma("act", rows_ap_fn(S0 + q0, S0 + q0 + n), src[0:npart, 0:n], dr.b, src.b)
        else:
            kb.op("dve", lambda E: E.tensor_copy(out=tmp[0:npart, 0:n], in_=src[0:npart, 0:n][:, ::-1]), reads=[src.b], writes=[tmp.b])
            kb.dma("act", rows_ap_fn(S1 - q0 - n, S1 - q0), tmp[0:npart, 0:n], dr.b, tmp.b)

    def mixer_rwkv(e):
        C = 64
        NBK = 512
        with ExitStack() as st:
            kb.open_scope()
            ev = Tl(kb, st, "ev", [128, 96])
            kb.dma("sp", ev[:], evec_d.ap[:, e * 96:(e + 1) * 96], ev.b, None)
            lwt = Tl(kb, st, "lwt", [96, 8, 64])
            for i in range(8):
                kb.dma("sp", lwt[:, i, :], lora_d.ap[(e * 8 + i) * 96:(e * 8 + i + 1) * 96, :], lwt.b, None)
            omk = Tl(kb, st, "omk", [64, 4])
            for i in range(4):
                kb.op("dve", lambda E: E.tensor_scalar(out=omk[:, i:i + 1], in0=ev[0:64, 40 + i * 8 + 6:40 + i * 8 + 7], scalar1=-1.0, scalar2=1.0,
                                                       op0=ALU.mult, op1=ALU.add), reads=[ev.b], writes=[omk.b])
            MK = Tl(kb, st, "MK", [64, 128])
            ML = Tl(kb, st, "ML", [64, 64])
            on64 = Tl(kb, st, "on64", [64, NBK])
            kb.op("pool", lambda E: E.memset(on64[:], 1.0), writes=[on64.b])
            kb.op("pool", lambda E: E.affine_select(out=MK[:, 0:64], in_=on64[:, 0:64], pattern=[[1, 64]], compare_op=ALU.is_gt, fill=0.0, base=0,
                                                    channel_multiplier=-1), reads=[on64.b], writes=[MK.b])
            kb.op("pool", lambda E: E.affine_select(out=MK[:, 64:128], in_=on64[:, 0:64], pattern=[[1, 64]], compare_op=ALU.is_ge, fill=0.0, base=0,
                                                    channel_multiplier=-1), reads=[on64.b], writes=[MK.b])
            kb.op("pool", lambda E: E.affine_select(out=ML[:], in_=on64[:, 0:64], pattern=[[-1, 64]], compare_op=ALU.is_gt, fill=0.0, base=0,
                                                    channel_multiplier=1), reads=[on64.b], writes=[ML.b])
            W = []
            for hh in range(2):
                w = {}
                for nm, shp in (("F", [64, 3, NBK + 1]), ("G", [64, 3, NBK + 1]), ("FL", [96, 2, NBK + 1]), ("GL", [96, 2, NBK + 1]),
                                ("f", [64, 3, NBK]), ("fl", [96, 2, NBK]), ("t1", [64, 3, NBK]), ("tl", [96, 2, NBK]),
                                ("lgw", [64, NBK]), ("a", [64, NBK]), ("kap", [64, NBK]), ("kp", [64, NBK]), ("bet", [64, NBK]), ("cs", [64, NBK]),
                                ("tA", [64, NBK]), ("tB", [64, NBK]), ("OB", [64, NBK]), ("OT", [64, NBK]),
                                ("lw", [64, C]), ("lm", [64, C]), ("g", [64, C]), ("gi", [64, C]), ("gm", [64, C]),
                                ("KR", [64, 2 * C]), ("kt", [64, C]), ("bt", [64, C]), ("AA0", [64, 128]), ("AA1", [64, 128]), ("BbT", [64, C]),
                                ("PBm", [64, 128]), ("X", [64, 128]), ("TM", [64, 192]), ("W2n", [64, 64]), ("RhT", [64, C]), ("MTn", [64, 64]),
                                ("Ha", [64, 64]), ("Hb", [64, 64]), ("ht", [64, 64])):
                    w[nm] = Tl(kb, st, f"{nm}{hh}", shp)
                W.append(w)
            zero = cst[0:64, 2:3]

            def mm(ps_ap, lhsT, rhs, R, Wb, start=True, stop=True):
                kb.op("pe", lambda E: E.matmul(ps_ap, lhsT=lhsT, rhs=rhs, start=start, stop=stop), reads=R, writes=Wb)

            def dve_tt(out, in0, in1, op, R, Wb):
                kb.op("dve", lambda E: E.tensor_tensor(out=out, in0=in0, in1=in1, op=op), reads=R, writes=Wb)

            for b in range(2):
                for d in range(2):
                    for hh in range(2):
                        kb.op("pool", lambda E: E.memset(W[hh]["Ha"][:], 0.0), writes=[W[hh]["Ha"].b])
                    Hcur = [W[0]["Ha"], W[1]["Ha"]]
                    Hnxt = [W[0]["Hb"], W[1]["Hb"]]
                    for (S0, S1, q0, n) in scan_blocks(NBK):
                        for hh in range(2):
                            w = W[hh]
                            vo = 40 + (d * 2 + hh) * 8
                            col = lambda j: ev[0:64, vo + j:vo + j + 1]
                            load_scan(w["F"], w["G"], lambda lo, hi: pt_v[hh * 64:hh * 64 + 64, 0:3, b, lo:hi], d, S0, S1, q0, n, 1, 64)
                            load_scan(w["FL"], w["GL"], lambda lo, hi: pt_v[0:96, 3:5, b, lo:hi], d, S0, S1, q0, n, 1, 96)
                            F, FL, f, fl, t1, tl = w["F"], w["FL"], w["f"], w["fl"], w["t1"], w["tl"]
                            dve_tt(t1[:, :, 0:n], F[:, :, 0:n], F[:, :, 1:n + 1], ALU.subtract, [F.b], [t1.b])
                            for q in range(3):
                                kb.op("dve", lambda E: E.scalar_tensor_tensor(out=f[:, q, 0:n], in0=t1[:, q, 0:n], scalar=col(q), in1=F[:, q, 1:n + 1],
                                                                              op0=ALU.mult, op1=ALU.add), reads=[t1.b, F.b, ev.b], writes=[f.b])
                            dve_tt(tl[:, :, 0:n], FL[:, :, 0:n], FL[:, :, 1:n + 1], ALU.subtract, [FL.b], [tl.b])
                            for q in range(2):
                                kb.op("dve", lambda E: E.scalar_tensor_tensor(out=fl[:, q, 0:n], in0=tl[:, q, 0:n], scalar=ev[0:96, 72 + d * 2 + q:73 + d * 2 + q],
                                                                              in1=FL[:, q, 1:n + 1], op0=ALU.mult, op1=ALU.add), reads=[tl.b, FL.b, ev.b], writes=[fl.b])
                            kb.op("act", lambda E: E.activation(out=tl[:, 0, 0:n], in_=fl[:, 0, 0:n], func=AF.Tanh), reads=[fl.b], writes=[tl.b])
                            ps = nextps()
                            mm(ps[0:64, 0:n], lwt[:, (d * 2 + hh) * 2 + 0, :], tl[:, 0, 0:n], [lwt.b, tl.b], [ps.b])
                            kb.op("act", lambda E: E.activation(out=w["lgw"][:, 0:n], in_=ps[0:64, 0:n], func=AF.Sigmoid, bias=col(3)), reads=[ps.b, ev.b], writes=[w["lgw"].b])
                            ps = nextps()
                            mm(ps[0:64, 0:n], lwt[:, (d * 2 + hh) * 2 + 1, :], fl[:, 1, 0:n], [lwt.b, fl.b], [ps.b])
                            kb.op("act", lambda E: E.activation(out=w["a"][:, 0:n], in_=ps[0:64, 0:n], func=AF.Sigmoid, bias=col(4)), reads=[ps.b, ev.b], writes=[w["a"].b])
                            kb.op("act", lambda E: E.activation(out=w["tA"][:, 0:n], in_=f[:, 1, 0:n], func=AF.Square, scale=col(5)), reads=[f.b, ev.b], writes=[w["tA"].b])
                            ps = nextps()
                            mm(ps[0:64, 0:n], on64[:, 0:64], w["tA"][:, 0:n], [on64.b, w["tA"].b], [ps.b])
                            kb.op("act", lambda E: E.activation(out=w["tB"][:, 0:n], in_=ps[0:64, 0:n], func=AF.Sqrt), reads=[ps.b], writes=[w["tB"].b])
                            kb.op("dve", lambda E: E.tensor_scalar(out=w["tB"][:, 0:n], in0=w["tB"][:, 0:n], scalar1=1e-12, scalar2=None, op0=ALU.max), reads=[w["tB"].b], writes=[w["tB"].b])
                            kb.op("dve", lambda E: E.reciprocal(out=w["tB"][:, 0:n], in_=w["tB"][:, 0:n]), reads=[w["tB"].b], writes=[w["tB"].b])
                            kb.op("dve", lambda E: E.scalar_tensor_tensor(out=w["kap"][:, 0:n], in0=f[:, 1, 0:n], scalar=col(5), in1=w["tB"][:, 0:n], op0=ALU.mult, op1=ALU.mult),
                                  reads=[f.b, ev.b, w["tB"].b], writes=[w["kap"].b])
                            kb.op("act", lambda E: E.activation(out=w["tA"][:, 0:n], in_=w["a"][:, 0:n], func=AF.Identity, scale=col(6), bias=omk[:, d * 2 + hh:d * 2 + hh + 1]),
                                  reads=[w["a"].b, ev.b, omk.b], writes=[w["tA"].b])
                            dve_tt(w["kp"][:, 0:n], f[:, 1, 0:n], w["tA"][:, 0:n], ALU.mult, [f.b, w["tA"].b], [w["kp"].b])
                            dve_tt(w["bet"][:, 0:n], w["kap"][:, 0:n], w["a"][:, 0:n], ALU.mult, [w["kap"].b, w["a"].b], [w["bet"].b])
                            kb.op("dve", lambda E: E.scalar_tensor_tensor(out=w["tA"][:, 0:n], in0=f[:, 0, 0:n], scalar=col(7), in1=w["kp"][:, 0:n], op0=ALU.mult, op1=ALU.mult),
                                  reads=[f.b, ev.b, w["kp"].b], writes=[w["tA"].b])
                            ps = nextps()
                            mm(ps[0:64, 0:n], on64[:, 0:64], w["tA"][:, 0:n], [on64.b, w["tA"].b], [ps.b])
                            dve_tt(w["tB"][:, 0:n], ps[0:64, 0:n], f[:, 2, 0:n], ALU.mult, [ps.b, f.b], [w["tB"].b])
                            store_scan(BN[d], lambda lo, hi: BN[d].ap[hh * 64:hh * 64 + 64, b * LS + lo:b * LS + hi], w["tB"], w["OT"], d, S0, S1, q0, n, 64)
                            kb.op("dve", lambda E: E.tensor_tensor_scan(out=w["cs"][:, 0:n], data0=on64[:, 0:n], data1=w["lgw"][:, 0:n], initial=0.0,
                                                                        op0=ALU.mult, op1=ALU.add), reads=[on64.b, w["lgw"].b], writes=[w["cs"].b])
                        for c0 in (range(0, n, C) if "nochunk" not in DBG else []):
                            for hh in range(2):
                                w = W[hh]
                                f = w["f"]
                                H = Hcur[hh]
                                Hn = Hnxt[hh]
                                cs_ = slice(c0, c0 + C)
                                off = w["cs"][:, c0 - 1:c0] if c0 > 0 else zero
                                kb.op("dve", lambda E: E.tensor_scalar(out=w["lw"][:], in0=w["cs"][:, cs_], scalar1=off, scalar2=-E05, op0=ALU.subtract, op1=ALU.mult),
                                      reads=[w["cs"].b, cst.b], writes=[w["lw"].b])
                                kb.op("dve", lambda E: E.scalar_tensor_tensor(out=w["lm"][:], in0=w["lgw"][:, cs_], scalar=E05, in1=w["lw"][:], op0=ALU.mult, op1=ALU.add),
                                      reads=[w["lgw"].b, w["lw"].b], writes=[w["lm"].b])
                                kb.op("act", lambda E: E.activation(out=w["g"][:], in_=w["lw"][:], func=AF.Exp), reads=[w["lw"].b], writes=[w["g"].b])
                                kb.op("act", lambda E: E.activation(out=w["gi"][:], in_=w["lw"][:], func=AF.Exp, scale=-1.0), reads=[w["lw"].b], writes=[w["gi"].b])
                                kb.op("act", lambda E: E.activation(out=w["gm"][:], in_=w["lm"][:], func=AF.Exp), reads=[w["lm"].b], writes=[w["gm"].b])
                                KR, kt, bt = w["KR"], w["kt"], w["bt"]
                                dve_tt(KR[:, 0:C], w["kap"][:, cs_], w["gm"][:], ALU.mult, [w["kap"].b, w["gm"].b], [KR.b])
                                dve_tt(KR[:, C:2 * C], f[:, 0, cs_], w["g"][:], ALU.mult, [f.b, w["g"].b], [KR.b])
                                dve_tt(kt[:], w["kp"][:, cs_], w["gi"][:], ALU.mult, [w["kp"].b, w["gi"].b], [kt.b])
                                dve_tt(bt[:], w["bet"][:, cs_], w["gi"][:], ALU.mult, [w["bet"].b, w["gi"].b], [bt.b])
                                CK = int(DBG[DBG.index("ck") + 2]) if "ck" in DBG else 9
                                if CK < 2:
                                    continue
                                pa, pb, pc = nextps(), nextps(), nextps()
                                mm(pa[0:64, 0:128], bt[:], KR[:], [bt.b, KR.b], [pa.b])
                                mm(pb[0:64, 0:128], kt[:], KR[:], [kt.b, KR.b], [pb.b])
                                mm(pc[0:64, 0:64], KR[:, 0:C], bt[:], [KR.b, bt.b], [pc.b])
                                AA = w["AA0"]
                                dve_tt(AA[:, 0:64], pa[0:64, 0:64], MK[:, 0:64], ALU.mult, [pa.b, MK.b], [AA.b])
                                dve_tt(w["BbT"][:], pa[0:64, 64:128], MK[:, 64:128], ALU.mult, [pa.b, MK.b], [w["BbT"].b])
                                dve_tt(w["PBm"][:], pb[0:64, 0:128], MK[:], ALU.mult, [pb.b, MK.b], [w["PBm"].b])
                                dve_tt(AA[:, 64:128], pc[0:64, 0:64], ML[:], ALU.mult, [pc.b, ML.b], [AA.b])
                                if CK < 3:
                                    continue
                                pt = nextps()
                                for i, src in enumerate((KR[:, 0:C], kt[:], bt[:], f[:, 2, cs_])):
                                    srcb = [KR.b, kt.b, bt.b, f.b][i]
                                    kb.op("pe", lambda E: E.matmul(pt[0:64, i * 64:(i + 1) * 64], lhsT=src, rhs=ident[0:64, 0:64], start=True, stop=True), reads=[srcb, ident.b], writes=[pt.b])
                                X, TM = w["X"], w["TM"]
                                if "ck3a" in DBG:
                                    continue
                                evac(X[:, 0:64], pt[0:64, 0:64], [pt.b], [X.b])
                                evac(TM[:], pt[0:64, 64:256], [pt.b], [TM.b])
                                Kt_, Bt_, V_ = TM[:, 0:64], TM[:, 64:128], TM[:, 128:192]
                                if "ck3b" in DBG:
                                    continue
                                pk = nextps()
                                mm(pk[0:64, 0:64], w["PBm"][:, 0:64], V_, [w["PBm"].b, TM.b], [pk.b])
                                evac(X[:, 64:128], pk[0:64, 0:64], [pk.b], [X.b])
                                if CK < 4:
                                    continue
                                for lev in range(6):
                                    if lev > 0:
                                        AAn = w["AA1"] if AA is w["AA0"] else w["AA0"]
                                        pq = nextps()
                                        mm(pq[0:64, 0:64], AA[:, 64:128], AA[:, 0:64], [AA.b], [pq.b])
                                        if lev < 5:
                                            mm(pq[0:64, 64:128], AA[:, 0:64], AA[:, 64:128], [AA.b], [pq.b])
                                            evac(AAn[:], pq[0:64, 0:128], [pq.b], [AAn.b])
                                        else:
                                            evac(AAn[:, 0:64], pq[0:64, 0:64], [pq.b], [AAn.b])
                                        AA = AAn
                                    px = nextps()
                                    mm(px[0:64, 0:128], AA[:, 0:64], X[:], [AA.b, X.b], [px.b])
                                    dve_tt(X[:], X[:], px[0:64, 0:128], ALU.subtract if lev == 0 else ALU.add, [X.b, px.b], [X.b])
                                if CK < 5:
                                    continue
                                kb.op("dve", lambda E: E.tensor_scalar(out=w["W2n"][:], in0=X[:, 64:128], scalar1=-1.0, scalar2=None, op0=ALU.mult), reads=[X.b], writes=[w["W2n"].b])
                                W1 = X[:, 0:64]
                                pr = nextps()
                                mm(pr[0:64, 0:64], W1, w["BbT"][:], [X.b, w["BbT"].b], [pr.b])
                                dve_tt(w["RhT"][:], KR[:, C:2 * C], pr[0:64, 0:64], ALU.subtract, [KR.b, pr.b], [w["RhT"].b])
                                pm = nextps()
                                mm(pm[0:64, 0:64], W1, Bt_, [X.b, TM.b], [pm.b])
                                kb.op("dve", lambda E: E.tensor_scalar(out=w["MTn"][:], in0=pm[0:64, 0:64], scalar1=-1.0, scalar2=None, op0=ALU.mult), reads=[pm.b], writes=[w["MTn"].b])
                                pg = nextps()
                                mm(pg[0:64, 0:64], Kt_, V_, [TM.b], [pg.b], start=True, stop=False)
                                mm(pg[0:64, 0:64], Bt_, w["W2n"][:], [TM.b, w["W2n"].b], [pg.b], start=False, stop=False)
                                mm(pg[0:64, 0:64], w["MTn"][:], H[:], [w["MTn"].b, H.b], [pg.b], start=False, stop=True)
                                py = nextps()
                                mm(py[0:64, 0:64], V_, w["PBm"][:, 64:128], [TM.b, w["PBm"].b], [py.b], start=True, stop=False)
                                mm(py[0:64, 0:64], w["W2n"][:], w["BbT"][:], [w["W2n"].b, w["BbT"].b], [py.b], start=False, stop=False)
                                mm(py[0:64, 0:64], H[:], w["RhT"][:], [H.b, w["RhT"].b], [py.b], start=False, stop=True)
                                evac(w["OB"][:, cs_], py[0:64, 0:64], [py.b], [w["OB"].b])
                                dve_tt(w["ht"][:], H[:], pg[0:64, 0:64], ALU.add, [H.b, pg.b], [w["ht"].b])
                                kb.op("dve", lambda E: E.tensor_scalar(out=Hn[:], in0=w["ht"][:], scalar1=w["g"][:, C - 1:C], scalar2=None, op0=ALU.mult),
                                      reads=[w["ht"].b, w["g"].b], writes=[Hn.b])
                                Hcur[hh], Hnxt[hh] = Hn, H
                        for hh in range(2):
                            w = W[hh]
                            store_scan(YR[d], lambda lo, hi: YR[d].ap[hh * 64:hh * 64 + 64, b * LS + lo:b * LS + hi], w["OB"], w["OT"], d, S0, S1, q0, n, 64)
            kb.barrier()
            kb.close_scope()

    def mixer_ssd(e):
        C = 128
        NBK = 512
        with ExitStack() as st:
            kb.open_scope()
            ev = Tl(kb, st, "evs", [128, 96])
            kb.dma("sp", ev[:], evec_d.ap[:, e * 96:(e + 1) * 96], ev.b, None)
            dv = Tl(kb, st, "dv", [128, 8])
            kb.dma("sp", dv[0:64, :], dvec_d.ap[:, e * 8:(e + 1) * 8], dv.b, None)
            kb.dma("sp", dv[64:128, :], dvec_d.ap[:, e * 8:(e + 1) * 8], dv.b, None)
            s4 = Tl(kb, st, "s4", [4, 512])
            kb.dma("sp", s4[:], sel4_d.ap, s4.b, None)
            selt = Tl(kb, st, "seltm", [128, 12])
            kb.dma("sp", selt[:], sel_d.ap, selt.b, None)
            MU = Tl(kb, st, "MU", [128, 128])
            on = Tl(kb, st, "onS", [128, 128])
            kb.op("pool", lambda E: E.memset(on[:], 1.0), writes=[on.b])
            kb.op("pool", lambda E: E.affine_select(out=MU[:], in_=on[:], pattern=[[1, 128]], compare_op=ALU.is_ge, fill=0.0, base=0, channel_multiplier=-1),
                  reads=[on.b], writes=[MU.b])
            F = Tl(kb, st, "Fs", [128, 4, NBK + 3])
            G0 = Tl(kb, st, "G0s", [128, 4, NBK + 3])
            G1 = Tl(kb, st, "G1s", [128, 4, NBK + 3])
            Fd = Tl(kb, st, "Fd", [4, 1, NBK])
            Gd0 = Tl(kb, st, "Gd0", [4, 1, NBK])
            Gd1 = Tl(kb, st, "Gd1", [4, 1, NBK])
            xc = Tl(kb, st, "xcs", [128, 4, NBK])
            xs = Tl(kb, st, "xss", [128, 4, NBK])
            dtt = Tl(kb, st, "dtt", [4, NBK])
            dta = Tl(kb, st, "dta", [4, NBK])
            acol = Tl(kb, st, "acolS", [4, 2])
            acs = Tl(kb, st, "acs", [4, C])
            DTA = Tl(kb, st, "DTA", [128, 8])
            CBm = Tl(kb, st, "CBm", [128, C])
            Btok = Tl(kb, st, "Btok", [128, 128])
            sg = [Tl(kb, st, f"sgS{i}", [128, C]) for i in range(2)]
            MT = [Tl(kb, st, f"MTs{i}", [128, C]) for i in range(2)]
            gb = [Tl(kb, st, f"gbS{i}", [128, C]) for i in range(2)]
            rT = [Tl(kb, st, f"rTs{i}", [128, C]) for i in range(2)]
            al = [Tl(kb, st, f"alS{i}", [128, 2]) for i in range(2)]
            xdt = [Tl(kb, st, f"xdt{i}", [128, 64]) for i in range(2)]
            xdw = [Tl(kb, st, f"xdw{i}", [128, 64]) for i in range(2)]
            xdd = [Tl(kb, st, f"xdd{i}", [128, 64]) for i in range(2)]
            Hs = [Tl(kb, st, f"Hs{i}", [128, 64]) for i in range(4)]
            OB = [Tl(kb, st, f"OBs{i}", [64, NBK]) for i in range(4)]
            OT = Tl(kb, st, "OTs", [64, NBK])

            def mm(ps_ap, lhsT, rhs, R, Wb, start=True, stop=True):
                kb.op("pe", lambda E: E.matmul(ps_ap, lhsT=lhsT, rhs=rhs, start=start, stop=stop), reads=R, writes=Wb)

            def blend(dst, g0, g1, npart, width):
                kb.op("dve", lambda E: E.tensor_scalar(out=dst[0:npart, :, 0:width], in0=g0[0:npart, :, 0:width], scalar1=selt[0:npart, 10:11], scalar2=None, op0=ALU.mult),
                      reads=[g0.b, selt.b], writes=[dst.b])
                kb.op("dve", lambda E: E.scalar_tensor_tensor(out=dst[0:npart, :, 0:width], in0=g1[0:npart, :, 0:width], scalar=selt[0:npart, 11:12], in1=dst[0:npart, :, 0:width],
                                                              op0=ALU.mult, op1=ALU.add), reads=[g1.b, selt.b, dst.b], writes=[dst.b])

            for d in range(2):
                kb.op("act", lambda E: E.activation(out=acol[:, d:d + 1], in_=ev[0:4, 80 + d * 2 + 1:80 + d * 2 + 2], func=AF.Exp), reads=[ev.b], writes=[acol.b])
                kb.op("dve", lambda E: E.tensor_scalar(out=acol[:, d:d + 1], in0=acol[:, d:d + 1], scalar1=-1.0, scalar2=None, op0=ALU.mult), reads=[acol.b], writes=[acol.b])
                for hd in range(4):
                    kb.op("pool", lambda E: E.memset(Hs[hd][:], 0.0), writes=[Hs[hd].b])
                for (S0, S1, q0, n) in scan_blocks(NBK):
                    if d == 0:
                        lo, hi = S0 + q0, S0 + q0 + n
                        h = min(3, q0)
                        for bsel, Gx, Gdx in ((0, G0, Gd0), (1, G1, Gd1)):
                            if h < 3:
                                kb.op("pool", lambda E: E.memset(Gx[:, :, 0:3 - h], 0.0), writes=[Gx.b])
                            kb.dma("sp", Gx[:, :, 3 - h:3 + n], pt_v[:, 9:13, bsel, lo - h:hi], Gx.b, PT.b)
                            kb.dma("sp", Gdx[:, :, 0:n], pt_v[0:4, 13:14, bsel, lo:hi], Gdx.b, PT.b)
                        blend(F, G0, G1, 128, n + 3)
                        blend(Fd, Gd0, Gd1, 4, n)
                    else:
                        hi = S1 - q0
                        lo = hi - n
                        h = min(3, q0)
                        for bsel, Gx, Gdx in ((0, G0, Gd0), (1, G1, Gd1)):
                            if h < 3:
                                kb.op("pool", lambda E: E.memset(Gx[:, :, n + h:n + 3], 0.0), writes=[Gx.b])
                            kb.dma("sp", Gx[:, :, 0:n + h], pt_v[:, 9:13, bsel, lo:hi + h], Gx.b, PT.b)
                            kb.dma("sp", Gdx[:, :, 0:n], pt_v[0:4, 13:14, bsel, lo:hi], Gdx.b, PT.b)
                        blend(G0, G0, G1, 128, n + 3)
                        blend(Gd0, Gd0, Gd1, 4, n)
                        kb.op("dve", lambda E: E.tensor_copy(out=F[:, :, 0:n + 3], in_=G0[:, :, 0:n + 3][:, :, ::-1]), reads=[G0.b], writes=[F.b])
                        kb.op("dve", lambda E: E.tensor_copy(out=Fd[:, :, 0:n], in_=Gd0[:, :, 0:n][:, :, ::-1]), reads=[Gd0.b], writes=[Fd.b])
                    for q in range(4):
                        vo = d * 20 + q * 5
                        kb.op("act", lambda E: E.activation(out=xc[:, q, 0:n], in_=F[:, q, 3:3 + n], func=AF.Identity, scale=ev[:, vo + 3:vo + 4], bias=ev[:, vo + 4:vo + 5]),
                              reads=[F.b, ev.b], writes=[xc.b])
                        for k in range(1, 4):
                            kb.op("dve", lambda E: E.scalar_tensor_tensor(out=xc[:, q, 0:n], in0=F[:, q, 3 - k:3 - k + n], scalar=ev[:, vo + 3 - k:vo + 4 - k], in1=xc[:, q, 0:n],
                                                                          op0=ALU.mult, op1=ALU.add), reads=[F.b, ev.b, xc.b], writes=[xc.b])
                    kb.op("act", lambda E: E.activation(out=xs[:, :, 0:n], in_=xc[:, :, 0:n], func=AF.Silu), reads=[xc.b], writes=[xs.b])
                    kb.op("act", lambda E: E.activation(out=dtt[:, 0:n], in_=Fd[:, 0, 0:n], func=AF.Exp, bias=ev[0:4, 80 + d * 2:80 + d * 2 + 1]), reads=[Fd.b, ev.b], writes=[dtt.b])
                    kb.op("act", lambda E: E.activation(out=dtt[:, 0:n], in_=dtt[:, 0:n], func=AF.Ln, bias=cst[0:4, 1:2]), reads=[dtt.b, cst.b], writes=[dtt.b])
                    kb.op("dve", lambda E: E.tensor_scalar(out=dta[:, 0:n], in0=dtt[:, 0:n], scalar1=acol[:, d:d + 1], scalar2=None, op0=ALU.mult), reads=[dtt.b, acol.b], writes=[dta.b])
                    for c0 in range(0, n, C):
                        cs_ = slice(c0, c0 + C)
                        kb.op("dve", lambda E: E.tensor_tensor_scan(out=acs[:], data0=on[0:4, 0:C], data1=dta[:, cs_], initial=0.0, op0=ALU.mult, op1=ALU.add),
                              reads=[on.b, dta.b], writes=[acs.b])
                        pt = nextps()
                        kb.op("pe", lambda E: E.matmul(pt[:, 0:4], lhsT=dtt[:, cs_], rhs=ident[0:4, 0:4], start=True, stop=True), reads=[dtt.b, ident.b], writes=[pt.b])
                        kb.op("pe", lambda E: E.matmul(pt[:, 4:8], lhsT=acs[:], rhs=ident[0:4, 0:4], start=True, stop=True), reads=[acs.b, ident.b], writes=[pt.b])
                        evac(DTA[:], pt[:, 0:8], [pt.b], [DTA.b])
                        pcb = nextps()
                        mm(pcb[:, 0:C], xs[:, 2, cs_], xs[:, 3, cs_], [xs.b], [pcb.b])
                        kb.op("dve", lambda E: E.tensor_tensor(out=CBm[:], in0=pcb[:, 0:C], in1=MU[:], op=ALU.mult), reads=[pcb.b, MU.b], writes=[CBm.b])
                        pbt = nextps()
                        kb.op("pe", lambda E: E.matmul(pbt[:, 0:128], lhsT=xs[:, 2, cs_], rhs=ident[:], start=True, stop=True), reads=[xs.b, ident.b], writes=[pbt.b])
                        evac(Btok[:], pbt[:, 0:128], [pbt.b], [Btok.b])
                        for hd in range(4):
                            i2 = hd % 2
                            H = Hs[hd]
                            pab = nextps()
                            mm(pab[:, 0:C], s4[:, hd * 128:(hd + 1) * 128], acs[:], [s4.b, acs.b], [pab.b])
                            kb.op("dve", lambda E: E.tensor_scalar(out=sg[i2][:], in0=pab[:, 0:C], scalar1=DTA[:, 4 + hd:5 + hd], scalar2=0.0, op0=ALU.subtract, op1=ALU.min),
                                  reads=[pab.b, DTA.b], writes=[sg[i2].b])
                            kb.op("act", lambda E: E.activation(out=sg[i2][:], in_=sg[i2][:], func=AF.Exp), reads=[sg[i2].b], writes=[sg[i2].b])
                            kb.op("dve", lambda E: E.tensor_tensor(out=MT[i2][:], in0=sg[i2][:], in1=CBm[:], op=ALU.mult), reads=[sg[i2].b, CBm.b], writes=[MT[i2].b])
                            kb.op("act", lambda E: E.activation(out=gb[i2][:], in_=pab[:, 0:C], func=AF.Exp), reads=[pab.b], writes=[gb[i2].b])
                            kb.op("dve", lambda E: E.tensor_tensor(out=rT[i2][:], in0=xs[:, 3, cs_], in1=gb[i2][:], op=ALU.mult), reads=[xs.b, gb[i2].b], writes=[rT[i2].b])
                            kb.op("dve", lambda E: E.tensor_copy(out=al[i2][:, 0:1], in_=pab[:, C - 1:C]), reads=[pab.b], writes=[al[i2].b])
                            kb.op("act", lambda E: E.activation(out=al[i2][:, 1:2], in_=DTA[:, 4 + hd:5 + hd], func=AF.Exp, scale=-1.0, bias=al[i2][:, 0:1]),
                                  reads=[DTA.b, al[i2].b], writes=[al[i2].b])
                            pxt = nextps()
                            pb0 = (hd % 2) * 64
                            kb.op("pe", lambda E: E.matmul(pxt[:, 0:64], lhsT=xs[pb0:pb0 + 64, hd // 2, cs_], rhs=ident[pb0:pb0 + 64, pb0:pb0 + 64], start=True, stop=True), reads=[xs.b, ident.b], writes=[pxt.b])
                            kb.op("dve", lambda E: E.tensor_scalar(out=xdt[i2][:], in0=pxt[:, 0:64], scalar1=DTA[:, hd:hd + 1], scalar2=None, op0=ALU.mult), reads=[pxt.b, DTA.b], writes=[xdt[i2].b])
                            kb.op("dve", lambda E: E.tensor_scalar(out=xdd[i2][:], in0=pxt[:, 0:64], scalar1=dv[:, d * 4 + hd:d * 4 + hd + 1], scalar2=None, op0=ALU.mult), reads=[pxt.b, dv.b], writes=[xdd[i2].b])
                            kb.op("dve", lambda E: E.tensor_scalar(out=xdw[i2][:], in0=xdt[i2][:], scalar1=al[i2][:, 1:2], scalar2=None, op0=ALU.mult), reads=[xdt[i2].b, al[i2].b], writes=[xdw[i2].b])
                            py = nextps()
                            mm(py[0:64, 0:C], xdt[i2][:], MT[i2][:], [xdt[i2].b, MT[i2].b], [py.b], start=True, stop=False)
                            mm(py[0:64, 0:C], xdd[i2][:], ident[:], [xdd[i2].b, ident.b], [py.b], start=False, stop=False)
                            mm(py[0:64, 0:C], H[:], rT[i2][:], [H.b, rT[i2].b], [py.b], start=False, stop=True)
                            evac(OB[hd][:, cs_], py[0:64, 0:C], [py.b], [OB[hd].b])
                            ph = nextps()
                            mm(ph[:, 0:64], Btok[:], xdw[i2][:], [Btok.b, xdw[i2].b], [ph.b])
                            kb.op("act", lambda E: E.activation(out=gb[i2][:, 0:1], in_=al[i2][:, 0:1], func=AF.Exp), reads=[al[i2].b, rT[i2].b], writes=[gb[i2].b])
                            kb.op("dve", lambda E: E.scalar_tensor_tensor(out=H[:], in0=H[:], scalar=gb[i2][:, 0:1], in1=ph[:, 0:64], op0=ALU.mult, op1=ALU.add),
                                  reads=[H.b, gb[i2].b, ph.b], writes=[H.b])
                    for hd in range(4):
                        r0 = (hd // 2) * 128 + (hd % 2) * 64
                        store_scan(YM[d], lambda lo_, hi_: YM[d].ap[r0:r0 + 64, lo_:hi_], OB[hd], OT, d, S0, S1, q0, n, 64)
            kb.barrier()
            kb.close_scope()

    def phase_C_even(l, e):
        with ExitStack() as st:
            kb.open_scope()
            W = Tl(kb, st, "WoutE", [128, 3, D], BF16)
            load_w_bf16(W, lambda k: W[:, k, :], lambda k: woute_d.ap[e * 384 + k * 128:e * 384 + (k + 1) * 128, :], 3)
            G2w = Tl(kb, st, "G2w", [128, 2, 128], BF16)
            load_w_bf16(G2w, lambda k: G2w[:, k, :], lambda k: g2_d.ap[e * 256 + k * 128:e * 256 + (k + 1) * 128, :], 2)
            ev = Tl(kb, st, "evC", [128, 96])
            kb.dma("sp", ev[:], evec_d.ap[:, e * 96:(e + 1) * 96], ev.b, None)
            selt = Tl(kb, st, "seltC", [128, 12])
            kb.dma("sp", selt[:], sel_d.ap, selt.b, None)
            on = Tl(kb, st, "onC", [128, 128])
            bo = Tl(kb, st, "boC", [128, 128])
            kb.op("dve", lambda E: E.memset(on[:], 1.0), writes=[on.b])
            kb.op("dve", lambda E: E.memset(bo[:], 0.0), writes=[bo.b])
            kb.op("dve", lambda E: E.memset(bo[0:64, 0:64], 1.0), writes=[bo.b])
            kb.op("dve", lambda E: E.memset(bo[64:128, 64:128], 1.0), writes=[bo.b])
            kb.op("dve", lambda E: E.memset(cst[:, 3:4], 1e-5), writes=[cst.b])
            kb.op("dve", lambda E: E.memset(cst[:, 4:5], 64e-5), writes=[cst.b])
            mns = Tl(kb, st, "mns", [128, 4])
            for q in range(2):
                for b in range(2):
                    kb.op("dve", lambda E: E.tensor_tensor(out=mns[:, q * 2 + b:q * 2 + b + 1], in0=ev[:, 78 + q:79 + q], in1=selt[:, 10 + b:11 + b], op=ALU.mult),
                          reads=[ev.b, selt.b], writes=[mns.b])
            ym = [[Tl(kb, st, f"ym{i}{d}", [128, 2, T]) for d in range(2)] for i in range(2)]
            zz = [Tl(kb, st, f"zz{i}", [128, 2, T]) for i in range(2)]
            gl = [Tl(kb, st, f"glC{i}", [128, 2, T]) for i in range(2)]
            yr = [[Tl(kb, st, f"yr{i}{d}", [128, T]) for d in range(2)] for i in range(2)]
            bn = [[Tl(kb, st, f"bn{i}{d}", [128, T]) for d in range(2)] for i in range(2)]
            sq2 = Tl(kb, st, "sq2C", [128, 2, T])
            rs = Tl(kb, st, "rsC", [128, T])
            t1 = Tl(kb, st, "t1E", [128, T])
            t2 = Tl(kb, st, "t2E", [128, T])
            sgl = Tl(kb, st, "sglC", [128, 2, T], BF16)
            mbf = [Tl(kb, st, f"mE{i}", [128, 3, T], BF16) for i in range(2)]
            stg = [Tl(kb, st, f"stE{i}", [128, FC, T]) for i in range(2)]
            ymv = [YM[d].ap.rearrange("(q p) s -> p q s", p=128) for d in range(2)]
            for ti, tile in enumerate(tiles):
                n, b, pos = tile["n"], tile["b"], tile["pos"]
                i = ti % 2
                m, sg = mbf[i], stg[i]

                def ld(tj):
                    tl_ = tiles[tj]
                    i_, n_, b_, p_ = tj % 2, tl_["n"], tl_["b"], tl_["pos"]
                    for d in range(2):
                        kb.dma("sp", ym[i_][d][:, :, 0:n_], ymv[d][:, :, p_:p_ + n_], ym[i_][d].b, YM[d].b)
                        kb.dma("sp", yr[i_][d][:, 0:n_], YR[d].ap[:, b_ * LS + p_:b_ * LS + p_ + n_], yr[i_][d].b, YR[d].b)
                        kb.dma("sp", bn[i_][d][:, 0:n_], BN[d].ap[:, b_ * LS + p_:b_ * LS + p_ + n_], bn[i_][d].b, BN[d].b)
                    kb.dma("sp", zz[i_][:, :, 0:n_], scr_ap(PT, 2, tl_, 7), zz[i_].b, PT.b)
                    kb.dma("sp", gl[i_][:, :, 0:n_], scr_ap(PT, 2, tl_, 5), gl[i_].b, PT.b)
                if ti == 0:
                    ld(0)
                if ti + 1 < len(tiles):
                    ld(ti + 1)
                y0 = ym[i][0]
                kb.op("dve", lambda E: E.tensor_tensor(out=y0[:, :, 0:n], in0=y0[:, :, 0:n], in1=ym[i][1][:, :, 0:n], op=ALU.add), reads=[y0.b, ym[i][1].b], writes=[y0.b])
                kb.op("act", lambda E: E.activation(out=zz[i][:, :, 0:n], in_=zz[i][:, :, 0:n], func=AF.Silu), reads=[zz[i].b], writes=[zz[i].b])
                kb.op("dve", lambda E: E.tensor_tensor(out=y0[:, :, 0:n], in0=y0[:, :, 0:n], in1=zz[i][:, :, 0:n], op=ALU.mult), reads=[y0.b, zz[i].b], writes=[y0.b])
                kb.op("act", lambda E: E.activation(out=sq2[:, :, 0:n], in_=y0[:, :, 0:n], func=AF.Square), reads=[y0.b], writes=[sq2.b])
                ps = nextps()
                for q in range(2):
                    kb.op("pe", lambda E: E.matmul(ps[:, 0:n], lhsT=on[:], rhs=sq2[:, q, 0:n], start=(q == 0), stop=(q == 1)), reads=[on.b, sq2.b], writes=[ps.b])
                kb.op("act", lambda E: E.activation(out=rs[:, 0:n], in_=ps[:, 0:n], func=AF.Sqrt, scale=1.0 / 256, bias=cst[:, 3:4]), reads=[ps.b, cst.b], writes=[rs.b])
                kb.op("dve", lambda E: E.reciprocal(out=rs[:, 0:n], in_=rs[:, 0:n]), reads=[rs.b], writes=[rs.b])
                for q in range(2):
                    kb.op("dve", lambda E: E.scalar_tensor_tensor(out=m[:, q, 0:n], in0=y0[:, q, 0:n], scalar=mns[:, q * 2 + b:q * 2 + b + 1], in1=rs[:, 0:n],
                                                                  op0=ALU.mult, op1=ALU.mult), reads=[y0.b, mns.b, rs.b], writes=[m.b])
                yy = yr[i][0]
                kb.op("dve", lambda E: E.tensor_tensor(out=yy[:, 0:n], in0=yy[:, 0:n], in1=yr[i][1][:, 0:n], op=ALU.add), reads=[yy.b, yr[i][1].b], writes=[yy.b])
                ps = nextps()
                kb.op("pe", lambda E: E.matmul(ps[:, 0:n], lhsT=bo[:], rhs=yy[:, 0:n], start=True, stop=True), reads=[bo.b, yy.b], writes=[ps.b])
                kb.op("dve", lambda E: E.scalar_tensor_tensor(out=t1[:, 0:n], in0=ps[:, 0:n], scalar=-1.0 / 64, in1=yy[:, 0:n], op0=ALU.mult, op1=ALU.add),
                      reads=[ps.b, yy.b], writes=[t1.b])
                kb.op("act", lambda E: E.activation(out=t2[:, 0:n], in_=t1[:, 0:n], func=AF.Square), reads=[t1.b], writes=[t2.b])
                ps = nextps()
                kb.op("pe", lambda E: E.matmul(ps[:, 0:n], lhsT=bo[:], rhs=t2[:, 0:n], start=True, stop=True), reads=[bo.b, t2.b], writes=[ps.b])
                kb.op("act", lambda E: E.activation(out=t2[:, 0:n], in_=ps[:, 0:n], func=AF.Sqrt, scale=1.0 / 64, bias=cst[:, 4:5]), reads=[ps.b, cst.b], writes=[t2.b])
                kb.op("dve", lambda E: E.reciprocal(out=t2[:, 0:n], in_=t2[:, 0:n]), reads=[t2.b], writes=[t2.b])
                kb.op("dve", lambda E: E.tensor_tensor(out=t1[:, 0:n], in0=t1[:, 0:n], in1=t2[:, 0:n], op=ALU.mult), reads=[t1.b, t2.b], writes=[t1.b])
                kb.op("act", lambda E: E.activation(out=t1[:, 0:n], in_=t1[:, 0:n], func=AF.Identity, scale=ev[:, 76:77], bias=ev[:, 77:78]), reads=[t1.b, ev.b], writes=[t1.b])
                kb.op("dve", lambda E: E.tensor_tensor(out=t1[:, 0:n], in0=t1[:, 0:n], in1=bn[i][0][:, 0:n], op=ALU.add), reads=[t1.b, bn[i][0].b], writes=[t1.b])
                kb.op("dve", lambda E: E.tensor_tensor(out=t1[:, 0:n], in0=t1[:, 0:n], in1=bn[i][1][:, 0:n], op=ALU.add), reads=[t1.b, bn[i][1].b], writes=[t1.b])
                kb.op("act", lambda E: E.activation(out=sgl[:, :, 0:n], in_=gl[i][:, :, 0:n], func=AF.Sigmoid), reads=[gl[i].b], writes=[sgl.b])
                ps = nextps()
                for q in range(2):
                    kb.op("pe", lambda E: E.matmul(ps[:, 0:n], lhsT=G2w[:, q, :], rhs=sgl[:, q, 0:n], start=(q == 0), stop=(q == 1)), reads=[G2w.b, sgl.b], writes=[ps.b])
                kb.op("dve", lambda E: E.tensor_tensor(out=m[:, 2, 0:n], in0=t1[:, 0:n], in1=ps[:, 0:n], op=ALU.mult), reads=[t1.b, ps.b], writes=[m.b])
                out_proj(W, 3, [128, 128, 128], m, sg, n, tile)
            kb.barrier()
            kb.close_scope()

    ECH = [(0, 128), (128, 128), (256, 128), (384, 96), (480, 96), (576, 128), (704, 128), (832, 128), (960, 128), (1088, 128),
           (1216, 128), (1344, 128), (1472, 128), (1600, 4)]

    e_i = o_i = 0
    for l, lt in enumerate(LT):
        pend = (l - 1) if l > 0 else None
        stop = cfg.get("stop", "")
        if stop == "setup":
            break
        if lt == "O":
            phase_A(l, wino_d, o_i * D, 512, pend)
            if stop == "A":
                break
            mixer_rglru(o_i)
            if stop == "mix":
                break
            phase_C_odd(l, o_i)
            o_i += 1
        else:
            phase_A(l, wine_d, e_i * D, 1604, pend, chunks=ECH)
            if stop == "A":
                break
            ev_dve[0] = True
            if "nossd" not in DBG:
                mixer_ssd(e_i)
            if "norwkv" not in DBG:
                mixer_rwkv(e_i)
            ev_dve[0] = False
            if stop == "mix":
                break
            phase_C_even(l, e_i)
            e_i += 1
        if stop == "C":
            break
        if stop == "AR":
            break
        phase_D(l)

    with ExitStack() as st:
        kb.open_scope()
        selt = Tl(kb, st, "selt", [128, 12])
        g2c = Tl(kb, st, "g2c", [128, FC])
        kb.dma("sp", selt[:], sel_d.ap, selt.b, None)
        o5 = ((NL - 1) * 6 + 5) * FC * 3
        g2v = lambda j: mod[:, o5:o5 + FC * 3].rearrange("p (f j) -> p f j", j=3)[:, :, j]
        kb.op("dve", lambda E: E.tensor_scalar(out=g2c[:], in0=g2v(0), scalar1=selt[:, 8:9], scalar2=None, op0=ALU.mult), reads=[mod.b, selt.b], writes=[g2c.b])
        kb.op("dve", lambda E: E.scalar_tensor_tensor(out=g2c[:], in0=g2v(1), scalar=selt[:, 9:10], in1=g2c[:], op0=ALU.mult, op1=ALU.add),
              reads=[mod.b, selt.b, g2c.b], writes=[g2c.b])
        xts = [Tl(kb, st, f"xtF{i}", [128, FC, T]) for i in range(2)]
        rts = [Tl(kb, st, f"rtF{i}", [128, FC, T]) for i in range(2)]
        xa = Tl(kb, st, "xaF", [128, FC, T])
        ra = Tl(kb, st, "raF", [128, FC, T])
        sq = Tl(kb, st, "sqF", [128, FC, T], BF16)
        rs = Tl(kb, st, "rsF", [128, T])
        li = 0
        for t0 in range(0, TR, T):
            for r in range(8):
                xt, rt = xts[li % 2], rts[li % 2]
                li += 1
                kb.dma("sp", xt[:], lat_view(XT)[:, r, :, t0:t0 + T], xt.b, XT.bs(r * D, (r + 1) * D))
                kb.dma("sp", rt[:], lat_view(REDL)[:, r, :, t0:t0 + T], rt.b, REDL.bs(r * D, (r + 1) * D))
                if r == 0:
                    kb.op("dve", lambda E: E.tensor_scalar(out=xa[:], in0=xt[:], scalar1=selt[:, 0:1], scalar2=None, op0=ALU.mult), reads=[xt.b, selt.b], writes=[xa.b])
                    kb.op("pool", lambda E: E.tensor_scalar(out=ra[:], in0=rt[:], scalar1=selt[:, 0:1], scalar2=None, op0=ALU.mult), reads=[rt.b, selt.b], writes=[ra.b])
                else:
                    kb.op("dve", lambda E: E.scalar_tensor_tensor(out=xa[:], in0=xt[:], scalar=selt[:, r:r + 1], in1=xa[:], op0=ALU.mult, op1=ALU.add),
                          reads=[xt.b, selt.b, xa.b], writes=[xa.b])
                    kb.op("dve", lambda E: E.scalar_tensor_tensor(out=ra[:], in0=rt[:], scalar=selt[:, r:r + 1], in1=ra[:], op0=ALU.mult, op1=ALU.add),
                          reads=[rt.b, selt.b, ra.b], writes=[ra.b])
            for fc in range(FC):
                kb.op("dve", lambda E: E.scalar_tensor_tensor(out=xa[:, fc, :], in0=ra[:, fc, :], scalar=g2c[:, fc:fc + 1], in1=xa[:, fc, :],
                                                              op0=ALU.mult, op1=ALU.add), reads=[ra.b, g2c.b, xa.b], writes=[xa.b])
            kb.op("act", lambda E: E.activation(out=sq[:], in_=xa[:], func=AF.Square), reads=[xa.b], writes=[sq.b])
            ps = nextps()
            for fc in range(FC):
                kb.op("pe", lambda E: E.matmul(ps[:, 0:T], lhsT=ones_bf[:], rhs=sq[:, fc, :], start=(fc == 0), stop=(fc == FC - 1)),
                      reads=[ones_bf.b, sq.b], writes=[ps.b])
            kb.op("act", lambda E: E.activation(out=rs[:], in_=ps[:, 0:T], func=AF.Sqrt, scale=1.0 / D, bias=cst[:, 0:1]), reads=[ps.b, cst.b], writes=[rs.b])
            kb.op("dve", lambda E: E.reciprocal(out=rs[:], in_=rs[:]), reads=[rs.b], writes=[rs.b])
            for fc in range(FC):
                kb.op("dve", lambda E: E.scalar_tensor_tensor(out=ra[:, fc, :], in0=xa[:, fc, :], scalar=nw[:, 2 * NL * FC + fc:2 * NL * FC + fc + 1],
                                                              in1=rs[:], op0=ALU.mult, op1=ALU.mult), reads=[xa.b, rs.b, nw.b], writes=[ra.b])
            kb.dma("act", yout.ap.rearrange("(fc p) t -> p fc t", p=128)[:, :, t0:t0 + T], ra[:], yout.b, ra.b)
    if cfg.get("dbg"):
        kb.barrier()
        extra = ([("YM0", YM[0]), ("YM1", YM[1]), ("YR0", YR[0]), ("YR1", YR[1]), ("BN0", BN[0]), ("BN1", BN[1])] if NE else [])
        for nm, dr in [("MODR", MODR), ("PT", PT), ("HS", HS), ("REDL", REDL), ("REDC", REDC), ("XT", XT), ("XCi", XC)] + extra:
            od = Dr(nc, "dbg_" + nm, list(dr.t.shape), F32, kind="ExternalOutput")
            kb.dma("sp", od.ap, dr.ap, od.b, None)
        kb.barrier()
    deps = {yout.b.wr[0]: yout.b.wr[1]}
    kb._wait("sp", deps)
    kb._wait("act", deps)
    kb.barrier()
    ninst = kb.ninst
    kb.stack.close()
    return nc, ninst


def pack_inputs(inp, cfg):
    SEQ = cfg["SEQ"]
    LT = cfg["ltypes"]
    NL = len(LT)
    TR = SEQ // 4
    f32 = lambda a: np.ascontiguousarray(np.asarray(a, dtype=np.float32))
    x = f32(inp["x"]).reshape(2 * SEQ, D)
    xc = f32(f32(inp["ctx"]).reshape(2 * CTX, D).T)
    cc = np.stack([f32(inp["c"])[0], f32(inp["c"])[1], f32(inp["c_ctx"])], 0)
    ada_w = f32(inp["ada_w"])[:NL]
    ada_b = f32(inp["ada_b"])[:NL]
    adab = f32(ada_b.reshape(NL, 96, 128).transpose(2, 0, 1).reshape(128, NL * 96))
    nwv = np.zeros((128, (2 * NL + 1) * FC), np.float32)
    for l in range(NL):
        nwv[:, (l * 2) * FC:(l * 2 + 1) * FC] = f32(inp["norm1_w"])[l].reshape(FC, 128).T
        nwv[:, (l * 2 + 1) * FC:(l * 2 + 2) * FC] = f32(inp["norm2_w"])[l].reshape(FC, 128).T
    nwv[:, 2 * NL * FC:] = f32(inp["final_norm_w"]).reshape(FC, 128).T
    wgate, wup, wdown = f32(inp["ffn_w_gate"])[:NL], f32(inp["ffn_w_up"])[:NL], f32(inp["ffn_w_down"])[:NL]
    o_idx = [i for i, t in enumerate(LT) if t == "O"]
    NO = len(o_idx)
    NE = len(LT) - NO
    maps = []
    for c in range(NCORE):
        m = {}
        m["xs"] = f32(x[c * TR:(c + 1) * TR].T)
        m["xc"] = xc
        cm = np.zeros((128, 6), np.float32)
        for kc in range(2):
            cm[:, kc * 3:(kc + 1) * 3] = cc[:, c * 256 + kc * 128:c * 256 + (kc + 1) * 128].T
        m["cmine"] = cm
        m["adaw"] = f32(ada_w[:, c * 256:(c + 1) * 256, :].reshape(NL * 256, 6 * D))
        m["adab"] = adab
        m["nwv"] = nwv
        m["wg"] = f32(wgate[:, :, c * DFFC:(c + 1) * DFFC].reshape(NL * D, DFFC))
        m["wu"] = f32(wup[:, :, c * DFFC:(c + 1) * DFFC].reshape(NL * D, DFFC))
        m["wd"] = f32(wdown[:, c * DFFC:(c + 1) * DFFC, :].reshape(NL * DFFC, D))
        if NO:
            cw_in, cw_out = f32(inp["c_w_in"]), f32(inp["c_w_out"])
            m["wino"] = f32(np.concatenate([np.concatenate([cw_in[o][:, c * 256:(c + 1) * 256], cw_in[o][:, D + c * 256:D + (c + 1) * 256]], 1)
                                            for o in range(NO)], 0))
            m["wouto"] = f32(np.concatenate([cw_out[o][c * 256:(c + 1) * 256, :] for o in range(NO)], 0))
            gws = []
            ov = np.zeros((128, NO * 32), np.float32)
            for o in range(NO):
                for d in range(2):
                    gws.append(f32(inp["c_wa"])[o, d, c])
                    gws.append(f32(inp["c_wx"])[o, d, c])
                    for ci in range(2):
                        ch = slice(c * 256 + ci * 128, c * 256 + (ci + 1) * 128)
                        base = o * 32 + (d * 2 + ci) * 8
                        ov[:, base:base + 4] = f32(inp["c_conv_w"])[o, d][:, ch].T
                        ov[:, base + 4] = f32(inp["c_conv_b"])[o, d, ch]
                        ov[:, base + 5] = f32(inp["c_ba"])[o, d, ch]
                        ov[:, base + 6] = f32(inp["c_bx"])[o, d, ch]
                        ov[:, base + 7] = f32(inp["c_lambda"])[o, d, ch]
            m["gw"] = f32(np.concatenate(gws, 0))
            m["ovec"] = ov
        if NE:
            g, bm = c // 2, c % 2
            abw = f32(inp["ab_w_in"])
            cols = np.concatenate([np.arange(3088 + c * 128, 3088 + (c + 1) * 128), np.arange(4112 + c * 128, 4112 + (c + 1) * 128),
                                   np.arange(5136 + c * 128, 5136 + (c + 1) * 128), np.arange(6160, 6256), np.arange(6256, 6352), np.arange(6352, 6608),
                                   np.arange(g * 256, (g + 1) * 256), np.arange(1024 + g * 256, 1024 + (g + 1) * 256),
                                   np.arange(2048 + g * 128, 2048 + (g + 1) * 128), np.arange(2560 + g * 128, 2560 + (g + 1) * 128),
                                   np.arange(3072 + g * 4, 3072 + (g + 1) * 4)])
            m["wine"] = f32(np.concatenate([abw[e][:, cols] for e in range(NE)], 0))
            abo = f32(inp["ab_w_out"])
            m["woute"] = f32(np.concatenate([np.concatenate([abo[e][g * 256:(g + 1) * 256], abo[e][1024 + c * 128:1024 + (c + 1) * 128]], 0) for e in range(NE)], 0))
            evv = np.zeros((128, NE * 96), np.float32)
            lora = []
            dvec = np.zeros((64, NE * 8), np.float32)
            mch = [np.arange(g * 256, g * 256 + 128), np.arange(g * 256 + 128, (g + 1) * 256), np.arange(1024 + g * 128, 1024 + (g + 1) * 128),
                   np.arange(1536 + g * 128, 1536 + (g + 1) * 128)]
            for e in range(NE):
                o = e * 96
                for d in range(2):
                    for q in range(4):
                        evv[:, o + d * 20 + q * 5:o + d * 20 + q * 5 + 4] = f32(inp["m_conv_w"])[e, d][:, mch[q]].T
                        evv[:, o + d * 20 + q * 5 + 4] = f32(inp["m_conv_b"])[e, d, mch[q]]
                    mu = f32(inp["r_mu"])[e, d]
                    for hh in range(2):
                        ch = np.arange(c * 128 + hh * 64, c * 128 + (hh + 1) * 64)
                        base = o + 40 + (d * 2 + hh) * 8
                        evv[:64, base + 0] = mu[ch]
                        evv[:64, base + 1] = mu[1024 + ch]
                        evv[:64, base + 2] = mu[2048 + ch]
                        evv[:64, base + 3] = f32(inp["r_w0"])[e, d, ch]
                        evv[:64, base + 4] = f32(inp["r_a0"])[e, d, ch]
                        evv[:64, base + 5] = f32(inp["r_kk"])[e, d, ch]
                        evv[:64, base + 6] = f32(inp["r_ka"])[e, d, ch]
                        evv[:64, base + 7] = f32(inp["r_rk"])[e, d].reshape(-1)[ch]
                        lora.append(f32(inp["r_w2"])[e, d][:, ch])
                        lora.append(f32(inp["r_a2"])[e, d][:, ch])
                    evv[:96, o + 72 + d * 2] = mu[3072:3168]
                    evv[:96, o + 72 + d * 2 + 1] = mu[3168:3264]
                    evv[:4, o + 80 + d * 2] = f32(inp["m_dt_bias"])[e, d, g * 4:(g + 1) * 4]
                    evv[:4, o + 80 + d * 2 + 1] = f32(inp["m_a_log"])[e, d, g * 4:(g + 1) * 4]
                    for hd in range(4):
                        dvec[:, e * 8 + d * 4 + hd] = f32(inp["m_d"])[e, d, g * 4 + hd]
                evv[:, o + 76] = f32(inp["r_lnx_w"])[e, c * 128:(c + 1) * 128]
                evv[:, o + 77] = f32(inp["r_lnx_b"])[e, c * 128:(c + 1) * 128]
                evv[:, o + 78] = f32(inp["m_norm_w"])[e, g * 256:g * 256 + 128]
                evv[:, o + 79] = f32(inp["m_norm_w"])[e, g * 256 + 128:(g + 1) * 256]
            m["evec"] = evv
            m["lora"] = f32(np.concatenate(lora, 0))
            m["g2"] = f32(np.concatenate([f32(inp["r_g2"])[e][:, c * 128:(c + 1) * 128] for e in range(NE)], 0))
            m["dvec"] = dvec
            s4 = np.zeros((4, 512), np.float32)
            for hd in range(4):
                s4[hd, hd * 128:(hd + 1) * 128] = 1.0
            m["sel4"] = s4
        sel = np.zeros((128, 12), np.float32)
        sel[:, c] = 1.0
        sel[:, 8 + c // 4] = 1.0
        sel[:, 10 + c % 2] = 1.0
        m["sel"] = sel
        maps.append(m)
    return maps


_CACHE = {}


def run(inp, cfg, trace=False):
    key = (cfg["SEQ"], cfg["ltypes"], cfg["T"], cfg.get("stop", ""), cfg.get("dbg", False))
    if key not in _CACHE:
        _CACHE[key] = build(cfg)
    nc, ninst = _CACHE[key]
    maps = pack_inputs(inp, cfg)
    res = run_bass_kernel_spmd(nc, maps, core_ids=list(range(NCORE)), **({"trace": True} if trace else {}))
    SEQ = cfg["SEQ"]
    TR = SEQ // 4
    out = np.zeros((2 * SEQ, D), np.float32)
    for c in range(NCORE):
        out[c * TR:(c + 1) * TR] = np.asarray(res.results[c]["yout"]).T
    return out.reshape(2, SEQ, D), res


def kernel(**inputs):
    cfg = make_cfg()
    out, _ = run(inputs, cfg)
    return out
```
